# Optimizing a Trainium2 kernel written in Bass

```python
import math
import jax
import jax.numpy as jnp
from jax import lax
import numpy as np

D_MODEL = 1024
BATCH = 4
SEQ = 4096
DEPTH = 2

GRID_W = 64
CTX_LEN = 256
EPS = 1e-6
N_MOD = 6

MLA_HEADS = 4
Q_LORA = 256
KV_LORA = 128
QK_NOPE = 64
QK_ROPE = 32
V_HEAD = 64
MLA_WIDTH = MLA_HEADS * V_HEAD
MLA_SCALE = (QK_NOPE + QK_ROPE) ** -0.5
ROPE_THETA = 10000.0
Q_BLOCK = 128

SC_WIDTH = 256
SC_KERNEL = 3

SSD_HEADS = 8
SSD_HEAD_DIM = 64
SSD_WIDTH = SSD_HEADS * SSD_HEAD_DIM
SSD_GROUPS = 2
SSD_STATE = 64
SSD_CONV = 3
SSD_CHUNK = 128
SSD_GN = SSD_GROUPS * SSD_STATE
SSD_XBC = SSD_WIDTH + 2 * SSD_GN

D_MIX = MLA_WIDTH + SC_WIDTH + SSD_WIDTH
IN_MLA = Q_LORA + KV_LORA + QK_ROPE
IN_SC = 3 * SC_WIDTH
IN_SSD = SSD_WIDTH + SSD_XBC + 2 * SSD_HEADS
D_IN = IN_MLA + IN_SC + IN_SSD
D_FF = 4 * D_MODEL

kernel_name = 'hybrid_mla_shortconv_ssd_dit_block'


def _rms_norm(x, g):
    xf = x.astype(jnp.float32)
    y = xf * lax.rsqrt(jnp.mean(xf * xf, axis=-1, keepdims=True) + EPS)
    return (y * g.astype(jnp.float32)).astype(x.dtype)


def _modulate(h, shift, scale):
    return h * (1 + scale) + shift


def _dwconv(x, w):
    k = w.shape[0]
    return lax.conv_general_dilated(
        x, w[:, None, :].astype(x.dtype), window_strides=(1,),
        padding=[(k // 2, k // 2)], dimension_numbers=('NWC', 'WIO', 'NWC'),
        feature_group_count=x.shape[-1])


def _axial_rope_tables(rows):
    half = QK_ROPE // 2
    inv_freq = ROPE_THETA ** (-jnp.arange(0, half, 2, dtype=jnp.float32) / half)
    row = jnp.repeat(jnp.arange(rows, dtype=jnp.float32), GRID_W)
    col = jnp.tile(jnp.arange(GRID_W, dtype=jnp.float32), rows)
    ang_r = row[:, None] * inv_freq
    ang_c = col[:, None] * inv_freq
    ang = jnp.concatenate([ang_r, ang_r, ang_c, ang_c], axis=-1)
    return jnp.cos(ang), jnp.sin(ang)


def _apply_rope(x, cos, sin):
    xf = x.astype(jnp.float32)
    a, b, c, d = jnp.split(xf, 4, axis=-1)
    rot = jnp.concatenate([-b, a, -d, c], axis=-1)
    return (xf * cos + rot * sin).astype(x.dtype)


def _split_in(u):
    return u[..., :IN_MLA], u[..., IN_MLA:IN_MLA + IN_SC], u[..., IN_MLA + IN_SC:]


def _mla_kv(u, kv_g, w_ukv, rope):
    b, l, _ = u.shape
    ckv = _rms_norm(u[..., Q_LORA:Q_LORA + KV_LORA], kv_g)
    kv = (ckv @ w_ukv).reshape(b, l, MLA_HEADS, QK_NOPE + V_HEAD)
    k_nope, v = kv[..., :QK_NOPE], kv[..., QK_NOPE:]
    k_rope = u[..., Q_LORA + KV_LORA:]
    if rope is not None:
        k_rope = _apply_rope(k_rope, rope[0], rope[1])
    k_rope = jnp.broadcast_to(k_rope[:, :, None, :], (b, l, MLA_HEADS, QK_ROPE))
    return jnp.concatenate([k_nope, k_rope], axis=-1), v


def _mla_q(u, q_g, w_uq, rope):
    b, l, _ = u.shape
    cq = _rms_norm(u[..., :Q_LORA], q_g)
    q = (cq @ w_uq).reshape(b, l, MLA_HEADS, QK_NOPE + QK_ROPE)
    if rope is not None:
        q_rope = _apply_rope(q[..., QK_NOPE:], rope[0][:, None, :], rope[1][:, None, :])
        q = jnp.concatenate([q[..., :QK_NOPE], q_rope], axis=-1)
    return q


def _softmax_attend(q, k, v):
    s = jnp.einsum('bqhd,bkhd->bhqk', q, k).astype(jnp.float32) * MLA_SCALE
    p = jax.nn.softmax(s, axis=-1).astype(v.dtype)
    return jnp.einsum('bhqk,bkhd->bqhd', p, v)


def _blocked_attention(q, k, v):
    b, l, h, d = q.shape
    nb = l // Q_BLOCK
    qb = q.reshape(b, nb, Q_BLOCK, h, d).transpose(1, 0, 2, 3, 4)
    o = lax.map(lambda qi: _softmax_attend(qi, k, v), qb)
    return o.transpose(1, 0, 2, 3, 4).reshape(b, l, h * V_HEAD)


def _short_conv(u, w):
    gate_b, gate_c, val = jnp.split(u, 3, axis=-1)
    return gate_b * _dwconv(gate_c * val, w)


def _segsum_exp(cs):
    t = cs.shape[-1]
    diff = cs[..., :, None] - cs[..., None, :]
    mask = jnp.tril(jnp.ones((t, t), dtype=bool))
    return jnp.exp(jnp.where(mask, diff, -jnp.inf))


def _ssd_chunked(xs, dt, a, bm, cm, h0):
    b, l, h, p = xs.shape
    nc = l // SSD_CHUNK
    rep = h // SSD_GROUPS
    f32 = jnp.float32
    xdt = (xs.astype(f32) * dt[..., None]).reshape(b, nc, SSD_CHUNK, h, p)
    bc = jnp.repeat(bm.astype(f32), rep, axis=2).reshape(b, nc, SSD_CHUNK, h, SSD_STATE)
    cc = jnp.repeat(cm.astype(f32), rep, axis=2).reshape(b, nc, SSD_CHUNK, h, SSD_STATE)
    la = (dt * a).reshape(b, nc, SSD_CHUNK, h).transpose(0, 3, 1, 2)
    cs = jnp.cumsum(la, axis=-1)
    scores = jnp.einsum('bclhn,bcshn->bhcls', cc, bc) * _segsum_exp(cs)
    y_diag = jnp.einsum('bhcls,bcshp->bclhp', scores, xdt)
    decay_states = jnp.exp(cs[..., -1:] - cs).transpose(0, 2, 3, 1)
    states = jnp.einsum('bclhn,bclhp->bchpn', bc, xdt * decay_states[..., None])
    states = jnp.concatenate([h0[:, None], states], axis=1)
    chunk_tot = jnp.pad(cs[..., -1], ((0, 0), (0, 0), (1, 0)))
    decay_chunk = _segsum_exp(jnp.cumsum(chunk_tot, axis=-1))
    new_states = jnp.einsum('bhzc,bchpn->bzhpn', decay_chunk, states)
    state_decay = jnp.exp(cs).transpose(0, 2, 3, 1)[..., None]
    y_off = jnp.einsum('bclhn,bchpn->bclhp', cc, new_states[:, :-1]) * state_decay
    y = (y_diag + y_off).reshape(b, l, h, p)
    return y, new_states[:, -1]


def _ssd_prepare(u, conv_w, conv_b, dt_bias):
    b, l, _ = u.shape
    z = u[..., :SSD_WIDTH]
    xbc = jax.nn.silu(_dwconv(u[..., SSD_WIDTH:SSD_WIDTH + SSD_XBC], conv_w) + conv_b)
    xs = xbc[..., :SSD_WIDTH].reshape(b, l, SSD_HEADS, SSD_HEAD_DIM)
    bm = xbc[..., SSD_WIDTH:SSD_WIDTH + SSD_GN].reshape(b, l, SSD_GROUPS, SSD_STATE)
    cm = xbc[..., SSD_WIDTH + SSD_GN:].reshape(b, l, SSD_GROUPS, SSD_STATE)
    dt_raw = u[..., SSD_WIDTH + SSD_XBC:].reshape(b, l, 2, SSD_HEADS).astype(jnp.float32)
    dt = jax.nn.softplus(dt_raw + dt_bias.astype(jnp.float32))
    return z, xs, bm, cm, dt


def _bidir_ssd(xs, bm, cm, dt, a, h0):
    flip = lambda t: jnp.flip(t, axis=1)
    xs2 = jnp.stack([xs, flip(xs)])
    bm2 = jnp.stack([bm, flip(bm)])
    cm2 = jnp.stack([cm, flip(cm)])
    dt2 = jnp.stack([dt[:, :, 0], flip(dt[:, :, 1])])
    y2, h_final = jax.vmap(_ssd_chunked)(xs2, dt2, a, bm2, cm2, h0)
    return y2[0] + flip(y2[1]), h_final


def _ssd_output(y, xs, z, d_skip, norm_g):
    b, l = z.shape[0], z.shape[1]
    y = (y + d_skip.astype(jnp.float32)[:, None] * xs.astype(jnp.float32)).reshape(b, l, SSD_WIDTH)
    g = (y * jax.nn.silu(z.astype(jnp.float32))).reshape(b, l, SSD_GROUPS, SSD_WIDTH // SSD_GROUPS)
    g = g * lax.rsqrt(jnp.mean(g * g, axis=-1, keepdims=True) + EPS)
    return (g.reshape(b, l, SSD_WIDTH) * norm_g.astype(jnp.float32)).astype(z.dtype)


def _ffn_sublayer(x, shift, scale, gate, p):
    h = _modulate(_rms_norm(x, p['g_pre_ffn']), shift, scale)
    f = jnp.square(jax.nn.relu(h @ p['w_ff1'])) @ p['w_ff2']
    return x + gate * _rms_norm(f, p['g_post_ffn'])


def _layer(x, xc, mod, mod_c, rope, need_ctx, p):
    sh1, sc1, g1, sh2, sc2, g2 = jnp.split(mod[:, None, :], N_MOD, axis=-1)
    csh1, csc1, cg1, csh2, csc2, cg2 = jnp.split(mod_c, N_MOD, axis=-1)
    bsz, lc = xc.shape[0], xc.shape[1]

    h = _modulate(_rms_norm(x, p['g_pre_mix']), sh1, sc1)
    hc = _modulate(_rms_norm(xc, p['g_pre_mix']), csh1, csc1)
    ua, us, um = _split_in(h @ p['w_in'])
    uca, ucs, ucm = _split_in(hc @ p['w_in'])

    k, v = _mla_kv(ua, p['mla_kv_norm'], p['w_ukv'], rope)
    kc, vc = _mla_kv(uca, p['mla_kv_norm'], p['w_ukv'], None)
    q = _mla_q(ua, p['mla_q_norm'], p['w_uq'], rope)
    y_att = _blocked_attention(q, jnp.concatenate([kc, k], axis=1), jnp.concatenate([vc, v], axis=1))

    y_sc = _short_conv(us, p['sc_conv_w'])

    a = -jnp.exp(p['ssd_a_log'].astype(jnp.float32))
    zc, xsc, bmc, cmc, dtc = _ssd_prepare(ucm, p['ssd_conv_w'], p['ssd_conv_b'], p['ssd_dt_bias'])
    z, xs, bm, cm, dt = _ssd_prepare(um, p['ssd_conv_w'], p['ssd_conv_b'], p['ssd_dt_bias'])
    h0 = jnp.zeros((2, bsz, SSD_HEADS, SSD_HEAD_DIM, SSD_STATE), jnp.float32)
    yc_scan, h_ctx = _bidir_ssd(xsc, bmc, cmc, dtc, a, h0)
    y_scan, _ = _bidir_ssd(xs, bm, cm, dt, a, h_ctx)
    y_ssd = _ssd_output(y_scan, xs, z, p['ssd_d'], p['ssd_norm'])

    y = jnp.concatenate([y_att, y_sc, y_ssd], axis=-1) @ p['w_out']
    x = x + g1 * _rms_norm(y, p['g_post_mix'])
    x = _ffn_sublayer(x, sh2, sc2, g2, p)

    if need_ctx:
        qc = _mla_q(uca, p['mla_q_norm'], p['w_uq'], None)
        yc_att = _softmax_attend(qc, kc, vc).reshape(bsz, lc, MLA_WIDTH)
        yc_sc = _short_conv(ucs, p['sc_conv_w'])
        yc_ssd = _ssd_output(yc_scan, xsc, zc, p['ssd_d'], p['ssd_norm'])
        yc = jnp.concatenate([yc_att, yc_sc, yc_ssd], axis=-1) @ p['w_out']
        xc = xc + cg1 * _rms_norm(yc, p['g_post_mix'])
        xc = _ffn_sublayer(xc, csh2, csc2, cg2, p)
    return x, xc


def setup_inputs(seed: int = 0) -> dict:
    key = jax.random.key(seed)
    k = jax.random.split(key, 25)
    f32 = jnp.float32
    L = DEPTH

    def nrm(i, shape, scale):
        return jax.random.normal(k[i], shape, f32) * scale

    def gain(i, shape):
        return 1.0 + 0.1 * jax.random.normal(k[i], shape, f32)

    dt0 = jnp.exp(jax.random.uniform(k[16], (L, 2, SSD_HEADS), f32, math.log(1e-3), math.log(1e-1)))
    return {
        'x': nrm(0, (BATCH, SEQ, D_MODEL), 1.0),
        'c': nrm(1, (BATCH, D_MODEL), 1.0),
        'ctx': nrm(2, (BATCH, CTX_LEN, D_MODEL), 1.0),
        'c_ctx': nrm(3, (D_MODEL,), 1.0),
        'w_mod': nrm(4, (L, D_MODEL, N_MOD * D_MODEL), 0.5 * D_MODEL ** -0.5),
        'b_mod': nrm(5, (L, N_MOD * D_MODEL), 0.02),
        'g_pre_mix': gain(6, (L, D_MODEL)),
        'w_in': nrm(7, (L, D_MODEL, D_IN), D_MODEL ** -0.5),
        'mla_q_norm': gain(8, (L, Q_LORA)),
        'w_uq': nrm(9, (L, Q_LORA, MLA_HEADS * (QK_NOPE + QK_ROPE)), Q_LORA ** -0.5),
        'mla_kv_norm': gain(10, (L, KV_LORA)),
        'w_ukv': nrm(11, (L, KV_LORA, MLA_HEADS * (QK_NOPE + V_HEAD)), KV_LORA ** -0.5),
        'sc_conv_w': nrm(12, (L, SC_KERNEL, SC_WIDTH), SC_KERNEL ** -0.5),
        'ssd_conv_w': nrm(13, (L, SSD_CONV, SSD_XBC), SSD_CONV ** -0.5),
        'ssd_conv_b': nrm(14, (L, SSD_XBC), 0.02),
        'ssd_a_log': jnp.log(jax.random.uniform(k[15], (L, 2, SSD_HEADS), f32, 1.0, 16.0)),
        'ssd_dt_bias': dt0 + jnp.log(-jnp.expm1(-dt0)),
        'ssd_d': gain(17, (L, SSD_HEADS)),
        'ssd_norm': gain(18, (L, SSD_WIDTH)),
        'w_out': nrm(19, (L, D_MIX, D_MODEL), D_MIX ** -0.5),
        'g_post_mix': gain(20, (L, D_MODEL)),
        'g_pre_ffn': gain(21, (L, D_MODEL)),
        'w_ff1': nrm(22, (L, D_MODEL, D_FF), D_MODEL ** -0.5),
        'w_ff2': nrm(23, (L, D_FF, D_MODEL), D_FF ** -0.5),
        'g_post_ffn': gain(24, (L, D_MODEL)),
    }


def reference(x, c, ctx, c_ctx, w_mod, b_mod, g_pre_mix, w_in, mla_q_norm, w_uq, mla_kv_norm, w_ukv,
              sc_conv_w, ssd_conv_w, ssd_conv_b, ssd_a_log, ssd_dt_bias, ssd_d, ssd_norm, w_out,
              g_post_mix, g_pre_ffn, w_ff1, w_ff2, g_post_ffn):
    rows = x.shape[1] // GRID_W
    rope = _axial_rope_tables(rows)
    xc = ctx
    s_lat = jax.nn.silu(c)
    s_ctx = jax.nn.silu(c_ctx)
    for i in range(DEPTH):
        p = {
            'g_pre_mix': g_pre_mix[i], 'w_in': w_in[i],
            'mla_q_norm': mla_q_norm[i], 'w_uq': w_uq[i],
            'mla_kv_norm': mla_kv_norm[i], 'w_ukv': w_ukv[i],
            'sc_conv_w': sc_conv_w[i],
            'ssd_conv_w': ssd_conv_w[i], 'ssd_conv_b': ssd_conv_b[i],
            'ssd_a_log': ssd_a_log[i], 'ssd_dt_bias': ssd_dt_bias[i],
            'ssd_d': ssd_d[i], 'ssd_norm': ssd_norm[i],
            'w_out': w_out[i], 'g_post_mix': g_post_mix[i],
            'g_pre_ffn': g_pre_ffn[i], 'w_ff1': w_ff1[i], 'w_ff2': w_ff2[i],
            'g_post_ffn': g_post_ffn[i],
        }
        mod = s_lat @ w_mod[i] + b_mod[i]
        mod_c = s_ctx @ w_mod[i] + b_mod[i]
        x, xc = _layer(x, xc, mod, mod_c, rope, i < DEPTH - 1, p)
    return x
```

```python
import numpy as np
import ml_dtypes
from contextlib import ExitStack
import concourse.bass as bass
import concourse.mybir as mybir
from concourse.bass_utils import run_bass_kernel_spmd

F32 = mybir.dt.float32
BF16 = mybir.dt.bfloat16
AF = mybir.ActivationFunctionType
ALU = mybir.AluOpType

NCORES = 8
D = 1024
DEPTH = 2
TC = 256
TL = 2048
T = TC + TL
NT = T // 128
BLOCKS = [(0, 2), (2, 4), (6, 4), (10, 4), (14, 4)]
NU = 2512
EPS = 1e-6
MLA_SCALE = 96.0 ** -0.5
NEG = -30000.0
FM = [(0, 128), (128, 128), (256, 128), (384, 32), (416, 32)] + [(448 + 128 * i, 128) for i in range(12)]
Z_OFF = 1984
DT_OFF = 2496
SM_W = 544

ENGS = ("pe", "act", "dve", "pool", "sp")


class Res:
    __slots__ = ("name", "w", "r", "sem", "ndma")

    def __init__(self, name):
        self.name = name
        self.w = None
        self.r = []
        self.sem = None
        self.ndma = 0


class Prog:
    def __init__(self, nc, strict=True):
        self.nc = nc
        self.ops = {e: [] for e in ENGS}
        self.strict = strict
        self.nres = 0
        self.dma_res = []
        self.slot_count = []
        self.slot_cls = []
        self.free_slots = {}

    def res(self, name=None):
        self.nres += 1
        return Res(name or f"r{self.nres}")

    def _deps(self, reads, writes, eng=None):
        ev = []
        for r in reads:
            if r.w is not None:
                ev.append(r.w)
        for w in writes:
            for e in ([w.w] if w.w is not None else []) + w.r:
                if eng is not None and e[0] == "E" and e[1] == eng:
                    continue
                ev.append(e)
        return ev

    def op(self, eng, fn, reads=(), writes=()):
        lst = self.ops[eng]
        idx = len(lst)
        waits = self._collect(eng, self._deps(reads, writes, eng))
        lst.append({"fn": fn, "waits": waits, "signal": False, "dma": None})
        me = ("E", eng, idx)
        for r in reads:
            r.r.append(me)
        for w in writes:
            w.w = me
            w.r = []

    def dma(self, eng, fn, src, dst, inc=16):
        srcs = src if isinstance(src, (list, tuple)) else [src]
        dsts = dst if isinstance(dst, (list, tuple)) else [dst]
        main = dsts[0]
        cls = eng if inc == 16 else "cc"
        if main.sem is None:
            fl = self.free_slots.setdefault(cls, [])
            if fl and cls != "cc":
                main.sem = fl.pop()
            else:
                main.sem = len(self.slot_count)
                self.slot_count.append(0)
                self.slot_cls.append(cls)
            main.ndma = self.slot_count[main.sem]
            self.dma_res.append(main)
        assert self.slot_cls[main.sem] == cls, (main.name, self.slot_cls[main.sem], cls)
        waits = self._collect(eng, self._deps(srcs, dsts))
        main.ndma += inc
        self.slot_count[main.sem] = main.ndma
        self.ops[eng].append({"fn": fn, "waits": waits, "signal": False, "dma": main.sem, "inc": inc})
        me = ("D", main.sem, main.ndma)
        for r in srcs:
            r.r.append(me)
        for w in dsts:
            w.w = me
            w.r = []

    def release(self, resources):
        for r in resources:
            if r.sem is not None:
                self.free_slots.setdefault(self.slot_cls[r.sem], []).append(r.sem)
                self.dma_res.remove(r)
                r.sem = None

    def _collect(self, eng, events, all_same=False):
        out = {}
        for ev in events:
            if ev[0] == "E":
                _, e2, j = ev
                if e2 == eng and (eng == "pe" or not self.strict) and not all_same:
                    continue
                key = ("E", e2)
            else:
                key = ("D", ev[1])
            if key not in out or out[key][2] < ev[2]:
                out[key] = ev
        evs = list(out.values())
        for ev in evs:
            if ev[0] == "E":
                self.ops[ev[1]][ev[2]]["signal"] = True
        return evs

    def final_wait(self, eng, resources):
        evs = []
        for r in resources:
            if r.w is not None:
                evs.append(r.w)
            evs.extend(r.r)
        self.ops[eng].append({"fn": None, "waits": self._collect(eng, evs), "signal": False, "dma": None})

    def barrier(self):
        evs = []
        for e in ENGS:
            for idx in range(len(self.ops[e]) - 1, -1, -1):
                rec = self.ops[e][idx]
                if rec["dma"] is None and rec["fn"] is not None:
                    evs.append(("E", e, idx))
                    break
        for r in self.dma_res:
            evs.append(("D", r.sem, r.ndma))
        for e in ENGS:
            self.ops[e].append({"fn": None, "waits": self._collect(e, evs, all_same=(e != "pe")),
                                "signal": False, "dma": None})

    def emit(self):
        nc = self.nc
        esem = {e: nc.alloc_semaphore(f"s_{e}") for e in ENGS}
        dsem = [nc.alloc_semaphore(f"d{i}") for i in range(len(self.slot_count))]
        cnt = {}
        for e in ENGS:
            c = 0
            arr = []
            for rec in self.ops[e]:
                if rec["signal"] and rec["dma"] is None:
                    c += 1
                arr.append(c)
            cnt[e] = arr
        ops = self.ops

        def run(e, eh):
            waited = {}
            for rec in ops[e]:
                for ev in rec["waits"]:
                    if ev[0] == "E":
                        sem = esem[ev[1]]
                        val = cnt[ev[1]][ev[2]]
                    else:
                        sem = dsem[ev[1]]
                        val = ev[2]
                    if waited.get(sem.num, 0) >= val:
                        continue
                    waited[sem.num] = val
                    eh.wait_ge(sem, val)
                if rec["fn"] is None:
                    continue
                ins = rec["fn"](eh)
                if rec["dma"] is not None:
                    ins.then_inc(dsem[rec["dma"]], rec["inc"])
                elif rec["signal"]:
                    ins.then_inc(esem[e], 1)

        with nc.Block() as block:
            @block.tensor
            def _(eh):
                run("pe", eh)

            @block.scalar
            def _(eh):
                run("act", eh)

            @block.vector
            def _(eh):
                run("dve", eh)

            @block.gpsimd
            def _(eh):
                run("pool", eh)

            @block.sync
            def _(eh):
                run("sp", eh)


class Buf:
    def __init__(self, t, r):
        self.t = t
        self.r = r

    def __getitem__(self, k):
        return self.t[k]


def bc(ap, shape):
    return ap.unsqueeze(len(ap.shape)).to_broadcast(list(shape))


def bcm(ap, shape):
    return ap.unsqueeze(1).to_broadcast(list(shape))


class Builder:
    def __init__(self, dbg=None, nlayers=DEPTH, stop_after=None, fake_cc=False, ncores=NCORES):
        self.fake_cc = fake_cc
        self.ncores = ncores
        self.nc = bass.Bass("TRN2", target_bir_lowering=False)
        self.P = Prog(self.nc)
        self.dbg = dbg or []
        self.dbg_out = {}
        self.nlayers = nlayers
        self.stop_after = stop_after
        self.scope = None
        self.scope_ab = None
        self.scope_res = []
        self.ab_res = []
        self.pcur = 0
        self.pinned = set()
        self.uid = 0

    def din(self, name, shape, dt=F32):
        return Buf(self.nc.dram_tensor(name, list(shape), dt, kind="ExternalInput").ap(), self.P.res(name))

    def dout(self, name, shape, dt=F32):
        return Buf(self.nc.dram_tensor(name, list(shape), dt, kind="ExternalOutput").ap(), self.P.res(name))

    def dscr(self, name, shape, dt=F32):
        return Buf(self.nc.dram_tensor(name, list(shape), dt, kind="Internal").ap(), self.P.res(name))

    def sb(self, name, shape, dt=F32, persist=False, ab=False):
        self.uid += 1
        nm = f"{name}_{self.uid}"
        r = self.P.res(name)
        if persist:
            t = self.nc.alloc_sbuf_tensor(nm, list(shape), dt)
        elif ab:
            t = self.scope_ab.enter_context(self.nc.sbuf_tensor(nm, list(shape), dt))
            self.ab_res.append(r)
        else:
            t = self.scope.enter_context(self.nc.sbuf_tensor(nm, list(shape), dt))
            self.scope_res.append(r)
        return Buf(t, r)

    def rot(self, name, n, shape, dt=F32):
        return Rot([self.sb(f"{name}{i}", shape, dt) for i in range(n)])

    def pbank(self, n=1, pin=False):
        while True:
            if self.pcur % n:
                self.pcur += n - self.pcur % n
            if self.pcur + n > 8:
                self.pcur = 0
            b0 = self.pcur
            self.pcur = (self.pcur + n) % 8
            if not any((b0 + i) in self.pinned for i in range(n)):
                break
        if pin:
            for i in range(n):
                self.pinned.add(b0 + i)
        self.last_b0 = b0
        return self.ps_all[:, b0 * 512:(b0 + n) * 512], [self.ps_res[b0 + i] for i in range(n)]

    def unpin(self, res_list):
        for r in res_list:
            self.pinned.discard(self.ps_res.index(r))

    def mm(self, out, lhsT, rhs, start, stop, R, W):
        self.P.op("pe", lambda e: e.matmul(out, lhsT, rhs, start=start, stop=stop), R, W)

    def tr(self, out, in_, ident, R, W):
        self.P.op("pe", lambda e: e.transpose(out, in_, ident), R, W)

    def act(self, out, in_, func, R, W, bias=None, scale=None, accum=None, eng="act"):
        kw = {}
        if bias is not None:
            kw["bias"] = bias
        if scale is not None:
            kw["scale"] = scale
        if accum is not None:
            kw["accum_out"] = accum
        self.P.op("act", lambda e: e.activation(out, in_, func, **kw), R, W)

    def tt(self, eng, out, a, b, op, R, W):
        self.P.op(eng, lambda e: e.tensor_tensor(out, a, b, op), R, W)

    def ts(self, eng, out, a, s1, s2, op0, op1, R, W):
        if op1 is None:
            self.P.op(eng, lambda e: e.tensor_scalar(out, a, s1, s2, op0), R, W)
        else:
            self.P.op(eng, lambda e: e.tensor_scalar(out, a, s1, s2, op0, op1), R, W)

    def stt(self, eng, out, a, s, b, op0, op1, R, W):
        self.P.op(eng, lambda e: e.scalar_tensor_tensor(out, a, s, b, op0, op1), R, W)

    def cp(self, eng, out, in_, R, W):
        if eng == "act":
            self.P.op("act", lambda e: e.copy(out, in_), R, W)
        else:
            self.P.op(eng, lambda e: e.tensor_copy(out, in_), R, W)

    def ms(self, eng, ap, val, W):
        self.P.op(eng, lambda e: e.memset(ap, val), [], W)

    def recip(self, out, in_, R, W):
        self.P.op("dve", lambda e: e.reciprocal(out, in_), R, W)

    def dma(self, eng, out, in_, R, W, slow_ok=False):
        if slow_ok:
            self.P.dma(eng, lambda e: e.dma_start(out=out, in_=in_, allow_slow_non_contiguous=True), R, W)
        else:
            self.P.dma(eng, lambda e: e.dma_start(out=out, in_=in_), R, W)

    def rstd(self, out, ms_ap, R, W, tmp):
        self.act(tmp[:], ms_ap, AF.Ln, R, [tmp.r], bias=self.eps_t[:, 0:1])
        self.act(out, tmp[:], AF.Exp, [tmp.r], W, scale=-0.5)

    def silu(self, out, out_res, in_, in_res, sg, bias=None, negbias=None, mul_eng="dve"):
        kw = {} if negbias is None else {"bias": negbias[0]}
        rr = list(in_res) + ([] if negbias is None else [negbias[1]])
        self.act(sg[0], in_, AF.Exp, rr, [sg[1]], scale=-1.0, **kw)
        self.act(sg[0], sg[0], AF.Ln, [sg[1]], [sg[1]], bias=1.0)
        self.act(sg[0], sg[0], AF.Exp, [sg[1]], [sg[1]], scale=-1.0)
        if bias is not None:
            self.stt(mul_eng, out, in_, bias[0], sg[0], ALU.add, ALU.mult, list(in_res) + [bias[1], sg[1]], out_res)
        else:
            self.tt(mul_eng, out, in_, sg[0], ALU.mult, list(in_res) + [sg[1]], out_res)

    def dump(self, key, src_ap, src_res, shape, dt=F32):
        if key not in self.dbg or key in self.dbg_out:
            return
        o = self.dout("dbg_" + key, shape, dt)
        self.dma("pool", o.t, src_ap, [src_res] if not isinstance(src_res, list) else src_res, [o.r])
        self.dbg_out[key] = o

    def build(self):
        nc, P = self.nc, self.P
        L = DEPTH
        self.x_in = self.din("x_in", [T, D])
        self.cvecT = self.din("cvecT", [D, 2])
        self.w_mod = self.din("w_mod", [L, D, 6 * D])
        self.b_mod = self.din("b_mod", [L, 6 * D])
        self.gvecs = self.din("gvecs", [L, 4, D])
        self.w_in = self.din("w_in_aug", [L, D, NU])
        self.q_norm = self.din("q_norm", [L, 256])
        self.w_uq = self.din("w_uq_aug", [L, 256, 640])
        self.kv_norm = self.din("kv_norm", [L, 128])
        self.w_ukv_k = self.din("w_ukv_k", [L, 128, 512])
        self.w_ukv_v = self.din("w_ukv_v", [L, 128, 256])
        self.sc_cw = self.din("sc_cw", [L, 256, 3])
        self.ssd_cw = self.din("ssd_cw", [L, 768, 3])
        self.ssd_cb = self.din("ssd_cb", [L, 768])
        self.a_log = self.din("a_log", [L, 16])
        self.dt_bias = self.din("dt_bias", [L, 16])
        self.ssd_d = self.din("ssd_d", [L, 8])
        self.ssd_norm = self.din("ssd_norm", [L, 512])
        self.w_out = self.din("w_out", [L, D, D])
        self.w_ff1 = self.din("w_ff1", [L, D, 4 * D])
        self.w_ff2 = self.din("w_ff2", [L, 4 * D, D])
        self.ropeT = self.din("ropeT", [2, 32, T])
        self.consts = self.din("consts", [128, 3 * 128 + 4])
        self.y_out = self.dout("y_out", [TL, D])
        self.xres1 = self.dscr("xres1", [T, D])
        self.x1s = self.dscr("x1s", [T, D])
        self.sc_z = self.dscr("sc_z", [T, 512])
        self.sc_xs = self.dscr("sc_xs", [T, 512])
        self.sc_bt = self.dscr("sc_bt", [T, 128], BF16)
        self.bv = self.dscr("bv", [12, 128, D])
        self.kv_ctx_k = self.dscr("kv_ctx_k", [128, 4, TC], BF16)
        self.kv_ctx_v = self.dscr("kv_ctx_v", [128, 2, 260], BF16)
        self.xk = self.dscr("xk", [512, TL], BF16)
        self.xk_g = self.dscr("xk_g", [1024, TL], BF16)
        self.xv = self.dscr("xv", [2048, 260], BF16)
        self.xv_g = self.dscr("xv_g", [4096, 260], BF16)
        self.xs_ = self.dscr("xsm", [128, SM_W])
        self.xs_g = self.dscr("xsm_g", [256, SM_W])

        self.ps_all = nc.alloc_psum_tensor("ps_all", [128, 8 * 512], F32)
        self.ps_res = [P.res(f"bank{i}") for i in range(8)]

        cst = self.sb("cst", [128, 3 * 128 + 4], persist=True)
        self.dma("sp", cst[:], self.consts.t, [self.consts.r], [cst.r])
        self.cst = cst
        self.ident_f = Buf(cst.t[:, 0:128], cst.r)
        self.triE = Buf(cst.t[:, 128:256], cst.r)
        self.triL = Buf(cst.t[:, 256:384], cst.r)
        self.lastmask = Buf(cst.t[:, 384:385], cst.r)
        self.esel = Buf(cst.t[:, 385:387], cst.r)
        self.ident_b = self.sb("ident_b", [128, 128], BF16, persist=True)
        self.cp("dve", self.ident_b[:], self.ident_f[:], [cst.r], [self.ident_b.r])
        self.ones_f = self.sb("ones_f", [128, 128], persist=True)
        self.ms("pool", self.ones_f[:], 1.0, [self.ones_f.r])
        self.eps_t = self.sb("eps_t", [128, 1], persist=True)
        self.ms("pool", self.eps_t[:], EPS, [self.eps_t.r])
        self.negm = self.sb("negm", [128, 2, 512], BF16, persist=True)
        for d_, tri in enumerate((self.triE, self.triL)):
            self.ts("dve", self.negm[:, d_, :].rearrange("p (a b) -> p a b", a=4), bcm(tri[:], [128, 4, 128]),
                    -1.0, -NEG, ALU.add, ALU.mult, [cst.r], [self.negm.r])
        self.sT = self.sb("sT", [128, 8, 2], persist=True)
        cv = self.sb("cv", [128, 8, 2], persist=True)
        self.dma("sp", cv[:], self.cvecT.t.rearrange("(k p) r -> p k r", p=128), [self.cvecT.r], [cv.r])
        self.act(self.sT[:], cv[:], AF.Silu, [cv.r], [self.sT.r])
        self.dt_all = self.sb("dt_all", [128, NT, 16], persist=True)

        out_res = []
        xres = self.x_in
        for l in range(self.nlayers):
            last = (l == DEPTH - 1)
            xnext = None if last else self.xres1
            self.layer(l, xres, xnext, last)
            xres = xnext
            if self.stop_after is not None and self.stop_after[0] == l:
                break
        P.final_wait("sp", [self.y_out.r] + [o.r for o in self.dbg_out.values()])
        P.final_wait("pool", [self.y_out.r] + [o.r for o in self.dbg_out.values()])
        P.emit()
        return nc

    def end_scope(self):
        if self.scope is not None:
            self.P.barrier()
            self.scope.close()
            self.scope = None
            self.P.release(self.scope_res)
            self.scope_res = []

    def new_scope(self):
        self.end_scope()
        self.scope = ExitStack()

    def open_ab(self):
        self.end_scope()
        self.scope_ab = ExitStack()

    def close_ab(self):
        self.end_scope()
        if self.scope_ab is not None:
            self.scope_ab.close()
            self.scope_ab = None
            self.P.release(self.ab_res)
            self.ab_res = []

    def stop(self, l, tag):
        return self.stop_after is not None and self.stop_after == (l, tag)

    def layer(self, l, xres, xnext, last):
        self.phase_mod(l)
        if self.stop(l, "mod"):
            return
        self.phase_a(l, xres, last)
        if self.stop(l, "a"):
            return
        self.exchange(l)
        self.phase_b(l, xres, last)
        if self.stop_after is not None and self.stop_after[0] == l and self.stop_after[1] in ("b", "b0", "b1", "b2", "t1", "t2", "t3", "t4", "u1", "u2", "u3", "v1", "v2"):
            return
        self.phase_c(l, xnext, last)

    def phase_mod(self, l):
        self.new_scope()
        modsb = self.sb("modsb", [2, 6 * D])
        bm = self.sb("bm", [2, 6 * D])
        self.dma("sp", bm[:], self.b_mod.t[l].partition_broadcast(2), [self.b_mod.r], [bm.r])
        gv = self.sb("gv", [2, 4 * D])
        self.dma("sp", gv[:], self.gvecs.t[l].rearrange("a d -> (a d)").partition_broadcast(2),
                 [self.gvecs.r], [gv.r])
        slabs = self.rot("wm", 2, [128, 8, 512])
        wmv = self.w_mod.t[l].rearrange("(k p) n -> p k n", p=128)
        for n in range(12):
            sl = slabs.next()
            self.dma("sp", sl[:], wmv[:, :, n * 512:(n + 1) * 512], [self.w_mod.r], [sl.r])
            pb, pr = self.pbank()
            for k in range(8):
                self.mm(pb[0:2, :], self.sT[:, k, :], sl[:, k, :], k == 0, k == 7, [self.sT.r, sl.r], pr)
            self.tt("dve", modsb[:, n * 512:(n + 1) * 512], pb[0:2, :], bm[:, n * 512:(n + 1) * 512], ALU.add,
                    pr + [bm.r], [modsb.r])
        cmb = self.sb("cmb", [2, 6, D])
        m = lambda j: modsb[:, j * D:(j + 1) * D]
        g = lambda j: gv[:, j * D:(j + 1) * D]
        R = [modsb.r, gv.r]
        self.stt("dve", cmb[:, 0, :], m(1), 1.0, g(0), ALU.add, ALU.mult, R, [cmb.r])
        self.cp("dve", cmb[:, 1, :], m(0), R, [cmb.r])
        self.tt("dve", cmb[:, 2, :], m(2), g(1), ALU.mult, R, [cmb.r])
        self.stt("dve", cmb[:, 3, :], m(4), 1.0, g(2), ALU.add, ALU.mult, R, [cmb.r])
        self.cp("dve", cmb[:, 4, :], m(3), R, [cmb.r])
        self.tt("dve", cmb[:, 5, :], m(5), g(3), ALU.mult, R, [cmb.r])
        sel = self.sb("sel", [2, 2, 128])
        self.cp("dve", sel[:, 0, :], self.ident_f.t[0:2, 0:1].to_broadcast([2, 128]), [self.cst.r], [sel.r])
        self.cp("dve", sel[:, 1, :], self.ident_f.t[0:2, 1:2].to_broadcast([2, 128]), [self.cst.r], [sel.r])
        stg = self.rot("bvst", 2, [128, D])
        for s in range(2):
            for j in range(6):
                st = stg.next()
                for hlf in range(2):
                    pb, pr = self.pbank()
                    self.mm(pb[:, :], sel[:, s, :], cmb[:, j, hlf * 512:(hlf + 1) * 512], True, True,
                            [sel.r, cmb.r], pr)
                    self.cp("act", st[:, hlf * 512:(hlf + 1) * 512], pb[:, :], pr, [st.r])
                self.dma("sp", self.bv.t[6 * s + j], st[:], [st.r], [self.bv.r])
        self.dump(f"bv{l}", self.bv.t, self.bv.r, [12, 128, D])

    def esel_rows(self):
        return self.ident_f.t[0:2, 0:2]

    def load_bv(self, idx, name):
        b = self.sb(name, [128, D])
        self.dma("sp", b[:], self.bv.t[idx], [self.bv.r], [b.r])
        return b

    def norm_tile(self, xt, A, Bv, hb, tmp, small):
        ms_, ln_, rs_ = small
        self.act(hb[:], xt[:], AF.Square, [xt.r], [hb.r, ms_.r], scale=1.0 / 32.0, accum=ms_[:])
        self.rstd(rs_[:], ms_[:], [ms_.r], [rs_.r], ln_)
        self.stt("dve", tmp[:], xt[:], rs_[:, 0:1], A[:], ALU.mult, ALU.mult, [xt.r, rs_.r, A.r], [tmp.r])
        self.tt("dve", hb[:], tmp[:], Bv[:], ALU.add, [tmp.r, Bv.r], [hb.r])

    def post_tile(self, pw, prw, xt, G, tmp, small, xo):
        ms_, ln_, rs_ = small
        self.act(tmp[:], pw[:, :], AF.Square, prw, [tmp.r, ms_.r], scale=1.0 / 32.0, accum=ms_[:])
        self.rstd(rs_[:], ms_[:], [ms_.r], [rs_.r], ln_)
        self.stt("dve", tmp[:], pw[:, :], rs_[:, 0:1], G[:], ALU.mult, ALU.mult, prw + [rs_.r, G.r], [tmp.r])
        self.tt("dve", xo[:], xt[:], tmp[:], ALU.add, [xt.r, tmp.r], [xo.r])

    def transpose_tile(self, hb, hT, tl):
        pb, pr = self.pbank()
        pbb = pb.bitcast(BF16)
        for k in range(8):
            self.tr(pbb[:, k * 128:(k + 1) * 128], hb[:, k * 128:(k + 1) * 128], self.ident_b[:],
                    [hb.r, self.ident_b.r], pr)
        self.cp("act", hT[:, :, tl * 128:(tl + 1) * 128], pbb[:, :].rearrange("p (k t) -> p k t", k=8), pr, [hT.r])

    def small3(self, n):
        return Rot([(self.sb("ms", [128, 1]), self.sb("ln", [128, 1]), self.sb("rs", [128, 1])) for _ in range(n)])

    def phase_a(self, l, xres, last):
        nc, P = self.nc, self.P
        self.open_ab()
        self.qT = self.sb("qT", [128, 4, T], BF16, ab=True)
        self.ysc = self.sb("ysc", [128, 2, T], BF16, ab=True)
        self.BT = self.sb("BT", [128, T], BF16, ab=True)
        self.CT = self.sb("CT", [128, T], BF16, ab=True)
        self.HEs = self.sb("HEs", [128, NT, 256], BF16, ab=True)
        self.fix = self.sb("fix", [128, 16], ab=True)
        self.xsT_last = self.sb("xsT_last", [128, 4, 128], ab=True)
        self.new_scope()
        NB = 256
        win = self.sb("win", [128, 8, NU], BF16)
        for k in range(8):
            self.dma("pool", win[:, k, :], self.w_in.t[l, k * 128:(k + 1) * 128, :], [self.w_in.r], [win.r])
        wst = self.sb("wst", [128, 2, 640])
        self.dma("sp", wst[:], self.w_uq.t[l].rearrange("(c p) n -> p c n", p=128), [self.w_uq.r], [wst.r])
        gq = self.sb("gq", [128, 2])
        self.dma("sp", gq[:], self.q_norm.t[l].rearrange("(c p) -> p c", p=128), [self.q_norm.r], [gq.r], slow_ok=True)
        wuq = self.sb("wuq", [128, 2, 640], BF16)
        for c in range(2):
            self.ts("dve", wuq[:, c, :], wst[:, c, :], gq[:, c:c + 1], None, ALU.mult, None, [wst.r, gq.r], [wuq.r])
        wst2 = self.sb("wst2", [128, 768])
        self.dma("sp", wst2[:, 0:512], self.w_ukv_k.t[l], [self.w_ukv_k.r], [wst2.r])
        self.dma("sp", wst2[:, 512:768], self.w_ukv_v.t[l], [self.w_ukv_v.r], [wst2.r])
        gkv = self.sb("gkv", [128, 1])
        self.dma("sp", gkv[:], self.kv_norm.t[l].rearrange("(p o) -> p o", o=1), [self.kv_norm.r], [gkv.r])
        wkv = self.sb("wkv", [128, 768], BF16)
        self.ts("dve", wkv[:], wst2[:], gkv[:, 0:1], None, ALU.mult, None, [wst2.r, gkv.r], [wkv.r])
        sccw, cw, cb = self.conv_consts(l)
        st = self.ssd_consts(l)
        A1 = self.load_bv(6, "A1")
        B1 = self.load_bv(7, "B1")
        HE = self.sb("HE", [128, 256])
        self.ms("dve", HE[:], 0.0, [HE.r])

        xts = self.rot("xt", 2, [128, D])
        tmp = self.sb("tmp", [128, D])
        hb = self.sb("hb", [128, D], BF16)
        hT = self.sb("hT", [128, 8, NB], BF16)
        smalls = self.small3(2)
        zs = self.rot("zs", 2, [128, 512])
        dtt = self.sb("dtt", [128, 16])
        rqb = self.sb("rqb", [128, NB])
        rkvb = self.sb("rkvb", [128, NB])
        lnb = self.sb("lnb", [128, NB])
        cosr = self.sb("cosr", [32, 2, NB])
        t12 = self.sb("t12", [32, 2, NB])
        rkc = self.sb("rkc", [128, 4])
        rkl = self.sb("rkl", [128, 4])
        KTb = self.sb("KTb", [128, 4, NB], BF16)
        Vb = self.sb("Vb", [128, 2, 260], BF16)
        self.ms("pool", Vb[:], 1.0, [Vb.r])
        xbc = self.rot("xbc", 2, [128, 6, NB + 2])
        pbuf = self.rot("pbuf", 2, [128, 2, NB + 2])
        gbb = self.rot("gbb", 2, [128, 2, NB])
        gcs = self.sb("gcs", [128, NB])
        W = dict(sg=self.rot("sg", 2, [128, NB]), cvo=self.rot("cvo", 2, [128, NB]), xsT=self.sb("xsT", [128, 4, NB]), BCf=self.sb("BCf", [128, 2, NB]),
                 xstm=self.rot("xstm", 2, [128, 512]), btm=self.rot("btm", 2, [128, 128], BF16),
                 sm16=self.sb("sm16", [128, 16]), vt=self.sb("vt", [128, 16]), wE=self.sb("wE", [128, 8]),
                 decE=self.sb("decE", [128, 4]), xdtw=self.rot("xdtw", 2, [128, 512], BF16),
                 hetmp=self.sb("hetmp", [128, 256]), HE=HE, sccw=sccw, cw=cw, cb=cb, st=st)

        blocks = [(2 * i, 2) for i in range(NT // 2)]
        nb = len(blocks)
        S = dict(win=win, wuq=wuq, wkv=wkv, A1=A1, B1=B1, xts=xts, tmp=tmp, hb=hb, hT=hT, smalls=smalls, zs=zs, dtt=dtt,
                 st=st, rqb=rqb, rkvb=rkvb, lnb=lnb, cosr=cosr, t12=t12, rkc=rkc, rkl=rkl, KTb=KTb, Vb=Vb, gcs=gcs,
                 mla=[dict(cqT=self.sb("cqT", [128, 2, NB], BF16), cqsq=self.sb("cqsq", [128, 2, NB]),
                           ckvT=self.sb("ckvT", [128, NB], BF16), ckvsq=self.sb("ckvsq", [128, NB]),
                           krs=self.sb("krs", [32, 2, NB]), rope=self.sb("rope", [32, 2, NB])) for _ in range(2)],
                 slot=[dict(xb=xbc.next(), pbf=pbuf.next(), gb=gbb.next()) for _ in range(2)])

        def front1(bi):
            t0, nt = blocks[bi]
            N = nt * 128
            c0 = t0 * 128
            M = S["mla"][bi % 2]
            if bi == 1:
                self.dma("sp", A1[:], self.bv.t[0], [self.bv.r], [A1.r])
                self.dma("sp", B1[:], self.bv.t[1], [self.bv.r], [B1.r])
            for tl in range(nt):
                t = t0 + tl
                xt = xts.next()
                self.dma("sp", xt[:], xres.t[t * 128:(t + 1) * 128, :], [xres.r], [xt.r])
                self.norm_tile(xt, A1, B1, hb, tmp, smalls.next())
                self.transpose_tile(hb, hT, tl)
                yield
            for tl in range(nt):
                t = t0 + tl
                pb, pr = self.pbank()
                for k in range(8):
                    self.mm(pb[:, :], hT[:, k, tl * 128:(tl + 1) * 128], win[:, k, Z_OFF:Z_OFF + 512], k == 0, k == 7,
                            [hT.r, win.r], pr)
                z = zs.next()
                self.cp("act", z[:], pb[:, :], pr, [z.r])
                self.dma("pool", self.sc_z.t[t * 128:(t + 1) * 128, :], z[:], [z.r], [self.sc_z.r])
                pb, pr = self.pbank()
                for k in range(8):
                    self.mm(pb[:, 0:16], hT[:, k, tl * 128:(tl + 1) * 128], win[:, k, DT_OFF:DT_OFF + 16], k == 0, k == 7,
                            [hT.r, win.r], pr)
                self.tt("dve", dtt[:], pb[:, 0:16], st["dtb"][:], ALU.add, pr + [st["dtb"].r], [dtt.r])
                self.act(dtt[:], dtt[:], AF.Exp, [dtt.r], [dtt.r])
                self.act(self.dt_all[:, t, :], dtt[:], AF.Ln, [dtt.r], [self.dt_all.r], bias=1.0)
                yield
            self.dma("sp", M["rope"][:, :, 0:N], self.ropeT.t[:, :, c0:c0 + N].rearrange("a p t -> p a t"),
                     [self.ropeT.r], [M["rope"].r])
            for ci in range(5):
                off, wdt = FM[ci]
                pb, pr = self.pbank()
                for k in range(8):
                    self.mm(pb[0:wdt, 0:N], win[:, k, off:off + wdt], hT[:, k, 0:N], k == 0, k == 7, [hT.r, win.r], pr)
                src = pb[0:wdt, 0:N]
                if ci < 2:
                    self.cp("act", M["cqT"][:, ci, 0:N], src, pr, [M["cqT"].r])
                    self.act(M["cqsq"][:, ci, 0:N], src, AF.Square, pr, [M["cqsq"].r])
                elif ci == 2:
                    self.cp("act", M["ckvT"][:, 0:N], src, pr, [M["ckvT"].r])
                    self.act(M["ckvsq"][:, 0:N], src, AF.Square, pr, [M["ckvsq"].r])
                else:
                    self.cp("act", M["krs"][:, ci - 3, 0:N], src, pr, [M["krs"].r])
                yield

        def front2(bi):
            t0, nt = blocks[bi]
            N = nt * 128
            sl = S["slot"][bi % 2]
            xb, pbf, gb = sl["xb"], sl["pbf"], sl["gb"]
            for ci in (5, 6, 7, 8, 11, 12, 13, 14, 15, 16):
                off, wdt = FM[ci]
                pb, pr = self.pbank()
                for k in range(8):
                    self.mm(pb[0:wdt, 0:N], win[:, k, off:off + wdt], hT[:, k, 0:N], k == 0, k == 7, [hT.r, win.r], pr)
                src = pb[0:wdt, 0:N]
                if ci < 7:
                    self.cp("act", gb[:, ci - 5, 0:N], src, pr, [gb.r])
                elif ci < 9:
                    c = ci - 7
                    pbv, prv = self.pbank()
                    o2, w2 = FM[ci + 2]
                    for k in range(8):
                        self.mm(pbv[:, 0:N], win[:, k, o2:o2 + w2], hT[:, k, 0:N], k == 0, k == 7, [hT.r, win.r], prv)
                    self.cp("act", gcs[:, 0:N], src, pr, [gcs.r])
                    self.tt("dve", pbf[:, c, 1:N + 1], gcs[:, 0:N], pbv[:, 0:N], ALU.mult, [gcs.r] + prv, [pbf.r])
                else:
                    self.cp("act", xb[:, ci - 11, 1:N + 1], src, pr, [xb.r])
                yield
            self.ms("pool", xb[:, :, 0:1], 0.0, [xb.r])
            self.ms("pool", pbf[:, :, 0:1], 0.0, [pbf.r])
            self.ms("pool", xb[:, :, N + 1:N + 2], 0.0, [xb.r])
            self.ms("pool", pbf[:, :, N + 1:N + 2], 0.0, [pbf.r])
            yield

        def mla_post(bi):
            t0, nt = blocks[bi]
            N = nt * 128
            c0 = t0 * 128
            M = S["mla"][bi % 2]
            cqT, cqsq, ckvT, ckvsq, krs, rope = M["cqT"], M["cqsq"], M["ckvT"], M["ckvsq"], M["krs"], M["rope"]
            pb, pr = self.pbank()
            for c in range(2):
                self.mm(pb[:, 0:N], self.ones_f[:], cqsq[:, c, 0:N], c == 0, c == 1, [self.ones_f.r, cqsq.r], pr)
            self.act(lnb[:, 0:N], pb[:, 0:N], AF.Ln, pr + [self.eps_t.r], [lnb.r], bias=self.eps_t[:, 0:1], scale=1.0 / 256)
            self.act(rqb[:, 0:N], lnb[:, 0:N], AF.Exp, [lnb.r], [rqb.r], scale=-0.5)
            pb, pr = self.pbank()
            self.mm(pb[:, 0:N], self.ones_f[:], ckvsq[:, 0:N], True, True, [self.ones_f.r, ckvsq.r], pr)
            self.act(lnb[:, 0:N], pb[:, 0:N], AF.Ln, pr + [self.eps_t.r], [lnb.r], bias=self.eps_t[:, 0:1], scale=1.0 / 128)
            self.act(rkvb[:, 0:N], lnb[:, 0:N], AF.Exp, [lnb.r], [rkvb.r], scale=-0.5)
            pb, pr = self.pbank()
            for tl in range(nt):
                self.mm(pb[:, tl:tl + 1], ckvsq[:, tl * 128:(tl + 1) * 128], self.ones_f[:, 0:1], True, True,
                        [ckvsq.r, self.ones_f.r], pr)
            self.act(rkl[:, 0:nt], pb[:, 0:nt], AF.Ln, pr + [self.eps_t.r], [rkl.r], bias=self.eps_t[:, 0:1], scale=1.0 / 128)
            self.act(rkc[:, 0:nt], rkl[:, 0:nt], AF.Exp, [rkl.r], [rkc.r], scale=-0.5)
            yield
            for a in range(2):
                self.tt("pool", cosr[:, a, 0:N], rope[:, a, 0:N], rqb[0:32, 0:N], ALU.mult, [rope.r, rqb.r], [cosr.r])
            for h in range(4):
                pa, pra = self.pbank()
                for c in range(2):
                    self.mm(pa[:, 0:N], wuq[:, c, h * 160:h * 160 + 128], cqT[:, c, 0:N], c == 0, c == 1, [wuq.r, cqT.r], pra)
                pq, prq = self.pbank()
                for c in range(2):
                    self.mm(pq[0:32, 0:N], wuq[:, c, h * 160 + 128:h * 160 + 160], cqT[:, c, 0:N], c == 0, c == 1,
                            [wuq.r, cqT.r], prq)
                self.tt("dve", self.qT[:, h, c0:c0 + N], pa[:, 0:N], rqb[:, 0:N], ALU.mult, pra + [rqb.r], [self.qT.r])
                self.tt("dve", t12[:, 0, 0:N], pa[0:32, 0:N], cosr[:, 0, 0:N], ALU.mult, pra + [cosr.r], [t12.r])
                self.tt("dve", t12[:, 1, 0:N], pq[0:32, 0:N], cosr[:, 1, 0:N], ALU.mult, prq + [cosr.r], [t12.r])
                self.tt("dve", self.qT[0:32, h, c0:c0 + N], t12[:, 0, 0:N], t12[:, 1, 0:N], ALU.add, [t12.r], [self.qT.r])
                yield
            self.tt("pool", t12[:, 0, 0:N], krs[:, 0, 0:N], rope[:, 0, 0:N], ALU.mult, [krs.r, rope.r], [t12.r])
            self.tt("pool", t12[:, 1, 0:N], krs[:, 1, 0:N], rope[:, 1, 0:N], ALU.mult, [krs.r, rope.r], [t12.r])
            self.tt("pool", t12[:, 0, 0:N], t12[:, 0, 0:N], t12[:, 1, 0:N], ALU.add, [t12.r], [t12.r])
            for h in range(4):
                pk, prk = self.pbank()
                self.mm(pk[:, 0:N], wkv[:, h * 128:(h + 1) * 128], ckvT[:, 0:N], True, True, [wkv.r, ckvT.r], prk)
                self.tt("dve", KTb[:, h, 0:N], pk[:, 0:N], rkvb[:, 0:N], ALU.mult, prk + [rkvb.r], [KTb.r])
                self.cp("pool", KTb[0:32, h, 0:N], t12[:, 0, 0:N], [t12.r], [KTb.r])
            yield
            if bi == 0:
                self.dma("pool", self.kv_ctx_k.t, KTb[:, :, 0:N], [KTb.r], [self.kv_ctx_k.r])
            else:
                lo = c0 - TC
                self.dma("pool", self.xk.t.rearrange("(h p) t -> p h t", p=128)[:, :, lo:lo + N], KTb[:, :, 0:N],
                         [KTb.r], [self.xk.r])
            for tl in range(nt):
                pv, prv = self.pbank()
                self.mm(pv[:, 0:256], ckvT[:, tl * 128:(tl + 1) * 128], wkv[:, 512:768], True, True, [ckvT.r, wkv.r], prv)
                self.ts("dve", Vb[:, tl, :].rearrange("p (h e) -> p h e", h=4)[:, :, 0:64],
                        pv[:, 0:256].rearrange("p (h e) -> p h e", h=4), rkc[:, tl:tl + 1], None, ALU.mult, None,
                        prv + [rkc.r], [Vb.r])
            if bi == 0:
                self.dma("pool", self.kv_ctx_v.t, Vb[:, 0:2, :], [Vb.r], [self.kv_ctx_v.r])
            else:
                lo = t0 - 2
                self.dma("pool", self.xv.t.rearrange("(t p) e -> p t e", p=128)[:, lo:lo + nt, :], Vb[:, 0:nt, :],
                         [Vb.r], [self.xv.r])
            yield

        def back1(bi):
            if bi == 0:
                return
            p = S["slot"][(bi - 1) % 2]
            pt0, pnt = blocks[bi - 1]
            pN = pnt * 128
            if 2 <= bi < nb:
                c = S["slot"][bi % 2]
                self.cp("pool", p["xb"][:, :, pN + 1:pN + 2], c["xb"][:, :, 1:2], [c["xb"].r], [p["xb"].r])
                self.cp("pool", p["pbf"][:, :, pN + 1:pN + 2], c["pbf"][:, :, 1:2], [c["pbf"].r], [p["pbf"].r])
                self.cp("pool", c["xb"][:, :, 0:1], p["xb"][:, :, pN:pN + 1], [p["xb"].r], [c["xb"].r])
                self.cp("pool", c["pbf"][:, :, 0:1], p["pbf"][:, :, pN:pN + 1], [p["pbf"].r], [c["pbf"].r])
            yield from self.stage_a2_conv(l, p["xb"], p["pbf"], p["gb"], pt0, pnt, W, bi == nb)

        def back2(bi):
            if bi < nb:
                yield from mla_post(bi)
            if bi >= 1:
                pt0, pnt = blocks[bi - 1]
                yield from self.stage_a2_scan(l, pt0, pnt, W, bi == nb)

        run_gen(front1(0))
        run_gen(front2(0))
        for bi in range(nb + 1):
            f1 = front1(bi + 1) if bi + 1 < nb else None
            f2 = front2(bi + 1) if bi + 1 < nb else None
            interleave((f1, 9), (back1(bi), 8))
            interleave((f2, 11), (back2(bi), 10))
        pxb, ppb = S["slot"][(nb - 1) % 2]["xb"], S["slot"][(nb - 1) % 2]["pbf"]
        pnt = blocks[-1][1]
        N = pnt * 128
        sm = self.sb("sm", [128, SM_W])
        self.ms("dve", sm[:], 0.0, [sm.r])
        self.cp("dve", sm[:, 0:256], HE[:], [HE.r], [sm.r])
        self.cp("dve", sm[:, 512:524].rearrange("p (c k) -> p c k", c=6), pxb[:, :, N - 1:N + 1], [pxb.r], [sm.r])
        self.cp("dve", sm[:, 524:526], ppb[:, :, N], [ppb.r], [sm.r])
        self.cp("dve", self.fix[:, 10:16], pxb[:, :, N], [pxb.r], [self.fix.r])
        self.dma("pool", self.xs_.t, sm[:], [sm.r], [self.xs_.r])
        self.dma("pool", self.xs_.t[0:1, 528:536], self.dt_all[127:128, NT - 1, 0:8], [self.dt_all.r], [self.xs_.r])
        self.dump(f"qT{l}", self.qT[:], self.qT.r, [128, 4, T], BF16)
        self.dump(f"ysc{l}", self.ysc[:], self.ysc.r, [128, 2, T], BF16)
        self.dump(f"dt{l}", self.dt_all[:], self.dt_all.r, [128, NT, 16])
        self.dump(f"HE{l}", HE[:], HE.r, [128, 256])
        self.dump(f"BT{l}", self.BT[:], self.BT.r, [128, T], BF16)
        self.dump(f"CT{l}", self.CT[:], self.CT.r, [128, T], BF16)
        self.dump(f"xk{l}", self.xk.t, self.xk.r, [512, TL], BF16)
        self.dump(f"xv{l}", self.xv.t, self.xv.r, [2048, 260], BF16)
        self.dump(f"scxs{l}", self.sc_xs.t, self.sc_xs.r, [T, 512])
        self.dump(f"scz{l}", self.sc_z.t, self.sc_z.r, [T, 512])

    def conv_consts(self, l):
        sccw = self.sb("sccw", [128, 2, 3])
        self.dma("sp", sccw[:], self.sc_cw.t[l].rearrange("(c p) k -> p c k", p=128), [self.sc_cw.r], [sccw.r])
        cw = self.sb("cw", [128, 6, 3])
        self.dma("sp", cw[:], self.ssd_cw.t[l].rearrange("(c p) k -> p c k", p=128), [self.ssd_cw.r], [cw.r])
        cb = self.sb("cb", [128, 6])
        self.dma("sp", cb[:], self.ssd_cb.t[l].rearrange("(c p) -> p c", p=128), [self.ssd_cb.r], [cb.r], slow_ok=True)
        self.ncb = self.sb("ncb", [128, 6])
        self.ts("dve", self.ncb[:], cb[:], -1.0, None, ALU.mult, None, [cb.r], [self.ncb.r])
        return sccw, cw, cb

    def ssd_consts(self, l):
        st = {}
        al = self.sb("al", [128, 16])
        self.dma("sp", al[:], self.a_log.t[l].partition_broadcast(128), [self.a_log.r], [al.r])
        ab = self.sb("ab", [128, 16])
        self.act(ab[:], al[:], AF.Exp, [al.r], [ab.r])
        self.ts("dve", ab[:], ab[:], -1.0, None, ALU.mult, None, [ab.r], [ab.r])
        st["ab"] = ab
        dtb = self.sb("dtb", [128, 16])
        self.dma("sp", dtb[:], self.dt_bias.t[l].partition_broadcast(128), [self.dt_bias.r], [dtb.r])
        st["dtb"] = dtb
        return st

    def stage_a2_conv(self, l, xb, pbf, gb, t0, nt, W, is_last):
        N = nt * 128
        c0 = t0 * 128
        xsT, BCf, sccw, cw, cb = W["xsT"], W["BCf"], W["sccw"], W["cw"], W["cb"]
        for c in range(2):
            cvo = W["cvo"].next()
            self.conv3(cvo, pbf, c, sccw, N)
            self.tt("dve", self.ysc[:, c, c0:c0 + N], cvo[:, 0:N], gb[:, c, 0:N], ALU.mult, [cvo.r, gb.r], [self.ysc.r])
            if is_last:
                self.cp("dve", self.fix[:, 6 + c:7 + c], cvo[:, N - 1:N], [cvo.r], [self.fix.r])
                self.cp("dve", self.fix[:, 8 + c:9 + c], gb[:, c, N - 1:N], [gb.r], [self.fix.r])
            yield
        for c in range(6):
            cvo = W["cvo"].next()
            self.conv3(cvo, xb, c, cw, N)
            if is_last:
                self.cp("dve", self.fix[:, c:c + 1], cvo[:, N - 1:N], [cvo.r], [self.fix.r])
            dst = xsT[:, c, 0:N] if c < 4 else BCf[:, c - 4, 0:N]
            dres = xsT.r if c < 4 else BCf.r
            sg = W["sg"].next()
            self.silu(dst, [dres], cvo[:, 0:N], [cvo.r], (sg[:, 0:N], sg.r), bias=(cb[:, c:c + 1], cb.r),
                      negbias=(self.ncb[:, c:c + 1], self.ncb.r))
            yield
        self.cp("pool", self.BT[:, c0:c0 + N], BCf[:, 0, 0:N], [BCf.r], [self.BT.r])
        self.cp("pool", self.CT[:, c0:c0 + N], BCf[:, 1, 0:N], [BCf.r], [self.CT.r])
        if is_last:
            self.cp("pool", self.xsT_last[:], xsT[:, :, N - 128:N], [xsT.r], [self.xsT_last.r])

    def stage_a2_scan(self, l, t0, nt, W, is_last):
        xsT, BCf = W["xsT"], W["BCf"]
        for tl in range(nt):
            t = t0 + tl
            xs_tm, b_tm = self.tokmajor_xs_b(xsT, BCf, tl, t, W["xstm"], W["btm"])
            self.cp("act", self.HEs[:, t, :], W["HE"][:], [W["HE"].r], [self.HEs.r])
            yield
            self.state_update(W["st"], xs_tm, b_tm, t, 0, W["sm16"], W["vt"], W["wE"], W["decE"], W["xdtw"], W["hetmp"],
                              W["HE"], mask_last=(is_last and tl == nt - 1))
            yield

    def conv3(self, cvo, buf, c, wts, N):
        self.ts("dve", cvo[:, 0:N], buf[:, c, 1:N + 1], wts[:, c, 1:2], None, ALU.mult, None, [buf.r, wts.r], [cvo.r])
        self.stt("dve", cvo[:, 0:N], buf[:, c, 0:N], wts[:, c, 0:1], cvo[:, 0:N], ALU.mult, ALU.add,
                 [buf.r, wts.r, cvo.r], [cvo.r])
        self.stt("dve", cvo[:, 0:N], buf[:, c, 2:N + 2], wts[:, c, 2:3], cvo[:, 0:N], ALU.mult, ALU.add,
                 [buf.r, wts.r, cvo.r], [cvo.r])

    def tokmajor_xs_b(self, xsT, BCf, tl, t, xstm, btm):
        pb, pr = self.pbank()
        for c in range(4):
            self.tr(pb[:, c * 128:(c + 1) * 128], xsT[:, c, tl * 128:(tl + 1) * 128], self.ident_f[:],
                    [xsT.r, self.cst.r], pr)
        xs_tm = xstm.next()
        self.cp("act", xs_tm[:], pb[:, :], pr, [xs_tm.r])
        self.dma("pool", self.sc_xs.t[t * 128:(t + 1) * 128, :], xs_tm[:], [xs_tm.r], [self.sc_xs.r])
        pb2, pr2 = self.pbank()
        self.tr(pb2[:, 0:128], BCf[:, 0, tl * 128:(tl + 1) * 128], self.ident_f[:], [BCf.r, self.cst.r], pr2)
        b_tm = btm.next()
        self.cp("act", b_tm[:], pb2[:, 0:128], pr2, [b_tm.r])
        self.dma("pool", self.sc_bt.t[t * 128:(t + 1) * 128, :], b_tm[:], [b_tm.r], [self.sc_bt.r])
        return xs_tm, b_tm

    def state_update(self, st, xs_tm, b_tm, t, d_, sm16, vt, wE, decE, xdtw, hetmp, H, mask_last=False, vt_ready=None):
        tri = self.triE if d_ == 0 else self.triL
        o = d_ * 8
        if vt_ready is None:
            la = sm16
            self.tt("dve", la[:, 0:8], self.dt_all[:, t, o:o + 8], st["ab"][:, o:o + 8], ALU.mult,
                    [self.dt_all.r, st["ab"].r], [la.r])
            pb, pr = self.pbank()
            self.mm(pb[:, 0:8], tri[:], la[:, 0:8], True, True, [self.cst.r, la.r], pr)
            self.mm(pb[:, 8:16], self.ones_f[:], la[:, 0:8], True, True, [self.ones_f.r, la.r], pr)
            self.cp("act", vt[:, 0:16], pb[:, 0:16], pr, [vt.r])
            vcol, tot = vt[:, 0:8], vt[:, 8:16]
            totlo, tothi = vt[0:64, 8:12], vt[64:128, 12:16]
            vr = vt.r
        else:
            vcol, tot, totlo, tothi, vr = vt_ready
        self.tt("dve", wE[:], tot, vcol, ALU.subtract, [vr], [wE.r])
        self.act(wE[:], wE[:], AF.Exp, [wE.r], [wE.r])
        self.tt("dve", wE[:], wE[:], self.dt_all[:, t, o:o + 8], ALU.mult, [wE.r, self.dt_all.r], [wE.r])
        if mask_last:
            self.ts("dve", wE[:], wE[:], self.lastmask[:, 0:1], None, ALU.mult, None, [wE.r, self.cst.r], [wE.r])
        self.act(decE[0:64, :], totlo, AF.Exp, [vr], [decE.r])
        self.act(decE[64:128, :], tothi, AF.Exp, [vr], [decE.r])
        xd = xdtw.next()
        self.tt("dve", xd[:].rearrange("p (h e) -> p h e", h=8), xs_tm[:].rearrange("p (h e) -> p h e", h=8),
                bc(wE[:], [128, 8, 64]), ALU.mult, [xs_tm.r, wE.r], [xd.r])
        pb, pr = self.pbank()
        self.mm(pb[:, :], b_tm[:], xd[:], True, True, [b_tm.r, xd.r], pr)
        self.tt("dve", hetmp[:].rearrange("p (h e) -> p h e", h=4), H[:].rearrange("p (h e) -> p h e", h=4),
                bc(decE[:], [128, 4, 64]), ALU.mult, [H.r, decE.r], [hetmp.r])
        self.tt("dve", H[0:64, :], hetmp[0:64, :], pb[0:64, 0:256], ALU.add, [hetmp.r] + pr, [H.r])
        self.tt("dve", H[64:128, :], hetmp[64:128, :], pb[64:128, 256:512], ALU.add, [hetmp.r] + pr, [H.r])

    def exchange(self, l):
        groups = [[2 * i, 2 * i + 1] for i in range(self.ncores // 2)]
        for src, dst in ((self.xk, self.xk_g), (self.xv, self.xv_g), (self.xs_, self.xs_g)):
            self.cc(src, dst, groups)

    def cc(self, src, dst, groups):
        if self.fake_cc:
            n = src.t.shape[0]
            for r in range(2):
                self.dma("pool", dst.t[r * n:(r + 1) * n, :], src.t, [src.r], [dst.r])
            return
        self.P.dma("pool", lambda e: e.collective_compute("AllGather", ALU.bypass, replica_groups=groups,
                                                           ins=[src.t.opt()], outs=[dst.t.opt()]),
                   [src.r], [dst.r], inc=1)

    def phase_b(self, l, xres, last):
        nc, P = self.nc, self.P
        self.new_scope()
        st = self.ssd_consts(l)
        sccw, cw, cb = self.conv_consts(l)
        wo_att = self.sb("wo_att", [64, 4, D], BF16)
        self.dma("pool", wo_att[:], self.w_out.t[l, 0:256, :].rearrange("(h p) n -> p h n", p=64), [self.w_out.r], [wo_att.r])
        wo_r = self.sb("wo_r", [128, 6, D], BF16)
        self.dma("pool", wo_r[:], self.w_out.t[l, 256:1024, :].rearrange("(c p) n -> p c n", p=128), [self.w_out.r], [wo_r.r])
        G1 = self.load_bv(2, "G1")
        dsk8 = self.sb("dsk8", [128, 8])
        self.dma("sp", dsk8[:], self.ssd_d.t[l].partition_broadcast(128), [self.ssd_d.r], [dsk8.r])
        ngb = self.sb("ngb", [128, 512])
        self.dma("sp", ngb[:], self.ssd_norm.t[l].partition_broadcast(128), [self.ssd_norm.r], [ngb.r])
        VA = self.sb("VA", [128, 34, 260], BF16)
        self.dma("sp", VA[:, 0:2, :], self.kv_ctx_v.t, [self.kv_ctx_v.r], [VA.r])
        xvv = self.xv_g.t.rearrange("(t p) e -> p t e", p=128)
        for q in range(4):
            self.dma("sp", VA[:, 2 + 8 * q:10 + 8 * q, :], xvv[:, 8 * q:8 * q + 8, :], [self.xv_g.r], [VA.r])
        KTs = self.rot("KT", 2, [128, TC + 2 * TL], BF16)
        smg = self.sb("smg", [128, 2, SM_W])
        self.dma("sp", smg[:], self.xs_g.t.rearrange("(r p) c -> p r c", p=128), [self.xs_g.r], [smg.r])
        smp = self.sb("smp", [128, SM_W])
        self.ts("dve", smp[:], smg[:, 1, :], self.esel[:, 0:1], None, ALU.mult, None, [smg.r, self.cst.r], [smp.r])
        self.stt("dve", smp[:], smg[:, 0, :], self.esel[:, 1:2], smp[:], ALU.mult, ALU.add, [smg.r, self.cst.r, smp.r], [smp.r])

        xt = self.sb("xt", [128, D])
        tmp = self.sb("tmp", [128, D])
        smalls = self.small3(2)
        x1 = self.sb("x1", [128, D])
        la = self.sb("la", [128, 16])
        vt = self.sb("vt", [128, 32])
        ev = self.sb("ev", [128, 16])
        wL = self.sb("wL", [128, 8])
        decL = self.sb("decL", [128, 4])
        Rm = self.sb("Rm", [128, 8, 128])
        D1 = self.sb("D1", [128, 8, 128])
        MT = self.sb("MT", [128, 16, 128], BF16)
        xdt = self.sb("xdt", [128, 16, 64], BF16)
        xs_r = self.rot("xsr", 2, [128, 512])
        bt_r = self.rot("btr", 2, [128, 128], BF16)
        z_r = self.rot("zr", 2, [128, 512])
        ya = self.sb("ya", [128, 512])
        yb = self.sb("yb", [128, 512])
        sz = self.sb("sz", [128, 512])
        yss = self.sb("yss", [128, 512], BF16)
        jk = self.sb("jk", [128, 512], BF16)
        jk32 = self.sb("jk32", [128, 512])
        ms2 = self.sb("ms2", [128, 2])
        ln2 = self.sb("ln2", [128, 2])
        rs2 = self.sb("rs2", [128, 2])
        xdtw = self.rot("xdtw", 2, [128, 512], BF16)
        hetmp = self.sb("hetmp", [128, 256])
        HL = self.sb("HL", [128, 256])
        HLb = self.sb("HLbd", [128, 512], BF16)
        self.ms("pool", HLb[:], 0.0, [HLb.r])
        Cbd = self.sb("Cbd", [128, 256], BF16)
        self.ms("pool", Cbd[:], 0.0, [Cbd.r])
        HbdE = self.sb("HbdE", [128, 512], BF16)
        self.ms("pool", HbdE[:], 0.0, [HbdE.r])
        ymx = self.sb("ymx", [128, 4, 512], BF16)
        yatt = self.sb("yatt", [64, 4, 512], BF16)
        PTs = self.rot("PT", 2, [128, 1024], BF16)
        Osb = self.sb("Osb", [64, 512])
        rec = self.sb("rec", [128, 512])
        dskb = self.sb("dskb", [128, 512])
        self.cp("dve", dskb[:].rearrange("p (h e) -> p h e", h=8), bc(dsk8[:], [128, 8, 64]), [dsk8.r], [dskb.r])

        self.l_init(l, st, smp, cw, cb, HL, HLb)
        self.fix_last(l, smp, sccw, cw, cb)
        self.dump(f"HL0{l}", HL[:], HL.r, [128, 256])
        if self.stop(l, "b0"):
            return

        order = [4, 3, 2, 1] + ([] if last else [0])
        ymxs = [ymx, self.sb("ymx2", [128, 4, 512], BF16)]
        yatts = [yatt, self.sb("yatt2", [64, 4, 512], BF16)]

        def geom(bi):
            t0, nt = BLOCKS[bi]
            return t0, nt, nt * 128, t0 * 128, (1 if bi == 0 else 0)

        def ssd_gen(bi, ymx):
            t0, nt, N, c0, isctx = geom(bi)
            if isctx:
                self.ms("dve", HL[:], 0.0, [HL.r])
                self.ms("dve", HLb[:], 0.0, [HLb.r])
            for tl in range(nt - 1, -1, -1):
                t = t0 + tl
                cs = slice(t * 128, (t + 1) * 128)
                xs = xs_r.next()
                self.dma("pool", xs[:], self.sc_xs.t[cs, :], [self.sc_xs.r], [xs.r])
                btk = bt_r.next()
                self.dma("pool", btk[:], self.sc_bt.t[cs, :], [self.sc_bt.r], [btk.r])
                z = z_r.next()
                self.dma("pool", z[:], self.sc_z.t[cs, :], [self.sc_z.r], [z.r])
                self.tt("dve", la[:], self.dt_all[:, t, :], st["ab"][:], ALU.mult, [self.dt_all.r, st["ab"].r], [la.r])
                pb, pr = self.pbank()
                self.mm(pb[:, 0:8], self.triE[:], la[:, 0:8], True, True, [self.cst.r, la.r], pr)
                self.mm(pb[:, 8:16], self.triL[:], la[:, 8:16], True, True, [self.cst.r, la.r], pr)
                self.mm(pb[:, 16:32], self.ones_f[:], la[:, 0:16], True, True, [self.ones_f.r, la.r], pr)
                self.cp("act", vt[:], pb[:, 0:32], pr, [vt.r])
                self.act(ev[:], vt[:, 0:16], AF.Exp, [vt.r], [ev.r])
                yield
                pg, prg = self.pbank(pin=True)
                for g in range(2):
                    self.cp("pool", Cbd[g * 64:(g + 1) * 64, g * 128:(g + 1) * 128], self.CT[g * 64:(g + 1) * 64, cs],
                            [self.CT.r], [Cbd.r])
                self.mm(pg[:, 0:256], self.BT[:, cs], Cbd[:], True, True, [self.BT.r, Cbd.r], prg)
                for d_ in range(2):
                    self.tt("dve", xdt[:, d_ * 8:(d_ + 1) * 8, :], xs[:].rearrange("p (h e) -> p h e", h=8),
                            bc(self.dt_all[:, t, d_ * 8:(d_ + 1) * 8], [128, 8, 64]), ALU.mult,
                            [xs.r, self.dt_all.r], [xdt.r])
                yield
                for d_, tri in enumerate((self.triE, self.triL)):
                    self.tt("pool", Rm[:], bcm(tri[:], [128, 8, 128]), bc(la[:, d_ * 8:(d_ + 1) * 8], [128, 8, 128]),
                            ALU.mult, [self.cst.r, la.r], [Rm.r])
                    pd, prd = self.pbank(2)
                    for q in range(2):
                        self.mm(pd[:, q * 512:(q + 1) * 512], self.ones_f[:],
                                Rm[:, q * 4:(q + 1) * 4, :].rearrange("p a b -> p (a b)"), True, False,
                                [self.ones_f.r, Rm.r], prd)
                        self.mm(pd[:, q * 512:(q + 1) * 512], self.ident_b[:], self.negm[:, d_, :], False, True,
                                [self.ident_b.r, self.negm.r], prd)
                    self.tt("dve", D1[:], pd[:, :].rearrange("p (a b) -> p a b", a=8),
                            bc(vt[:, d_ * 8:(d_ + 1) * 8], [128, 8, 128]), ALU.subtract, prd + [vt.r], [D1.r])
                    self.act(D1[:], D1[:], AF.Exp, [D1.r], [D1.r])
                    self.tt("dve", MT[:, d_ * 8:(d_ + 1) * 8, :].rearrange("p (g h) b -> p g h b", g=2),
                            D1[:].rearrange("p (g h) b -> p g h b", g=2),
                            pg[:, 0:256].rearrange("p (g b) -> p g b", g=2).unsqueeze(2).to_broadcast([128, 2, 4, 128]),
                            ALU.mult, [D1.r] + prg, [MT.r])
                    if d_ == 1:
                        self.unpin(prg)
                    yield
                yield
                py, pry = self.pbank()
                for h in range(8):
                    self.mm(py[:, h * 64:(h + 1) * 64], MT[:, h, :], xdt[:, h, :], True, False, [MT.r, xdt.r], pry)
                    self.mm(py[:, h * 64:(h + 1) * 64], MT[:, 8 + h, :], xdt[:, 8 + h, :], False, True, [MT.r, xdt.r], pry)
                pe_, pre = self.pbank()
                for g in range(2):
                    self.cp("act", HbdE[g * 64:(g + 1) * 64, g * 256:(g + 1) * 256], self.HEs[g * 64:(g + 1) * 64, t, :],
                            [self.HEs.r], [HbdE.r])
                self.mm(pe_[:, :], self.CT[:, cs], HbdE[:], True, True, [self.CT.r, HbdE.r], pre)
                self.tt("dve", ya[:].rearrange("p (h e) -> p h e", h=8), pe_[:, :].rearrange("p (h e) -> p h e", h=8),
                        bc(ev[:, 0:8], [128, 8, 64]), ALU.mult, pre + [ev.r], [ya.r])
                pl, prl = self.pbank()
                self.mm(pl[:, :], self.CT[:, cs], HLb[:], True, True, [self.CT.r, HLb.r], prl)
                self.tt("dve", yb[:].rearrange("p (h e) -> p h e", h=8), pl[:, :].rearrange("p (h e) -> p h e", h=8),
                        bc(ev[:, 8:16], [128, 8, 64]), ALU.mult, prl + [ev.r], [yb.r])
                self.tt("dve", ya[:], ya[:], yb[:], ALU.add, [ya.r, yb.r], [ya.r])
                self.tt("dve", ya[:], ya[:], py[:, :], ALU.add, [ya.r] + pry, [ya.r])
                self.tt("dve", yb[:], xs[:], dskb[:], ALU.mult, [xs.r, dskb.r], [yb.r])
                self.tt("dve", ya[:], ya[:], yb[:], ALU.add, [ya.r, yb.r], [ya.r])
                if l == 0 and t == NT - 1:
                    self.dump("yscan0", ya[:], ya.r, [128, 512])
                yield
                self.silu(sz[:], [sz.r], z[:], [z.r], (jk32[:], jk32.r))
                self.tt("dve", ya[:], ya[:], sz[:], ALU.mult, [ya.r, sz.r], [ya.r])
                for g in range(2):
                    self.act(jk[:, g * 256:(g + 1) * 256], ya[:, g * 256:(g + 1) * 256], AF.Square, [ya.r],
                             [jk.r, ms2.r], scale=1.0 / 16.0, accum=ms2[:, g:g + 1])
                self.act(ln2[:], ms2[:], AF.Ln, [ms2.r, self.eps_t.r], [ln2.r], bias=self.eps_t[:, 0:1])
                self.act(rs2[:], ln2[:], AF.Exp, [ln2.r], [rs2.r], scale=-0.5)
                for g in range(2):
                    self.stt("dve", yss[:, g * 256:(g + 1) * 256], ya[:, g * 256:(g + 1) * 256], rs2[:, g:g + 1],
                             ngb[:, g * 256:(g + 1) * 256], ALU.mult, ALU.mult, [ya.r, rs2.r, ngb.r], [yss.r])
                pb, pr = self.pbank()
                pbb = pb.bitcast(BF16)
                for c in range(4):
                    self.tr(pbb[:, c * 128:(c + 1) * 128], yss[:, c * 128:(c + 1) * 128], self.ident_b[:],
                            [yss.r, self.ident_b.r], pr)
                self.cp("act", ymx[:, :, tl * 128:(tl + 1) * 128], pbb[:, 0:512].rearrange("p (c t) -> p c t", c=4), pr, [ymx.r])
                yield
                vready = (vt[:, 8:16], vt[:, 24:32], vt[0:64, 24:28], vt[64:128, 28:32], vt.r)
                self.state_update(st, xs, btk, t, 1, None, vt, wL, decL, xdtw, hetmp, HL, vt_ready=vready)
                self.hl_bd(HL, HLb)
                yield

        def att_gen(bi, yatt):
            t0, nt, N, c0, isctx = geom(bi)
            nk = 2 if isctx else 34
            nkc = nk * 128
            for h in range(4):
                KT = KTs.next()
                self.dma("sp", KT[:, 0:TC], self.kv_ctx_k.t[:, h, :], [self.kv_ctx_k.r], [KT.r])
                if not isctx:
                    for r in range(2):
                        self.dma("sp", KT[:, TC + r * TL:TC + (r + 1) * TL],
                                 self.xk_g.t[r * 512 + h * 128:r * 512 + (h + 1) * 128, :], [self.xk_g.r], [KT.r])
                po, pro = self.pbank(pin=True)
                npair = nk // 2

                def qk_pair(j):
                    ps2, pr2 = self.pbank(2, pin=True)
                    for a in range(2):
                        kt = 2 * j + a
                        self.mm(ps2[:, a * 512:a * 512 + N], KT[:, kt * 128:(kt + 1) * 128], self.qT[:, h, c0:c0 + N],
                                True, True, [KT.r, self.qT.r], pr2)
                    return ps2, pr2

                cur = qk_pair(0)
                for j in range(npair):
                    ps2, pr2 = cur
                    pt_ = PTs.next()
                    self.act(pt_[:].rearrange("p (a n) -> p a n", a=2)[:, :, 0:N],
                             ps2[:, :].rearrange("p (a n) -> p a n", a=2)[:, :, 0:N], AF.Exp, pr2, [pt_.r], scale=MLA_SCALE)
                    self.unpin(pr2)
                    if j + 1 < npair:
                        cur = qk_pair(j + 1)
                    for a in range(2):
                        kt = 2 * j + a
                        self.mm(po[0:65, 0:N], VA[:, kt, h * 65:(h + 1) * 65], pt_[:, a * 512:a * 512 + N], kt == 0,
                                kt == nk - 1, [VA.r, pt_.r], pro)
                    yield
                self.unpin(pro)
                self.act(rec[64:65, 0:N], po[64:65, 0:N], AF.Ln, pro, [rec.r])
                self.act(rec[64:65, 0:N], rec[64:65, 0:N], AF.Exp, [rec.r], [rec.r], scale=-1.0)
                self.cp("act", Osb[:, 0:N], po[0:64, 0:N], pro, [Osb.r])
                yield
                pbc, prb = self.pbank()
                self.mm(pbc[0:64, 0:N], self.ones_f[64:65, 0:64], rec[64:65, 0:N], True, True, [self.ones_f.r, rec.r], prb)
                self.tt("dve", yatt[:, h, 0:N], pbc[0:64, 0:N], Osb[:, 0:N], ALU.mult, prb + [Osb.r], [yatt.r])
                yield

        def wout_gen(bi, ymx, yatt):
            t0, nt, N, c0, isctx = geom(bi)
            if isctx:
                self.dma("sp", G1[:], self.bv.t[8], [self.bv.r], [G1.r])
            for tl in range(nt):
                t = t0 + tl
                cs = slice(t * 128, (t + 1) * 128)
                ls = slice(tl * 128, (tl + 1) * 128)
                pw, prw = self.pbank(2)
                for hf in range(2):
                    ns = slice(hf * 512, (hf + 1) * 512)
                    for h in range(4):
                        self.mm(pw[:, ns], yatt[:, h, ls], wo_att[:, h, ns], h == 0, False, [yatt.r, wo_att.r], prw)
                    for c in range(2):
                        self.mm(pw[:, ns], self.ysc[:, c, cs], wo_r[:, c, ns], False, False, [self.ysc.r, wo_r.r], prw)
                    for c in range(4):
                        self.mm(pw[:, ns], ymx[:, c, ls], wo_r[:, 2 + c, ns], False, c == 3, [ymx.r, wo_r.r], prw)
                self.dma("sp", xt[:], xres.t[cs, :], [xres.r], [xt.r])
                self.post_tile(pw, prw, xt, G1, tmp, smalls.next(), x1)
                self.dma("sp", self.x1s.t[cs, :], x1[:], [x1.r], [self.x1s.r])
                yield

        prev = None
        for it, bi in enumerate(order):
            ymx_c, yatt_c = ymxs[it % 2], yatts[it % 2]
            nk_ = 2 if bi == 0 else 34
            gens = [(ssd_gen(bi, ymx_c), 9 * BLOCKS[bi][1]), (att_gen(bi, yatt_c), 4 * (nk_ // 2 + 2))]
            if prev is not None:
                gens.append((wout_gen(*prev), BLOCKS[prev[0]][1]))
            interleave(*gens)
            if l == 0 and bi == 4:
                self.dump("yatt0", yatt_c[:], yatt_c.r, [64, 4, 512], BF16)
                self.dump("ymx0", ymx_c[:], ymx_c.r, [128, 4, 512], BF16)
            if self.stop(l, "b1") or self.stop(l, "b2"):
                return
            prev = (bi, ymx_c, yatt_c)
        run_gen(wout_gen(*prev))
        self.dump(f"x1_{l}", self.x1s.t, self.x1s.r, [T, D])

    def l_init(self, l, st, smp, cw, cb, HL, HLb):
        c3 = self.sb("c3", [128, 6, 3])
        pv = smp[:, 512:524].rearrange("p (c k) -> p c k", c=6)
        self.cp("dve", c3[:, :, 0], self.fix[:, 10:16], [self.fix.r], [c3.r])
        self.cp("dve", c3[:, :, 1], pv[:, :, 1], [smp.r], [c3.r])
        self.cp("dve", c3[:, :, 2], pv[:, :, 0], [smp.r], [c3.r])
        t6 = self.sb("t6", [128, 6, 3])
        cv = self.sb("cv6", [128, 6])
        self.tt("dve", t6[:], c3[:], cw[:], ALU.mult, [c3.r, cw.r], [t6.r])
        self.tt("dve", cv[:], t6[:, :, 0], t6[:, :, 1], ALU.add, [t6.r], [cv.r])
        self.tt("dve", cv[:], cv[:], t6[:, :, 2], ALU.add, [t6.r, cv.r], [cv.r])
        self.tt("dve", cv[:], cv[:], cb[:], ALU.add, [cv.r, cb.r], [cv.r])
        sv = self.sb("sv6", [128, 6])
        sg6 = self.sb("sg6", [128, 6])
        self.silu(sv[:], [sv.r], cv[:], [cv.r], (sg6[:], sg6.r))
        pb, pr = self.pbank()
        for c in range(4):
            self.tr(pb[0:1, c * 128:(c + 1) * 128], sv[:, c:c + 1], self.ident_f[:], [sv.r, self.cst.r], pr)
        pb2, pr2 = self.pbank()
        self.tr(pb2[0:1, 0:128], sv[:, 4:5], self.ident_f[:], [sv.r, self.cst.r], pr2)
        row = self.sb("row", [1, 640])
        self.cp("act", row[:, 0:512], pb[0:1, 0:512], pr, [row.r])
        self.cp("act", row[:, 512:640], pb2[0:1, 0:128], pr2, [row.r])
        dtr = self.sb("dtr", [1, 8])
        self.cp("dve", dtr[:], smp[0:1, 528:536], [smp.r], [dtr.r])
        xr = self.sb("xr", [1, 512])
        self.tt("dve", xr[:].rearrange("p (h e) -> p h e", h=8), row[:, 0:512].rearrange("p (h e) -> p h e", h=8),
                bc(dtr[:], [1, 8, 64]), ALU.mult, [row.r, dtr.r], [xr.r])
        pb3, pr3 = self.pbank()
        self.mm(pb3[:, :], row[:, 512:640], xr[:], True, True, [row.r, xr.r], pr3)
        self.tt("dve", HL[0:64, :], smp[0:64, 0:256], pb3[0:64, 0:256], ALU.add, [smp.r] + pr3, [HL.r])
        self.tt("dve", HL[64:128, :], smp[64:128, 0:256], pb3[64:128, 256:512], ALU.add, [smp.r] + pr3, [HL.r])
        self.hl_bd(HL, HLb)

    def hl_bd(self, HL, HLb):
        for g in range(2):
            self.cp("act", HLb[g * 64:(g + 1) * 64, g * 256:(g + 1) * 256], HL[g * 64:(g + 1) * 64, :], [HL.r], [HLb.r])

    def fix_last(self, l, smp, sccw, cw, cb):
        tcol = T - 1
        f2 = self.sb("f2", [128, 2])
        self.tt("dve", f2[:], smp[:, 524:526], sccw[:, :, 2], ALU.mult, [smp.r, sccw.r], [f2.r])
        self.tt("dve", f2[:], f2[:], self.fix[:, 6:8], ALU.add, [f2.r, self.fix.r], [f2.r])
        self.tt("dve", self.ysc[:, :, tcol], f2[:], self.fix[:, 8:10], ALU.mult, [f2.r, self.fix.r], [self.ysc.r])
        f6 = self.sb("f6", [128, 6])
        pv = smp[:, 512:524].rearrange("p (c k) -> p c k", c=6)
        self.tt("dve", f6[:], pv[:, :, 1], cw[:, :, 2], ALU.mult, [smp.r, cw.r], [f6.r])
        self.tt("dve", f6[:], f6[:], self.fix[:, 0:6], ALU.add, [f6.r, self.fix.r], [f6.r])
        self.tt("dve", f6[:], f6[:], cb[:], ALU.add, [f6.r, cb.r], [f6.r])
        s6 = self.sb("s6", [128, 6])
        sg6b = self.sb("sg6b", [128, 6])
        self.silu(s6[:], [s6.r], f6[:], [f6.r], (sg6b[:], sg6b.r))
        self.cp("dve", self.xsT_last[:, :, 127], s6[:, 0:4], [s6.r], [self.xsT_last.r])
        self.cp("dve", self.BT[:, tcol:tcol + 1], s6[:, 4:5], [s6.r], [self.BT.r])
        self.cp("dve", self.CT[:, tcol:tcol + 1], s6[:, 5:6], [s6.r], [self.CT.r])
        bl = self.sb("bl", [128, 128])
        self.cp("dve", bl[:], self.BT[:, T - 128:T], [self.BT.r], [bl.r])
        self.cp("dve", bl[:, 127:128], s6[:, 4:5], [s6.r], [bl.r])
        pb, pr = self.pbank()
        for c in range(4):
            self.tr(pb[:, c * 128:(c + 1) * 128], self.xsT_last[:, c, :], self.ident_f[:], [self.xsT_last.r, self.cst.r], pr)
        xl = self.sb("xl", [128, 512])
        self.cp("act", xl[:], pb[:, :], pr, [xl.r])
        self.dma("pool", self.sc_xs.t[T - 128:T, :], xl[:], [xl.r], [self.sc_xs.r])
        pb2, pr2 = self.pbank()
        self.tr(pb2[:, 0:128], bl[:], self.ident_f[:], [bl.r, self.cst.r], pr2)
        bl2 = self.sb("bl2", [128, 128], BF16)
        self.cp("act", bl2[:], pb2[:, 0:128], pr2, [bl2.r])
        self.dma("pool", self.sc_bt.t[T - 128:T, :], bl2[:], [bl2.r], [self.sc_bt.r])

    def phase_c(self, l, xnext, last):
        self.close_ab()
        self.new_scope()
        w1 = self.sb("w1", [128, 8, 4 * D], BF16)
        for k in range(8):
            self.dma("pool", w1[:, k, :], self.w_ff1.t[l, k * 128:(k + 1) * 128, :], [self.w_ff1.r], [w1.r])
        w2 = self.sb("w2", [128, 32, D], BF16)
        w2v = self.w_ff2.t[l].rearrange("(c p) n -> p c n", p=128)
        for q in range(4):
            self.dma("pool", w2[:, q * 8:(q + 1) * 8, :], w2v[:, q * 8:(q + 1) * 8, :], [self.w_ff2.r], [w2.r])
        A2 = self.load_bv(3, "A2")
        B2 = self.load_bv(4, "B2")
        G2 = self.load_bv(5, "G2")
        A2c = B2c = G2c = None
        if not last:
            A2c = self.load_bv(9, "A2c")
            B2c = self.load_bv(10, "B2c")
            G2c = self.load_bv(11, "G2c")
        NBC = 256
        xts = self.rot("xt", 2, [128, D])
        xps = self.rot("xp", 1, [128, D])
        tmp = self.sb("tmp", [128, D])
        tmp2 = self.sb("tmp2", [128, D])
        hb = self.sb("hb", [128, D], BF16)
        hTs = [self.sb("hTa", [128, 8, NBC], BF16), self.sb("hTb", [128, 8, NBC], BF16)]
        a_r = self.rot("afo", 3, [128, NBC], BF16)
        rl_r = self.rot("rl", 2, [128, NBC])
        smalls = self.small3(2)
        smalls2 = self.small3(2)
        x2s = self.rot("x2", 2, [128, D])
        ysbs = self.rot("ysb", 1, [128, D])
        blocks = [(2 + 2 * i, 2) for i in range(8)] + ([] if last else [(0, 2)])

        def front(i):
            t0, nt = blocks[i]
            isctx = (t0 == 0)
            hT = hTs[i % 2]
            for tl in range(nt):
                t = t0 + tl
                xt = xts.next()
                self.dma("sp", xt[:], self.x1s.t[t * 128:(t + 1) * 128, :], [self.x1s.r], [xt.r])
                self.norm_tile(xt, A2c if isctx else A2, B2c if isctx else B2, hb, tmp, smalls.next())
                self.transpose_tile(hb, hT, tl)
                yield

        def ffn(i):
            t0, nt = blocks[i]
            isctx = (t0 == 0)
            N = nt * 128
            hT = hTs[i % 2]
            pws = [self.pbank(2, pin=True) for _ in range(nt)]

            def ffn1(fo):
                pb, pr = self.pbank(pin=True)
                for k in range(8):
                    self.mm(pb[:, 0:N], w1[:, k, fo * 128:(fo + 1) * 128], hT[:, k, 0:N], k == 0, k == 7, [w1.r, hT.r], pr)
                return pb, pr

            cur = ffn1(0)
            for fo in range(32):
                pb, pr = cur
                rl = rl_r.next()
                a = a_r.next()
                self.act(rl[:, 0:N], pb[:, 0:N], AF.Relu, pr, [rl.r])
                self.unpin(pr)
                if fo + 1 < 32:
                    cur = ffn1(fo + 1)
                self.tt("dve", a[:, 0:N], rl[:, 0:N], rl[:, 0:N], ALU.mult, [rl.r], [a.r])
                for tl in range(nt):
                    pw, prw = pws[tl]
                    for hf in range(2):
                        ns = slice(hf * 512, (hf + 1) * 512)
                        self.mm(pw[:, ns], a[:, tl * 128:(tl + 1) * 128], w2[:, fo, ns], fo == 0, fo == 31, [a.r, w2.r], prw)
                yield
            for tl in range(nt):
                t = t0 + tl
                cs = slice(t * 128, (t + 1) * 128)
                pw, prw = pws[tl]
                ysb = ysbs.next()
                self.cp("act", ysb[:], pw[:, :], prw, [ysb.r])
                self.unpin(prw)
                xp = xps.next()
                self.dma("sp", xp[:], self.x1s.t[cs, :], [self.x1s.r], [xp.r])
                x2 = x2s.next()
                self.post_tile(ysb, [ysb.r], xp, G2c if isctx else G2, tmp2, smalls2.next(), x2)
                if last:
                    self.dma("pool", self.y_out.t[(t - 2) * 128:(t - 1) * 128, :], x2[:], [x2.r], [self.y_out.r])
                else:
                    self.dma("pool", xnext.t[cs, :], x2[:], [x2.r], [xnext.r])
                yield

        run_gen(front(0))
        for i in range(len(blocks)):
            f = front(i + 1) if i + 1 < len(blocks) else None
            interleave((f, 2), (ffn(i), 34))
        if not last:
            self.dump(f"xres1_{l}", xnext.t, xnext.r, [T, D])


def run_gen(g):
    if g is not None:
        for _ in g:
            pass


def interleave(*pairs):
    live = [[g, 0, max(1, n)] for g, n in pairs if g is not None]
    while live:
        live.sort(key=lambda x: x[1] / x[2])
        it = live[0]
        try:
            next(it[0])
            it[1] += 1
        except StopIteration:
            live.remove(it)


class Rot:
    def __init__(self, items):
        self.items = items
        self.i = -1

    def next(self):
        self.i = (self.i + 1) % len(self.items)
        return self.items[self.i]


_PERM = np.r_[8:16, 0:8, 24:32, 16:24]


def _rope_tables(pos):
    half = 16
    inv = (10000.0 ** (-(np.arange(0, half, 2, dtype=np.float32)) / half)).astype(np.float32)
    row = (pos // 64).astype(np.float32)
    col = (pos % 64).astype(np.float32)
    ar = row[:, None] * inv
    ac = col[:, None] * inv
    ang = np.concatenate([ar, ar, ac, ac], -1).astype(np.float32)
    sign = np.concatenate([-np.ones(8), np.ones(8), -np.ones(8), np.ones(8)]).astype(np.float32)
    return np.cos(ang).astype(np.float32), (np.sin(ang) * sign).astype(np.float32)


def host_layout(inp):
    f = lambda a: np.ascontiguousarray(np.asarray(a, dtype=np.float32))
    x, c, ctx, c_ctx = f(inp["x"]), f(inp["c"]), f(inp["ctx"]), f(inp["c_ctx"])
    L = DEPTH
    w_in = f(inp["w_in"])
    s0 = 416 + 768
    shared = {}
    per_rev = {}
    for rev in (0, 1):
        wa = []
        for l in range(L):
            w = w_in[l]
            dtc = w[:, s0 + 512 + 768:s0 + 512 + 768 + 16]
            if rev:
                dtc = np.concatenate([dtc[:, 8:16], dtc[:, 0:8]], 1)
            wa.append(np.concatenate([w[:, 0:256], w[:, 256:384], w[:, 384:416], w[:, 384 + _PERM],
                                      w[:, 416:416 + 768], w[:, s0 + 512:s0 + 512 + 768], w[:, s0:s0 + 512], dtc], 1))
        d = {"w_in_aug": f(np.stack(wa))}
        al = f(inp["ssd_a_log"]); db = f(inp["ssd_dt_bias"])
        order = [1, 0] if rev else [0, 1]
        d["a_log"] = f(np.concatenate([al[:, order[0]], al[:, order[1]]], -1))
        d["dt_bias"] = f(np.concatenate([db[:, order[0]], db[:, order[1]]], -1))
        scw = f(inp["sc_conv_w"]); sw = f(inp["ssd_conv_w"])
        if rev:
            scw = scw[:, ::-1]; sw = sw[:, ::-1]
        d["sc_cw"] = f(np.transpose(scw, (0, 2, 1)))
        d["ssd_cw"] = f(np.transpose(sw, (0, 2, 1)))
        per_rev[rev] = d
    wuq = f(inp["w_uq"])
    wq = np.zeros((L, 256, 4, 160), np.float32)
    for h in range(4):
        wq[:, :, h, 0:32] = wuq[:, :, h * 96 + 64:h * 96 + 96]
        wq[:, :, h, 64:128] = wuq[:, :, h * 96:h * 96 + 64]
        wq[:, :, h, 128:160] = wuq[:, :, h * 96 + 64 + _PERM]
    shared["w_uq_aug"] = f(wq.reshape(L, 256, 640))
    wukv = f(inp["w_ukv"])
    wk = np.zeros((L, 128, 4, 128), np.float32)
    wv = np.zeros((L, 128, 4, 64), np.float32)
    for h in range(4):
        wk[:, :, h, 64:128] = wukv[:, :, h * 128:h * 128 + 64]
        wv[:, :, h, :] = wukv[:, :, h * 128 + 64:h * 128 + 128]
    shared["w_ukv_k"] = f(wk.reshape(L, 128, 512))
    shared["w_ukv_v"] = f(wv.reshape(L, 128, 256))
    shared["q_norm"] = f(inp["mla_q_norm"])
    shared["kv_norm"] = f(inp["mla_kv_norm"])
    shared["w_mod"] = f(inp["w_mod"])
    shared["b_mod"] = f(inp["b_mod"])
    shared["gvecs"] = f(np.stack([inp["g_pre_mix"], inp["g_post_mix"], inp["g_pre_ffn"], inp["g_post_ffn"]], 1))
    shared["ssd_cb"] = f(inp["ssd_conv_b"])
    shared["ssd_d"] = f(inp["ssd_d"])
    shared["ssd_norm"] = f(inp["ssd_norm"])
    shared["w_out"] = f(inp["w_out"])
    shared["w_ff1"] = f(inp["w_ff1"])
    shared["w_ff2"] = f(inp["w_ff2"])
    j = np.arange(128)
    triE = (j[:, None] <= j[None, :]).astype(np.float32)
    triL = (j[:, None] >= j[None, :]).astype(np.float32)
    in_maps = []
    for core in range(NCORES):
        b, hf = core // 2, core % 2
        xl = x[b, hf * TL:(hf + 1) * TL]
        cl = ctx[b]
        pos = np.arange(hf * TL, (hf + 1) * TL)
        if hf:
            xl, cl, pos = xl[::-1], cl[::-1], pos[::-1]
        cos, sin = _rope_tables(pos)
        rope = np.zeros((2, 32, T), np.float32)
        rope[0, :, :TC] = 1.0
        rope[0, :, TC:] = cos.T
        rope[1, :, TC:] = sin.T
        cst = np.zeros((128, 3 * 128 + 4), np.float32)
        cst[:, 0:128] = np.eye(128)
        cst[:, 128:256] = triE
        cst[:, 256:384] = triL
        cst[:, 384] = 1.0
        cst[127, 384] = 0.0
        cst[:, 385] = 1.0 - hf
        cst[:, 386] = float(hf)
        m = {"x_in": f(np.concatenate([cl, xl], 0)), "cvecT": f(np.stack([c[b], c_ctx], 1)), "ropeT": rope, "consts": cst}
        m.update(shared)
        m.update(per_rev[hf])
        in_maps.append(m)
    return in_maps


_NC_CACHE = {}


def kernel(**inputs):
    in_maps = host_layout(inputs)
    if "nc" not in _NC_CACHE:
        _NC_CACHE["nc"] = Builder().build()
    res = run_bass_kernel_spmd(_NC_CACHE["nc"], in_maps, core_ids=list(range(NCORES)))
    out = np.zeros((4, 2 * TL, D), np.float32)
    for core in range(NCORES):
        b, hf = core // 2, core % 2
        y = np.asarray(res.results[core]["y_out"], dtype=np.float32)
        out[b, hf * TL:(hf + 1) * TL] = y[::-1] if hf else y
    return out
```

```python
import numpy as np
import ml_dtypes
from contextlib import ExitStack
import concourse.bass as bass
import concourse.mybir as mybir
from concourse.bass_utils import run_bass_kernel_spmd

F32 = mybir.dt.float32
BF16 = mybir.dt.bfloat16
AF = mybir.ActivationFunctionType
ALU = mybir.AluOpType

NCORES = 8
D = 1024
DEPTH = 2
TC = 256
TL = 2048
T = TC + TL
NT = T // 128
BLOCKS = [(0, 2), (2, 4), (6, 4), (10, 4), (14, 4)]
NU = 2512
EPS = 1e-6
MLA_SCALE = 96.0 ** -0.5
NEG = -30000.0
FM = [(0, 128), (128, 128), (256, 128), (384, 32), (416, 32)] + [(448 + 128 * i, 128) for i in range(12)]
Z_OFF = 1984
DT_OFF = 2496
SM_W = 544

ENGS = ("pe", "act", "dve", "pool", "sp")


class Res:
    __slots__ = ("name", "w", "r", "sem", "ndma")

    def __init__(self, name):
        self.name = name
        self.w = None
        self.r = []
        self.sem = None
        self.ndma = 0


class Prog:
    def __init__(self, nc, strict=True):
        self.nc = nc
        self.ops = {e: [] for e in ENGS}
        self.strict = strict
        self.nres = 0
        self.dma_res = []
        self.slot_count = []
        self.slot_cls = []
        self.free_slots = {}

    def res(self, name=None):
        self.nres += 1
        return Res(name or f"r{self.nres}")

    def _deps(self, reads, writes, eng=None):
        ev = []
        for r in reads:
            if r.w is not None:
                ev.append(r.w)
        for w in writes:
            for e in ([w.w] if w.w is not None else []) + w.r:
                if eng is not None and e[0] == "E" and e[1] == eng:
                    continue
                ev.append(e)
        return ev

    def op(self, eng, fn, reads=(), writes=()):
        lst = self.ops[eng]
        idx = len(lst)
        waits = self._collect(eng, self._deps(reads, writes, eng))
        lst.append({"fn": fn, "waits": waits, "signal": False, "dma": None})
        me = ("E", eng, idx)
        for r in reads:
            r.r.append(me)
        for w in writes:
            w.w = me
            w.r = []

    def dma(self, eng, fn, src, dst, inc=16):
        srcs = src if isinstance(src, (list, tuple)) else [src]
        dsts = dst if isinstance(dst, (list, tuple)) else [dst]
        main = dsts[0]
        cls = eng if inc == 16 else "cc"
        if main.sem is None:
            fl = self.free_slots.setdefault(cls, [])
            if fl and cls != "cc":
                main.sem = fl.pop()
            else:
                main.sem = len(self.slot_count)
                self.slot_count.append(0)
                self.slot_cls.append(cls)
            main.ndma = self.slot_count[main.sem]
            self.dma_res.append(main)
        assert self.slot_cls[main.sem] == cls, (main.name, self.slot_cls[main.sem], cls)
        waits = self._collect(eng, self._deps(srcs, dsts))
        main.ndma += inc
        self.slot_count[main.sem] = main.ndma
        self.ops[eng].append({"fn": fn, "waits": waits, "signal": False, "dma": main.sem, "inc": inc})
        me = ("D", main.sem, main.ndma)
        for r in srcs:
            r.r.append(me)
        for w in dsts:
            w.w = me
            w.r = []

    def release(self, resources):
        for r in resources:
            if r.sem is not None:
                self.free_slots.setdefault(self.slot_cls[r.sem], []).append(r.sem)
                self.dma_res.remove(r)
                r.sem = None

    def _collect(self, eng, events, all_same=False):
        out = {}
        for ev in events:
            if ev[0] == "E":
                _, e2, j = ev
                if e2 == eng and (eng == "pe" or not self.strict) and not all_same:
                    continue
                key = ("E", e2)
            else:
                key = ("D", ev[1])
            if key not in out or out[key][2] < ev[2]:
                out[key] = ev
        evs = list(out.values())
        for ev in evs:
            if ev[0] == "E":
                self.ops[ev[1]][ev[2]]["signal"] = True
        return evs

    def final_wait(self, eng, resources):
        evs = []
        for r in resources:
            if r.w is not None:
                evs.append(r.w)
            evs.extend(r.r)
        self.ops[eng].append({"fn": None, "waits": self._collect(eng, evs), "signal": False, "dma": None})

    def barrier(self):
        evs = []
        for e in ENGS:
            for idx in range(len(self.ops[e]) - 1, -1, -1):
                rec = self.ops[e][idx]
                if rec["dma"] is None and rec["fn"] is not None:
                    evs.append(("E", e, idx))
                    break
        for r in self.dma_res:
            evs.append(("D", r.sem, r.ndma))
        for e in ENGS:
            self.ops[e].append({"fn": None, "waits": self._collect(e, evs, all_same=(e != "pe")),
                                "signal": False, "dma": None})

    def emit(self):
        nc = self.nc
        esem = {e: nc.alloc_semaphore(f"s_{e}") for e in ENGS}
        dsem = [nc.alloc_semaphore(f"d{i}") for i in range(len(self.slot_count))]
        cnt = {}
        for e in ENGS:
            c = 0
            arr = []
            for rec in self.ops[e]:
                if rec["signal"] and rec["dma"] is None:
                    c += 1
                arr.append(c)
            cnt[e] = arr
        ops = self.ops

        def run(e, eh):
            waited = {}
            for rec in ops[e]:
                for ev in rec["waits"]:
                    if ev[0] == "E":
                        sem = esem[ev[1]]
                        val = cnt[ev[1]][ev[2]]
                    else:
                        sem = dsem[ev[1]]
                        val = ev[2]
                    if waited.get(sem.num, 0) >= val:
                        continue
                    waited[sem.num] = val
                    eh.wait_ge(sem, val)
                if rec["fn"] is None:
                    continue
                ins = rec["fn"](eh)
                if rec["dma"] is not None:
                    ins.then_inc(dsem[rec["dma"]], rec["inc"])
                elif rec["signal"]:
                    ins.then_inc(esem[e], 1)

        with nc.Block() as block:
            @block.tensor
            def _(eh):
                run("pe", eh)

            @block.scalar
            def _(eh):
                run("act", eh)

            @block.vector
            def _(eh):
                run("dve", eh)

            @block.gpsimd
            def _(eh):
                run("pool", eh)

            @block.sync
            def _(eh):
                run("sp", eh)


class Buf:
    def __init__(self, t, r):
        self.t = t
        self.r = r

    def __getitem__(self, k):
        return self.t[k]


def bc(ap, shape):
    return ap.unsqueeze(len(ap.shape)).to_broadcast(list(shape))


def bcm(ap, shape):
    return ap.unsqueeze(1).to_broadcast(list(shape))


class Builder:
    def __init__(self, dbg=None, nlayers=DEPTH, stop_after=None, fake_cc=False, ncores=NCORES):
        self.fake_cc = fake_cc
        self.ncores = ncores
        self.nc = bass.Bass("TRN2", target_bir_lowering=False)
        self.P = Prog(self.nc)
        self.dbg = dbg or []
        self.dbg_out = {}
        self.nlayers = nlayers
        self.stop_after = stop_after
        self.scope = None
        self.scope_ab = None
        self.scope_res = []
        self.ab_res = []
        self.pcur = 0
        self.pinned = set()
        self.uid = 0

    def din(self, name, shape, dt=F32):
        return Buf(self.nc.dram_tensor(name, list(shape), dt, kind="ExternalInput").ap(), self.P.res(name))

    def dout(self, name, shape, dt=F32):
        return Buf(self.nc.dram_tensor(name, list(shape), dt, kind="ExternalOutput").ap(), self.P.res(name))

    def dscr(self, name, shape, dt=F32):
        return Buf(self.nc.dram_tensor(name, list(shape), dt, kind="Internal").ap(), self.P.res(name))

    def sb(self, name, shape, dt=F32, persist=False, ab=False):
        self.uid += 1
        nm = f"{name}_{self.uid}"
        r = self.P.res(name)
        if persist:
            t = self.nc.alloc_sbuf_tensor(nm, list(shape), dt)
        elif ab:
            t = self.scope_ab.enter_context(self.nc.sbuf_tensor(nm, list(shape), dt))
            self.ab_res.append(r)
        else:
            t = self.scope.enter_context(self.nc.sbuf_tensor(nm, list(shape), dt))
            self.scope_res.append(r)
        return Buf(t, r)

    def rot(self, name, n, shape, dt=F32):
        return Rot([self.sb(f"{name}{i}", shape, dt) for i in range(n)])

    def pbank(self, n=1, pin=False):
        while True:
            if self.pcur % n:
                self.pcur += n - self.pcur % n
            if self.pcur + n > 8:
                self.pcur = 0
            b0 = self.pcur
            self.pcur = (self.pcur + n) % 8
            if not any((b0 + i) in self.pinned for i in range(n)):
                break
        if pin:
            for i in range(n):
                self.pinned.add(b0 + i)
        self.last_b0 = b0
        return self.ps_all[:, b0 * 512:(b0 + n) * 512], [self.ps_res[b0 + i] for i in range(n)]

    def unpin(self, res_list):
        for r in res_list:
            self.pinned.discard(self.ps_res.index(r))

    def mm(self, out, lhsT, rhs, start, stop, R, W):
        self.P.op("pe", lambda e: e.matmul(out, lhsT, rhs, start=start, stop=stop), R, W)

    def tr(self, out, in_, ident, R, W):
        self.P.op("pe", lambda e: e.transpose(out, in_, ident), R, W)

    def act(self, out, in_, func, R, W, bias=None, scale=None, accum=None, eng="act"):
        kw = {}
        if bias is not None:
            kw["bias"] = bias
        if scale is not None:
            kw["scale"] = scale
        if accum is not None:
            kw["accum_out"] = accum
        self.P.op("act", lambda e: e.activation(out, in_, func, **kw), R, W)

    def tt(self, eng, out, a, b, op, R, W):
        self.P.op(eng, lambda e: e.tensor_tensor(out, a, b, op), R, W)

    def ts(self, eng, out, a, s1, s2, op0, op1, R, W):
        if op1 is None:
            self.P.op(eng, lambda e: e.tensor_scalar(out, a, s1, s2, op0), R, W)
        else:
            self.P.op(eng, lambda e: e.tensor_scalar(out, a, s1, s2, op0, op1), R, W)

    def stt(self, eng, out, a, s, b, op0, op1, R, W):
        self.P.op(eng, lambda e: e.scalar_tensor_tensor(out, a, s, b, op0, op1), R, W)

    def cp(self, eng, out, in_, R, W):
        if eng == "act":
            self.P.op("act", lambda e: e.copy(out, in_), R, W)
        else:
            self.P.op(eng, lambda e: e.tensor_copy(out, in_), R, W)

    def ms(self, eng, ap, val, W):
        self.P.op(eng, lambda e: e.memset(ap, val), [], W)

    def recip(self, out, in_, R, W):
        self.P.op("dve", lambda e: e.reciprocal(out, in_), R, W)

    def dma(self, eng, out, in_, R, W, slow_ok=False):
        if slow_ok:
            self.P.dma(eng, lambda e: e.dma_start(out=out, in_=in_, allow_slow_non_contiguous=True), R, W)
        else:
            self.P.dma(eng, lambda e: e.dma_start(out=out, in_=in_), R, W)

    def rstd(self, out, ms_ap, R, W, tmp):
        self.act(tmp[:], ms_ap, AF.Ln, R, [tmp.r], bias=self.eps_t[:, 0:1])
        self.act(out, tmp[:], AF.Exp, [tmp.r], W, scale=-0.5)

    def silu(self, out, out_res, in_, in_res, sg, bias=None, negbias=None, mul_eng="dve"):
        kw = {} if negbias is None else {"bias": negbias[0]}
        rr = list(in_res) + ([] if negbias is None else [negbias[1]])
        self.act(sg[0], in_, AF.Exp, rr, [sg[1]], scale=-1.0, **kw)
        self.act(sg[0], sg[0], AF.Ln, [sg[1]], [sg[1]], bias=1.0)
        self.act(sg[0], sg[0], AF.Exp, [sg[1]], [sg[1]], scale=-1.0)
        if bias is not None:
            self.stt(mul_eng, out, in_, bias[0], sg[0], ALU.add, ALU.mult, list(in_res) + [bias[1], sg[1]], out_res)
        else:
            self.tt(mul_eng, out, in_, sg[0], ALU.mult, list(in_res) + [sg[1]], out_res)

    def dump(self, key, src_ap, src_res, shape, dt=F32):
        if key not in self.dbg or key in self.dbg_out:
            return
        o = self.dout("dbg_" + key, shape, dt)
        self.dma("pool", o.t, src_ap, [src_res] if not isinstance(src_res, list) else src_res, [o.r])
        self.dbg_out[key] = o

    def build(self):
        nc, P = self.nc, self.P
        L = DEPTH
        self.x_in = self.din("x_in", [T, D])
        self.cvecT = self.din("cvecT", [D, 2])
        self.w_mod = self.din("w_mod", [L, D, 6 * D])
        self.b_mod = self.din("b_mod", [L, 6 * D])
        self.gvecs = self.din("gvecs", [L, 4, D])
        self.w_in = self.din("w_in_aug", [L, D, NU])
        self.q_norm = self.din("q_norm", [L, 256])
        self.w_uq = self.din("w_uq_aug", [L, 256, 640])
        self.kv_norm = self.din("kv_norm", [L, 128])
        self.w_ukv_k = self.din("w_ukv_k", [L, 128, 512])
        self.w_ukv_v = self.din("w_ukv_v", [L, 128, 256])
        self.sc_cw = self.din("sc_cw", [L, 256, 3])
        self.ssd_cw = self.din("ssd_cw", [L, 768, 3])
        self.ssd_cb = self.din("ssd_cb", [L, 768])
        self.a_log = self.din("a_log", [L, 16])
        self.dt_bias = self.din("dt_bias", [L, 16])
        self.ssd_d = self.din("ssd_d", [L, 8])
        self.ssd_norm = self.din("ssd_norm", [L, 512])
        self.w_out = self.din("w_out", [L, D, D])
        self.w_ff1 = self.din("w_ff1", [L, D, 4 * D])
        self.w_ff2 = self.din("w_ff2", [L, 4 * D, D])
        self.ropeT = self.din("ropeT", [2, 32, T])
        self.consts = self.din("consts", [128, 3 * 128 + 4])
        self.y_out = self.dout("y_out", [TL, D])
        self.xres1 = self.dscr("xres1", [T, D])
        self.x1s = self.dscr("x1s", [T, D])
        self.sc_z = self.dscr("sc_z", [T, 512])
        self.sc_xs = self.dscr("sc_xs", [T, 512])
        self.sc_bt = self.dscr("sc_bt", [T, 128], BF16)
        self.bv = self.dscr("bv", [12, 128, D])
        self.kv_ctx_k = self.dscr("kv_ctx_k", [128, 4, TC], BF16)
        self.kv_ctx_v = self.dscr("kv_ctx_v", [128, 2, 260], BF16)
        self.xk = self.dscr("xk", [512, TL], BF16)
        self.xk_g = self.dscr("xk_g", [1024, TL], BF16)
        self.xv = self.dscr("xv", [2048, 260], BF16)
        self.xv_g = self.dscr("xv_g", [4096, 260], BF16)
        self.xs_ = self.dscr("xsm", [128, SM_W])
        self.xs_g = self.dscr("xsm_g", [256, SM_W])

        self.ps_all = nc.alloc_psum_tensor("ps_all", [128, 8 * 512], F32)
        self.ps_res = [P.res(f"bank{i}") for i in range(8)]

        cst = self.sb("cst", [128, 3 * 128 + 4], persist=True)
        self.dma("sp", cst[:], self.consts.t, [self.consts.r], [cst.r])
        self.cst = cst
        self.ident_f = Buf(cst.t[:, 0:128], cst.r)
        self.triE = Buf(cst.t[:, 128:256], cst.r)
        self.triL = Buf(cst.t[:, 256:384], cst.r)
        self.lastmask = Buf(cst.t[:, 384:385], cst.r)
        self.esel = Buf(cst.t[:, 385:387], cst.r)
        self.ident_b = self.sb("ident_b", [128, 128], BF16, persist=True)
        self.cp("dve", self.ident_b[:], self.ident_f[:], [cst.r], [self.ident_b.r])
        self.ones_f = self.sb("ones_f", [128, 128], persist=True)
        self.ms("pool", self.ones_f[:], 1.0, [self.ones_f.r])
        self.eps_t = self.sb("eps_t", [128, 1], persist=True)
        self.ms("pool", self.eps_t[:], EPS, [self.eps_t.r])
        self.negm = self.sb("negm", [128, 2, 512], BF16, persist=True)
        for d_, tri in enumerate((self.triE, self.triL)):
            self.ts("dve", self.negm[:, d_, :].rearrange("p (a b) -> p a b", a=4), bcm(tri[:], [128, 4, 128]),
                    -1.0, -NEG, ALU.add, ALU.mult, [cst.r], [self.negm.r])
        self.sT = self.sb("sT", [128, 8, 2], persist=True)
        cv = self.sb("cv", [128, 8, 2], persist=True)
        self.dma("sp", cv[:], self.cvecT.t.rearrange("(k p) r -> p k r", p=128), [self.cvecT.r], [cv.r])
        self.act(self.sT[:], cv[:], AF.Silu, [cv.r], [self.sT.r])
        self.dt_all = self.sb("dt_all", [128, NT, 16], persist=True)

        out_res = []
        xres = self.x_in
        for l in range(self.nlayers):
            last = (l == DEPTH - 1)
            xnext = None if last else self.xres1
            self.layer(l, xres, xnext, last)
            xres = xnext
            if self.stop_after is not None and self.stop_after[0] == l:
                break
        P.final_wait("sp", [self.y_out.r] + [o.r for o in self.dbg_out.values()])
        P.final_wait("pool", [self.y_out.r] + [o.r for o in self.dbg_out.values()])
        P.emit()
        return nc

    def end_scope(self):
        if self.scope is not None:
            self.P.barrier()
            self.scope.close()
            self.scope = None
            self.P.release(self.scope_res)
            self.scope_res = []

    def new_scope(self):
        self.end_scope()
        self.scope = ExitStack()

    def open_ab(self):
        self.end_scope()
        self.scope_ab = ExitStack()

    def close_ab(self):
        self.end_scope()
        if self.scope_ab is not None:
            self.scope_ab.close()
            self.scope_ab = None
            self.P.release(self.ab_res)
            self.ab_res = []

    def stop(self, l, tag):
        return self.stop_after is not None and self.stop_after == (l, tag)

    def layer(self, l, xres, xnext, last):
        self.phase_mod(l)
        if self.stop(l, "mod"):
            return
        self.phase_a(l, xres, last)
        if self.stop(l, "a"):
            return
        self.exchange(l)
        self.phase_b(l, xres, last)
        if self.stop_after is not None and self.stop_after[0] == l and self.stop_after[1] in ("b", "b0", "b1", "b2", "t1", "t2", "t3", "t4", "u1", "u2", "u3", "v1", "v2"):
            return
        self.phase_c(l, xnext, last)

    def phase_mod(self, l):
        self.new_scope()
        modsb = self.sb("modsb", [2, 6 * D])
        bm = self.sb("bm", [2, 6 * D])
        self.dma("sp", bm[:], self.b_mod.t[l].partition_broadcast(2), [self.b_mod.r], [bm.r])
        gv = self.sb("gv", [2, 4 * D])
        self.dma("sp", gv[:], self.gvecs.t[l].rearrange("a d -> (a d)").partition_broadcast(2),
                 [self.gvecs.r], [gv.r])
        slabs = self.rot("wm", 2, [128, 8, 512])
        wmv = self.w_mod.t[l].rearrange("(k p) n -> p k n", p=128)
        for n in range(12):
            sl = slabs.next()
            self.dma("sp", sl[:], wmv[:, :, n * 512:(n + 1) * 512], [self.w_mod.r], [sl.r])
            pb, pr = self.pbank()
            for k in range(8):
                self.mm(pb[0:2, :], self.sT[:, k, :], sl[:, k, :], k == 0, k == 7, [self.sT.r, sl.r], pr)
            self.tt("dve", modsb[:, n * 512:(n + 1) * 512], pb[0:2, :], bm[:, n * 512:(n + 1) * 512], ALU.add,
                    pr + [bm.r], [modsb.r])
        cmb = self.sb("cmb", [2, 6, D])
        m = lambda j: modsb[:, j * D:(j + 1) * D]
        g = lambda j: gv[:, j * D:(j + 1) * D]
        R = [modsb.r, gv.r]
        self.stt("dve", cmb[:, 0, :], m(1), 1.0, g(0), ALU.add, ALU.mult, R, [cmb.r])
        self.cp("dve", cmb[:, 1, :], m(0), R, [cmb.r])
        self.tt("dve", cmb[:, 2, :], m(2), g(1), ALU.mult, R, [cmb.r])
        self.stt("dve", cmb[:, 3, :], m(4), 1.0, g(2), ALU.add, ALU.mult, R, [cmb.r])
        self.cp("dve", cmb[:, 4, :], m(3), R, [cmb.r])
        self.tt("dve", cmb[:, 5, :], m(5), g(3), ALU.mult, R, [cmb.r])
        sel = self.sb("sel", [2, 2, 128])
        self.cp("dve", sel[:, 0, :], self.ident_f.t[0:2, 0:1].to_broadcast([2, 128]), [self.cst.r], [sel.r])
        self.cp("dve", sel[:, 1, :], self.ident_f.t[0:2, 1:2].to_broadcast([2, 128]), [self.cst.r], [sel.r])
        stg = self.rot("bvst", 2, [128, D])
        for s in range(2):
            for j in range(6):
                st = stg.next()
                for hlf in range(2):
                    pb, pr = self.pbank()
                    self.mm(pb[:, :], sel[:, s, :], cmb[:, j, hlf * 512:(hlf + 1) * 512], True, True,
                            [sel.r, cmb.r], pr)
                    self.cp("act", st[:, hlf * 512:(hlf + 1) * 512], pb[:, :], pr, [st.r])
                self.dma("sp", self.bv.t[6 * s + j], st[:], [st.r], [self.bv.r])
        self.dump(f"bv{l}", self.bv.t, self.bv.r, [12, 128, D])

    def esel_rows(self):
        return self.ident_f.t[0:2, 0:2]

    def load_bv(self, idx, name):
        b = self.sb(name, [128, D])
        self.dma("sp", b[:], self.bv.t[idx], [self.bv.r], [b.r])
        return b

    def norm_tile(self, xt, A, Bv, hb, tmp, small):
        ms_, ln_, rs_ = small
        self.act(hb[:], xt[:], AF.Square, [xt.r], [hb.r, ms_.r], scale=1.0 / 32.0, accum=ms_[:])
        self.rstd(rs_[:], ms_[:], [ms_.r], [rs_.r], ln_)
        self.stt("dve", tmp[:], xt[:], rs_[:, 0:1], A[:], ALU.mult, ALU.mult, [xt.r, rs_.r, A.r], [tmp.r])
        self.tt("dve", hb[:], tmp[:], Bv[:], ALU.add, [tmp.r, Bv.r], [hb.r])

    def post_tile(self, pw, prw, xt, G, tmp, small, xo):
        ms_, ln_, rs_ = small
        self.act(tmp[:], pw[:, :], AF.Square, prw, [tmp.r, ms_.r], scale=1.0 / 32.0, accum=ms_[:])
        self.rstd(rs_[:], ms_[:], [ms_.r], [rs_.r], ln_)
        self.stt("dve", tmp[:], pw[:, :], rs_[:, 0:1], G[:], ALU.mult, ALU.mult, prw + [rs_.r, G.r], [tmp.r])
        self.tt("dve", xo[:], xt[:], tmp[:], ALU.add, [xt.r, tmp.r], [xo.r])

    def transpose_tile(self, hb, hT, tl):
        pb, pr = self.pbank()
        pbb = pb.bitcast(BF16)
        for k in range(8):
            self.tr(pbb[:, k * 128:(k + 1) * 128], hb[:, k * 128:(k + 1) * 128], self.ident_b[:],
                    [hb.r, self.ident_b.r], pr)
        self.cp("act", hT[:, :, tl * 128:(tl + 1) * 128], pbb[:, :].rearrange("p (k t) -> p k t", k=8), pr, [hT.r])

    def small3(self, n):
        return Rot([(self.sb("ms", [128, 1]), self.sb("ln", [128, 1]), self.sb("rs", [128, 1])) for _ in range(n)])

    def phase_a(self, l, xres, last):
        nc, P = self.nc, self.P
        self.open_ab()
        self.qT = self.sb("qT", [128, 4, T], BF16, ab=True)
        self.ysc = self.sb("ysc", [128, 2, T], BF16, ab=True)
        self.BT = self.sb("BT", [128, T], BF16, ab=True)
        self.CT = self.sb("CT", [128, T], BF16, ab=True)
        self.HEs = self.sb("HEs", [128, NT, 256], BF16, ab=True)
        self.fix = self.sb("fix", [128, 16], ab=True)
        self.xsT_last = self.sb("xsT_last", [128, 4, 128], ab=True)
        self.new_scope()
        NB = 256
        win = self.sb("win", [128, 8, NU], BF16)
        for k in range(8):
            self.dma("pool", win[:, k, :], self.w_in.t[l, k * 128:(k + 1) * 128, :], [self.w_in.r], [win.r])
        wst = self.sb("wst", [128, 2, 640])
        self.dma("sp", wst[:], self.w_uq.t[l].rearrange("(c p) n -> p c n", p=128), [self.w_uq.r], [wst.r])
        gq = self.sb("gq", [128, 2])
        self.dma("sp", gq[:], self.q_norm.t[l].rearrange("(c p) -> p c", p=128), [self.q_norm.r], [gq.r], slow_ok=True)
        wuq = self.sb("wuq", [128, 2, 640], BF16)
        for c in range(2):
            self.ts("dve", wuq[:, c, :], wst[:, c, :], gq[:, c:c + 1], None, ALU.mult, None, [wst.r, gq.r], [wuq.r])
        wst2 = self.sb("wst2", [128, 768])
        self.dma("sp", wst2[:, 0:512], self.w_ukv_k.t[l], [self.w_ukv_k.r], [wst2.r])
        self.dma("sp", wst2[:, 512:768], self.w_ukv_v.t[l], [self.w_ukv_v.r], [wst2.r])
        gkv = self.sb("gkv", [128, 1])
        self.dma("sp", gkv[:], self.kv_norm.t[l].rearrange("(p o) -> p o", o=1), [self.kv_norm.r], [gkv.r])
        wkv = self.sb("wkv", [128, 768], BF16)
        self.ts("dve", wkv[:], wst2[:], gkv[:, 0:1], None, ALU.mult, None, [wst2.r, gkv.r], [wkv.r])
        sccw, cw, cb = self.conv_consts(l)
        st = self.ssd_consts(l)
        A1 = self.load_bv(6, "A1")
        B1 = self.load_bv(7, "B1")
        HE = self.sb("HE", [128, 256])
        self.ms("dve", HE[:], 0.0, [HE.r])

        xts = self.rot("xt", 2, [128, D])
        tmp = self.sb("tmp", [128, D])
        hb = self.sb("hb", [128, D], BF16)
        hT = self.sb("hT", [128, 8, NB], BF16)
        smalls = self.small3(2)
        zs = self.rot("zs", 2, [128, 512])
        dtt = self.sb("dtt", [128, 16])
        rqb = self.sb("rqb", [128, NB])
        rkvb = self.sb("rkvb", [128, NB])
        lnb = self.sb("lnb", [128, NB])
        cosr = self.sb("cosr", [32, 2, NB])
        t12 = self.sb("t12", [32, 2, NB])
        rkc = self.sb("rkc", [128, 4])
        rkl = self.sb("rkl", [128, 4])
        KTb = self.sb("KTb", [128, 4, NB], BF16)
        Vb = self.sb("Vb", [128, 2, 260], BF16)
        self.ms("pool", Vb[:], 1.0, [Vb.r])
        xbc = self.rot("xbc", 2, [128, 6, NB + 2])
        pbuf = self.rot("pbuf", 2, [128, 2, NB + 2])
        gbb = self.rot("gbb", 2, [128, 2, NB])
        gcs = self.sb("gcs", [128, NB])
        W = dict(sg=self.rot("sg", 2, [128, NB]), cvo=self.rot("cvo", 2, [128, NB]), xsT=self.sb("xsT", [128, 4, NB]), BCf=self.sb("BCf", [128, 2, NB]),
                 xstm=self.rot("xstm", 2, [128, 512]), btm=self.rot("btm", 2, [128, 128], BF16),
                 sm16=self.sb("sm16", [128, 16]), vt=self.sb("vt", [128, 16]), wE=self.sb("wE", [128, 8]),
                 decE=self.sb("decE", [128, 4]), xdtw=self.rot("xdtw", 2, [128, 512], BF16),
                 hetmp=self.sb("hetmp", [128, 256]), HE=HE, sccw=sccw, cw=cw, cb=cb, st=st)

        blocks = [(2 * i, 2) for i in range(NT // 2)]
        nb = len(blocks)
        S = dict(win=win, wuq=wuq, wkv=wkv, A1=A1, B1=B1, xts=xts, tmp=tmp, hb=hb, hT=hT, smalls=smalls, zs=zs, dtt=dtt,
                 st=st, rqb=rqb, rkvb=rkvb, lnb=lnb, cosr=cosr, t12=t12, rkc=rkc, rkl=rkl, KTb=KTb, Vb=Vb, gcs=gcs,
                 mla=[dict(cqT=self.sb("cqT", [128, 2, NB], BF16), cqsq=self.sb("cqsq", [128, 2, NB]),
                           ckvT=self.sb("ckvT", [128, NB], BF16), ckvsq=self.sb("ckvsq", [128, NB]),
                           krs=self.sb("krs", [32, 2, NB]), rope=self.sb("rope", [32, 2, NB])) for _ in range(2)],
                 slot=[dict(xb=xbc.next(), pbf=pbuf.next(), gb=gbb.next()) for _ in range(2)])

        def front1(bi):
            t0, nt = blocks[bi]
            N = nt * 128
            c0 = t0 * 128
            M = S["mla"][bi % 2]
            if bi == 1:
                self.dma("sp", A1[:], self.bv.t[0], [self.bv.r], [A1.r])
                self.dma("sp", B1[:], self.bv.t[1], [self.bv.r], [B1.r])
            for tl in range(nt):
                t = t0 + tl
                xt = xts.next()
                self.dma("sp", xt[:], xres.t[t * 128:(t + 1) * 128, :], [xres.r], [xt.r])
                self.norm_tile(xt, A1, B1, hb, tmp, smalls.next())
                self.transpose_tile(hb, hT, tl)
                yield
            for tl in range(nt):
                t = t0 + tl
                pb, pr = self.pbank()
                for k in range(8):
                    self.mm(pb[:, :], hT[:, k, tl * 128:(tl + 1) * 128], win[:, k, Z_OFF:Z_OFF + 512], k == 0, k == 7,
                            [hT.r, win.r], pr)
                z = zs.next()
                self.cp("act", z[:], pb[:, :], pr, [z.r])
                self.dma("pool", self.sc_z.t[t * 128:(t + 1) * 128, :], z[:], [z.r], [self.sc_z.r])
                pb, pr = self.pbank()
                for k in range(8):
                    self.mm(pb[:, 0:16], hT[:, k, tl * 128:(tl + 1) * 128], win[:, k, DT_OFF:DT_OFF + 16], k == 0, k == 7,
                            [hT.r, win.r], pr)
                self.tt("dve", dtt[:], pb[:, 0:16], st["dtb"][:], ALU.add, pr + [st["dtb"].r], [dtt.r])
                self.act(dtt[:], dtt[:], AF.Exp, [dtt.r], [dtt.r])
                self.act(self.dt_all[:, t, :], dtt[:], AF.Ln, [dtt.r], [self.dt_all.r], bias=1.0)
                yield
            self.dma("sp", M["rope"][:, :, 0:N], self.ropeT.t[:, :, c0:c0 + N].rearrange("a p t -> p a t"),
                     [self.ropeT.r], [M["rope"].r])
            for ci in range(5):
                off, wdt = FM[ci]
                pb, pr = self.pbank()
                for k in range(8):
                    self.mm(pb[0:wdt, 0:N], win[:, k, off:off + wdt], hT[:, k, 0:N], k == 0, k == 7, [hT.r, win.r], pr)
                src = pb[0:wdt, 0:N]
                if ci < 2:
                    self.cp("act", M["cqT"][:, ci, 0:N], src, pr, [M["cqT"].r])
                    self.act(M["cqsq"][:, ci, 0:N], src, AF.Square, pr, [M["cqsq"].r])
                elif ci == 2:
                    self.cp("act", M["ckvT"][:, 0:N], src, pr, [M["ckvT"].r])
                    self.act(M["ckvsq"][:, 0:N], src, AF.Square, pr, [M["ckvsq"].r])
                else:
                    self.cp("act", M["krs"][:, ci - 3, 0:N], src, pr, [M["krs"].r])
                yield

        def front2(bi):
            t0, nt = blocks[bi]
            N = nt * 128
            sl = S["slot"][bi % 2]
            xb, pbf, gb = sl["xb"], sl["pbf"], sl["gb"]
            for ci in (5, 6, 7, 8, 11, 12, 13, 14, 15, 16):
                off, wdt = FM[ci]
                pb, pr = self.pbank()
                for k in range(8):
                    self.mm(pb[0:wdt, 0:N], win[:, k, off:off + wdt], hT[:, k, 0:N], k == 0, k == 7, [hT.r, win.r], pr)
                src = pb[0:wdt, 0:N]
                if ci < 7:
                    self.cp("act", gb[:, ci - 5, 0:N], src, pr, [gb.r])
                elif ci < 9:
                    c = ci - 7
                    pbv, prv = self.pbank()
                    o2, w2 = FM[ci + 2]
                    for k in range(8):
                        self.mm(pbv[:, 0:N], win[:, k, o2:o2 + w2], hT[:, k, 0:N], k == 0, k == 7, [hT.r, win.r], prv)
                    self.cp("act", gcs[:, 0:N], src, pr, [gcs.r])
                    self.tt("dve", pbf[:, c, 1:N + 1], gcs[:, 0:N], pbv[:, 0:N], ALU.mult, [gcs.r] + prv, [pbf.r])
                else:
                    self.cp("act", xb[:, ci - 11, 1:N + 1], src, pr, [xb.r])
                yield
            self.ms("pool", xb[:, :, 0:1], 0.0, [xb.r])
            self.ms("pool", pbf[:, :, 0:1], 0.0, [pbf.r])
            self.ms("pool", xb[:, :, N + 1:N + 2], 0.0, [xb.r])
            self.ms("pool", pbf[:, :, N + 1:N + 2], 0.0, [pbf.r])
            yield

        def mla_post(bi):
            t0, nt = blocks[bi]
            N = nt * 128
            c0 = t0 * 128
            M = S["mla"][bi % 2]
            cqT, cqsq, ckvT, ckvsq, krs, rope = M["cqT"], M["cqsq"], M["ckvT"], M["ckvsq"], M["krs"], M["rope"]
            pb, pr = self.pbank()
            for c in range(2):
                self.mm(pb[:, 0:N], self.ones_f[:], cqsq[:, c, 0:N], c == 0, c == 1, [self.ones_f.r, cqsq.r], pr)
            self.act(lnb[:, 0:N], pb[:, 0:N], AF.Ln, pr + [self.eps_t.r], [lnb.r], bias=self.eps_t[:, 0:1], scale=1.0 / 256)
            self.act(rqb[:, 0:N], lnb[:, 0:N], AF.Exp, [lnb.r], [rqb.r], scale=-0.5)
            pb, pr = self.pbank()
            self.mm(pb[:, 0:N], self.ones_f[:], ckvsq[:, 0:N], True, True, [self.ones_f.r, ckvsq.r], pr)
            self.act(lnb[:, 0:N], pb[:, 0:N], AF.Ln, pr + [self.eps_t.r], [lnb.r], bias=self.eps_t[:, 0:1], scale=1.0 / 128)
            self.act(rkvb[:, 0:N], lnb[:, 0:N], AF.Exp, [lnb.r], [rkvb.r], scale=-0.5)
            pb, pr = self.pbank()
            for tl in range(nt):
                self.mm(pb[:, tl:tl + 1], ckvsq[:, tl * 128:(tl + 1) * 128], self.ones_f[:, 0:1], True, True,
                        [ckvsq.r, self.ones_f.r], pr)
            self.act(rkl[:, 0:nt], pb[:, 0:nt], AF.Ln, pr + [self.eps_t.r], [rkl.r], bias=self.eps_t[:, 0:1], scale=1.0 / 128)
            self.act(rkc[:, 0:nt], rkl[:, 0:nt], AF.Exp, [rkl.r], [rkc.r], scale=-0.5)
            yield
            for a in range(2):
                self.tt("pool", cosr[:, a, 0:N], rope[:, a, 0:N], rqb[0:32, 0:N], ALU.mult, [rope.r, rqb.r], [cosr.r])
            for h in range(4):
                pa, pra = self.pbank()
                for c in range(2):
                    self.mm(pa[:, 0:N], wuq[:, c, h * 160:h * 160 + 128], cqT[:, c, 0:N], c == 0, c == 1, [wuq.r, cqT.r], pra)
                pq, prq = self.pbank()
                for c in range(2):
                    self.mm(pq[0:32, 0:N], wuq[:, c, h * 160 + 128:h * 160 + 160], cqT[:, c, 0:N], c == 0, c == 1,
                            [wuq.r, cqT.r], prq)
                self.tt("dve", self.qT[:, h, c0:c0 + N], pa[:, 0:N], rqb[:, 0:N], ALU.mult, pra + [rqb.r], [self.qT.r])
                self.tt("dve", t12[:, 0, 0:N], pa[0:32, 0:N], cosr[:, 0, 0:N], ALU.mult, pra + [cosr.r], [t12.r])
                self.tt("dve", t12[:, 1, 0:N], pq[0:32, 0:N], cosr[:, 1, 0:N], ALU.mult, prq + [cosr.r], [t12.r])
                self.tt("dve", self.qT[0:32, h, c0:c0 + N], t12[:, 0, 0:N], t12[:, 1, 0:N], ALU.add, [t12.r], [self.qT.r])
                yield
            self.tt("pool", t12[:, 0, 0:N], krs[:, 0, 0:N], rope[:, 0, 0:N], ALU.mult, [krs.r, rope.r], [t12.r])
            self.tt("pool", t12[:, 1, 0:N], krs[:, 1, 0:N], rope[:, 1, 0:N], ALU.mult, [krs.r, rope.r], [t12.r])
            self.tt("pool", t12[:, 0, 0:N], t12[:, 0, 0:N], t12[:, 1, 0:N], ALU.add, [t12.r], [t12.r])
            for h in range(4):
                pk, prk = self.pbank()
                self.mm(pk[:, 0:N], wkv[:, h * 128:(h + 1) * 128], ckvT[:, 0:N], True, True, [wkv.r, ckvT.r], prk)
                self.tt("dve", KTb[:, h, 0:N], pk[:, 0:N], rkvb[:, 0:N], ALU.mult, prk + [rkvb.r], [KTb.r])
                self.cp("pool", KTb[0:32, h, 0:N], t12[:, 0, 0:N], [t12.r], [KTb.r])
            yield
            if bi == 0:
                self.dma("pool", self.kv_ctx_k.t, KTb[:, :, 0:N], [KTb.r], [self.kv_ctx_k.r])
            else:
                lo = c0 - TC
                self.dma("pool", self.xk.t.rearrange("(h p) t -> p h t", p=128)[:, :, lo:lo + N], KTb[:, :, 0:N],
                         [KTb.r], [self.xk.r])
            for tl in range(nt):
                pv, prv = self.pbank()
                self.mm(pv[:, 0:256], ckvT[:, tl * 128:(tl + 1) * 128], wkv[:, 512:768], True, True, [ckvT.r, wkv.r], prv)
                self.ts("dve", Vb[:, tl, :].rearrange("p (h e) -> p h e", h=4)[:, :, 0:64],
                        pv[:, 0:256].rearrange("p (h e) -> p h e", h=4), rkc[:, tl:tl + 1], None, ALU.mult, None,
                        prv + [rkc.r], [Vb.r])
            if bi == 0:
                self.dma("pool", self.kv_ctx_v.t, Vb[:, 0:2, :], [Vb.r], [self.kv_ctx_v.r])
            else:
                lo = t0 - 2
                self.dma("pool", self.xv.t.rearrange("(t p) e -> p t e", p=128)[:, lo:lo + nt, :], Vb[:, 0:nt, :],
                         [Vb.r], [self.xv.r])
            yield

        def back1(bi):
            if bi == 0:
                return
            p = S["slot"][(bi - 1) % 2]
            pt0, pnt = blocks[bi - 1]
            pN = pnt * 128
            if 2 <= bi < nb:
                c = S["slot"][bi % 2]
                self.cp("pool", p["xb"][:, :, pN + 1:pN + 2], c["xb"][:, :, 1:2], [c["xb"].r], [p["xb"].r])
                self.cp("pool", p["pbf"][:, :, pN + 1:pN + 2], c["pbf"][:, :, 1:2], [c["pbf"].r], [p["pbf"].r])
                self.cp("pool", c["xb"][:, :, 0:1], p["xb"][:, :, pN:pN + 1], [p["xb"].r], [c["xb"].r])
                self.cp("pool", c["pbf"][:, :, 0:1], p["pbf"][:, :, pN:pN + 1], [p["pbf"].r], [c["pbf"].r])
            yield from self.stage_a2_conv(l, p["xb"], p["pbf"], p["gb"], pt0, pnt, W, bi == nb)

        def back2(bi):
            if bi < nb:
                yield from mla_post(bi)
            if bi >= 1:
                pt0, pnt = blocks[bi - 1]
                yield from self.stage_a2_scan(l, pt0, pnt, W, bi == nb)

        run_gen(front1(0))
        run_gen(front2(0))
        for bi in range(nb + 1):
            f1 = front1(bi + 1) if bi + 1 < nb else None
            f2 = front2(bi + 1) if bi + 1 < nb else None
            interleave((f1, 9), (back1(bi), 8))
            interleave((f2, 11), (back2(bi), 10))
        pxb, ppb = S["slot"][(nb - 1) % 2]["xb"], S["slot"][(nb - 1) % 2]["pbf"]
        pnt = blocks[-1][1]
        N = pnt * 128
        sm = self.sb("sm", [128, SM_W])
        self.ms("dve", sm[:], 0.0, [sm.r])
        self.cp("dve", sm[:, 0:256], HE[:], [HE.r], [sm.r])
        self.cp("dve", sm[:, 512:524].rearrange("p (c k) -> p c k", c=6), pxb[:, :, N - 1:N + 1], [pxb.r], [sm.r])
        self.cp("dve", sm[:, 524:526], ppb[:, :, N], [ppb.r], [sm.r])
        self.cp("dve", self.fix[:, 10:16], pxb[:, :, N], [pxb.r], [self.fix.r])
        self.dma("pool", self.xs_.t, sm[:], [sm.r], [self.xs_.r])
        self.dma("pool", self.xs_.t[0:1, 528:536], self.dt_all[127:128, NT - 1, 0:8], [self.dt_all.r], [self.xs_.r])
        self.dump(f"qT{l}", self.qT[:], self.qT.r, [128, 4, T], BF16)
        self.dump(f"ysc{l}", self.ysc[:], self.ysc.r, [128, 2, T], BF16)
        self.dump(f"dt{l}", self.dt_all[:], self.dt_all.r, [128, NT, 16])
        self.dump(f"HE{l}", HE[:], HE.r, [128, 256])
        self.dump(f"BT{l}", self.BT[:], self.BT.r, [128, T], BF16)
        self.dump(f"CT{l}", self.CT[:], self.CT.r, [128, T], BF16)
        self.dump(f"xk{l}", self.xk.t, self.xk.r, [512, TL], BF16)
        self.dump(f"xv{l}", self.xv.t, self.xv.r, [2048, 260], BF16)
        self.dump(f"scxs{l}", self.sc_xs.t, self.sc_xs.r, [T, 512])
        self.dump(f"scz{l}", self.sc_z.t, self.sc_z.r, [T, 512])

    def conv_consts(self, l):
        sccw = self.sb("sccw", [128, 2, 3])
        self.dma("sp", sccw[:], self.sc_cw.t[l].rearrange("(c p) k -> p c k", p=128), [self.sc_cw.r], [sccw.r])
        cw = self.sb("cw", [128, 6, 3])
        self.dma("sp", cw[:], self.ssd_cw.t[l].rearrange("(c p) k -> p c k", p=128), [self.ssd_cw.r], [cw.r])
        cb = self.sb("cb", [128, 6])
        self.dma("sp", cb[:], self.ssd_cb.t[l].rearrange("(c p) -> p c", p=128), [self.ssd_cb.r], [cb.r], slow_ok=True)
        self.ncb = self.sb("ncb", [128, 6])
        self.ts("dve", self.ncb[:], cb[:], -1.0, None, ALU.mult, None, [cb.r], [self.ncb.r])
        return sccw, cw, cb

    def ssd_consts(self, l):
        st = {}
        al = self.sb("al", [128, 16])
        self.dma("sp", al[:], self.a_log.t[l].partition_broadcast(128), [self.a_log.r], [al.r])
        ab = self.sb("ab", [128, 16])
        self.act(ab[:], al[:], AF.Exp, [al.r], [ab.r])
        self.ts("dve", ab[:], ab[:], -1.0, None, ALU.mult, None, [ab.r], [ab.r])
        st["ab"] = ab
        dtb = self.sb("dtb", [128, 16])
        self.dma("sp", dtb[:], self.dt_bias.t[l].partition_broadcast(128), [self.dt_bias.r], [dtb.r])
        st["dtb"] = dtb
        return st

    def stage_a2_conv(self, l, xb, pbf, gb, t0, nt, W, is_last):
        N = nt * 128
        c0 = t0 * 128
        xsT, BCf, sccw, cw, cb = W["xsT"], W["BCf"], W["sccw"], W["cw"], W["cb"]
        for c in range(2):
            cvo = W["cvo"].next()
            self.conv3(cvo, pbf, c, sccw, N)
            self.tt("dve", self.ysc[:, c, c0:c0 + N], cvo[:, 0:N], gb[:, c, 0:N], ALU.mult, [cvo.r, gb.r], [self.ysc.r])
            if is_last:
                self.cp("dve", self.fix[:, 6 + c:7 + c], cvo[:, N - 1:N], [cvo.r], [self.fix.r])
                self.cp("dve", self.fix[:, 8 + c:9 + c], gb[:, c, N - 1:N], [gb.r], [self.fix.r])
            yield
        for c in range(6):
            cvo = W["cvo"].next()
            self.conv3(cvo, xb, c, cw, N)
            if is_last:
                self.cp("dve", self.fix[:, c:c + 1], cvo[:, N - 1:N], [cvo.r], [self.fix.r])
            dst = xsT[:, c, 0:N] if c < 4 else BCf[:, c - 4, 0:N]
            dres = xsT.r if c < 4 else BCf.r
            sg = W["sg"].next()
            self.silu(dst, [dres], cvo[:, 0:N], [cvo.r], (sg[:, 0:N], sg.r), bias=(cb[:, c:c + 1], cb.r),
                      negbias=(self.ncb[:, c:c + 1], self.ncb.r))
            yield
        self.cp("pool", self.BT[:, c0:c0 + N], BCf[:, 0, 0:N], [BCf.r], [self.BT.r])
        self.cp("pool", self.CT[:, c0:c0 + N], BCf[:, 1, 0:N], [BCf.r], [self.CT.r])
        if is_last:
            self.cp("pool", self.xsT_last[:], xsT[:, :, N - 128:N], [xsT.r], [self.xsT_last.r])

    def stage_a2_scan(self, l, t0, nt, W, is_last):
        xsT, BCf = W["xsT"], W["BCf"]
        for tl in range(nt):
            t = t0 + tl
            xs_tm, b_tm = self.tokmajor_xs_b(xsT, BCf, tl, t, W["xstm"], W["btm"])
            self.cp("act", self.HEs[:, t, :], W["HE"][:], [W["HE"].r], [self.HEs.r])
            yield
            self.state_update(W["st"], xs_tm, b_tm, t, 0, W["sm16"], W["vt"], W["wE"], W["decE"], W["xdtw"], W["hetmp"],
                              W["HE"], mask_last=(is_last and tl == nt - 1))
            yield

    def conv3(self, cvo, buf, c, wts, N):
        self.ts("dve", cvo[:, 0:N], buf[:, c, 1:N + 1], wts[:, c, 1:2], None, ALU.mult, None, [buf.r, wts.r], [cvo.r])
        self.stt("dve", cvo[:, 0:N], buf[:, c, 0:N], wts[:, c, 0:1], cvo[:, 0:N], ALU.mult, ALU.add,
                 [buf.r, wts.r, cvo.r], [cvo.r])
        self.stt("dve", cvo[:, 0:N], buf[:, c, 2:N + 2], wts[:, c, 2:3], cvo[:, 0:N], ALU.mult, ALU.add,
                 [buf.r, wts.r, cvo.r], [cvo.r])

    def tokmajor_xs_b(self, xsT, BCf, tl, t, xstm, btm):
        pb, pr = self.pbank()
        for c in range(4):
            self.tr(pb[:, c * 128:(c + 1) * 128], xsT[:, c, tl * 128:(tl + 1) * 128], self.ident_f[:],
                    [xsT.r, self.cst.r], pr)
        xs_tm = xstm.next()
        self.cp("act", xs_tm[:], pb[:, :], pr, [xs_tm.r])
        self.dma("pool", self.sc_xs.t[t * 128:(t + 1) * 128, :], xs_tm[:], [xs_tm.r], [self.sc_xs.r])
        pb2, pr2 = self.pbank()
        self.tr(pb2[:, 0:128], BCf[:, 0, tl * 128:(tl + 1) * 128], self.ident_f[:], [BCf.r, self.cst.r], pr2)
        b_tm = btm.next()
        self.cp("act", b_tm[:], pb2[:, 0:128], pr2, [b_tm.r])
        self.dma("pool", self.sc_bt.t[t * 128:(t + 1) * 128, :], b_tm[:], [b_tm.r], [self.sc_bt.r])
        return xs_tm, b_tm

    def state_update(self, st, xs_tm, b_tm, t, d_, sm16, vt, wE, decE, xdtw, hetmp, H, mask_last=False, vt_ready=None):
        tri = self.triE if d_ == 0 else self.triL
        o = d_ * 8
        if vt_ready is None:
            la = sm16
            self.tt("dve", la[:, 0:8], self.dt_all[:, t, o:o + 8], st["ab"][:, o:o + 8], ALU.mult,
                    [self.dt_all.r, st["ab"].r], [la.r])
            pb, pr = self.pbank()
            self.mm(pb[:, 0:8], tri[:], la[:, 0:8], True, True, [self.cst.r, la.r], pr)
            self.mm(pb[:, 8:16], self.ones_f[:], la[:, 0:8], True, True, [self.ones_f.r, la.r], pr)
            self.cp("act", vt[:, 0:16], pb[:, 0:16], pr, [vt.r])
            vcol, tot = vt[:, 0:8], vt[:, 8:16]
            totlo, tothi = vt[0:64, 8:12], vt[64:128, 12:16]
            vr = vt.r
        else:
            vcol, tot, totlo, tothi, vr = vt_ready
        self.tt("dve", wE[:], tot, vcol, ALU.subtract, [vr], [wE.r])
        self.act(wE[:], wE[:], AF.Exp, [wE.r], [wE.r])
        self.tt("dve", wE[:], wE[:], self.dt_all[:, t, o:o + 8], ALU.mult, [wE.r, self.dt_all.r], [wE.r])
        if mask_last:
            self.ts("dve", wE[:], wE[:], self.lastmask[:, 0:1], None, ALU.mult, None, [wE.r, self.cst.r], [wE.r])
        self.act(decE[0:64, :], totlo, AF.Exp, [vr], [decE.r])
        self.act(decE[64:128, :], tothi, AF.Exp, [vr], [decE.r])
        xd = xdtw.next()
        self.tt("dve", xd[:].rearrange("p (h e) -> p h e", h=8), xs_tm[:].rearrange("p (h e) -> p h e", h=8),
                bc(wE[:], [128, 8, 64]), ALU.mult, [xs_tm.r, wE.r], [xd.r])
        pb, pr = self.pbank()
        self.mm(pb[:, :], b_tm[:], xd[:], True, True, [b_tm.r, xd.r], pr)
        self.tt("dve", hetmp[:].rearrange("p (h e) -> p h e", h=4), H[:].rearrange("p (h e) -> p h e", h=4),
                bc(decE[:], [128, 4, 64]), ALU.mult, [H.r, decE.r], [hetmp.r])
        self.tt("dve", H[0:64, :], hetmp[0:64, :], pb[0:64, 0:256], ALU.add, [hetmp.r] + pr, [H.r])
        self.tt("dve", H[64:128, :], hetmp[64:128, :], pb[64:128, 256:512], ALU.add, [hetmp.r] + pr, [H.r])

    def exchange(self, l):
        groups = [[2 * i, 2 * i + 1] for i in range(self.ncores // 2)]
        for src, dst in ((self.xk, self.xk_g), (self.xv, self.xv_g), (self.xs_, self.xs_g)):
            self.cc(src, dst, groups)

    def cc(self, src, dst, groups):
        if self.fake_cc:
            n = src.t.shape[0]
            for r in range(2):
                self.dma("pool", dst.t[r * n:(r + 1) * n, :], src.t, [src.r], [dst.r])
            return
        self.P.dma("pool", lambda e: e.collective_compute("AllGather", ALU.bypass, replica_groups=groups,
                                                           ins=[src.t.opt()], outs=[dst.t.opt()]),
                   [src.r], [dst.r], inc=1)

    def phase_b(self, l, xres, last):
        nc, P = self.nc, self.P
        self.new_scope()
        st = self.ssd_consts(l)
        sccw, cw, cb = self.conv_consts(l)
        wo_att = self.sb("wo_att", [64, 4, D], BF16)
        self.dma("pool", wo_att[:], self.w_out.t[l, 0:256, :].rearrange("(h p) n -> p h n", p=64), [self.w_out.r], [wo_att.r])
        wo_r = self.sb("wo_r", [128, 6, D], BF16)
        self.dma("pool", wo_r[:], self.w_out.t[l, 256:1024, :].rearrange("(c p) n -> p c n", p=128), [self.w_out.r], [wo_r.r])
        G1 = self.load_bv(2, "G1")
        dsk8 = self.sb("dsk8", [128, 8])
        self.dma("sp", dsk8[:], self.ssd_d.t[l].partition_broadcast(128), [self.ssd_d.r], [dsk8.r])
        ngb = self.sb("ngb", [128, 512])
        self.dma("sp", ngb[:], self.ssd_norm.t[l].partition_broadcast(128), [self.ssd_norm.r], [ngb.r])
        VA = self.sb("VA", [128, 34, 260], BF16)
        self.dma("sp", VA[:, 0:2, :], self.kv_ctx_v.t, [self.kv_ctx_v.r], [VA.r])
        xvv = self.xv_g.t.rearrange("(t p) e -> p t e", p=128)
        for q in range(4):
            self.dma("sp", VA[:, 2 + 8 * q:10 + 8 * q, :], xvv[:, 8 * q:8 * q + 8, :], [self.xv_g.r], [VA.r])
        KTs = self.rot("KT", 2, [128, TC + 2 * TL], BF16)
        smg = self.sb("smg", [128, 2, SM_W])
        self.dma("sp", smg[:], self.xs_g.t.rearrange("(r p) c -> p r c", p=128), [self.xs_g.r], [smg.r])
        smp = self.sb("smp", [128, SM_W])
        self.ts("dve", smp[:], smg[:, 1, :], self.esel[:, 0:1], None, ALU.mult, None, [smg.r, self.cst.r], [smp.r])
        self.stt("dve", smp[:], smg[:, 0, :], self.esel[:, 1:2], smp[:], ALU.mult, ALU.add, [smg.r, self.cst.r, smp.r], [smp.r])

        xt = self.sb("xt", [128, D])
        tmp = self.sb("tmp", [128, D])
        smalls = self.small3(2)
        x1 = self.sb("x1", [128, D])
        la = self.sb("la", [128, 16])
        vt = self.sb("vt", [128, 32])
        ev = self.sb("ev", [128, 16])
        wL = self.sb("wL", [128, 8])
        decL = self.sb("decL", [128, 4])
        Rm = self.sb("Rm", [128, 8, 128])
        D1 = self.sb("D1", [128, 8, 128])
        MT = self.sb("MT", [128, 16, 128], BF16)
        xdt = self.sb("xdt", [128, 16, 64], BF16)
        xs_r = self.rot("xsr", 2, [128, 512])
        bt_r = self.rot("btr", 2, [128, 128], BF16)
        z_r = self.rot("zr", 2, [128, 512])
        ya = self.sb("ya", [128, 512])
        yb = self.sb("yb", [128, 512])
        sz = self.sb("sz", [128, 512])
        yss = self.sb("yss", [128, 512], BF16)
        jk = self.sb("jk", [128, 512], BF16)
        jk32 = self.sb("jk32", [128, 512])
        ms2 = self.sb("ms2", [128, 2])
        ln2 = self.sb("ln2", [128, 2])
        rs2 = self.sb("rs2", [128, 2])
        xdtw = self.rot("xdtw", 2, [128, 512], BF16)
        hetmp = self.sb("hetmp", [128, 256])
        HL = self.sb("HL", [128, 256])
        HLb = self.sb("HLbd", [128, 512], BF16)
        self.ms("pool", HLb[:], 0.0, [HLb.r])
        Cbd = self.sb("Cbd", [128, 256], BF16)
        self.ms("pool", Cbd[:], 0.0, [Cbd.r])
        HbdE = self.sb("HbdE", [128, 512], BF16)
        self.ms("pool", HbdE[:], 0.0, [HbdE.r])
        ymx = self.sb("ymx", [128, 4, 512], BF16)
        yatt = self.sb("yatt", [64, 4, 512], BF16)
        PTs = self.rot("PT", 2, [128, 1024], BF16)
        Osb = self.sb("Osb", [64, 512])
        rec = self.sb("rec", [128, 512])
        dskb = self.sb("dskb", [128, 512])
        self.cp("dve", dskb[:].rearrange("p (h e) -> p h e", h=8), bc(dsk8[:], [128, 8, 64]), [dsk8.r], [dskb.r])

        self.l_init(l, st, smp, cw, cb, HL, HLb)
        self.fix_last(l, smp, sccw, cw, cb)
        self.dump(f"HL0{l}", HL[:], HL.r, [128, 256])
        if self.stop(l, "b0"):
            return

        order = [4, 3, 2, 1] + ([] if last else [0])
        ymxs = [ymx, self.sb("ymx2", [128, 4, 512], BF16)]
        yatts = [yatt, self.sb("yatt2", [64, 4, 512], BF16)]

        def geom(bi):
            t0, nt = BLOCKS[bi]
            return t0, nt, nt * 128, t0 * 128, (1 if bi == 0 else 0)

        def ssd_gen(bi, ymx):
            t0, nt, N, c0, isctx = geom(bi)
            if isctx:
                self.ms("dve", HL[:], 0.0, [HL.r])
                self.ms("dve", HLb[:], 0.0, [HLb.r])
            for tl in range(nt - 1, -1, -1):
                t = t0 + tl
                cs = slice(t * 128, (t + 1) * 128)
                xs = xs_r.next()
                self.dma("pool", xs[:], self.sc_xs.t[cs, :], [self.sc_xs.r], [xs.r])
                btk = bt_r.next()
                self.dma("pool", btk[:], self.sc_bt.t[cs, :], [self.sc_bt.r], [btk.r])
                z = z_r.next()
                self.dma("pool", z[:], self.sc_z.t[cs, :], [self.sc_z.r], [z.r])
                self.tt("dve", la[:], self.dt_all[:, t, :], st["ab"][:], ALU.mult, [self.dt_all.r, st["ab"].r], [la.r])
                pb, pr = self.pbank()
                self.mm(pb[:, 0:8], self.triE[:], la[:, 0:8], True, True, [self.cst.r, la.r], pr)
                self.mm(pb[:, 8:16], self.triL[:], la[:, 8:16], True, True, [self.cst.r, la.r], pr)
                self.mm(pb[:, 16:32], self.ones_f[:], la[:, 0:16], True, True, [self.ones_f.r, la.r], pr)
                self.cp("act", vt[:], pb[:, 0:32], pr, [vt.r])
                self.act(ev[:], vt[:, 0:16], AF.Exp, [vt.r], [ev.r])
                yield
                pg, prg = self.pbank(pin=True)
                for g in range(2):
                    self.cp("pool", Cbd[g * 64:(g + 1) * 64, g * 128:(g + 1) * 128], self.CT[g * 64:(g + 1) * 64, cs],
                            [self.CT.r], [Cbd.r])
                self.mm(pg[:, 0:256], self.BT[:, cs], Cbd[:], True, True, [self.BT.r, Cbd.r], prg)
                for d_ in range(2):
                    self.tt("dve", xdt[:, d_ * 8:(d_ + 1) * 8, :], xs[:].rearrange("p (h e) -> p h e", h=8),
                            bc(self.dt_all[:, t, d_ * 8:(d_ + 1) * 8], [128, 8, 64]), ALU.mult,
                            [xs.r, self.dt_all.r], [xdt.r])
                yield
                for d_, tri in enumerate((self.triE, self.triL)):
                    self.tt("pool", Rm[:], bcm(tri[:], [128, 8, 128]), bc(la[:, d_ * 8:(d_ + 1) * 8], [128, 8, 128]),
                            ALU.mult, [self.cst.r, la.r], [Rm.r])
                    pd, prd = self.pbank(2)
                    for q in range(2):
                        self.mm(pd[:, q * 512:(q + 1) * 512], self.ones_f[:],
                                Rm[:, q * 4:(q + 1) * 4, :].rearrange("p a b -> p (a b)"), True, False,
                                [self.ones_f.r, Rm.r], prd)
                        self.mm(pd[:, q * 512:(q + 1) * 512], self.ident_b[:], self.negm[:, d_, :], False, True,
                                [self.ident_b.r, self.negm.r], prd)
                    self.tt("dve", D1[:], pd[:, :].rearrange("p (a b) -> p a b", a=8),
                            bc(vt[:, d_ * 8:(d_ + 1) * 8], [128, 8, 128]), ALU.subtract, prd + [vt.r], [D1.r])
                    self.act(D1[:], D1[:], AF.Exp, [D1.r], [D1.r])
                    self.tt("dve", MT[:, d_ * 8:(d_ + 1) * 8, :].rearrange("p (g h) b -> p g h b", g=2),
                            D1[:].rearrange("p (g h) b -> p g h b", g=2),
                            pg[:, 0:256].rearrange("p (g b) -> p g b", g=2).unsqueeze(2).to_broadcast([128, 2, 4, 128]),
                            ALU.mult, [D1.r] + prg, [MT.r])
                    if d_ == 1:
                        self.unpin(prg)
                    yield
                yield
                py, pry = self.pbank()
                for h in range(8):
                    self.mm(py[:, h * 64:(h + 1) * 64], MT[:, h, :], xdt[:, h, :], True, False, [MT.r, xdt.r], pry)
                    self.mm(py[:, h * 64:(h + 1) * 64], MT[:, 8 + h, :], xdt[:, 8 + h, :], False, True, [MT.r, xdt.r], pry)
                pe_, pre = self.pbank()
                for g in range(2):
                    self.cp("act", HbdE[g * 64:(g + 1) * 64, g * 256:(g + 1) * 256], self.HEs[g * 64:(g + 1) * 64, t, :],
                            [self.HEs.r], [HbdE.r])
                self.mm(pe_[:, :], self.CT[:, cs], HbdE[:], True, True, [self.CT.r, HbdE.r], pre)
                self.tt("dve", ya[:].rearrange("p (h e) -> p h e", h=8), pe_[:, :].rearrange("p (h e) -> p h e", h=8),
                        bc(ev[:, 0:8], [128, 8, 64]), ALU.mult, pre + [ev.r], [ya.r])
                pl, prl = self.pbank()
                self.mm(pl[:, :], self.CT[:, cs], HLb[:], True, True, [self.CT.r, HLb.r], prl)
                self.tt("dve", yb[:].rearrange("p (h e) -> p h e", h=8), pl[:, :].rearrange("p (h e) -> p h e", h=8),
                        bc(ev[:, 8:16], [128, 8, 64]), ALU.mult, prl + [ev.r], [yb.r])
                self.tt("dve", ya[:], ya[:], yb[:], ALU.add, [ya.r, yb.r], [ya.r])
                self.tt("dve", ya[:], ya[:], py[:, :], ALU.add, [ya.r] + pry, [ya.r])
                self.tt("dve", yb[:], xs[:], dskb[:], ALU.mult, [xs.r, dskb.r], [yb.r])
                self.tt("dve", ya[:], ya[:], yb[:], ALU.add, [ya.r, yb.r], [ya.r])
                if l == 0 and t == NT - 1:
                    self.dump("yscan0", ya[:], ya.r, [128, 512])
                yield
                self.silu(sz[:], [sz.r], z[:], [z.r], (jk32[:], jk32.r))
                self.tt("dve", ya[:], ya[:], sz[:], ALU.mult, [ya.r, sz.r], [ya.r])
                for g in range(2):
                    self.act(jk[:, g * 256:(g + 1) * 256], ya[:, g * 256:(g + 1) * 256], AF.Square, [ya.r],
                             [jk.r, ms2.r], scale=1.0 / 16.0, accum=ms2[:, g:g + 1])
                self.act(ln2[:], ms2[:], AF.Ln, [ms2.r, self.eps_t.r], [ln2.r], bias=self.eps_t[:, 0:1])
                self.act(rs2[:], ln2[:], AF.Exp, [ln2.r], [rs2.r], scale=-0.5)
                for g in range(2):
                    self.stt("dve", yss[:, g * 256:(g + 1) * 256], ya[:, g * 256:(g + 1) * 256], rs2[:, g:g + 1],
                             ngb[:, g * 256:(g + 1) * 256], ALU.mult, ALU.mult, [ya.r, rs2.r, ngb.r], [yss.r])
                pb, pr = self.pbank()
                pbb = pb.bitcast(BF16)
                for c in range(4):
                    self.tr(pbb[:, c * 128:(c + 1) * 128], yss[:, c * 128:(c + 1) * 128], self.ident_b[:],
                            [yss.r, self.ident_b.r], pr)
                self.cp("act", ymx[:, :, tl * 128:(tl + 1) * 128], pbb[:, 0:512].rearrange("p (c t) -> p c t", c=4), pr, [ymx.r])
                yield
                vready = (vt[:, 8:16], vt[:, 24:32], vt[0:64, 24:28], vt[64:128, 28:32], vt.r)
                self.state_update(st, xs, btk, t, 1, None, vt, wL, decL, xdtw, hetmp, HL, vt_ready=vready)
                self.hl_bd(HL, HLb)
                yield

        def att_gen(bi, yatt):
            t0, nt, N, c0, isctx = geom(bi)
            nk = 2 if isctx else 34
            nkc = nk * 128
            for h in range(4):
                KT = KTs.next()
                self.dma("sp", KT[:, 0:TC], self.kv_ctx_k.t[:, h, :], [self.kv_ctx_k.r], [KT.r])
                if not isctx:
                    for r in range(2):
                        self.dma("sp", KT[:, TC + r * TL:TC + (r + 1) * TL],
                                 self.xk_g.t[r * 512 + h * 128:r * 512 + (h + 1) * 128, :], [self.xk_g.r], [KT.r])
                po, pro = self.pbank(pin=True)
                npair = nk // 2

                def qk_pair(j):
                    ps2, pr2 = self.pbank(2, pin=True)
                    for a in range(2):
                        kt = 2 * j + a
                        self.mm(ps2[:, a * 512:a * 512 + N], KT[:, kt * 128:(kt + 1) * 128], self.qT[:, h, c0:c0 + N],
                                True, True, [KT.r, self.qT.r], pr2)
                    return ps2, pr2

                cur = qk_pair(0)
                for j in range(npair):
                    ps2, pr2 = cur
                    pt_ = PTs.next()
                    self.act(pt_[:].rearrange("p (a n) -> p a n", a=2)[:, :, 0:N],
                             ps2[:, :].rearrange("p (a n) -> p a n", a=2)[:, :, 0:N], AF.Exp, pr2, [pt_.r], scale=MLA_SCALE)
                    self.unpin(pr2)
                    if j + 1 < npair:
                        cur = qk_pair(j + 1)
                    for a in range(2):
                        kt = 2 * j + a
                        self.mm(po[0:65, 0:N], VA[:, kt, h * 65:(h + 1) * 65], pt_[:, a * 512:a * 512 + N], kt == 0,
                                kt == nk - 1, [VA.r, pt_.r], pro)
                    yield
                self.unpin(pro)
                self.act(rec[64:65, 0:N], po[64:65, 0:N], AF.Ln, pro, [rec.r])
                self.act(rec[64:65, 0:N], rec[64:65, 0:N], AF.Exp, [rec.r], [rec.r], scale=-1.0)
                self.cp("act", Osb[:, 0:N], po[0:64, 0:N], pro, [Osb.r])
                yield
                pbc, prb = self.pbank()
                self.mm(pbc[0:64, 0:N], self.ones_f[64:65, 0:64], rec[64:65, 0:N], True, True, [self.ones_f.r, rec.r], prb)
                self.tt("dve", yatt[:, h, 0:N], pbc[0:64, 0:N], Osb[:, 0:N], ALU.mult, prb + [Osb.r], [yatt.r])
                yield

        def wout_gen(bi, ymx, yatt):
            t0, nt, N, c0, isctx = geom(bi)
            if isctx:
                self.dma("sp", G1[:], self.bv.t[8], [self.bv.r], [G1.r])
            for tl in range(nt):
                t = t0 + tl
                cs = slice(t * 128, (t + 1) * 128)
                ls = slice(tl * 128, (tl + 1) * 128)
                pw, prw = self.pbank(2)
                for hf in range(2):
                    ns = slice(hf * 512, (hf + 1) * 512)
                    for h in range(4):
                        self.mm(pw[:, ns], yatt[:, h, ls], wo_att[:, h, ns], h == 0, False, [yatt.r, wo_att.r], prw)
                    for c in range(2):
                        self.mm(pw[:, ns], self.ysc[:, c, cs], wo_r[:, c, ns], False, False, [self.ysc.r, wo_r.r], prw)
                    for c in range(4):
                        self.mm(pw[:, ns], ymx[:, c, ls], wo_r[:, 2 + c, ns], False, c == 3, [ymx.r, wo_r.r], prw)
                self.dma("sp", xt[:], xres.t[cs, :], [xres.r], [xt.r])
                self.post_tile(pw, prw, xt, G1, tmp, smalls.next(), x1)
                self.dma("sp", self.x1s.t[cs, :], x1[:], [x1.r], [self.x1s.r])
                yield

        prev = None
        for it, bi in enumerate(order):
            ymx_c, yatt_c = ymxs[it % 2], yatts[it % 2]
            nk_ = 2 if bi == 0 else 34
            gens = [(ssd_gen(bi, ymx_c), 9 * BLOCKS[bi][1]), (att_gen(bi, yatt_c), 4 * (nk_ // 2 + 2))]
            if prev is not None:
                gens.append((wout_gen(*prev), BLOCKS[prev[0]][1]))
            interleave(*gens)
            if l == 0 and bi == 4:
                self.dump("yatt0", yatt_c[:], yatt_c.r, [64, 4, 512], BF16)
                self.dump("ymx0", ymx_c[:], ymx_c.r, [128, 4, 512], BF16)
            if self.stop(l, "b1") or self.stop(l, "b2"):
                return
            prev = (bi, ymx_c, yatt_c)
        run_gen(wout_gen(*prev))
        self.dump(f"x1_{l}", self.x1s.t, self.x1s.r, [T, D])

    def l_init(self, l, st, smp, cw, cb, HL, HLb):
        c3 = self.sb("c3", [128, 6, 3])
        pv = smp[:, 512:524].rearrange("p (c k) -> p c k", c=6)
        self.cp("dve", c3[:, :, 0], self.fix[:, 10:16], [self.fix.r], [c3.r])
        self.cp("dve", c3[:, :, 1], pv[:, :, 1], [smp.r], [c3.r])
        self.cp("dve", c3[:, :, 2], pv[:, :, 0], [smp.r], [c3.r])
        t6 = self.sb("t6", [128, 6, 3])
        cv = self.sb("cv6", [128, 6])
        self.tt("dve", t6[:], c3[:], cw[:], ALU.mult, [c3.r, cw.r], [t6.r])
        self.tt("dve", cv[:], t6[:, :, 0], t6[:, :, 1], ALU.add, [t6.r], [cv.r])
        self.tt("dve", cv[:], cv[:], t6[:, :, 2], ALU.add, [t6.r, cv.r], [cv.r])
        self.tt("dve", cv[:], cv[:], cb[:], ALU.add, [cv.r, cb.r], [cv.r])
        sv = self.sb("sv6", [128, 6])
        sg6 = self.sb("sg6", [128, 6])
        self.silu(sv[:], [sv.r], cv[:], [cv.r], (sg6[:], sg6.r))
        pb, pr = self.pbank()
        for c in range(4):
            self.tr(pb[0:1, c * 128:(c + 1) * 128], sv[:, c:c + 1], self.ident_f[:], [sv.r, self.cst.r], pr)
        pb2, pr2 = self.pbank()
        self.tr(pb2[0:1, 0:128], sv[:, 4:5], self.ident_f[:], [sv.r, self.cst.r], pr2)
        row = self.sb("row", [1, 640])
        self.cp("act", row[:, 0:512], pb[0:1, 0:512], pr, [row.r])
        self.cp("act", row[:, 512:640], pb2[0:1, 0:128], pr2, [row.r])
        dtr = self.sb("dtr", [1, 8])
        self.cp("dve", dtr[:], smp[0:1, 528:536], [smp.r], [dtr.r])
        xr = self.sb("xr", [1, 512])
        self.tt("dve", xr[:].rearrange("p (h e) -> p h e", h=8), row[:, 0:512].rearrange("p (h e) -> p h e", h=8),
                bc(dtr[:], [1, 8, 64]), ALU.mult, [row.r, dtr.r], [xr.r])
        pb3, pr3 = self.pbank()
        self.mm(pb3[:, :], row[:, 512:640], xr[:], True, True, [row.r, xr.r], pr3)
        self.tt("dve", HL[0:64, :], smp[0:64, 0:256], pb3[0:64, 0:256], ALU.add, [smp.r] + pr3, [HL.r])
        self.tt("dve", HL[64:128, :], smp[64:128, 0:256], pb3[64:128, 256:512], ALU.add, [smp.r] + pr3, [HL.r])
        self.hl_bd(HL, HLb)

    def hl_bd(self, HL, HLb):
        for g in range(2):
            self.cp("act", HLb[g * 64:(g + 1) * 64, g * 256:(g + 1) * 256], HL[g * 64:(g + 1) * 64, :], [HL.r], [HLb.r])

    def fix_last(self, l, smp, sccw, cw, cb):
        tcol = T - 1
        f2 = self.sb("f2", [128, 2])
        self.tt("dve", f2[:], smp[:, 524:526], sccw[:, :, 2], ALU.mult, [smp.r, sccw.r], [f2.r])
        self.tt("dve", f2[:], f2[:], self.fix[:, 6:8], ALU.add, [f2.r, self.fix.r], [f2.r])
        self.tt("dve", self.ysc[:, :, tcol], f2[:], self.fix[:, 8:10], ALU.mult, [f2.r, self.fix.r], [self.ysc.r])
        f6 = self.sb("f6", [128, 6])
        pv = smp[:, 512:524].rearrange("p (c k) -> p c k", c=6)
        self.tt("dve", f6[:], pv[:, :, 1], cw[:, :, 2], ALU.mult, [smp.r, cw.r], [f6.r])
        self.tt("dve", f6[:], f6[:], self.fix[:, 0:6], ALU.add, [f6.r, self.fix.r], [f6.r])
        self.tt("dve", f6[:], f6[:], cb[:], ALU.add, [f6.r, cb.r], [f6.r])
        s6 = self.sb("s6", [128, 6])
        sg6b = self.sb("sg6b", [128, 6])
        self.silu(s6[:], [s6.r], f6[:], [f6.r], (sg6b[:], sg6b.r))
        self.cp("dve", self.xsT_last[:, :, 127], s6[:, 0:4], [s6.r], [self.xsT_last.r])
        self.cp("dve", self.BT[:, tcol:tcol + 1], s6[:, 4:5], [s6.r], [self.BT.r])
        self.cp("dve", self.CT[:, tcol:tcol + 1], s6[:, 5:6], [s6.r], [self.CT.r])
        bl = self.sb("bl", [128, 128])
        self.cp("dve", bl[:], self.BT[:, T - 128:T], [self.BT.r], [bl.r])
        self.cp("dve", bl[:, 127:128], s6[:, 4:5], [s6.r], [bl.r])
        pb, pr = self.pbank()
        for c in range(4):
            self.tr(pb[:, c * 128:(c + 1) * 128], self.xsT_last[:, c, :], self.ident_f[:], [self.xsT_last.r, self.cst.r], pr)
        xl = self.sb("xl", [128, 512])
        self.cp("act", xl[:], pb[:, :], pr, [xl.r])
        self.dma("pool", self.sc_xs.t[T - 128:T, :], xl[:], [xl.r], [self.sc_xs.r])
        pb2, pr2 = self.pbank()
        self.tr(pb2[:, 0:128], bl[:], self.ident_f[:], [bl.r, self.cst.r], pr2)
        bl2 = self.sb("bl2", [128, 128], BF16)
        self.cp("act", bl2[:], pb2[:, 0:128], pr2, [bl2.r])
        self.dma("pool", self.sc_bt.t[T - 128:T, :], bl2[:], [bl2.r], [self.sc_bt.r])

    def phase_c(self, l, xnext, last):
        self.close_ab()
        self.new_scope()
        w1 = self.sb("w1", [128, 8, 4 * D], BF16)
        for k in range(8):
            self.dma("pool", w1[:, k, :], self.w_ff1.t[l, k * 128:(k + 1) * 128, :], [self.w_ff1.r], [w1.r])
        w2 = self.sb("w2", [128, 32, D], BF16)
        w2v = self.w_ff2.t[l].rearrange("(c p) n -> p c n", p=128)
        for q in range(4):
            self.dma("pool", w2[:, q * 8:(q + 1) * 8, :], w2v[:, q * 8:(q + 1) * 8, :], [self.w_ff2.r], [w2.r])
        A2 = self.load_bv(3, "A2")
        B2 = self.load_bv(4, "B2")
        G2 = self.load_bv(5, "G2")
        A2c = B2c = G2c = None
        if not last:
            A2c = self.load_bv(9, "A2c")
            B2c = self.load_bv(10, "B2c")
            G2c = self.load_bv(11, "G2c")
        NBC = 256
        xts = self.rot("xt", 2, [128, D])
        xps = self.rot("xp", 1, [128, D])
        tmp = self.sb("tmp", [128, D])
        tmp2 = self.sb("tmp2", [128, D])
        hb = self.sb("hb", [128, D], BF16)
        hTs = [self.sb("hTa", [128, 8, NBC], BF16), self.sb("hTb", [128, 8, NBC], BF16)]
        a_r = self.rot("afo", 3, [128, NBC], BF16)
        rl_r = self.rot("rl", 2, [128, NBC])
        smalls = self.small3(2)
        smalls2 = self.small3(2)
        x2s = self.rot("x2", 2, [128, D])
        ysbs = self.rot("ysb", 1, [128, D])
        blocks = [(2 + 2 * i, 2) for i in range(8)] + ([] if last else [(0, 2)])

        def front(i):
            t0, nt = blocks[i]
            isctx = (t0 == 0)
            hT = hTs[i % 2]
            for tl in range(nt):
                t = t0 + tl
                xt = xts.next()
                self.dma("sp", xt[:], self.x1s.t[t * 128:(t + 1) * 128, :], [self.x1s.r], [xt.r])
                self.norm_tile(xt, A2c if isctx else A2, B2c if isctx else B2, hb, tmp, smalls.next())
                self.transpose_tile(hb, hT, tl)
                yield

        def ffn(i):
            t0, nt = blocks[i]
            isctx = (t0 == 0)
            N = nt * 128
            hT = hTs[i % 2]
            pws = [self.pbank(2, pin=True) for _ in range(nt)]

            def ffn1(fo):
                pb, pr = self.pbank(pin=True)
                for k in range(8):
                    self.mm(pb[:, 0:N], w1[:, k, fo * 128:(fo + 1) * 128], hT[:, k, 0:N], k == 0, k == 7, [w1.r, hT.r], pr)
                return pb, pr

            cur = ffn1(0)
            for fo in range(32):
                pb, pr = cur
                rl = rl_r.next()
                a = a_r.next()
                self.act(rl[:, 0:N], pb[:, 0:N], AF.Relu, pr, [rl.r])
                self.unpin(pr)
                if fo + 1 < 32:
                    cur = ffn1(fo + 1)
                self.tt("dve", a[:, 0:N], rl[:, 0:N], rl[:, 0:N], ALU.mult, [rl.r], [a.r])
                for tl in range(nt):
                    pw, prw = pws[tl]
                    for hf in range(2):
                        ns = slice(hf * 512, (hf + 1) * 512)
                        self.mm(pw[:, ns], a[:, tl * 128:(tl + 1) * 128], w2[:, fo, ns], fo == 0, fo == 31, [a.r, w2.r], prw)
                yield
            for tl in range(nt):
                t = t0 + tl
                cs = slice(t * 128, (t + 1) * 128)
                pw, prw = pws[tl]
                ysb = ysbs.next()
                self.cp("act", ysb[:], pw[:, :], prw, [ysb.r])
                self.unpin(prw)
                xp = xps.next()
                self.dma("sp", xp[:], self.x1s.t[cs, :], [self.x1s.r], [xp.r])
                x2 = x2s.next()
                self.post_tile(ysb, [ysb.r], xp, G2c if isctx else G2, tmp2, smalls2.next(), x2)
                if last:
                    self.dma("pool", self.y_out.t[(t - 2) * 128:(t - 1) * 128, :], x2[:], [x2.r], [self.y_out.r])
                else:
                    self.dma("pool", xnext.t[cs, :], x2[:], [x2.r], [xnext.r])
                yield

        run_gen(front(0))
        for i in range(len(blocks)):
            f = front(i + 1) if i + 1 < len(blocks) else None
            interleave((f, 2, 0.7), (ffn(i), 34))
        if not last:
            self.dump(f"xres1_{l}", xnext.t, xnext.r, [T, D])


def run_gen(g):
    if g is not None:
        for _ in g:
            pass


def interleave(*pairs):
    live = [[p[0], (p[2] if len(p) > 2 else 0.0), max(1, p[1])] for p in pairs if p[0] is not None]
    while live:
        live.sort(key=lambda x: x[1] / x[2])
        it = live[0]
        try:
            next(it[0])
            it[1] += 1
        except StopIteration:
            live.remove(it)


class Rot:
    def __init__(self, items):
        self.items = items
        self.i = -1

    def next(self):
        self.i = (self.i + 1) % len(self.items)
        return self.items[self.i]


_PERM = np.r_[8:16, 0:8, 24:32, 16:24]


def _rope_tables(pos):
    half = 16
    inv = (10000.0 ** (-(np.arange(0, half, 2, dtype=np.float32)) / half)).astype(np.float32)
    row = (pos // 64).astype(np.float32)
    col = (pos % 64).astype(np.float32)
    ar = row[:, None] * inv
    ac = col[:, None] * inv
    ang = np.concatenate([ar, ar, ac, ac], -1).astype(np.float32)
    sign = np.concatenate([-np.ones(8), np.ones(8), -np.ones(8), np.ones(8)]).astype(np.float32)
    return np.cos(ang).astype(np.float32), (np.sin(ang) * sign).astype(np.float32)


def host_layout(inp):
    f = lambda a: np.ascontiguousarray(np.asarray(a, dtype=np.float32))
    x, c, ctx, c_ctx = f(inp["x"]), f(inp["c"]), f(inp["ctx"]), f(inp["c_ctx"])
    L = DEPTH
    w_in = f(inp["w_in"])
    s0 = 416 + 768
    shared = {}
    per_rev = {}
    for rev in (0, 1):
        wa = []
        for l in range(L):
            w = w_in[l]
            dtc = w[:, s0 + 512 + 768:s0 + 512 + 768 + 16]
            if rev:
                dtc = np.concatenate([dtc[:, 8:16], dtc[:, 0:8]], 1)
            wa.append(np.concatenate([w[:, 0:256], w[:, 256:384], w[:, 384:416], w[:, 384 + _PERM],
                                      w[:, 416:416 + 768], w[:, s0 + 512:s0 + 512 + 768], w[:, s0:s0 + 512], dtc], 1))
        d = {"w_in_aug": f(np.stack(wa))}
        al = f(inp["ssd_a_log"]); db = f(inp["ssd_dt_bias"])
        order = [1, 0] if rev else [0, 1]
        d["a_log"] = f(np.concatenate([al[:, order[0]], al[:, order[1]]], -1))
        d["dt_bias"] = f(np.concatenate([db[:, order[0]], db[:, order[1]]], -1))
        scw = f(inp["sc_conv_w"]); sw = f(inp["ssd_conv_w"])
        if rev:
            scw = scw[:, ::-1]; sw = sw[:, ::-1]
        d["sc_cw"] = f(np.transpose(scw, (0, 2, 1)))
        d["ssd_cw"] = f(np.transpose(sw, (0, 2, 1)))
        per_rev[rev] = d
    wuq = f(inp["w_uq"])
    wq = np.zeros((L, 256, 4, 160), np.float32)
    for h in range(4):
        wq[:, :, h, 0:32] = wuq[:, :, h * 96 + 64:h * 96 + 96]
        wq[:, :, h, 64:128] = wuq[:, :, h * 96:h * 96 + 64]
        wq[:, :, h, 128:160] = wuq[:, :, h * 96 + 64 + _PERM]
    shared["w_uq_aug"] = f(wq.reshape(L, 256, 640))
    wukv = f(inp["w_ukv"])
    wk = np.zeros((L, 128, 4, 128), np.float32)
    wv = np.zeros((L, 128, 4, 64), np.float32)
    for h in range(4):
        wk[:, :, h, 64:128] = wukv[:, :, h * 128:h * 128 + 64]
        wv[:, :, h, :] = wukv[:, :, h * 128 + 64:h * 128 + 128]
    shared["w_ukv_k"] = f(wk.reshape(L, 128, 512))
    shared["w_ukv_v"] = f(wv.reshape(L, 128, 256))
    shared["q_norm"] = f(inp["mla_q_norm"])
    shared["kv_norm"] = f(inp["mla_kv_norm"])
    shared["w_mod"] = f(inp["w_mod"])
    shared["b_mod"] = f(inp["b_mod"])
    shared["gvecs"] = f(np.stack([inp["g_pre_mix"], inp["g_post_mix"], inp["g_pre_ffn"], inp["g_post_ffn"]], 1))
    shared["ssd_cb"] = f(inp["ssd_conv_b"])
    shared["ssd_d"] = f(inp["ssd_d"])
    shared["ssd_norm"] = f(inp["ssd_norm"])
    shared["w_out"] = f(inp["w_out"])
    shared["w_ff1"] = f(inp["w_ff1"])
    shared["w_ff2"] = f(inp["w_ff2"])
    j = np.arange(128)
    triE = (j[:, None] <= j[None, :]).astype(np.float32)
    triL = (j[:, None] >= j[None, :]).astype(np.float32)
    in_maps = []
    for core in range(NCORES):
        b, hf = core // 2, core % 2
        xl = x[b, hf * TL:(hf + 1) * TL]
        cl = ctx[b]
        pos = np.arange(hf * TL, (hf + 1) * TL)
        if hf:
            xl, cl, pos = xl[::-1], cl[::-1], pos[::-1]
        cos, sin = _rope_tables(pos)
        rope = np.zeros((2, 32, T), np.float32)
        rope[0, :, :TC] = 1.0
        rope[0, :, TC:] = cos.T
        rope[1, :, TC:] = sin.T
        cst = np.zeros((128, 3 * 128 + 4), np.float32)
        cst[:, 0:128] = np.eye(128)
        cst[:, 128:256] = triE
        cst[:, 256:384] = triL
        cst[:, 384] = 1.0
        cst[127, 384] = 0.0
        cst[:, 385] = 1.0 - hf
        cst[:, 386] = float(hf)
        m = {"x_in": f(np.concatenate([cl, xl], 0)), "cvecT": f(np.stack([c[b], c_ctx], 1)), "ropeT": rope, "consts": cst}
        m.update(shared)
        m.update(per_rev[hf])
        in_maps.append(m)
    return in_maps


_NC_CACHE = {}


def kernel(**inputs):
    in_maps = host_layout(inputs)
    if "nc" not in _NC_CACHE:
        _NC_CACHE["nc"] = Builder().build()
    res = run_bass_kernel_spmd(_NC_CACHE["nc"], in_maps, core_ids=list(range(NCORES)))
    out = np.zeros((4, 2 * TL, D), np.float32)
    for core in range(NCORES):
        b, hf = core // 2, core % 2
        y = np.asarray(res.results[core]["y_out"], dtype=np.float32)
        out[b, hf * TL:(hf + 1) * TL] = y[::-1] if hf else y
    return out
```

```python
import numpy as np
import ml_dtypes
from contextlib import ExitStack
import concourse.bass as bass
import concourse.mybir as mybir
from concourse.bass_utils import run_bass_kernel_spmd

F32 = mybir.dt.float32
BF16 = mybir.dt.bfloat16
AF = mybir.ActivationFunctionType
ALU = mybir.AluOpType

NCORES = 8
D = 1024
DEPTH = 2
TC = 256
TL = 2048
T = TC + TL
NT = T // 128
BLOCKS = [(0, 2), (2, 4), (6, 4), (10, 4), (14, 4)]
NU = 2512
EPS = 1e-6
MLA_SCALE = 96.0 ** -0.5
NEG = -30000.0
FM = [(0, 128), (128, 128), (256, 128), (384, 32), (416, 32)] + [(448 + 128 * i, 128) for i in range(12)]
Z_OFF = 1984
DT_OFF = 2496
SM_W = 544

ENGS = ("pe", "act", "dve", "pool", "sp")


class Res:
    __slots__ = ("name", "w", "r", "sem", "ndma")

    def __init__(self, name):
        self.name = name
        self.w = None
        self.r = []
        self.sem = None
        self.ndma = 0


class Prog:
    def __init__(self, nc, strict=True):
        self.nc = nc
        self.ops = {e: [] for e in ENGS}
        self.strict = strict
        self.nres = 0
        self.dma_res = []
        self.slot_count = []
        self.slot_cls = []
        self.free_slots = {}

    def res(self, name=None):
        self.nres += 1
        return Res(name or f"r{self.nres}")

    def _deps(self, reads, writes, eng=None):
        ev = []
        for r in reads:
            if r.w is not None:
                ev.append(r.w)
        for w in writes:
            for e in ([w.w] if w.w is not None else []) + w.r:
                if eng is not None and e[0] == "E" and e[1] == eng:
                    continue
                ev.append(e)
        return ev

    def op(self, eng, fn, reads=(), writes=()):
        lst = self.ops[eng]
        idx = len(lst)
        waits = self._collect(eng, self._deps(reads, writes, eng))
        lst.append({"fn": fn, "waits": waits, "signal": False, "dma": None})
        me = ("E", eng, idx)
        for r in reads:
            r.r.append(me)
        for w in writes:
            w.w = me
            w.r = []

    def dma(self, eng, fn, src, dst, inc=16):
        srcs = src if isinstance(src, (list, tuple)) else [src]
        dsts = dst if isinstance(dst, (list, tuple)) else [dst]
        main = dsts[0]
        cls = eng if inc == 16 else "cc"
        if main.sem is None:
            fl = self.free_slots.setdefault(cls, [])
            if fl and cls != "cc":
                main.sem = fl.pop()
            else:
                main.sem = len(self.slot_count)
                self.slot_count.append(0)
                self.slot_cls.append(cls)
            main.ndma = self.slot_count[main.sem]
            self.dma_res.append(main)
        assert self.slot_cls[main.sem] == cls, (main.name, self.slot_cls[main.sem], cls)
        waits = self._collect(eng, self._deps(srcs, dsts))
        main.ndma += inc
        self.slot_count[main.sem] = main.ndma
        self.ops[eng].append({"fn": fn, "waits": waits, "signal": False, "dma": main.sem, "inc": inc})
        me = ("D", main.sem, main.ndma)
        for r in srcs:
            r.r.append(me)
        for w in dsts:
            w.w = me
            w.r = []

    def release(self, resources):
        for r in resources:
            if r.sem is not None:
                self.free_slots.setdefault(self.slot_cls[r.sem], []).append(r.sem)
                self.dma_res.remove(r)
                r.sem = None

    def _collect(self, eng, events, all_same=False):
        out = {}
        for ev in events:
            if ev[0] == "E":
                _, e2, j = ev
                if e2 == eng and (eng == "pe" or not self.strict) and not all_same:
                    continue
                key = ("E", e2)
            else:
                key = ("D", ev[1])
            if key not in out or out[key][2] < ev[2]:
                out[key] = ev
        evs = list(out.values())
        for ev in evs:
            if ev[0] == "E":
                self.ops[ev[1]][ev[2]]["signal"] = True
        return evs

    def final_wait(self, eng, resources):
        evs = []
        for r in resources:
            if r.w is not None:
                evs.append(r.w)
            evs.extend(r.r)
        self.ops[eng].append({"fn": None, "waits": self._collect(eng, evs), "signal": False, "dma": None})

    def barrier(self):
        evs = []
        for e in ENGS:
            for idx in range(len(self.ops[e]) - 1, -1, -1):
                rec = self.ops[e][idx]
                if rec["dma"] is None and rec["fn"] is not None:
                    evs.append(("E", e, idx))
                    break
        for r in self.dma_res:
            evs.append(("D", r.sem, r.ndma))
        for e in ENGS:
            self.ops[e].append({"fn": None, "waits": self._collect(e, evs, all_same=(e != "pe")),
                                "signal": False, "dma": None})

    def emit(self):
        nc = self.nc
        esem = {e: nc.alloc_semaphore(f"s_{e}") for e in ENGS}
        dsem = [nc.alloc_semaphore(f"d{i}") for i in range(len(self.slot_count))]
        cnt = {}
        for e in ENGS:
            c = 0
            arr = []
            for rec in self.ops[e]:
                if rec["signal"] and rec["dma"] is None:
                    c += 1
                arr.append(c)
            cnt[e] = arr
        ops = self.ops

        def run(e, eh):
            waited = {}
            for rec in ops[e]:
                for ev in rec["waits"]:
                    if ev[0] == "E":
                        sem = esem[ev[1]]
                        val = cnt[ev[1]][ev[2]]
                    else:
                        sem = dsem[ev[1]]
                        val = ev[2]
                    if waited.get(sem.num, 0) >= val:
                        continue
                    waited[sem.num] = val
                    eh.wait_ge(sem, val)
                if rec["fn"] is None:
                    continue
                ins = rec["fn"](eh)
                if rec["dma"] is not None:
                    ins.then_inc(dsem[rec["dma"]], rec["inc"])
                elif rec["signal"]:
                    ins.then_inc(esem[e], 1)

        with nc.Block() as block:
            @block.tensor
            def _(eh):
                run("pe", eh)

            @block.scalar
            def _(eh):
                run("act", eh)

            @block.vector
            def _(eh):
                run("dve", eh)

            @block.gpsimd
            def _(eh):
                run("pool", eh)

            @block.sync
            def _(eh):
                run("sp", eh)


class Buf:
    def __init__(self, t, r):
        self.t = t
        self.r = r

    def __getitem__(self, k):
        return self.t[k]


def bc(ap, shape):
    return ap.unsqueeze(len(ap.shape)).to_broadcast(list(shape))


def bcm(ap, shape):
    return ap.unsqueeze(1).to_broadcast(list(shape))


class Builder:
    def __init__(self, dbg=None, nlayers=DEPTH, stop_after=None, fake_cc=False, ncores=NCORES):
        self.fake_cc = fake_cc
        self.ncores = ncores
        self.nc = bass.Bass("TRN2", target_bir_lowering=False)
        self.P = Prog(self.nc)
        self.dbg = dbg or []
        self.dbg_out = {}
        self.nlayers = nlayers
        self.stop_after = stop_after
        self.scope = None
        self.scope_ab = None
        self.scope_res = []
        self.ab_res = []
        self.pcur = 0
        self.pinned = set()
        self.uid = 0

    def din(self, name, shape, dt=F32):
        return Buf(self.nc.dram_tensor(name, list(shape), dt, kind="ExternalInput").ap(), self.P.res(name))

    def dout(self, name, shape, dt=F32):
        return Buf(self.nc.dram_tensor(name, list(shape), dt, kind="ExternalOutput").ap(), self.P.res(name))

    def dscr(self, name, shape, dt=F32):
        return Buf(self.nc.dram_tensor(name, list(shape), dt, kind="Internal").ap(), self.P.res(name))

    def sb(self, name, shape, dt=F32, persist=False, ab=False):
        self.uid += 1
        nm = f"{name}_{self.uid}"
        r = self.P.res(name)
        if persist:
            t = self.nc.alloc_sbuf_tensor(nm, list(shape), dt)
        elif ab:
            t = self.scope_ab.enter_context(self.nc.sbuf_tensor(nm, list(shape), dt))
            self.ab_res.append(r)
        else:
            t = self.scope.enter_context(self.nc.sbuf_tensor(nm, list(shape), dt))
            self.scope_res.append(r)
        return Buf(t, r)

    def rot(self, name, n, shape, dt=F32):
        return Rot([self.sb(f"{name}{i}", shape, dt) for i in range(n)])

    def pbank(self, n=1, pin=False):
        while True:
            if self.pcur % n:
                self.pcur += n - self.pcur % n
            if self.pcur + n > 8:
                self.pcur = 0
            b0 = self.pcur
            self.pcur = (self.pcur + n) % 8
            if not any((b0 + i) in self.pinned for i in range(n)):
                break
        if pin:
            for i in range(n):
                self.pinned.add(b0 + i)
        self.last_b0 = b0
        return self.ps_all[:, b0 * 512:(b0 + n) * 512], [self.ps_res[b0 + i] for i in range(n)]

    def unpin(self, res_list):
        for r in res_list:
            self.pinned.discard(self.ps_res.index(r))

    def mm(self, out, lhsT, rhs, start, stop, R, W):
        self.P.op("pe", lambda e: e.matmul(out, lhsT, rhs, start=start, stop=stop), R, W)

    def tr(self, out, in_, ident, R, W):
        self.P.op("pe", lambda e: e.transpose(out, in_, ident), R, W)

    def act(self, out, in_, func, R, W, bias=None, scale=None, accum=None, eng="act"):
        kw = {}
        if bias is not None:
            kw["bias"] = bias
        if scale is not None:
            kw["scale"] = scale
        if accum is not None:
            kw["accum_out"] = accum
        self.P.op("act", lambda e: e.activation(out, in_, func, **kw), R, W)

    def tt(self, eng, out, a, b, op, R, W):
        self.P.op(eng, lambda e: e.tensor_tensor(out, a, b, op), R, W)

    def ts(self, eng, out, a, s1, s2, op0, op1, R, W):
        if op1 is None:
            self.P.op(eng, lambda e: e.tensor_scalar(out, a, s1, s2, op0), R, W)
        else:
            self.P.op(eng, lambda e: e.tensor_scalar(out, a, s1, s2, op0, op1), R, W)

    def stt(self, eng, out, a, s, b, op0, op1, R, W):
        self.P.op(eng, lambda e: e.scalar_tensor_tensor(out, a, s, b, op0, op1), R, W)

    def cp(self, eng, out, in_, R, W):
        if eng == "act":
            self.P.op("act", lambda e: e.copy(out, in_), R, W)
        else:
            self.P.op(eng, lambda e: e.tensor_copy(out, in_), R, W)

    def ms(self, eng, ap, val, W):
        self.P.op(eng, lambda e: e.memset(ap, val), [], W)

    def recip(self, out, in_, R, W):
        self.P.op("dve", lambda e: e.reciprocal(out, in_), R, W)

    def dma(self, eng, out, in_, R, W, slow_ok=False):
        if slow_ok:
            self.P.dma(eng, lambda e: e.dma_start(out=out, in_=in_, allow_slow_non_contiguous=True), R, W)
        else:
            self.P.dma(eng, lambda e: e.dma_start(out=out, in_=in_), R, W)

    def rstd(self, out, ms_ap, R, W, tmp):
        self.act(tmp[:], ms_ap, AF.Ln, R, [tmp.r], bias=self.eps_t[:, 0:1])
        self.act(out, tmp[:], AF.Exp, [tmp.r], W, scale=-0.5)

    def silu(self, out, out_res, in_, in_res, sg, bias=None, negbias=None, mul_eng="dve"):
        kw = {} if negbias is None else {"bias": negbias[0]}
        rr = list(in_res) + ([] if negbias is None else [negbias[1]])
        self.act(sg[0], in_, AF.Exp, rr, [sg[1]], scale=-1.0, **kw)
        self.act(sg[0], sg[0], AF.Ln, [sg[1]], [sg[1]], bias=1.0)
        self.act(sg[0], sg[0], AF.Exp, [sg[1]], [sg[1]], scale=-1.0)
        if bias is not None:
            self.stt(mul_eng, out, in_, bias[0], sg[0], ALU.add, ALU.mult, list(in_res) + [bias[1], sg[1]], out_res)
        else:
            self.tt(mul_eng, out, in_, sg[0], ALU.mult, list(in_res) + [sg[1]], out_res)

    def dump(self, key, src_ap, src_res, shape, dt=F32):
        if key not in self.dbg or key in self.dbg_out:
            return
        o = self.dout("dbg_" + key, shape, dt)
        self.dma("pool", o.t, src_ap, [src_res] if not isinstance(src_res, list) else src_res, [o.r])
        self.dbg_out[key] = o

    def build(self):
        nc, P = self.nc, self.P
        L = DEPTH
        self.x_in = self.din("x_in", [T, D])
        self.cvecT = self.din("cvecT", [D, 2])
        self.w_mod = self.din("w_mod", [L, D, 6 * D])
        self.b_mod = self.din("b_mod", [L, 6 * D])
        self.gvecs = self.din("gvecs", [L, 4, D])
        self.w_in = self.din("w_in_aug", [L, D, NU])
        self.q_norm = self.din("q_norm", [L, 256])
        self.w_uq = self.din("w_uq_aug", [L, 256, 640])
        self.kv_norm = self.din("kv_norm", [L, 128])
        self.w_ukv_k = self.din("w_ukv_k", [L, 128, 512])
        self.w_ukv_v = self.din("w_ukv_v", [L, 128, 256])
        self.sc_cw = self.din("sc_cw", [L, 256, 3])
        self.ssd_cw = self.din("ssd_cw", [L, 768, 3])
        self.ssd_cb = self.din("ssd_cb", [L, 768])
        self.a_log = self.din("a_log", [L, 16])
        self.dt_bias = self.din("dt_bias", [L, 16])
        self.ssd_d = self.din("ssd_d", [L, 8])
        self.ssd_norm = self.din("ssd_norm", [L, 512])
        self.w_out = self.din("w_out", [L, D, D])
        self.w_ff1 = self.din("w_ff1", [L, D, 4 * D])
        self.w_ff2 = self.din("w_ff2", [L, 4 * D, D])
        self.ropeT = self.din("ropeT", [2, 32, T])
        self.consts = self.din("consts", [128, 3 * 128 + 4])
        self.y_out = self.dout("y_out", [TL, D])
        self.xres1 = self.dscr("xres1", [T, D])
        self.x1s = self.dscr("x1s", [T, D])
        self.sc_z = self.dscr("sc_z", [T, 512])
        self.sc_xs = self.dscr("sc_xs", [T, 512])
        self.sc_bt = self.dscr("sc_bt", [T, 128], BF16)
        self.bv = self.dscr("bv", [12, 128, D])
        self.kv_ctx_k = self.dscr("kv_ctx_k", [128, 4, TC], BF16)
        self.kv_ctx_v = self.dscr("kv_ctx_v", [128, 2, 260], BF16)
        self.xk = self.dscr("xk", [512, TL], BF16)
        self.xk_g = self.dscr("xk_g", [1024, TL], BF16)
        self.xv = self.dscr("xv", [2048, 260], BF16)
        self.xv_g = self.dscr("xv_g", [4096, 260], BF16)
        self.xs_ = self.dscr("xsm", [128, SM_W])
        self.xs_g = self.dscr("xsm_g", [256, SM_W])

        self.ps_all = nc.alloc_psum_tensor("ps_all", [128, 8 * 512], F32)
        self.ps_res = [P.res(f"bank{i}") for i in range(8)]

        cst = self.sb("cst", [128, 3 * 128 + 4], persist=True)
        self.dma("sp", cst[:], self.consts.t, [self.consts.r], [cst.r])
        self.cst = cst
        self.ident_f = Buf(cst.t[:, 0:128], cst.r)
        self.triE = Buf(cst.t[:, 128:256], cst.r)
        self.triL = Buf(cst.t[:, 256:384], cst.r)
        self.lastmask = Buf(cst.t[:, 384:385], cst.r)
        self.esel = Buf(cst.t[:, 385:387], cst.r)
        self.ident_b = self.sb("ident_b", [128, 128], BF16, persist=True)
        self.cp("dve", self.ident_b[:], self.ident_f[:], [cst.r], [self.ident_b.r])
        self.ones_f = self.sb("ones_f", [128, 128], persist=True)
        self.ms("pool", self.ones_f[:], 1.0, [self.ones_f.r])
        self.eps_t = self.sb("eps_t", [128, 1], persist=True)
        self.ms("pool", self.eps_t[:], EPS, [self.eps_t.r])
        self.negm = self.sb("negm", [128, 2, 512], BF16, persist=True)
        for d_, tri in enumerate((self.triE, self.triL)):
            self.ts("dve", self.negm[:, d_, :].rearrange("p (a b) -> p a b", a=4), bcm(tri[:], [128, 4, 128]),
                    -1.0, -NEG, ALU.add, ALU.mult, [cst.r], [self.negm.r])
        self.sT = self.sb("sT", [128, 8, 2], persist=True)
        cv = self.sb("cv", [128, 8, 2], persist=True)
        self.dma("sp", cv[:], self.cvecT.t.rearrange("(k p) r -> p k r", p=128), [self.cvecT.r], [cv.r])
        self.act(self.sT[:], cv[:], AF.Silu, [cv.r], [self.sT.r])
        self.dt_all = self.sb("dt_all", [128, NT, 16], persist=True)

        out_res = []
        xres = self.x_in
        for l in range(self.nlayers):
            last = (l == DEPTH - 1)
            xnext = None if last else self.xres1
            self.layer(l, xres, xnext, last)
            xres = xnext
            if self.stop_after is not None and self.stop_after[0] == l:
                break
        P.final_wait("sp", [self.y_out.r] + [o.r for o in self.dbg_out.values()])
        P.final_wait("pool", [self.y_out.r] + [o.r for o in self.dbg_out.values()])
        P.emit()
        return nc

    def end_scope(self):
        if self.scope is not None:
            self.P.barrier()
            self.scope.close()
            self.scope = None
            self.P.release(self.scope_res)
            self.scope_res = []

    def new_scope(self):
        self.end_scope()
        self.scope = ExitStack()

    def open_ab(self):
        self.end_scope()
        self.scope_ab = ExitStack()

    def close_ab(self):
        self.end_scope()
        if self.scope_ab is not None:
            self.scope_ab.close()
            self.scope_ab = None
            self.P.release(self.ab_res)
            self.ab_res = []

    def stop(self, l, tag):
        return self.stop_after is not None and self.stop_after == (l, tag)

    def layer(self, l, xres, xnext, last):
        self.phase_mod(l)
        if self.stop(l, "mod"):
            return
        self.phase_a(l, xres, last)
        if self.stop(l, "a"):
            return
        self.exchange(l)
        self.phase_b(l, xres, last)
        if self.stop_after is not None and self.stop_after[0] == l and self.stop_after[1] in ("b", "b0", "b1", "b2", "t1", "t2", "t3", "t4", "u1", "u2", "u3", "v1", "v2"):
            return
        self.phase_c(l, xnext, last)

    def phase_mod(self, l):
        self.new_scope()
        modsb = self.sb("modsb", [2, 6 * D])
        bm = self.sb("bm", [2, 6 * D])
        self.dma("sp", bm[:], self.b_mod.t[l].partition_broadcast(2), [self.b_mod.r], [bm.r])
        gv = self.sb("gv", [2, 4 * D])
        self.dma("sp", gv[:], self.gvecs.t[l].rearrange("a d -> (a d)").partition_broadcast(2),
                 [self.gvecs.r], [gv.r])
        slabs = self.rot("wm", 2, [128, 8, 512])
        wmv = self.w_mod.t[l].rearrange("(k p) n -> p k n", p=128)
        for n in range(12):
            sl = slabs.next()
            self.dma("sp", sl[:], wmv[:, :, n * 512:(n + 1) * 512], [self.w_mod.r], [sl.r])
            pb, pr = self.pbank()
            for k in range(8):
                self.mm(pb[0:2, :], self.sT[:, k, :], sl[:, k, :], k == 0, k == 7, [self.sT.r, sl.r], pr)
            self.tt("dve", modsb[:, n * 512:(n + 1) * 512], pb[0:2, :], bm[:, n * 512:(n + 1) * 512], ALU.add,
                    pr + [bm.r], [modsb.r])
        cmb = self.sb("cmb", [2, 6, D])
        m = lambda j: modsb[:, j * D:(j + 1) * D]
        g = lambda j: gv[:, j * D:(j + 1) * D]
        R = [modsb.r, gv.r]
        self.stt("dve", cmb[:, 0, :], m(1), 1.0, g(0), ALU.add, ALU.mult, R, [cmb.r])
        self.cp("dve", cmb[:, 1, :], m(0), R, [cmb.r])
        self.tt("dve", cmb[:, 2, :], m(2), g(1), ALU.mult, R, [cmb.r])
        self.stt("dve", cmb[:, 3, :], m(4), 1.0, g(2), ALU.add, ALU.mult, R, [cmb.r])
        self.cp("dve", cmb[:, 4, :], m(3), R, [cmb.r])
        self.tt("dve", cmb[:, 5, :], m(5), g(3), ALU.mult, R, [cmb.r])
        sel = self.sb("sel", [2, 2, 128])
        self.cp("dve", sel[:, 0, :], self.ident_f.t[0:2, 0:1].to_broadcast([2, 128]), [self.cst.r], [sel.r])
        self.cp("dve", sel[:, 1, :], self.ident_f.t[0:2, 1:2].to_broadcast([2, 128]), [self.cst.r], [sel.r])
        stg = self.rot("bvst", 2, [128, D])
        for s in range(2):
            for j in range(6):
                st = stg.next()
                for hlf in range(2):
                    pb, pr = self.pbank()
                    self.mm(pb[:, :], sel[:, s, :], cmb[:, j, hlf * 512:(hlf + 1) * 512], True, True,
                            [sel.r, cmb.r], pr)
                    self.cp("act", st[:, hlf * 512:(hlf + 1) * 512], pb[:, :], pr, [st.r])
                self.dma("sp", self.bv.t[6 * s + j], st[:], [st.r], [self.bv.r])
        self.dump(f"bv{l}", self.bv.t, self.bv.r, [12, 128, D])

    def esel_rows(self):
        return self.ident_f.t[0:2, 0:2]

    def load_bv(self, idx, name):
        b = self.sb(name, [128, D])
        self.dma("sp", b[:], self.bv.t[idx], [self.bv.r], [b.r])
        return b

    def norm_tile(self, xt, A, Bv, hb, tmp, small):
        ms_, ln_, rs_ = small
        self.act(hb[:], xt[:], AF.Square, [xt.r], [hb.r, ms_.r], scale=1.0 / 32.0, accum=ms_[:])
        self.rstd(rs_[:], ms_[:], [ms_.r], [rs_.r], ln_)
        self.stt("dve", tmp[:], xt[:], rs_[:, 0:1], A[:], ALU.mult, ALU.mult, [xt.r, rs_.r, A.r], [tmp.r])
        self.tt("dve", hb[:], tmp[:], Bv[:], ALU.add, [tmp.r, Bv.r], [hb.r])

    def post_tile(self, pw, prw, xt, G, tmp, small, xo):
        ms_, ln_, rs_ = small
        self.act(tmp[:], pw[:, :], AF.Square, prw, [tmp.r, ms_.r], scale=1.0 / 32.0, accum=ms_[:])
        self.rstd(rs_[:], ms_[:], [ms_.r], [rs_.r], ln_)
        self.stt("dve", tmp[:], pw[:, :], rs_[:, 0:1], G[:], ALU.mult, ALU.mult, prw + [rs_.r, G.r], [tmp.r])
        self.tt("dve", xo[:], xt[:], tmp[:], ALU.add, [xt.r, tmp.r], [xo.r])

    def transpose_tile(self, hb, hT, tl):
        pb, pr = self.pbank()
        pbb = pb.bitcast(BF16)
        for k in range(8):
            self.tr(pbb[:, k * 128:(k + 1) * 128], hb[:, k * 128:(k + 1) * 128], self.ident_b[:],
                    [hb.r, self.ident_b.r], pr)
        self.cp("act", hT[:, :, tl * 128:(tl + 1) * 128], pbb[:, :].rearrange("p (k t) -> p k t", k=8), pr, [hT.r])

    def small3(self, n):
        return Rot([(self.sb("ms", [128, 1]), self.sb("ln", [128, 1]), self.sb("rs", [128, 1])) for _ in range(n)])

    def phase_a(self, l, xres, last):
        nc, P = self.nc, self.P
        self.open_ab()
        self.qT = self.sb("qT", [128, 4, T], BF16, ab=True)
        self.ysc = self.sb("ysc", [128, 2, T], BF16, ab=True)
        self.BT = self.sb("BT", [128, T], BF16, ab=True)
        self.CT = self.sb("CT", [128, T], BF16, ab=True)
        self.HEs = self.sb("HEs", [128, NT, 256], BF16, ab=True)
        self.fix = self.sb("fix", [128, 16], ab=True)
        self.xsT_last = self.sb("xsT_last", [128, 4, 128], ab=True)
        self.new_scope()
        NB = 256
        win = self.sb("win", [128, 8, NU], BF16)
        for k in range(8):
            self.dma("pool", win[:, k, :], self.w_in.t[l, k * 128:(k + 1) * 128, :], [self.w_in.r], [win.r])
        wst = self.sb("wst", [128, 2, 640])
        self.dma("sp", wst[:], self.w_uq.t[l].rearrange("(c p) n -> p c n", p=128), [self.w_uq.r], [wst.r])
        gq = self.sb("gq", [128, 2])
        self.dma("sp", gq[:], self.q_norm.t[l].rearrange("(c p) -> p c", p=128), [self.q_norm.r], [gq.r], slow_ok=True)
        wuq = self.sb("wuq", [128, 2, 640], BF16)
        for c in range(2):
            self.ts("dve", wuq[:, c, :], wst[:, c, :], gq[:, c:c + 1], None, ALU.mult, None, [wst.r, gq.r], [wuq.r])
        wst2 = self.sb("wst2", [128, 768])
        self.dma("sp", wst2[:, 0:512], self.w_ukv_k.t[l], [self.w_ukv_k.r], [wst2.r])
        self.dma("sp", wst2[:, 512:768], self.w_ukv_v.t[l], [self.w_ukv_v.r], [wst2.r])
        gkv = self.sb("gkv", [128, 1])
        self.dma("sp", gkv[:], self.kv_norm.t[l].rearrange("(p o) -> p o", o=1), [self.kv_norm.r], [gkv.r])
        wkv = self.sb("wkv", [128, 768], BF16)
        self.ts("dve", wkv[:], wst2[:], gkv[:, 0:1], None, ALU.mult, None, [wst2.r, gkv.r], [wkv.r])
        sccw, cw, cb = self.conv_consts(l)
        st = self.ssd_consts(l)
        A1 = self.load_bv(6, "A1")
        B1 = self.load_bv(7, "B1")
        HE = self.sb("HE", [128, 256])
        self.ms("dve", HE[:], 0.0, [HE.r])

        xts = self.rot("xt", 2, [128, D])
        tmp = self.sb("tmp", [128, D])
        hb = self.sb("hb", [128, D], BF16)
        hT = self.sb("hT", [128, 8, NB], BF16)
        smalls = self.small3(2)
        zs = self.rot("zs", 2, [128, 512])
        dtt = self.sb("dtt", [128, 16])
        rqb = self.sb("rqb", [128, NB])
        rkvb = self.sb("rkvb", [128, NB])
        lnb = self.sb("lnb", [128, NB])
        cosr = self.sb("cosr", [32, 2, NB])
        t12 = self.sb("t12", [32, 2, NB])
        rkc = self.sb("rkc", [128, 4])
        rkl = self.sb("rkl", [128, 4])
        KTb = self.sb("KTb", [128, 4, NB], BF16)
        Vb = self.sb("Vb", [128, 2, 260], BF16)
        self.ms("pool", Vb[:], 1.0, [Vb.r])
        xbc = self.rot("xbc", 2, [128, 6, NB + 2])
        pbuf = self.rot("pbuf", 2, [128, 2, NB + 2])
        gbb = self.rot("gbb", 2, [128, 2, NB])
        gcs = self.sb("gcs", [128, NB])
        W = dict(sg=self.rot("sg", 2, [128, NB]), cvo=self.rot("cvo", 2, [128, NB]), xsT=self.sb("xsT", [128, 4, NB]), BCf=self.sb("BCf", [128, 2, NB]),
                 xstm=self.rot("xstm", 2, [128, 512]), btm=self.rot("btm", 2, [128, 128], BF16),
                 sm16=self.sb("sm16", [128, 16]), vt=self.sb("vt", [128, 16]), wE=self.sb("wE", [128, 8]),
                 decE=self.sb("decE", [128, 4]), xdtw=self.rot("xdtw", 2, [128, 512], BF16),
                 hetmp=self.sb("hetmp", [128, 256]), HE=HE, sccw=sccw, cw=cw, cb=cb, st=st)

        blocks = [(2 * i, 2) for i in range(NT // 2)]
        nb = len(blocks)
        S = dict(win=win, wuq=wuq, wkv=wkv, A1=A1, B1=B1, xts=xts, tmp=tmp, hb=hb, hT=hT, smalls=smalls, zs=zs, dtt=dtt,
                 st=st, rqb=rqb, rkvb=rkvb, lnb=lnb, cosr=cosr, t12=t12, rkc=rkc, rkl=rkl, KTb=KTb, Vb=Vb, gcs=gcs,
                 mla=[dict(cqT=self.sb("cqT", [128, 2, NB], BF16), cqsq=self.sb("cqsq", [128, 2, NB]),
                           ckvT=self.sb("ckvT", [128, NB], BF16), ckvsq=self.sb("ckvsq", [128, NB]),
                           krs=self.sb("krs", [32, 2, NB]), rope=self.sb("rope", [32, 2, NB])) for _ in range(2)],
                 slot=[dict(xb=xbc.next(), pbf=pbuf.next(), gb=gbb.next()) for _ in range(2)])

        def front1(bi):
            t0, nt = blocks[bi]
            N = nt * 128
            c0 = t0 * 128
            M = S["mla"][bi % 2]
            if bi == 1:
                self.dma("sp", A1[:], self.bv.t[0], [self.bv.r], [A1.r])
                self.dma("sp", B1[:], self.bv.t[1], [self.bv.r], [B1.r])
            for tl in range(nt):
                t = t0 + tl
                xt = xts.next()
                self.dma("sp", xt[:], xres.t[t * 128:(t + 1) * 128, :], [xres.r], [xt.r])
                self.norm_tile(xt, A1, B1, hb, tmp, smalls.next())
                self.transpose_tile(hb, hT, tl)
                yield
            for tl in range(nt):
                t = t0 + tl
                pb, pr = self.pbank()
                for k in range(8):
                    self.mm(pb[:, :], hT[:, k, tl * 128:(tl + 1) * 128], win[:, k, Z_OFF:Z_OFF + 512], k == 0, k == 7,
                            [hT.r, win.r], pr)
                z = zs.next()
                self.cp("act", z[:], pb[:, :], pr, [z.r])
                self.dma("pool", self.sc_z.t[t * 128:(t + 1) * 128, :], z[:], [z.r], [self.sc_z.r])
                pb, pr = self.pbank()
                for k in range(8):
                    self.mm(pb[:, 0:16], hT[:, k, tl * 128:(tl + 1) * 128], win[:, k, DT_OFF:DT_OFF + 16], k == 0, k == 7,
                            [hT.r, win.r], pr)
                self.tt("dve", dtt[:], pb[:, 0:16], st["dtb"][:], ALU.add, pr + [st["dtb"].r], [dtt.r])
                self.act(dtt[:], dtt[:], AF.Exp, [dtt.r], [dtt.r])
                self.act(self.dt_all[:, t, :], dtt[:], AF.Ln, [dtt.r], [self.dt_all.r], bias=1.0)
                yield
            self.dma("sp", M["rope"][:, :, 0:N], self.ropeT.t[:, :, c0:c0 + N].rearrange("a p t -> p a t"),
                     [self.ropeT.r], [M["rope"].r])
            for ci in range(5):
                off, wdt = FM[ci]
                pb, pr = self.pbank()
                for k in range(8):
                    self.mm(pb[0:wdt, 0:N], win[:, k, off:off + wdt], hT[:, k, 0:N], k == 0, k == 7, [hT.r, win.r], pr)
                src = pb[0:wdt, 0:N]
                if ci < 2:
                    self.cp("act", M["cqT"][:, ci, 0:N], src, pr, [M["cqT"].r])
                    self.act(M["cqsq"][:, ci, 0:N], src, AF.Square, pr, [M["cqsq"].r])
                elif ci == 2:
                    self.cp("act", M["ckvT"][:, 0:N], src, pr, [M["ckvT"].r])
                    self.act(M["ckvsq"][:, 0:N], src, AF.Square, pr, [M["ckvsq"].r])
                else:
                    self.cp("act", M["krs"][:, ci - 3, 0:N], src, pr, [M["krs"].r])
                yield

        def front2(bi):
            t0, nt = blocks[bi]
            N = nt * 128
            sl = S["slot"][bi % 2]
            xb, pbf, gb = sl["xb"], sl["pbf"], sl["gb"]
            for ci in (5, 6, 7, 8, 11, 12, 13, 14, 15, 16):
                off, wdt = FM[ci]
                pb, pr = self.pbank()
                for k in range(8):
                    self.mm(pb[0:wdt, 0:N], win[:, k, off:off + wdt], hT[:, k, 0:N], k == 0, k == 7, [hT.r, win.r], pr)
                src = pb[0:wdt, 0:N]
                if ci < 7:
                    self.cp("act", gb[:, ci - 5, 0:N], src, pr, [gb.r])
                elif ci < 9:
                    c = ci - 7
                    pbv, prv = self.pbank()
                    o2, w2 = FM[ci + 2]
                    for k in range(8):
                        self.mm(pbv[:, 0:N], win[:, k, o2:o2 + w2], hT[:, k, 0:N], k == 0, k == 7, [hT.r, win.r], prv)
                    self.cp("act", gcs[:, 0:N], src, pr, [gcs.r])
                    self.tt("dve", pbf[:, c, 1:N + 1], gcs[:, 0:N], pbv[:, 0:N], ALU.mult, [gcs.r] + prv, [pbf.r])
                else:
                    self.cp("act", xb[:, ci - 11, 1:N + 1], src, pr, [xb.r])
                yield
            self.ms("pool", xb[:, :, 0:1], 0.0, [xb.r])
            self.ms("pool", pbf[:, :, 0:1], 0.0, [pbf.r])
            self.ms("pool", xb[:, :, N + 1:N + 2], 0.0, [xb.r])
            self.ms("pool", pbf[:, :, N + 1:N + 2], 0.0, [pbf.r])
            yield

        def mla_post(bi):
            t0, nt = blocks[bi]
            N = nt * 128
            c0 = t0 * 128
            M = S["mla"][bi % 2]
            cqT, cqsq, ckvT, ckvsq, krs, rope = M["cqT"], M["cqsq"], M["ckvT"], M["ckvsq"], M["krs"], M["rope"]
            pb, pr = self.pbank()
            for c in range(2):
                self.mm(pb[:, 0:N], self.ones_f[:], cqsq[:, c, 0:N], c == 0, c == 1, [self.ones_f.r, cqsq.r], pr)
            self.act(lnb[:, 0:N], pb[:, 0:N], AF.Ln, pr + [self.eps_t.r], [lnb.r], bias=self.eps_t[:, 0:1], scale=1.0 / 256)
            self.act(rqb[:, 0:N], lnb[:, 0:N], AF.Exp, [lnb.r], [rqb.r], scale=-0.5)
            pb, pr = self.pbank()
            self.mm(pb[:, 0:N], self.ones_f[:], ckvsq[:, 0:N], True, True, [self.ones_f.r, ckvsq.r], pr)
            self.act(lnb[:, 0:N], pb[:, 0:N], AF.Ln, pr + [self.eps_t.r], [lnb.r], bias=self.eps_t[:, 0:1], scale=1.0 / 128)
            self.act(rkvb[:, 0:N], lnb[:, 0:N], AF.Exp, [lnb.r], [rkvb.r], scale=-0.5)
            pb, pr = self.pbank()
            for tl in range(nt):
                self.mm(pb[:, tl:tl + 1], ckvsq[:, tl * 128:(tl + 1) * 128], self.ones_f[:, 0:1], True, True,
                        [ckvsq.r, self.ones_f.r], pr)
            self.act(rkl[:, 0:nt], pb[:, 0:nt], AF.Ln, pr + [self.eps_t.r], [rkl.r], bias=self.eps_t[:, 0:1], scale=1.0 / 128)
            self.act(rkc[:, 0:nt], rkl[:, 0:nt], AF.Exp, [rkl.r], [rkc.r], scale=-0.5)
            yield
            for a in range(2):
                self.tt("pool", cosr[:, a, 0:N], rope[:, a, 0:N], rqb[0:32, 0:N], ALU.mult, [rope.r, rqb.r], [cosr.r])
            for h in range(4):
                pa, pra = self.pbank()
                for c in range(2):
                    self.mm(pa[:, 0:N], wuq[:, c, h * 160:h * 160 + 128], cqT[:, c, 0:N], c == 0, c == 1, [wuq.r, cqT.r], pra)
                pq, prq = self.pbank()
                for c in range(2):
                    self.mm(pq[0:32, 0:N], wuq[:, c, h * 160 + 128:h * 160 + 160], cqT[:, c, 0:N], c == 0, c == 1,
                            [wuq.r, cqT.r], prq)
                self.tt("dve", self.qT[:, h, c0:c0 + N], pa[:, 0:N], rqb[:, 0:N], ALU.mult, pra + [rqb.r], [self.qT.r])
                self.tt("dve", t12[:, 0, 0:N], pa[0:32, 0:N], cosr[:, 0, 0:N], ALU.mult, pra + [cosr.r], [t12.r])
                self.tt("dve", t12[:, 1, 0:N], pq[0:32, 0:N], cosr[:, 1, 0:N], ALU.mult, prq + [cosr.r], [t12.r])
                self.tt("dve", self.qT[0:32, h, c0:c0 + N], t12[:, 0, 0:N], t12[:, 1, 0:N], ALU.add, [t12.r], [self.qT.r])
                yield
            self.tt("pool", t12[:, 0, 0:N], krs[:, 0, 0:N], rope[:, 0, 0:N], ALU.mult, [krs.r, rope.r], [t12.r])
            self.tt("pool", t12[:, 1, 0:N], krs[:, 1, 0:N], rope[:, 1, 0:N], ALU.mult, [krs.r, rope.r], [t12.r])
            self.tt("pool", t12[:, 0, 0:N], t12[:, 0, 0:N], t12[:, 1, 0:N], ALU.add, [t12.r], [t12.r])
            for h in range(4):
                pk, prk = self.pbank()
                self.mm(pk[:, 0:N], wkv[:, h * 128:(h + 1) * 128], ckvT[:, 0:N], True, True, [wkv.r, ckvT.r], prk)
                self.tt("dve", KTb[:, h, 0:N], pk[:, 0:N], rkvb[:, 0:N], ALU.mult, prk + [rkvb.r], [KTb.r])
                self.cp("pool", KTb[0:32, h, 0:N], t12[:, 0, 0:N], [t12.r], [KTb.r])
            yield
            if bi == 0:
                self.dma("pool", self.kv_ctx_k.t, KTb[:, :, 0:N], [KTb.r], [self.kv_ctx_k.r])
            else:
                lo = c0 - TC
                self.dma("pool", self.xk.t.rearrange("(h p) t -> p h t", p=128)[:, :, lo:lo + N], KTb[:, :, 0:N],
                         [KTb.r], [self.xk.r])
            for tl in range(nt):
                pv, prv = self.pbank()
                self.mm(pv[:, 0:256], ckvT[:, tl * 128:(tl + 1) * 128], wkv[:, 512:768], True, True, [ckvT.r, wkv.r], prv)
                self.ts("dve", Vb[:, tl, :].rearrange("p (h e) -> p h e", h=4)[:, :, 0:64],
                        pv[:, 0:256].rearrange("p (h e) -> p h e", h=4), rkc[:, tl:tl + 1], None, ALU.mult, None,
                        prv + [rkc.r], [Vb.r])
            if bi == 0:
                self.dma("pool", self.kv_ctx_v.t, Vb[:, 0:2, :], [Vb.r], [self.kv_ctx_v.r])
            else:
                lo = t0 - 2
                self.dma("pool", self.xv.t.rearrange("(t p) e -> p t e", p=128)[:, lo:lo + nt, :], Vb[:, 0:nt, :],
                         [Vb.r], [self.xv.r])
            yield

        def back1(bi):
            if bi == 0:
                return
            p = S["slot"][(bi - 1) % 2]
            pt0, pnt = blocks[bi - 1]
            pN = pnt * 128
            if 2 <= bi < nb:
                c = S["slot"][bi % 2]
                self.cp("pool", p["xb"][:, :, pN + 1:pN + 2], c["xb"][:, :, 1:2], [c["xb"].r], [p["xb"].r])
                self.cp("pool", p["pbf"][:, :, pN + 1:pN + 2], c["pbf"][:, :, 1:2], [c["pbf"].r], [p["pbf"].r])
                self.cp("pool", c["xb"][:, :, 0:1], p["xb"][:, :, pN:pN + 1], [p["xb"].r], [c["xb"].r])
                self.cp("pool", c["pbf"][:, :, 0:1], p["pbf"][:, :, pN:pN + 1], [p["pbf"].r], [c["pbf"].r])
            yield from self.stage_a2_conv(l, p["xb"], p["pbf"], p["gb"], pt0, pnt, W, bi == nb)

        def back2(bi):
            if bi < nb:
                yield from mla_post(bi)
            if bi >= 1:
                pt0, pnt = blocks[bi - 1]
                yield from self.stage_a2_scan(l, pt0, pnt, W, bi == nb)

        run_gen(front1(0))
        run_gen(front2(0))
        for bi in range(nb + 1):
            f1 = front1(bi + 1) if bi + 1 < nb else None
            f2 = front2(bi + 1) if bi + 1 < nb else None
            interleave((f1, 9), (back1(bi), 8))
            interleave((f2, 11), (back2(bi), 10))
        pxb, ppb = S["slot"][(nb - 1) % 2]["xb"], S["slot"][(nb - 1) % 2]["pbf"]
        pnt = blocks[-1][1]
        N = pnt * 128
        sm = self.sb("sm", [128, SM_W])
        self.ms("dve", sm[:], 0.0, [sm.r])
        self.cp("dve", sm[:, 0:256], HE[:], [HE.r], [sm.r])
        self.cp("dve", sm[:, 512:524].rearrange("p (c k) -> p c k", c=6), pxb[:, :, N - 1:N + 1], [pxb.r], [sm.r])
        self.cp("dve", sm[:, 524:526], ppb[:, :, N], [ppb.r], [sm.r])
        self.cp("dve", self.fix[:, 10:16], pxb[:, :, N], [pxb.r], [self.fix.r])
        self.dma("pool", self.xs_.t, sm[:], [sm.r], [self.xs_.r])
        self.dma("pool", self.xs_.t[0:1, 528:536], self.dt_all[127:128, NT - 1, 0:8], [self.dt_all.r], [self.xs_.r])
        self.dump(f"qT{l}", self.qT[:], self.qT.r, [128, 4, T], BF16)
        self.dump(f"ysc{l}", self.ysc[:], self.ysc.r, [128, 2, T], BF16)
        self.dump(f"dt{l}", self.dt_all[:], self.dt_all.r, [128, NT, 16])
        self.dump(f"HE{l}", HE[:], HE.r, [128, 256])
        self.dump(f"BT{l}", self.BT[:], self.BT.r, [128, T], BF16)
        self.dump(f"CT{l}", self.CT[:], self.CT.r, [128, T], BF16)
        self.dump(f"xk{l}", self.xk.t, self.xk.r, [512, TL], BF16)
        self.dump(f"xv{l}", self.xv.t, self.xv.r, [2048, 260], BF16)
        self.dump(f"scxs{l}", self.sc_xs.t, self.sc_xs.r, [T, 512])
        self.dump(f"scz{l}", self.sc_z.t, self.sc_z.r, [T, 512])

    def conv_consts(self, l):
        sccw = self.sb("sccw", [128, 2, 3])
        self.dma("sp", sccw[:], self.sc_cw.t[l].rearrange("(c p) k -> p c k", p=128), [self.sc_cw.r], [sccw.r])
        cw = self.sb("cw", [128, 6, 3])
        self.dma("sp", cw[:], self.ssd_cw.t[l].rearrange("(c p) k -> p c k", p=128), [self.ssd_cw.r], [cw.r])
        cb = self.sb("cb", [128, 6])
        self.dma("sp", cb[:], self.ssd_cb.t[l].rearrange("(c p) -> p c", p=128), [self.ssd_cb.r], [cb.r], slow_ok=True)
        self.ncb = self.sb("ncb", [128, 6])
        self.ts("dve", self.ncb[:], cb[:], -1.0, None, ALU.mult, None, [cb.r], [self.ncb.r])
        return sccw, cw, cb

    def ssd_consts(self, l):
        st = {}
        al = self.sb("al", [128, 16])
        self.dma("sp", al[:], self.a_log.t[l].partition_broadcast(128), [self.a_log.r], [al.r])
        ab = self.sb("ab", [128, 16])
        self.act(ab[:], al[:], AF.Exp, [al.r], [ab.r])
        self.ts("dve", ab[:], ab[:], -1.0, None, ALU.mult, None, [ab.r], [ab.r])
        st["ab"] = ab
        dtb = self.sb("dtb", [128, 16])
        self.dma("sp", dtb[:], self.dt_bias.t[l].partition_broadcast(128), [self.dt_bias.r], [dtb.r])
        st["dtb"] = dtb
        return st

    def stage_a2_conv(self, l, xb, pbf, gb, t0, nt, W, is_last):
        N = nt * 128
        c0 = t0 * 128
        xsT, BCf, sccw, cw, cb = W["xsT"], W["BCf"], W["sccw"], W["cw"], W["cb"]
        for c in range(2):
            cvo = W["cvo"].next()
            self.conv3(cvo, pbf, c, sccw, N)
            self.tt("dve", self.ysc[:, c, c0:c0 + N], cvo[:, 0:N], gb[:, c, 0:N], ALU.mult, [cvo.r, gb.r], [self.ysc.r])
            if is_last:
                self.cp("dve", self.fix[:, 6 + c:7 + c], cvo[:, N - 1:N], [cvo.r], [self.fix.r])
                self.cp("dve", self.fix[:, 8 + c:9 + c], gb[:, c, N - 1:N], [gb.r], [self.fix.r])
            yield
        for c in range(6):
            cvo = W["cvo"].next()
            self.conv3(cvo, xb, c, cw, N)
            if is_last:
                self.cp("dve", self.fix[:, c:c + 1], cvo[:, N - 1:N], [cvo.r], [self.fix.r])
            dst = xsT[:, c, 0:N] if c < 4 else BCf[:, c - 4, 0:N]
            dres = xsT.r if c < 4 else BCf.r
            sg = W["sg"].next()
            self.silu(dst, [dres], cvo[:, 0:N], [cvo.r], (sg[:, 0:N], sg.r), bias=(cb[:, c:c + 1], cb.r),
                      negbias=(self.ncb[:, c:c + 1], self.ncb.r))
            yield
        self.cp("pool", self.BT[:, c0:c0 + N], BCf[:, 0, 0:N], [BCf.r], [self.BT.r])
        self.cp("pool", self.CT[:, c0:c0 + N], BCf[:, 1, 0:N], [BCf.r], [self.CT.r])
        if is_last:
            self.cp("pool", self.xsT_last[:], xsT[:, :, N - 128:N], [xsT.r], [self.xsT_last.r])

    def stage_a2_scan(self, l, t0, nt, W, is_last):
        xsT, BCf = W["xsT"], W["BCf"]
        for tl in range(nt):
            t = t0 + tl
            xs_tm, b_tm = self.tokmajor_xs_b(xsT, BCf, tl, t, W["xstm"], W["btm"])
            self.cp("act", self.HEs[:, t, :], W["HE"][:], [W["HE"].r], [self.HEs.r])
            yield
            self.state_update(W["st"], xs_tm, b_tm, t, 0, W["sm16"], W["vt"], W["wE"], W["decE"], W["xdtw"], W["hetmp"],
                              W["HE"], mask_last=(is_last and tl == nt - 1))
            yield

    def conv3(self, cvo, buf, c, wts, N):
        self.ts("dve", cvo[:, 0:N], buf[:, c, 1:N + 1], wts[:, c, 1:2], None, ALU.mult, None, [buf.r, wts.r], [cvo.r])
        self.stt("dve", cvo[:, 0:N], buf[:, c, 0:N], wts[:, c, 0:1], cvo[:, 0:N], ALU.mult, ALU.add,
                 [buf.r, wts.r, cvo.r], [cvo.r])
        self.stt("dve", cvo[:, 0:N], buf[:, c, 2:N + 2], wts[:, c, 2:3], cvo[:, 0:N], ALU.mult, ALU.add,
                 [buf.r, wts.r, cvo.r], [cvo.r])

    def tokmajor_xs_b(self, xsT, BCf, tl, t, xstm, btm):
        pb, pr = self.pbank()
        for c in range(4):
            self.tr(pb[:, c * 128:(c + 1) * 128], xsT[:, c, tl * 128:(tl + 1) * 128], self.ident_f[:],
                    [xsT.r, self.cst.r], pr)
        xs_tm = xstm.next()
        self.cp("act", xs_tm[:], pb[:, :], pr, [xs_tm.r])
        self.dma("pool", self.sc_xs.t[t * 128:(t + 1) * 128, :], xs_tm[:], [xs_tm.r], [self.sc_xs.r])
        pb2, pr2 = self.pbank()
        self.tr(pb2[:, 0:128], BCf[:, 0, tl * 128:(tl + 1) * 128], self.ident_f[:], [BCf.r, self.cst.r], pr2)
        b_tm = btm.next()
        self.cp("act", b_tm[:], pb2[:, 0:128], pr2, [b_tm.r])
        self.dma("pool", self.sc_bt.t[t * 128:(t + 1) * 128, :], b_tm[:], [b_tm.r], [self.sc_bt.r])
        return xs_tm, b_tm

    def state_update(self, st, xs_tm, b_tm, t, d_, sm16, vt, wE, decE, xdtw, hetmp, H, mask_last=False, vt_ready=None):
        tri = self.triE if d_ == 0 else self.triL
        o = d_ * 8
        if vt_ready is None:
            la = sm16
            self.tt("dve", la[:, 0:8], self.dt_all[:, t, o:o + 8], st["ab"][:, o:o + 8], ALU.mult,
                    [self.dt_all.r, st["ab"].r], [la.r])
            pb, pr = self.pbank()
            self.mm(pb[:, 0:8], tri[:], la[:, 0:8], True, True, [self.cst.r, la.r], pr)
            self.mm(pb[:, 8:16], self.ones_f[:], la[:, 0:8], True, True, [self.ones_f.r, la.r], pr)
            self.cp("act", vt[:, 0:16], pb[:, 0:16], pr, [vt.r])
            vcol, tot = vt[:, 0:8], vt[:, 8:16]
            totlo, tothi = vt[0:64, 8:12], vt[64:128, 12:16]
            vr = vt.r
        else:
            vcol, tot, totlo, tothi, vr = vt_ready
        self.tt("dve", wE[:], tot, vcol, ALU.subtract, [vr], [wE.r])
        self.act(wE[:], wE[:], AF.Exp, [wE.r], [wE.r])
        self.tt("dve", wE[:], wE[:], self.dt_all[:, t, o:o + 8], ALU.mult, [wE.r, self.dt_all.r], [wE.r])
        if mask_last:
            self.ts("dve", wE[:], wE[:], self.lastmask[:, 0:1], None, ALU.mult, None, [wE.r, self.cst.r], [wE.r])
        self.act(decE[0:64, :], totlo, AF.Exp, [vr], [decE.r])
        self.act(decE[64:128, :], tothi, AF.Exp, [vr], [decE.r])
        xd = xdtw.next()
        self.tt("dve", xd[:].rearrange("p (h e) -> p h e", h=8), xs_tm[:].rearrange("p (h e) -> p h e", h=8),
                bc(wE[:], [128, 8, 64]), ALU.mult, [xs_tm.r, wE.r], [xd.r])
        pb, pr = self.pbank()
        self.mm(pb[:, :], b_tm[:], xd[:], True, True, [b_tm.r, xd.r], pr)
        self.tt("dve", hetmp[:].rearrange("p (h e) -> p h e", h=4), H[:].rearrange("p (h e) -> p h e", h=4),
                bc(decE[:], [128, 4, 64]), ALU.mult, [H.r, decE.r], [hetmp.r])
        self.tt("dve", H[0:64, :], hetmp[0:64, :], pb[0:64, 0:256], ALU.add, [hetmp.r] + pr, [H.r])
        self.tt("dve", H[64:128, :], hetmp[64:128, :], pb[64:128, 256:512], ALU.add, [hetmp.r] + pr, [H.r])

    def exchange(self, l):
        groups = [[2 * i, 2 * i + 1] for i in range(self.ncores // 2)]
        for src, dst in ((self.xk, self.xk_g), (self.xv, self.xv_g), (self.xs_, self.xs_g)):
            self.cc(src, dst, groups)

    def cc(self, src, dst, groups):
        if self.fake_cc:
            n = src.t.shape[0]
            for r in range(2):
                self.dma("pool", dst.t[r * n:(r + 1) * n, :], src.t, [src.r], [dst.r])
            return
        self.P.dma("pool", lambda e: e.collective_compute("AllGather", ALU.bypass, replica_groups=groups,
                                                           ins=[src.t.opt()], outs=[dst.t.opt()]),
                   [src.r], [dst.r], inc=1)

    def phase_b(self, l, xres, last):
        nc, P = self.nc, self.P
        self.new_scope()
        st = self.ssd_consts(l)
        sccw, cw, cb = self.conv_consts(l)
        wo_att = self.sb("wo_att", [64, 4, D], BF16)
        self.dma("pool", wo_att[:], self.w_out.t[l, 0:256, :].rearrange("(h p) n -> p h n", p=64), [self.w_out.r], [wo_att.r])
        wo_r = self.sb("wo_r", [128, 6, D], BF16)
        self.dma("pool", wo_r[:], self.w_out.t[l, 256:1024, :].rearrange("(c p) n -> p c n", p=128), [self.w_out.r], [wo_r.r])
        G1 = self.load_bv(2, "G1")
        dsk8 = self.sb("dsk8", [128, 8])
        self.dma("sp", dsk8[:], self.ssd_d.t[l].partition_broadcast(128), [self.ssd_d.r], [dsk8.r])
        ngb = self.sb("ngb", [128, 512])
        self.dma("sp", ngb[:], self.ssd_norm.t[l].partition_broadcast(128), [self.ssd_norm.r], [ngb.r])
        VA = self.sb("VA", [128, 34, 260], BF16)
        self.dma("sp", VA[:, 0:2, :], self.kv_ctx_v.t, [self.kv_ctx_v.r], [VA.r])
        xvv = self.xv_g.t.rearrange("(t p) e -> p t e", p=128)
        for q in range(4):
            self.dma("sp", VA[:, 2 + 8 * q:10 + 8 * q, :], xvv[:, 8 * q:8 * q + 8, :], [self.xv_g.r], [VA.r])
        KTs = self.rot("KT", 2, [128, TC + 2 * TL], BF16)
        smg = self.sb("smg", [128, 2, SM_W])
        self.dma("sp", smg[:], self.xs_g.t.rearrange("(r p) c -> p r c", p=128), [self.xs_g.r], [smg.r])
        smp = self.sb("smp", [128, SM_W])
        self.ts("dve", smp[:], smg[:, 1, :], self.esel[:, 0:1], None, ALU.mult, None, [smg.r, self.cst.r], [smp.r])
        self.stt("dve", smp[:], smg[:, 0, :], self.esel[:, 1:2], smp[:], ALU.mult, ALU.add, [smg.r, self.cst.r, smp.r], [smp.r])

        xt = self.sb("xt", [128, D])
        tmp = self.sb("tmp", [128, D])
        smalls = self.small3(2)
        x1 = self.sb("x1", [128, D])
        la = self.sb("la", [128, 16])
        vt = self.sb("vt", [128, 32])
        ev = self.sb("ev", [128, 16])
        wL = self.sb("wL", [128, 8])
        decL = self.sb("decL", [128, 4])
        Rm = self.sb("Rm", [128, 8, 128])
        D1 = self.sb("D1", [128, 8, 128])
        MT = self.sb("MT", [128, 16, 128], BF16)
        xdt = self.sb("xdt", [128, 16, 64], BF16)
        xs_r = self.rot("xsr", 2, [128, 512])
        bt_r = self.rot("btr", 2, [128, 128], BF16)
        z_r = self.rot("zr", 2, [128, 512])
        ya = self.sb("ya", [128, 512])
        yb = self.sb("yb", [128, 512])
        sz = self.sb("sz", [128, 512])
        yss = self.sb("yss", [128, 512], BF16)
        jk = self.sb("jk", [128, 512], BF16)
        jk32 = self.sb("jk32", [128, 512])
        ms2 = self.sb("ms2", [128, 2])
        ln2 = self.sb("ln2", [128, 2])
        rs2 = self.sb("rs2", [128, 2])
        xdtw = self.rot("xdtw", 2, [128, 512], BF16)
        hetmp = self.sb("hetmp", [128, 256])
        HL = self.sb("HL", [128, 256])
        HLb = self.sb("HLbd", [128, 512], BF16)
        self.ms("pool", HLb[:], 0.0, [HLb.r])
        Cbd = self.sb("Cbd", [128, 256], BF16)
        self.ms("pool", Cbd[:], 0.0, [Cbd.r])
        HbdE = self.sb("HbdE", [128, 512], BF16)
        self.ms("pool", HbdE[:], 0.0, [HbdE.r])
        ymx = self.sb("ymx", [128, 4, 512], BF16)
        yatt = self.sb("yatt", [64, 4, 512], BF16)
        PTs = self.rot("PT", 2, [128, 1024], BF16)
        Osb = self.sb("Osb", [64, 512])
        rec = self.sb("rec", [128, 512])
        dskb = self.sb("dskb", [128, 512])
        self.cp("dve", dskb[:].rearrange("p (h e) -> p h e", h=8), bc(dsk8[:], [128, 8, 64]), [dsk8.r], [dskb.r])

        self.l_init(l, st, smp, cw, cb, HL, HLb)
        self.fix_last(l, smp, sccw, cw, cb)
        self.dump(f"HL0{l}", HL[:], HL.r, [128, 256])
        if self.stop(l, "b0"):
            return

        order = [4, 3, 2, 1] + ([] if last else [0])
        ymxs = [ymx, self.sb("ymx2", [128, 4, 512], BF16)]
        yatts = [yatt, self.sb("yatt2", [64, 4, 512], BF16)]

        def geom(bi):
            t0, nt = BLOCKS[bi]
            return t0, nt, nt * 128, t0 * 128, (1 if bi == 0 else 0)

        def ssd_gen(bi, ymx):
            t0, nt, N, c0, isctx = geom(bi)
            if isctx:
                self.ms("dve", HL[:], 0.0, [HL.r])
                self.ms("dve", HLb[:], 0.0, [HLb.r])
            for tl in range(nt - 1, -1, -1):
                t = t0 + tl
                cs = slice(t * 128, (t + 1) * 128)
                xs = xs_r.next()
                self.dma("pool", xs[:], self.sc_xs.t[cs, :], [self.sc_xs.r], [xs.r])
                btk = bt_r.next()
                self.dma("pool", btk[:], self.sc_bt.t[cs, :], [self.sc_bt.r], [btk.r])
                z = z_r.next()
                self.dma("pool", z[:], self.sc_z.t[cs, :], [self.sc_z.r], [z.r])
                self.tt("dve", la[:], self.dt_all[:, t, :], st["ab"][:], ALU.mult, [self.dt_all.r, st["ab"].r], [la.r])
                pb, pr = self.pbank()
                self.mm(pb[:, 0:8], self.triE[:], la[:, 0:8], True, True, [self.cst.r, la.r], pr)
                self.mm(pb[:, 8:16], self.triL[:], la[:, 8:16], True, True, [self.cst.r, la.r], pr)
                self.mm(pb[:, 16:32], self.ones_f[:], la[:, 0:16], True, True, [self.ones_f.r, la.r], pr)
                self.cp("act", vt[:], pb[:, 0:32], pr, [vt.r])
                self.act(ev[:], vt[:, 0:16], AF.Exp, [vt.r], [ev.r])
                yield
                pg, prg = self.pbank(pin=True)
                for g in range(2):
                    self.cp("pool", Cbd[g * 64:(g + 1) * 64, g * 128:(g + 1) * 128], self.CT[g * 64:(g + 1) * 64, cs],
                            [self.CT.r], [Cbd.r])
                self.mm(pg[:, 0:256], self.BT[:, cs], Cbd[:], True, True, [self.BT.r, Cbd.r], prg)
                for d_ in range(2):
                    self.tt("dve", xdt[:, d_ * 8:(d_ + 1) * 8, :], xs[:].rearrange("p (h e) -> p h e", h=8),
                            bc(self.dt_all[:, t, d_ * 8:(d_ + 1) * 8], [128, 8, 64]), ALU.mult,
                            [xs.r, self.dt_all.r], [xdt.r])
                yield
                for d_, tri in enumerate((self.triE, self.triL)):
                    self.tt("pool", Rm[:], bcm(tri[:], [128, 8, 128]), bc(la[:, d_ * 8:(d_ + 1) * 8], [128, 8, 128]),
                            ALU.mult, [self.cst.r, la.r], [Rm.r])
                    pd, prd = self.pbank(2)
                    for q in range(2):
                        self.mm(pd[:, q * 512:(q + 1) * 512], self.ones_f[:],
                                Rm[:, q * 4:(q + 1) * 4, :].rearrange("p a b -> p (a b)"), True, False,
                                [self.ones_f.r, Rm.r], prd)
                        self.mm(pd[:, q * 512:(q + 1) * 512], self.ident_b[:], self.negm[:, d_, :], False, True,
                                [self.ident_b.r, self.negm.r], prd)
                    self.tt("dve", D1[:], pd[:, :].rearrange("p (a b) -> p a b", a=8),
                            bc(vt[:, d_ * 8:(d_ + 1) * 8], [128, 8, 128]), ALU.subtract, prd + [vt.r], [D1.r])
                    self.act(D1[:], D1[:], AF.Exp, [D1.r], [D1.r])
                    self.tt("dve", MT[:, d_ * 8:(d_ + 1) * 8, :].rearrange("p (g h) b -> p g h b", g=2),
                            D1[:].rearrange("p (g h) b -> p g h b", g=2),
                            pg[:, 0:256].rearrange("p (g b) -> p g b", g=2).unsqueeze(2).to_broadcast([128, 2, 4, 128]),
                            ALU.mult, [D1.r] + prg, [MT.r])
                    if d_ == 1:
                        self.unpin(prg)
                    yield
                yield
                py, pry = self.pbank()
                for h in range(8):
                    self.mm(py[:, h * 64:(h + 1) * 64], MT[:, h, :], xdt[:, h, :], True, False, [MT.r, xdt.r], pry)
                    self.mm(py[:, h * 64:(h + 1) * 64], MT[:, 8 + h, :], xdt[:, 8 + h, :], False, True, [MT.r, xdt.r], pry)
                pe_, pre = self.pbank()
                for g in range(2):
                    self.cp("act", HbdE[g * 64:(g + 1) * 64, g * 256:(g + 1) * 256], self.HEs[g * 64:(g + 1) * 64, t, :],
                            [self.HEs.r], [HbdE.r])
                self.mm(pe_[:, :], self.CT[:, cs], HbdE[:], True, True, [self.CT.r, HbdE.r], pre)
                self.tt("dve", ya[:].rearrange("p (h e) -> p h e", h=8), pe_[:, :].rearrange("p (h e) -> p h e", h=8),
                        bc(ev[:, 0:8], [128, 8, 64]), ALU.mult, pre + [ev.r], [ya.r])
                pl, prl = self.pbank()
                self.mm(pl[:, :], self.CT[:, cs], HLb[:], True, True, [self.CT.r, HLb.r], prl)
                self.tt("dve", yb[:].rearrange("p (h e) -> p h e", h=8), pl[:, :].rearrange("p (h e) -> p h e", h=8),
                        bc(ev[:, 8:16], [128, 8, 64]), ALU.mult, prl + [ev.r], [yb.r])
                self.tt("dve", ya[:], ya[:], yb[:], ALU.add, [ya.r, yb.r], [ya.r])
                self.tt("dve", ya[:], ya[:], py[:, :], ALU.add, [ya.r] + pry, [ya.r])
                self.tt("dve", yb[:], xs[:], dskb[:], ALU.mult, [xs.r, dskb.r], [yb.r])
                self.tt("dve", ya[:], ya[:], yb[:], ALU.add, [ya.r, yb.r], [ya.r])
                if l == 0 and t == NT - 1:
                    self.dump("yscan0", ya[:], ya.r, [128, 512])
                yield
                self.silu(sz[:], [sz.r], z[:], [z.r], (jk32[:], jk32.r))
                self.tt("dve", ya[:], ya[:], sz[:], ALU.mult, [ya.r, sz.r], [ya.r])
                for g in range(2):
                    self.act(jk[:, g * 256:(g + 1) * 256], ya[:, g * 256:(g + 1) * 256], AF.Square, [ya.r],
                             [jk.r, ms2.r], scale=1.0 / 16.0, accum=ms2[:, g:g + 1])
                self.act(ln2[:], ms2[:], AF.Ln, [ms2.r, self.eps_t.r], [ln2.r], bias=self.eps_t[:, 0:1])
                self.act(rs2[:], ln2[:], AF.Exp, [ln2.r], [rs2.r], scale=-0.5)
                for g in range(2):
                    self.stt("dve", yss[:, g * 256:(g + 1) * 256], ya[:, g * 256:(g + 1) * 256], rs2[:, g:g + 1],
                             ngb[:, g * 256:(g + 1) * 256], ALU.mult, ALU.mult, [ya.r, rs2.r, ngb.r], [yss.r])
                pb, pr = self.pbank()
                pbb = pb.bitcast(BF16)
                for c in range(4):
                    self.tr(pbb[:, c * 128:(c + 1) * 128], yss[:, c * 128:(c + 1) * 128], self.ident_b[:],
                            [yss.r, self.ident_b.r], pr)
                self.cp("act", ymx[:, :, tl * 128:(tl + 1) * 128], pbb[:, 0:512].rearrange("p (c t) -> p c t", c=4), pr, [ymx.r])
                yield
                vready = (vt[:, 8:16], vt[:, 24:32], vt[0:64, 24:28], vt[64:128, 28:32], vt.r)
                self.state_update(st, xs, btk, t, 1, None, vt, wL, decL, xdtw, hetmp, HL, vt_ready=vready)
                self.hl_bd(HL, HLb)
                yield

        def att_gen(bi, yatt):
            t0, nt, N, c0, isctx = geom(bi)
            nk = 2 if isctx else 34
            nkc = nk * 128
            for h in range(4):
                KT = KTs.next()
                self.dma("sp", KT[:, 0:TC], self.kv_ctx_k.t[:, h, :], [self.kv_ctx_k.r], [KT.r])
                if not isctx:
                    for r in range(2):
                        self.dma("sp", KT[:, TC + r * TL:TC + (r + 1) * TL],
                                 self.xk_g.t[r * 512 + h * 128:r * 512 + (h + 1) * 128, :], [self.xk_g.r], [KT.r])
                po, pro = self.pbank(pin=True)
                npair = nk // 2

                def qk_pair(j):
                    ps2, pr2 = self.pbank(2, pin=True)
                    for a in range(2):
                        kt = 2 * j + a
                        self.mm(ps2[:, a * 512:a * 512 + N], KT[:, kt * 128:(kt + 1) * 128], self.qT[:, h, c0:c0 + N],
                                True, True, [KT.r, self.qT.r], pr2)
                    return ps2, pr2

                cur = qk_pair(0)
                for j in range(npair):
                    ps2, pr2 = cur
                    pt_ = PTs.next()
                    self.act(pt_[:].rearrange("p (a n) -> p a n", a=2)[:, :, 0:N],
                             ps2[:, :].rearrange("p (a n) -> p a n", a=2)[:, :, 0:N], AF.Exp, pr2, [pt_.r], scale=MLA_SCALE)
                    self.unpin(pr2)
                    if j + 1 < npair:
                        cur = qk_pair(j + 1)
                    for a in range(2):
                        kt = 2 * j + a
                        self.mm(po[0:65, 0:N], VA[:, kt, h * 65:(h + 1) * 65], pt_[:, a * 512:a * 512 + N], kt == 0,
                                kt == nk - 1, [VA.r, pt_.r], pro)
                    yield
                self.unpin(pro)
                self.act(rec[64:65, 0:N], po[64:65, 0:N], AF.Ln, pro, [rec.r])
                self.act(rec[64:65, 0:N], rec[64:65, 0:N], AF.Exp, [rec.r], [rec.r], scale=-1.0)
                self.cp("act", Osb[:, 0:N], po[0:64, 0:N], pro, [Osb.r])
                yield
                pbc, prb = self.pbank()
                self.mm(pbc[0:64, 0:N], self.ones_f[64:65, 0:64], rec[64:65, 0:N], True, True, [self.ones_f.r, rec.r], prb)
                self.tt("dve", yatt[:, h, 0:N], pbc[0:64, 0:N], Osb[:, 0:N], ALU.mult, prb + [Osb.r], [yatt.r])
                yield

        def wout_gen(bi, ymx, yatt):
            t0, nt, N, c0, isctx = geom(bi)
            if isctx:
                self.dma("sp", G1[:], self.bv.t[8], [self.bv.r], [G1.r])
            for tl in range(nt):
                t = t0 + tl
                cs = slice(t * 128, (t + 1) * 128)
                ls = slice(tl * 128, (tl + 1) * 128)
                pw, prw = self.pbank(2)
                for hf in range(2):
                    ns = slice(hf * 512, (hf + 1) * 512)
                    for h in range(4):
                        self.mm(pw[:, ns], yatt[:, h, ls], wo_att[:, h, ns], h == 0, False, [yatt.r, wo_att.r], prw)
                    for c in range(2):
                        self.mm(pw[:, ns], self.ysc[:, c, cs], wo_r[:, c, ns], False, False, [self.ysc.r, wo_r.r], prw)
                    for c in range(4):
                        self.mm(pw[:, ns], ymx[:, c, ls], wo_r[:, 2 + c, ns], False, c == 3, [ymx.r, wo_r.r], prw)
                self.dma("sp", xt[:], xres.t[cs, :], [xres.r], [xt.r])
                self.post_tile(pw, prw, xt, G1, tmp, smalls.next(), x1)
                self.dma("sp", self.x1s.t[cs, :], x1[:], [x1.r], [self.x1s.r])
                yield

        prev = None
        for it, bi in enumerate(order):
            ymx_c, yatt_c = ymxs[it % 2], yatts[it % 2]
            nk_ = 2 if bi == 0 else 34
            gens = [(ssd_gen(bi, ymx_c), 9 * BLOCKS[bi][1]), (att_gen(bi, yatt_c), 4 * (nk_ // 2 + 2))]
            if prev is not None:
                gens.append((wout_gen(*prev), BLOCKS[prev[0]][1]))
            interleave(*gens)
            if l == 0 and bi == 4:
                self.dump("yatt0", yatt_c[:], yatt_c.r, [64, 4, 512], BF16)
                self.dump("ymx0", ymx_c[:], ymx_c.r, [128, 4, 512], BF16)
            if self.stop(l, "b1") or self.stop(l, "b2"):
                return
            prev = (bi, ymx_c, yatt_c)
        run_gen(wout_gen(*prev))
        self.dump(f"x1_{l}", self.x1s.t, self.x1s.r, [T, D])

    def l_init(self, l, st, smp, cw, cb, HL, HLb):
        c3 = self.sb("c3", [128, 6, 3])
        pv = smp[:, 512:524].rearrange("p (c k) -> p c k", c=6)
        self.cp("dve", c3[:, :, 0], self.fix[:, 10:16], [self.fix.r], [c3.r])
        self.cp("dve", c3[:, :, 1], pv[:, :, 1], [smp.r], [c3.r])
        self.cp("dve", c3[:, :, 2], pv[:, :, 0], [smp.r], [c3.r])
        t6 = self.sb("t6", [128, 6, 3])
        cv = self.sb("cv6", [128, 6])
        self.tt("dve", t6[:], c3[:], cw[:], ALU.mult, [c3.r, cw.r], [t6.r])
        self.tt("dve", cv[:], t6[:, :, 0], t6[:, :, 1], ALU.add, [t6.r], [cv.r])
        self.tt("dve", cv[:], cv[:], t6[:, :, 2], ALU.add, [t6.r, cv.r], [cv.r])
        self.tt("dve", cv[:], cv[:], cb[:], ALU.add, [cv.r, cb.r], [cv.r])
        sv = self.sb("sv6", [128, 6])
        sg6 = self.sb("sg6", [128, 6])
        self.silu(sv[:], [sv.r], cv[:], [cv.r], (sg6[:], sg6.r))
        pb, pr = self.pbank()
        for c in range(4):
            self.tr(pb[0:1, c * 128:(c + 1) * 128], sv[:, c:c + 1], self.ident_f[:], [sv.r, self.cst.r], pr)
        pb2, pr2 = self.pbank()
        self.tr(pb2[0:1, 0:128], sv[:, 4:5], self.ident_f[:], [sv.r, self.cst.r], pr2)
        row = self.sb("row", [1, 640])
        self.cp("act", row[:, 0:512], pb[0:1, 0:512], pr, [row.r])
        self.cp("act", row[:, 512:640], pb2[0:1, 0:128], pr2, [row.r])
        dtr = self.sb("dtr", [1, 8])
        self.cp("dve", dtr[:], smp[0:1, 528:536], [smp.r], [dtr.r])
        xr = self.sb("xr", [1, 512])
        self.tt("dve", xr[:].rearrange("p (h e) -> p h e", h=8), row[:, 0:512].rearrange("p (h e) -> p h e", h=8),
                bc(dtr[:], [1, 8, 64]), ALU.mult, [row.r, dtr.r], [xr.r])
        pb3, pr3 = self.pbank()
        self.mm(pb3[:, :], row[:, 512:640], xr[:], True, True, [row.r, xr.r], pr3)
        self.tt("dve", HL[0:64, :], smp[0:64, 0:256], pb3[0:64, 0:256], ALU.add, [smp.r] + pr3, [HL.r])
        self.tt("dve", HL[64:128, :], smp[64:128, 0:256], pb3[64:128, 256:512], ALU.add, [smp.r] + pr3, [HL.r])
        self.hl_bd(HL, HLb)

    def hl_bd(self, HL, HLb):
        for g in range(2):
            self.cp("act", HLb[g * 64:(g + 1) * 64, g * 256:(g + 1) * 256], HL[g * 64:(g + 1) * 64, :], [HL.r], [HLb.r])

    def fix_last(self, l, smp, sccw, cw, cb):
        tcol = T - 1
        f2 = self.sb("f2", [128, 2])
        self.tt("dve", f2[:], smp[:, 524:526], sccw[:, :, 2], ALU.mult, [smp.r, sccw.r], [f2.r])
        self.tt("dve", f2[:], f2[:], self.fix[:, 6:8], ALU.add, [f2.r, self.fix.r], [f2.r])
        self.tt("dve", self.ysc[:, :, tcol], f2[:], self.fix[:, 8:10], ALU.mult, [f2.r, self.fix.r], [self.ysc.r])
        f6 = self.sb("f6", [128, 6])
        pv = smp[:, 512:524].rearrange("p (c k) -> p c k", c=6)
        self.tt("dve", f6[:], pv[:, :, 1], cw[:, :, 2], ALU.mult, [smp.r, cw.r], [f6.r])
        self.tt("dve", f6[:], f6[:], self.fix[:, 0:6], ALU.add, [f6.r, self.fix.r], [f6.r])
        self.tt("dve", f6[:], f6[:], cb[:], ALU.add, [f6.r, cb.r], [f6.r])
        s6 = self.sb("s6", [128, 6])
        sg6b = self.sb("sg6b", [128, 6])
        self.silu(s6[:], [s6.r], f6[:], [f6.r], (sg6b[:], sg6b.r))
        self.cp("dve", self.xsT_last[:, :, 127], s6[:, 0:4], [s6.r], [self.xsT_last.r])
        self.cp("dve", self.BT[:, tcol:tcol + 1], s6[:, 4:5], [s6.r], [self.BT.r])
        self.cp("dve", self.CT[:, tcol:tcol + 1], s6[:, 5:6], [s6.r], [self.CT.r])
        bl = self.sb("bl", [128, 128])
        self.cp("dve", bl[:], self.BT[:, T - 128:T], [self.BT.r], [bl.r])
        self.cp("dve", bl[:, 127:128], s6[:, 4:5], [s6.r], [bl.r])
        pb, pr = self.pbank()
        for c in range(4):
            self.tr(pb[:, c * 128:(c + 1) * 128], self.xsT_last[:, c, :], self.ident_f[:], [self.xsT_last.r, self.cst.r], pr)
        xl = self.sb("xl", [128, 512])
        self.cp("act", xl[:], pb[:, :], pr, [xl.r])
        self.dma("pool", self.sc_xs.t[T - 128:T, :], xl[:], [xl.r], [self.sc_xs.r])
        pb2, pr2 = self.pbank()
        self.tr(pb2[:, 0:128], bl[:], self.ident_f[:], [bl.r, self.cst.r], pr2)
        bl2 = self.sb("bl2", [128, 128], BF16)
        self.cp("act", bl2[:], pb2[:, 0:128], pr2, [bl2.r])
        self.dma("pool", self.sc_bt.t[T - 128:T, :], bl2[:], [bl2.r], [self.sc_bt.r])

    def phase_c(self, l, xnext, last):
        self.close_ab()
        self.new_scope()
        w1g = [self.sb(f"w1g{g}", [128, 8, 512], BF16) for g in range(8)]
        w2g = [self.sb(f"w2g{q}", [128, 8, D], BF16) for q in range(4)]
        w1v = self.w_ff1.t[l].rearrange("(k p) n -> p k n", p=128)
        w2v = self.w_ff2.t[l].rearrange("(c p) n -> p c n", p=128)
        for kind, g in (("1", 0), ("2", 0), ("1", 1), ("1", 2), ("2", 1), ("1", 3), ("1", 4), ("2", 2), ("1", 5), ("1", 6),
                        ("2", 3), ("1", 7)):
            if kind == "1":
                self.dma("pool", w1g[g][:], w1v[:, :, g * 512:(g + 1) * 512], [self.w_ff1.r], [w1g[g].r])
            else:
                self.dma("pool", w2g[g][:], w2v[:, g * 8:(g + 1) * 8, :], [self.w_ff2.r], [w2g[g].r])
        A2 = self.load_bv(3, "A2")
        B2 = self.load_bv(4, "B2")
        G2 = self.load_bv(5, "G2")
        A2c = B2c = G2c = None
        if not last:
            A2c = self.load_bv(9, "A2c")
            B2c = self.load_bv(10, "B2c")
            G2c = self.load_bv(11, "G2c")
        NBC = 256
        xts = self.rot("xt", 2, [128, D])
        xps = self.rot("xp", 1, [128, D])
        tmp = self.sb("tmp", [128, D])
        tmp2 = self.sb("tmp2", [128, D])
        hb = self.sb("hb", [128, D], BF16)
        hTs = [self.sb("hTa", [128, 8, NBC], BF16), self.sb("hTb", [128, 8, NBC], BF16)]
        a_r = self.rot("afo", 3, [128, NBC], BF16)
        rl_r = self.rot("rl", 2, [128, NBC])
        smalls = self.small3(2)
        smalls2 = self.small3(2)
        x2s = self.rot("x2", 2, [128, D])
        ysbs = self.rot("ysb", 1, [128, D])
        blocks = [(2 + 2 * i, 2) for i in range(8)] + ([] if last else [(0, 2)])

        def front(i):
            t0, nt = blocks[i]
            isctx = (t0 == 0)
            hT = hTs[i % 2]
            for tl in range(nt):
                t = t0 + tl
                xt = xts.next()
                self.dma("sp", xt[:], self.x1s.t[t * 128:(t + 1) * 128, :], [self.x1s.r], [xt.r])
                self.norm_tile(xt, A2c if isctx else A2, B2c if isctx else B2, hb, tmp, smalls.next())
                self.transpose_tile(hb, hT, tl)
                yield

        def ffn(i):
            t0, nt = blocks[i]
            isctx = (t0 == 0)
            N = nt * 128
            hT = hTs[i % 2]
            pws = [self.pbank(2, pin=True) for _ in range(nt)]

            def ffn1(fo):
                pb, pr = self.pbank(pin=True)
                for k in range(8):
                    wg = w1g[fo // 4]
                    self.mm(pb[:, 0:N], wg[:, k, (fo % 4) * 128:(fo % 4 + 1) * 128], hT[:, k, 0:N], k == 0, k == 7,
                            [wg.r, hT.r], pr)
                return pb, pr

            cur = ffn1(0)
            for fo in range(32):
                pb, pr = cur
                rl = rl_r.next()
                a = a_r.next()
                self.act(rl[:, 0:N], pb[:, 0:N], AF.Relu, pr, [rl.r])
                self.unpin(pr)
                if fo + 1 < 32:
                    cur = ffn1(fo + 1)
                self.tt("dve", a[:, 0:N], rl[:, 0:N], rl[:, 0:N], ALU.mult, [rl.r], [a.r])
                for tl in range(nt):
                    pw, prw = pws[tl]
                    for hf in range(2):
                        ns = slice(hf * 512, (hf + 1) * 512)
                        self.mm(pw[:, ns], a[:, tl * 128:(tl + 1) * 128], w2g[fo // 8][:, fo % 8, ns], fo == 0, fo == 31,
                                [a.r, w2g[fo // 8].r], prw)
                yield
            for tl in range(nt):
                t = t0 + tl
                cs = slice(t * 128, (t + 1) * 128)
                pw, prw = pws[tl]
                ysb = ysbs.next()
                self.cp("act", ysb[:], pw[:, :], prw, [ysb.r])
                self.unpin(prw)
                xp = xps.next()
                self.dma("sp", xp[:], self.x1s.t[cs, :], [self.x1s.r], [xp.r])
                x2 = x2s.next()
                self.post_tile(ysb, [ysb.r], xp, G2c if isctx else G2, tmp2, smalls2.next(), x2)
                if last:
                    self.dma("pool", self.y_out.t[(t - 2) * 128:(t - 1) * 128, :], x2[:], [x2.r], [self.y_out.r])
                else:
                    self.dma("pool", xnext.t[cs, :], x2[:], [x2.r], [xnext.r])
                yield

        run_gen(front(0))
        for i in range(len(blocks)):
            f = front(i + 1) if i + 1 < len(blocks) else None
            interleave((f, 2, 0.7), (ffn(i), 34))
        if not last:
            self.dump(f"xres1_{l}", xnext.t, xnext.r, [T, D])


def run_gen(g):
    if g is not None:
        for _ in g:
            pass


def interleave(*pairs):
    live = [[p[0], (p[2] if len(p) > 2 else 0.0), max(1, p[1])] for p in pairs if p[0] is not None]
    while live:
        live.sort(key=lambda x: x[1] / x[2])
        it = live[0]
        try:
            next(it[0])
            it[1] += 1
        except StopIteration:
            live.remove(it)


class Rot:
    def __init__(self, items):
        self.items = items
        self.i = -1

    def next(self):
        self.i = (self.i + 1) % len(self.items)
        return self.items[self.i]


_PERM = np.r_[8:16, 0:8, 24:32, 16:24]


def _rope_tables(pos):
    half = 16
    inv = (10000.0 ** (-(np.arange(0, half, 2, dtype=np.float32)) / half)).astype(np.float32)
    row = (pos // 64).astype(np.float32)
    col = (pos % 64).astype(np.float32)
    ar = row[:, None] * inv
    ac = col[:, None] * inv
    ang = np.concatenate([ar, ar, ac, ac], -1).astype(np.float32)
    sign = np.concatenate([-np.ones(8), np.ones(8), -np.ones(8), np.ones(8)]).astype(np.float32)
    return np.cos(ang).astype(np.float32), (np.sin(ang) * sign).astype(np.float32)


def host_layout(inp):
    f = lambda a: np.ascontiguousarray(np.asarray(a, dtype=np.float32))
    x, c, ctx, c_ctx = f(inp["x"]), f(inp["c"]), f(inp["ctx"]), f(inp["c_ctx"])
    L = DEPTH
    w_in = f(inp["w_in"])
    s0 = 416 + 768
    shared = {}
    per_rev = {}
    for rev in (0, 1):
        wa = []
        for l in range(L):
            w = w_in[l]
            dtc = w[:, s0 + 512 + 768:s0 + 512 + 768 + 16]
            if rev:
                dtc = np.concatenate([dtc[:, 8:16], dtc[:, 0:8]], 1)
            wa.append(np.concatenate([w[:, 0:256], w[:, 256:384], w[:, 384:416], w[:, 384 + _PERM],
                                      w[:, 416:416 + 768], w[:, s0 + 512:s0 + 512 + 768], w[:, s0:s0 + 512], dtc], 1))
        d = {"w_in_aug": f(np.stack(wa))}
        al = f(inp["ssd_a_log"]); db = f(inp["ssd_dt_bias"])
        order = [1, 0] if rev else [0, 1]
        d["a_log"] = f(np.concatenate([al[:, order[0]], al[:, order[1]]], -1))
        d["dt_bias"] = f(np.concatenate([db[:, order[0]], db[:, order[1]]], -1))
        scw = f(inp["sc_conv_w"]); sw = f(inp["ssd_conv_w"])
        if rev:
            scw = scw[:, ::-1]; sw = sw[:, ::-1]
        d["sc_cw"] = f(np.transpose(scw, (0, 2, 1)))
        d["ssd_cw"] = f(np.transpose(sw, (0, 2, 1)))
        per_rev[rev] = d
    wuq = f(inp["w_uq"])
    wq = np.zeros((L, 256, 4, 160), np.float32)
    for h in range(4):
        wq[:, :, h, 0:32] = wuq[:, :, h * 96 + 64:h * 96 + 96]
        wq[:, :, h, 64:128] = wuq[:, :, h * 96:h * 96 + 64]
        wq[:, :, h, 128:160] = wuq[:, :, h * 96 + 64 + _PERM]
    shared["w_uq_aug"] = f(wq.reshape(L, 256, 640))
    wukv = f(inp["w_ukv"])
    wk = np.zeros((L, 128, 4, 128), np.float32)
    wv = np.zeros((L, 128, 4, 64), np.float32)
    for h in range(4):
        wk[:, :, h, 64:128] = wukv[:, :, h * 128:h * 128 + 64]
        wv[:, :, h, :] = wukv[:, :, h * 128 + 64:h * 128 + 128]
    shared["w_ukv_k"] = f(wk.reshape(L, 128, 512))
    shared["w_ukv_v"] = f(wv.reshape(L, 128, 256))
    shared["q_norm"] = f(inp["mla_q_norm"])
    shared["kv_norm"] = f(inp["mla_kv_norm"])
    shared["w_mod"] = f(inp["w_mod"])
    shared["b_mod"] = f(inp["b_mod"])
    shared["gvecs"] = f(np.stack([inp["g_pre_mix"], inp["g_post_mix"], inp["g_pre_ffn"], inp["g_post_ffn"]], 1))
    shared["ssd_cb"] = f(inp["ssd_conv_b"])
    shared["ssd_d"] = f(inp["ssd_d"])
    shared["ssd_norm"] = f(inp["ssd_norm"])
    shared["w_out"] = f(inp["w_out"])
    shared["w_ff1"] = f(inp["w_ff1"])
    shared["w_ff2"] = f(inp["w_ff2"])
    j = np.arange(128)
    triE = (j[:, None] <= j[None, :]).astype(np.float32)
    triL = (j[:, None] >= j[None, :]).astype(np.float32)
    in_maps = []
    for core in range(NCORES):
        b, hf = core // 2, core % 2
        xl = x[b, hf * TL:(hf + 1) * TL]
        cl = ctx[b]
        pos = np.arange(hf * TL, (hf + 1) * TL)
        if hf:
            xl, cl, pos = xl[::-1], cl[::-1], pos[::-1]
        cos, sin = _rope_tables(pos)
        rope = np.zeros((2, 32, T), np.float32)
        rope[0, :, :TC] = 1.0
        rope[0, :, TC:] = cos.T
        rope[1, :, TC:] = sin.T
        cst = np.zeros((128, 3 * 128 + 4), np.float32)
        cst[:, 0:128] = np.eye(128)
        cst[:, 128:256] = triE
        cst[:, 256:384] = triL
        cst[:, 384] = 1.0
        cst[127, 384] = 0.0
        cst[:, 385] = 1.0 - hf
        cst[:, 386] = float(hf)
        m = {"x_in": f(np.concatenate([cl, xl], 0)), "cvecT": f(np.stack([c[b], c_ctx], 1)), "ropeT": rope, "consts": cst}
        m.update(shared)
        m.update(per_rev[hf])
        in_maps.append(m)
    return in_maps


_NC_CACHE = {}


def kernel(**inputs):
    in_maps = host_layout(inputs)
    if "nc" not in _NC_CACHE:
        _NC_CACHE["nc"] = Builder().build()
    res = run_bass_kernel_spmd(_NC_CACHE["nc"], in_maps, core_ids=list(range(NCORES)))
    out = np.zeros((4, 2 * TL, D), np.float32)
    for core in range(NCORES):
        b, hf = core // 2, core % 2
        y = np.asarray(res.results[core]["y_out"], dtype=np.float32)
        out[b, hf * TL:(hf + 1) * TL] = y[::-1] if hf else y
    return out
```

```python
import numpy as np
import ml_dtypes
from contextlib import ExitStack
import concourse.bass as bass
import concourse.mybir as mybir
from concourse.bass_utils import run_bass_kernel_spmd

F32 = mybir.dt.float32
BF16 = mybir.dt.bfloat16
AF = mybir.ActivationFunctionType
ALU = mybir.AluOpType

NCORES = 8
D = 1024
DEPTH = 2
TC = 256
TL = 2048
T = TC + TL
NT = T // 128
BLOCKS = [(0, 2), (2, 4), (6, 4), (10, 4), (14, 4)]
NU = 2512
EPS = 1e-6
MLA_SCALE = 96.0 ** -0.5
NEG = -30000.0
FM = [(0, 128), (128, 128), (256, 128), (384, 32), (416, 32)] + [(448 + 128 * i, 128) for i in range(12)]
Z_OFF = 1984
DT_OFF = 2496
SM_W = 544

ENGS = ("pe", "act", "dve", "pool", "sp")


class Res:
    __slots__ = ("name", "w", "r", "sem", "ndma")

    def __init__(self, name):
        self.name = name
        self.w = None
        self.r = []
        self.sem = None
        self.ndma = 0


class Prog:
    def __init__(self, nc, strict=True):
        self.nc = nc
        self.ops = {e: [] for e in ENGS}
        self.strict = strict
        self.nres = 0
        self.dma_res = []
        self.slot_count = []
        self.slot_cls = []
        self.free_slots = {}

    def res(self, name=None):
        self.nres += 1
        return Res(name or f"r{self.nres}")

    def _deps(self, reads, writes, eng=None):
        ev = []
        for r in reads:
            if r.w is not None:
                ev.append(r.w)
        for w in writes:
            for e in ([w.w] if w.w is not None else []) + w.r:
                if eng is not None and e[0] == "E" and e[1] == eng:
                    continue
                ev.append(e)
        return ev

    def op(self, eng, fn, reads=(), writes=()):
        lst = self.ops[eng]
        idx = len(lst)
        waits = self._collect(eng, self._deps(reads, writes, eng))
        lst.append({"fn": fn, "waits": waits, "signal": False, "dma": None})
        me = ("E", eng, idx)
        for r in reads:
            r.r.append(me)
        for w in writes:
            w.w = me
            w.r = []

    def dma(self, eng, fn, src, dst, inc=16):
        srcs = src if isinstance(src, (list, tuple)) else [src]
        dsts = dst if isinstance(dst, (list, tuple)) else [dst]
        main = dsts[0]
        cls = eng if inc == 16 else "cc"
        if main.sem is None:
            fl = self.free_slots.setdefault(cls, [])
            if fl and cls != "cc":
                main.sem = fl.pop()
            else:
                main.sem = len(self.slot_count)
                self.slot_count.append(0)
                self.slot_cls.append(cls)
            main.ndma = self.slot_count[main.sem]
            self.dma_res.append(main)
        assert self.slot_cls[main.sem] == cls, (main.name, self.slot_cls[main.sem], cls)
        waits = self._collect(eng, self._deps(srcs, dsts))
        main.ndma += inc
        self.slot_count[main.sem] = main.ndma
        self.ops[eng].append({"fn": fn, "waits": waits, "signal": False, "dma": main.sem, "inc": inc})
        me = ("D", main.sem, main.ndma)
        for r in srcs:
            r.r.append(me)
        for w in dsts:
            w.w = me
            w.r = []

    def release(self, resources):
        for r in resources:
            if r.sem is not None:
                self.free_slots.setdefault(self.slot_cls[r.sem], []).append(r.sem)
                self.dma_res.remove(r)
                r.sem = None

    def _collect(self, eng, events, all_same=False):
        out = {}
        for ev in events:
            if ev[0] == "E":
                _, e2, j = ev
                if e2 == eng and (eng == "pe" or not self.strict) and not all_same:
                    continue
                key = ("E", e2)
            else:
                key = ("D", ev[1])
            if key not in out or out[key][2] < ev[2]:
                out[key] = ev
        evs = list(out.values())
        for ev in evs:
            if ev[0] == "E":
                self.ops[ev[1]][ev[2]]["signal"] = True
        return evs

    def final_wait(self, eng, resources):
        evs = []
        for r in resources:
            if r.w is not None:
                evs.append(r.w)
            evs.extend(r.r)
        self.ops[eng].append({"fn": None, "waits": self._collect(eng, evs), "signal": False, "dma": None})

    def barrier(self):
        evs = []
        for e in ENGS:
            for idx in range(len(self.ops[e]) - 1, -1, -1):
                rec = self.ops[e][idx]
                if rec["dma"] is None and rec["fn"] is not None:
                    evs.append(("E", e, idx))
                    break
        for r in self.dma_res:
            evs.append(("D", r.sem, r.ndma))
        for e in ENGS:
            self.ops[e].append({"fn": None, "waits": self._collect(e, evs, all_same=(e != "pe")),
                                "signal": False, "dma": None})

    def emit(self):
        nc = self.nc
        esem = {e: nc.alloc_semaphore(f"s_{e}") for e in ENGS}
        dsem = [nc.alloc_semaphore(f"d{i}") for i in range(len(self.slot_count))]
        cnt = {}
        for e in ENGS:
            c = 0
            arr = []
            for rec in self.ops[e]:
                if rec["signal"] and rec["dma"] is None:
                    c += 1
                arr.append(c)
            cnt[e] = arr
        ops = self.ops

        def run(e, eh):
            waited = {}
            for rec in ops[e]:
                for ev in rec["waits"]:
                    if ev[0] == "E":
                        sem = esem[ev[1]]
                        val = cnt[ev[1]][ev[2]]
                    else:
                        sem = dsem[ev[1]]
                        val = ev[2]
                    if waited.get(sem.num, 0) >= val:
                        continue
                    waited[sem.num] = val
                    eh.wait_ge(sem, val)
                if rec["fn"] is None:
                    continue
                ins = rec["fn"](eh)
                if rec["dma"] is not None:
                    ins.then_inc(dsem[rec["dma"]], rec["inc"])
                elif rec["signal"]:
                    ins.then_inc(esem[e], 1)

        with nc.Block() as block:
            @block.tensor
            def _(eh):
                run("pe", eh)

            @block.scalar
            def _(eh):
                run("act", eh)

            @block.vector
            def _(eh):
                run("dve", eh)

            @block.gpsimd
            def _(eh):
                run("pool", eh)

            @block.sync
            def _(eh):
                run("sp", eh)


class Buf:
    def __init__(self, t, r):
        self.t = t
        self.r = r

    def __getitem__(self, k):
        return self.t[k]


def bc(ap, shape):
    return ap.unsqueeze(len(ap.shape)).to_broadcast(list(shape))


def bcm(ap, shape):
    return ap.unsqueeze(1).to_broadcast(list(shape))


class Builder:
    def __init__(self, dbg=None, nlayers=DEPTH, stop_after=None, fake_cc=False, ncores=NCORES):
        self.fake_cc = fake_cc
        self.ncores = ncores
        self.nc = bass.Bass("TRN2", target_bir_lowering=False)
        self.P = Prog(self.nc)
        self.dbg = dbg or []
        self.dbg_out = {}
        self.nlayers = nlayers
        self.stop_after = stop_after
        self.scope = None
        self.scope_ab = None
        self.scope_res = []
        self.ab_res = []
        self.pcur = 0
        self.pinned = set()
        self.uid = 0

    def din(self, name, shape, dt=F32):
        return Buf(self.nc.dram_tensor(name, list(shape), dt, kind="ExternalInput").ap(), self.P.res(name))

    def dout(self, name, shape, dt=F32):
        return Buf(self.nc.dram_tensor(name, list(shape), dt, kind="ExternalOutput").ap(), self.P.res(name))

    def dscr(self, name, shape, dt=F32):
        return Buf(self.nc.dram_tensor(name, list(shape), dt, kind="Internal").ap(), self.P.res(name))

    def sb(self, name, shape, dt=F32, persist=False, ab=False):
        self.uid += 1
        nm = f"{name}_{self.uid}"
        r = self.P.res(name)
        if persist:
            t = self.nc.alloc_sbuf_tensor(nm, list(shape), dt)
        elif ab:
            t = self.scope_ab.enter_context(self.nc.sbuf_tensor(nm, list(shape), dt))
            self.ab_res.append(r)
        else:
            t = self.scope.enter_context(self.nc.sbuf_tensor(nm, list(shape), dt))
            self.scope_res.append(r)
        return Buf(t, r)

    def rot(self, name, n, shape, dt=F32):
        return Rot([self.sb(f"{name}{i}", shape, dt) for i in range(n)])

    def pbank(self, n=1, pin=False):
        while True:
            if self.pcur % n:
                self.pcur += n - self.pcur % n
            if self.pcur + n > 8:
                self.pcur = 0
            b0 = self.pcur
            self.pcur = (self.pcur + n) % 8
            if not any((b0 + i) in self.pinned for i in range(n)):
                break
        if pin:
            for i in range(n):
                self.pinned.add(b0 + i)
        self.last_b0 = b0
        return self.ps_all[:, b0 * 512:(b0 + n) * 512], [self.ps_res[b0 + i] for i in range(n)]

    def unpin(self, res_list):
        for r in res_list:
            self.pinned.discard(self.ps_res.index(r))

    def mm(self, out, lhsT, rhs, start, stop, R, W):
        self.P.op("pe", lambda e: e.matmul(out, lhsT, rhs, start=start, stop=stop), R, W)

    def tr(self, out, in_, ident, R, W):
        self.P.op("pe", lambda e: e.transpose(out, in_, ident), R, W)

    def act(self, out, in_, func, R, W, bias=None, scale=None, accum=None, eng="act"):
        kw = {}
        if bias is not None:
            kw["bias"] = bias
        if scale is not None:
            kw["scale"] = scale
        if accum is not None:
            kw["accum_out"] = accum
        self.P.op("act", lambda e: e.activation(out, in_, func, **kw), R, W)

    def tt(self, eng, out, a, b, op, R, W):
        self.P.op(eng, lambda e: e.tensor_tensor(out, a, b, op), R, W)

    def ts(self, eng, out, a, s1, s2, op0, op1, R, W):
        if op1 is None:
            self.P.op(eng, lambda e: e.tensor_scalar(out, a, s1, s2, op0), R, W)
        else:
            self.P.op(eng, lambda e: e.tensor_scalar(out, a, s1, s2, op0, op1), R, W)

    def stt(self, eng, out, a, s, b, op0, op1, R, W):
        self.P.op(eng, lambda e: e.scalar_tensor_tensor(out, a, s, b, op0, op1), R, W)

    def cp(self, eng, out, in_, R, W):
        if eng == "act":
            self.P.op("act", lambda e: e.copy(out, in_), R, W)
        else:
            self.P.op(eng, lambda e: e.tensor_copy(out, in_), R, W)

    def ms(self, eng, ap, val, W):
        self.P.op(eng, lambda e: e.memset(ap, val), [], W)

    def recip(self, out, in_, R, W):
        self.P.op("dve", lambda e: e.reciprocal(out, in_), R, W)

    def dma(self, eng, out, in_, R, W, slow_ok=False):
        if slow_ok:
            self.P.dma(eng, lambda e: e.dma_start(out=out, in_=in_, allow_slow_non_contiguous=True), R, W)
        else:
            self.P.dma(eng, lambda e: e.dma_start(out=out, in_=in_), R, W)

    def rstd(self, out, ms_ap, R, W, tmp):
        self.act(tmp[:], ms_ap, AF.Ln, R, [tmp.r], bias=self.eps_t[:, 0:1])
        self.act(out, tmp[:], AF.Exp, [tmp.r], W, scale=-0.5)

    def silu(self, out, out_res, in_, in_res, sg, bias=None, negbias=None, mul_eng="dve"):
        kw = {} if negbias is None else {"bias": negbias[0]}
        rr = list(in_res) + ([] if negbias is None else [negbias[1]])
        self.act(sg[0], in_, AF.Exp, rr, [sg[1]], scale=-1.0, **kw)
        self.act(sg[0], sg[0], AF.Ln, [sg[1]], [sg[1]], bias=1.0)
        self.act(sg[0], sg[0], AF.Exp, [sg[1]], [sg[1]], scale=-1.0)
        if bias is not None:
            self.stt(mul_eng, out, in_, bias[0], sg[0], ALU.add, ALU.mult, list(in_res) + [bias[1], sg[1]], out_res)
        else:
            self.tt(mul_eng, out, in_, sg[0], ALU.mult, list(in_res) + [sg[1]], out_res)

    def dump(self, key, src_ap, src_res, shape, dt=F32):
        if key not in self.dbg or key in self.dbg_out:
            return
        o = self.dout("dbg_" + key, shape, dt)
        self.dma("pool", o.t, src_ap, [src_res] if not isinstance(src_res, list) else src_res, [o.r])
        self.dbg_out[key] = o

    def build(self):
        nc, P = self.nc, self.P
        L = DEPTH
        self.x_in = self.din("x_in", [T, D])
        self.cvecT = self.din("cvecT", [D, 2])
        self.w_mod = self.din("w_mod", [L, D, 6 * D])
        self.b_mod = self.din("b_mod", [L, 6 * D])
        self.gvecs = self.din("gvecs", [L, 4, D])
        self.w_in = self.din("w_in_aug", [L, D, NU])
        self.q_norm = self.din("q_norm", [L, 256])
        self.w_uq = self.din("w_uq_aug", [L, 256, 640])
        self.kv_norm = self.din("kv_norm", [L, 128])
        self.w_ukv_k = self.din("w_ukv_k", [L, 128, 512])
        self.w_ukv_v = self.din("w_ukv_v", [L, 128, 256])
        self.sc_cw = self.din("sc_cw", [L, 256, 3])
        self.ssd_cw = self.din("ssd_cw", [L, 768, 3])
        self.ssd_cb = self.din("ssd_cb", [L, 768])
        self.a_log = self.din("a_log", [L, 16])
        self.dt_bias = self.din("dt_bias", [L, 16])
        self.ssd_d = self.din("ssd_d", [L, 8])
        self.ssd_norm = self.din("ssd_norm", [L, 512])
        self.w_out = self.din("w_out", [L, D, D])
        self.w_ff1 = self.din("w_ff1", [L, D, 4 * D])
        self.w_ff2 = self.din("w_ff2", [L, 4 * D, D])
        self.ropeT = self.din("ropeT", [2, 32, T])
        self.consts = self.din("consts", [128, 3 * 128 + 4])
        self.y_out = self.dout("y_out", [TL, D])
        self.xres1 = self.dscr("xres1", [T, D])
        self.x1s = self.dscr("x1s", [T, D])
        self.sc_z = self.dscr("sc_z", [T, 512])
        self.sc_xs = self.dscr("sc_xs", [T, 512])
        self.sc_bt = self.dscr("sc_bt", [T, 128], BF16)
        self.bv = self.dscr("bv", [12, 128, D])
        self.kv_ctx_k = self.dscr("kv_ctx_k", [128, 4, TC], BF16)
        self.kv_ctx_v = self.dscr("kv_ctx_v", [128, 2, 260], BF16)
        self.xk = self.dscr("xk", [512, TL], BF16)
        self.xk_g = self.dscr("xk_g", [1024, TL], BF16)
        self.xv = self.dscr("xv", [2048, 260], BF16)
        self.xv_g = self.dscr("xv_g", [4096, 260], BF16)
        self.xs_ = self.dscr("xsm", [128, SM_W])
        self.xs_g = self.dscr("xsm_g", [256, SM_W])

        self.ps_all = nc.alloc_psum_tensor("ps_all", [128, 8 * 512], F32)
        self.ps_res = [P.res(f"bank{i}") for i in range(8)]

        cst = self.sb("cst", [128, 3 * 128 + 4], persist=True)
        self.dma("sp", cst[:], self.consts.t, [self.consts.r], [cst.r])
        self.cst = cst
        self.ident_f = Buf(cst.t[:, 0:128], cst.r)
        self.triE = Buf(cst.t[:, 128:256], cst.r)
        self.triL = Buf(cst.t[:, 256:384], cst.r)
        self.lastmask = Buf(cst.t[:, 384:385], cst.r)
        self.esel = Buf(cst.t[:, 385:387], cst.r)
        self.ident_b = self.sb("ident_b", [128, 128], BF16, persist=True)
        self.cp("dve", self.ident_b[:], self.ident_f[:], [cst.r], [self.ident_b.r])
        self.ones_f = self.sb("ones_f", [128, 128], persist=True)
        self.ms("pool", self.ones_f[:], 1.0, [self.ones_f.r])
        self.eps_t = self.sb("eps_t", [128, 1], persist=True)
        self.ms("pool", self.eps_t[:], EPS, [self.eps_t.r])
        self.negm = self.sb("negm", [128, 2, 512], BF16, persist=True)
        for d_, tri in enumerate((self.triE, self.triL)):
            self.ts("dve", self.negm[:, d_, :].rearrange("p (a b) -> p a b", a=4), bcm(tri[:], [128, 4, 128]),
                    -1.0, -NEG, ALU.add, ALU.mult, [cst.r], [self.negm.r])
        self.sT = self.sb("sT", [128, 8, 2], persist=True)
        cv = self.sb("cv", [128, 8, 2], persist=True)
        self.dma("sp", cv[:], self.cvecT.t.rearrange("(k p) r -> p k r", p=128), [self.cvecT.r], [cv.r])
        self.act(self.sT[:], cv[:], AF.Silu, [cv.r], [self.sT.r])
        self.dt_all = self.sb("dt_all", [128, NT, 16], persist=True)

        out_res = []
        xres = self.x_in
        for l in range(self.nlayers):
            last = (l == DEPTH - 1)
            xnext = None if last else self.xres1
            self.layer(l, xres, xnext, last)
            xres = xnext
            if self.stop_after is not None and self.stop_after[0] == l:
                break
        P.final_wait("sp", [self.y_out.r] + [o.r for o in self.dbg_out.values()])
        P.final_wait("pool", [self.y_out.r] + [o.r for o in self.dbg_out.values()])
        P.emit()
        return nc

    def end_scope(self):
        if self.scope is not None:
            self.P.barrier()
            self.scope.close()
            self.scope = None
            self.P.release(self.scope_res)
            self.scope_res = []

    def new_scope(self):
        self.end_scope()
        self.scope = ExitStack()

    def open_ab(self):
        self.end_scope()
        self.scope_ab = ExitStack()

    def close_ab(self):
        self.end_scope()
        if self.scope_ab is not None:
            self.scope_ab.close()
            self.scope_ab = None
            self.P.release(self.ab_res)
            self.ab_res = []

    def stop(self, l, tag):
        return self.stop_after is not None and self.stop_after == (l, tag)

    def layer(self, l, xres, xnext, last):
        self.phase_mod(l)
        if self.stop(l, "mod"):
            return
        self.phase_a(l, xres, last)
        if self.stop(l, "a"):
            return
        self.exchange(l)
        self.phase_b(l, xres, last)
        if self.stop_after is not None and self.stop_after[0] == l and self.stop_after[1] in ("b", "b0", "b1", "b2", "t1", "t2", "t3", "t4", "u1", "u2", "u3", "v1", "v2"):
            return
        self.phase_c(l, xnext, last)

    def phase_mod(self, l):
        self.new_scope()
        modsb = self.sb("modsb", [2, 6 * D])
        bm = self.sb("bm", [2, 6 * D])
        self.dma("sp", bm[:], self.b_mod.t[l].partition_broadcast(2), [self.b_mod.r], [bm.r])
        gv = self.sb("gv", [2, 4 * D])
        self.dma("sp", gv[:], self.gvecs.t[l].rearrange("a d -> (a d)").partition_broadcast(2),
                 [self.gvecs.r], [gv.r])
        slabs = self.rot("wm", 2, [128, 8, 512])
        wmv = self.w_mod.t[l].rearrange("(k p) n -> p k n", p=128)
        for n in range(12):
            sl = slabs.next()
            self.dma("sp", sl[:], wmv[:, :, n * 512:(n + 1) * 512], [self.w_mod.r], [sl.r])
            pb, pr = self.pbank()
            for k in range(8):
                self.mm(pb[0:2, :], self.sT[:, k, :], sl[:, k, :], k == 0, k == 7, [self.sT.r, sl.r], pr)
            self.tt("dve", modsb[:, n * 512:(n + 1) * 512], pb[0:2, :], bm[:, n * 512:(n + 1) * 512], ALU.add,
                    pr + [bm.r], [modsb.r])
        cmb = self.sb("cmb", [2, 6, D])
        m = lambda j: modsb[:, j * D:(j + 1) * D]
        g = lambda j: gv[:, j * D:(j + 1) * D]
        R = [modsb.r, gv.r]
        self.stt("dve", cmb[:, 0, :], m(1), 1.0, g(0), ALU.add, ALU.mult, R, [cmb.r])
        self.cp("dve", cmb[:, 1, :], m(0), R, [cmb.r])
        self.tt("dve", cmb[:, 2, :], m(2), g(1), ALU.mult, R, [cmb.r])
        self.stt("dve", cmb[:, 3, :], m(4), 1.0, g(2), ALU.add, ALU.mult, R, [cmb.r])
        self.cp("dve", cmb[:, 4, :], m(3), R, [cmb.r])
        self.tt("dve", cmb[:, 5, :], m(5), g(3), ALU.mult, R, [cmb.r])
        sel = self.sb("sel", [2, 2, 128])
        self.cp("dve", sel[:, 0, :], self.ident_f.t[0:2, 0:1].to_broadcast([2, 128]), [self.cst.r], [sel.r])
        self.cp("dve", sel[:, 1, :], self.ident_f.t[0:2, 1:2].to_broadcast([2, 128]), [self.cst.r], [sel.r])
        stg = self.rot("bvst", 2, [128, D])
        for s in range(2):
            for j in range(6):
                st = stg.next()
                for hlf in range(2):
                    pb, pr = self.pbank()
                    self.mm(pb[:, :], sel[:, s, :], cmb[:, j, hlf * 512:(hlf + 1) * 512], True, True,
                            [sel.r, cmb.r], pr)
                    self.cp("act", st[:, hlf * 512:(hlf + 1) * 512], pb[:, :], pr, [st.r])
                self.dma("sp", self.bv.t[6 * s + j], st[:], [st.r], [self.bv.r])
        self.dump(f"bv{l}", self.bv.t, self.bv.r, [12, 128, D])

    def esel_rows(self):
        return self.ident_f.t[0:2, 0:2]

    def load_bv(self, idx, name):
        b = self.sb(name, [128, D])
        self.dma("sp", b[:], self.bv.t[idx], [self.bv.r], [b.r])
        return b

    def norm_tile(self, xt, A, Bv, hb, tmp, small):
        ms_, ln_, rs_ = small
        self.act(hb[:], xt[:], AF.Square, [xt.r], [hb.r, ms_.r], scale=1.0 / 32.0, accum=ms_[:])
        self.rstd(rs_[:], ms_[:], [ms_.r], [rs_.r], ln_)
        self.stt("dve", tmp[:], xt[:], rs_[:, 0:1], A[:], ALU.mult, ALU.mult, [xt.r, rs_.r, A.r], [tmp.r])
        self.tt("dve", hb[:], tmp[:], Bv[:], ALU.add, [tmp.r, Bv.r], [hb.r])

    def post_tile(self, pw, prw, xt, G, tmp, small, xo):
        ms_, ln_, rs_ = small
        self.act(tmp[:], pw[:, :], AF.Square, prw, [tmp.r, ms_.r], scale=1.0 / 32.0, accum=ms_[:])
        self.rstd(rs_[:], ms_[:], [ms_.r], [rs_.r], ln_)
        self.stt("dve", tmp[:], pw[:, :], rs_[:, 0:1], G[:], ALU.mult, ALU.mult, prw + [rs_.r, G.r], [tmp.r])
        self.tt("dve", xo[:], xt[:], tmp[:], ALU.add, [xt.r, tmp.r], [xo.r])

    def transpose_tile(self, hb, hT, tl):
        pb, pr = self.pbank()
        pbb = pb.bitcast(BF16)
        for k in range(8):
            self.tr(pbb[:, k * 128:(k + 1) * 128], hb[:, k * 128:(k + 1) * 128], self.ident_b[:],
                    [hb.r, self.ident_b.r], pr)
        self.cp("act", hT[:, :, tl * 128:(tl + 1) * 128], pbb[:, :].rearrange("p (k t) -> p k t", k=8), pr, [hT.r])

    def small3(self, n):
        return Rot([(self.sb("ms", [128, 1]), self.sb("ln", [128, 1]), self.sb("rs", [128, 1])) for _ in range(n)])

    def phase_a(self, l, xres, last):
        nc, P = self.nc, self.P
        self.open_ab()
        self.qT = self.sb("qT", [128, 4, T], BF16, ab=True)
        self.ysc = self.sb("ysc", [128, 2, T], BF16, ab=True)
        self.BT = self.sb("BT", [128, T], BF16, ab=True)
        self.CT = self.sb("CT", [128, T], BF16, ab=True)
        self.HEs = self.sb("HEs", [128, NT, 256], BF16, ab=True)
        self.fix = self.sb("fix", [128, 16], ab=True)
        self.xsT_last = self.sb("xsT_last", [128, 4, 128], ab=True)
        self.new_scope()
        NB = 256
        wgrp = [(Z_OFF, NU), (0, 448), (448, 1216), (1216, Z_OFF)]
        wslab = []
        winv = self.w_in.t[l].rearrange("(k p) n -> p k n", p=128)
        for gi, (a_, b_) in enumerate(wgrp):
            wt = self.sb(f"win{gi}", [128, 8, b_ - a_], BF16)
            self.dma("pool", wt[:], winv[:, :, a_:b_], [self.w_in.r], [wt.r])
            wslab.append((a_, b_, wt))

        class _Win:
            def cols(self_, k, off, w):
                for a_, b_, wt in wslab:
                    if a_ <= off and off + w <= b_:
                        return wt[:, k, off - a_:off - a_ + w], wt.r
                raise AssertionError((off, w))
        win = _Win()
        wst = self.sb("wst", [128, 2, 640])
        self.dma("sp", wst[:], self.w_uq.t[l].rearrange("(c p) n -> p c n", p=128), [self.w_uq.r], [wst.r])
        gq = self.sb("gq", [128, 2])
        self.dma("sp", gq[:], self.q_norm.t[l].rearrange("(c p) -> p c", p=128), [self.q_norm.r], [gq.r], slow_ok=True)
        wuq = self.sb("wuq", [128, 2, 640], BF16)
        for c in range(2):
            self.ts("dve", wuq[:, c, :], wst[:, c, :], gq[:, c:c + 1], None, ALU.mult, None, [wst.r, gq.r], [wuq.r])
        wst2 = self.sb("wst2", [128, 768])
        self.dma("sp", wst2[:, 0:512], self.w_ukv_k.t[l], [self.w_ukv_k.r], [wst2.r])
        self.dma("sp", wst2[:, 512:768], self.w_ukv_v.t[l], [self.w_ukv_v.r], [wst2.r])
        gkv = self.sb("gkv", [128, 1])
        self.dma("sp", gkv[:], self.kv_norm.t[l].rearrange("(p o) -> p o", o=1), [self.kv_norm.r], [gkv.r])
        wkv = self.sb("wkv", [128, 768], BF16)
        self.ts("dve", wkv[:], wst2[:], gkv[:, 0:1], None, ALU.mult, None, [wst2.r, gkv.r], [wkv.r])
        sccw, cw, cb = self.conv_consts(l)
        st = self.ssd_consts(l)
        A1 = self.load_bv(6, "A1")
        B1 = self.load_bv(7, "B1")
        HE = self.sb("HE", [128, 256])
        self.ms("dve", HE[:], 0.0, [HE.r])

        xts = self.rot("xt", 2, [128, D])
        tmp = self.sb("tmp", [128, D])
        hb = self.sb("hb", [128, D], BF16)
        hT = self.sb("hT", [128, 8, NB], BF16)
        smalls = self.small3(2)
        zs = self.rot("zs", 2, [128, 512])
        dtt = self.sb("dtt", [128, 16])
        rqb = self.sb("rqb", [128, NB])
        rkvb = self.sb("rkvb", [128, NB])
        lnb = self.sb("lnb", [128, NB])
        cosr = self.sb("cosr", [32, 2, NB])
        t12 = self.sb("t12", [32, 2, NB])
        rkc = self.sb("rkc", [128, 4])
        rkl = self.sb("rkl", [128, 4])
        KTb = self.sb("KTb", [128, 4, NB], BF16)
        Vb = self.sb("Vb", [128, 2, 260], BF16)
        self.ms("pool", Vb[:], 1.0, [Vb.r])
        xbc = self.rot("xbc", 2, [128, 6, NB + 2])
        pbuf = self.rot("pbuf", 2, [128, 2, NB + 2])
        gbb = self.rot("gbb", 2, [128, 2, NB])
        gcs = self.sb("gcs", [128, NB])
        W = dict(sg=self.rot("sg", 2, [128, NB]), cvo=self.rot("cvo", 2, [128, NB]), xsT=self.sb("xsT", [128, 4, NB]), BCf=self.sb("BCf", [128, 2, NB]),
                 xstm=self.rot("xstm", 2, [128, 512]), btm=self.rot("btm", 2, [128, 128], BF16),
                 sm16=self.sb("sm16", [128, 16]), vt=self.sb("vt", [128, 16]), wE=self.sb("wE", [128, 8]),
                 decE=self.sb("decE", [128, 4]), xdtw=self.rot("xdtw", 2, [128, 512], BF16),
                 hetmp=self.sb("hetmp", [128, 256]), HE=HE, sccw=sccw, cw=cw, cb=cb, st=st)

        blocks = [(2 * i, 2) for i in range(NT // 2)]
        nb = len(blocks)
        S = dict(win=win, wuq=wuq, wkv=wkv, A1=A1, B1=B1, xts=xts, tmp=tmp, hb=hb, hT=hT, smalls=smalls, zs=zs, dtt=dtt,
                 st=st, rqb=rqb, rkvb=rkvb, lnb=lnb, cosr=cosr, t12=t12, rkc=rkc, rkl=rkl, KTb=KTb, Vb=Vb, gcs=gcs,
                 mla=[dict(cqT=self.sb("cqT", [128, 2, NB], BF16), cqsq=self.sb("cqsq", [128, 2, NB]),
                           ckvT=self.sb("ckvT", [128, NB], BF16), ckvsq=self.sb("ckvsq", [128, NB]),
                           krs=self.sb("krs", [32, 2, NB]), rope=self.sb("rope", [32, 2, NB])) for _ in range(2)],
                 slot=[dict(xb=xbc.next(), pbf=pbuf.next(), gb=gbb.next()) for _ in range(2)])

        def front1(bi):
            t0, nt = blocks[bi]
            N = nt * 128
            c0 = t0 * 128
            M = S["mla"][bi % 2]
            if bi == 1:
                self.dma("sp", A1[:], self.bv.t[0], [self.bv.r], [A1.r])
                self.dma("sp", B1[:], self.bv.t[1], [self.bv.r], [B1.r])
            for tl in range(nt):
                t = t0 + tl
                xt = xts.next()
                self.dma("sp", xt[:], xres.t[t * 128:(t + 1) * 128, :], [xres.r], [xt.r])
                self.norm_tile(xt, A1, B1, hb, tmp, smalls.next())
                self.transpose_tile(hb, hT, tl)
                yield
            for tl in range(nt):
                t = t0 + tl
                pb, pr = self.pbank()
                for k in range(8):
                    wa, wr = win.cols(k, Z_OFF, 512)
                    self.mm(pb[:, :], hT[:, k, tl * 128:(tl + 1) * 128], wa, k == 0, k == 7, [hT.r, wr], pr)
                z = zs.next()
                self.cp("act", z[:], pb[:, :], pr, [z.r])
                self.dma("pool", self.sc_z.t[t * 128:(t + 1) * 128, :], z[:], [z.r], [self.sc_z.r])
                pb, pr = self.pbank()
                for k in range(8):
                    wa, wr = win.cols(k, DT_OFF, 16)
                    self.mm(pb[:, 0:16], hT[:, k, tl * 128:(tl + 1) * 128], wa, k == 0, k == 7, [hT.r, wr], pr)
                self.tt("dve", dtt[:], pb[:, 0:16], st["dtb"][:], ALU.add, pr + [st["dtb"].r], [dtt.r])
                self.act(dtt[:], dtt[:], AF.Exp, [dtt.r], [dtt.r])
                self.act(self.dt_all[:, t, :], dtt[:], AF.Ln, [dtt.r], [self.dt_all.r], bias=1.0)
                yield
            self.dma("sp", M["rope"][:, :, 0:N], self.ropeT.t[:, :, c0:c0 + N].rearrange("a p t -> p a t"),
                     [self.ropeT.r], [M["rope"].r])
            for ci in range(5):
                off, wdt = FM[ci]
                pb, pr = self.pbank()
                for k in range(8):
                    wa, wr = win.cols(k, off, wdt)
                    self.mm(pb[0:wdt, 0:N], wa, hT[:, k, 0:N], k == 0, k == 7, [hT.r, wr], pr)
                src = pb[0:wdt, 0:N]
                if ci < 2:
                    self.cp("act", M["cqT"][:, ci, 0:N], src, pr, [M["cqT"].r])
                    self.act(M["cqsq"][:, ci, 0:N], src, AF.Square, pr, [M["cqsq"].r])
                elif ci == 2:
                    self.cp("act", M["ckvT"][:, 0:N], src, pr, [M["ckvT"].r])
                    self.act(M["ckvsq"][:, 0:N], src, AF.Square, pr, [M["ckvsq"].r])
                else:
                    self.cp("act", M["krs"][:, ci - 3, 0:N], src, pr, [M["krs"].r])
                yield

        def front2(bi):
            t0, nt = blocks[bi]
            N = nt * 128
            sl = S["slot"][bi % 2]
            xb, pbf, gb = sl["xb"], sl["pbf"], sl["gb"]
            for ci in (5, 6, 7, 8, 11, 12, 13, 14, 15, 16):
                off, wdt = FM[ci]
                pb, pr = self.pbank()
                for k in range(8):
                    wa, wr = win.cols(k, off, wdt)
                    self.mm(pb[0:wdt, 0:N], wa, hT[:, k, 0:N], k == 0, k == 7, [hT.r, wr], pr)
                src = pb[0:wdt, 0:N]
                if ci < 7:
                    self.cp("act", gb[:, ci - 5, 0:N], src, pr, [gb.r])
                elif ci < 9:
                    c = ci - 7
                    pbv, prv = self.pbank()
                    o2, w2 = FM[ci + 2]
                    for k in range(8):
                        wa, wr = win.cols(k, o2, w2)
                        self.mm(pbv[:, 0:N], wa, hT[:, k, 0:N], k == 0, k == 7, [hT.r, wr], prv)
                    self.cp("act", gcs[:, 0:N], src, pr, [gcs.r])
                    self.tt("dve", pbf[:, c, 1:N + 1], gcs[:, 0:N], pbv[:, 0:N], ALU.mult, [gcs.r] + prv, [pbf.r])
                else:
                    self.cp("act", xb[:, ci - 11, 1:N + 1], src, pr, [xb.r])
                yield
            self.ms("pool", xb[:, :, 0:1], 0.0, [xb.r])
            self.ms("pool", pbf[:, :, 0:1], 0.0, [pbf.r])
            self.ms("pool", xb[:, :, N + 1:N + 2], 0.0, [xb.r])
            self.ms("pool", pbf[:, :, N + 1:N + 2], 0.0, [pbf.r])
            yield

        def mla_post(bi):
            t0, nt = blocks[bi]
            N = nt * 128
            c0 = t0 * 128
            M = S["mla"][bi % 2]
            cqT, cqsq, ckvT, ckvsq, krs, rope = M["cqT"], M["cqsq"], M["ckvT"], M["ckvsq"], M["krs"], M["rope"]
            pb, pr = self.pbank()
            for c in range(2):
                self.mm(pb[:, 0:N], self.ones_f[:], cqsq[:, c, 0:N], c == 0, c == 1, [self.ones_f.r, cqsq.r], pr)
            self.act(lnb[:, 0:N], pb[:, 0:N], AF.Ln, pr + [self.eps_t.r], [lnb.r], bias=self.eps_t[:, 0:1], scale=1.0 / 256)
            self.act(rqb[:, 0:N], lnb[:, 0:N], AF.Exp, [lnb.r], [rqb.r], scale=-0.5)
            pb, pr = self.pbank()
            self.mm(pb[:, 0:N], self.ones_f[:], ckvsq[:, 0:N], True, True, [self.ones_f.r, ckvsq.r], pr)
            self.act(lnb[:, 0:N], pb[:, 0:N], AF.Ln, pr + [self.eps_t.r], [lnb.r], bias=self.eps_t[:, 0:1], scale=1.0 / 128)
            self.act(rkvb[:, 0:N], lnb[:, 0:N], AF.Exp, [lnb.r], [rkvb.r], scale=-0.5)
            pb, pr = self.pbank()
            for tl in range(nt):
                self.mm(pb[:, tl:tl + 1], ckvsq[:, tl * 128:(tl + 1) * 128], self.ones_f[:, 0:1], True, True,
                        [ckvsq.r, self.ones_f.r], pr)
            self.act(rkl[:, 0:nt], pb[:, 0:nt], AF.Ln, pr + [self.eps_t.r], [rkl.r], bias=self.eps_t[:, 0:1], scale=1.0 / 128)
            self.act(rkc[:, 0:nt], rkl[:, 0:nt], AF.Exp, [rkl.r], [rkc.r], scale=-0.5)
            yield
            for a in range(2):
                self.tt("pool", cosr[:, a, 0:N], rope[:, a, 0:N], rqb[0:32, 0:N], ALU.mult, [rope.r, rqb.r], [cosr.r])
            for h in range(4):
                pa, pra = self.pbank()
                for c in range(2):
                    self.mm(pa[:, 0:N], wuq[:, c, h * 160:h * 160 + 128], cqT[:, c, 0:N], c == 0, c == 1, [wuq.r, cqT.r], pra)
                pq, prq = self.pbank()
                for c in range(2):
                    self.mm(pq[0:32, 0:N], wuq[:, c, h * 160 + 128:h * 160 + 160], cqT[:, c, 0:N], c == 0, c == 1,
                            [wuq.r, cqT.r], prq)
                self.tt("dve", self.qT[:, h, c0:c0 + N], pa[:, 0:N], rqb[:, 0:N], ALU.mult, pra + [rqb.r], [self.qT.r])
                self.tt("dve", t12[:, 0, 0:N], pa[0:32, 0:N], cosr[:, 0, 0:N], ALU.mult, pra + [cosr.r], [t12.r])
                self.tt("dve", t12[:, 1, 0:N], pq[0:32, 0:N], cosr[:, 1, 0:N], ALU.mult, prq + [cosr.r], [t12.r])
                self.tt("dve", self.qT[0:32, h, c0:c0 + N], t12[:, 0, 0:N], t12[:, 1, 0:N], ALU.add, [t12.r], [self.qT.r])
                yield
            self.tt("pool", t12[:, 0, 0:N], krs[:, 0, 0:N], rope[:, 0, 0:N], ALU.mult, [krs.r, rope.r], [t12.r])
            self.tt("pool", t12[:, 1, 0:N], krs[:, 1, 0:N], rope[:, 1, 0:N], ALU.mult, [krs.r, rope.r], [t12.r])
            self.tt("pool", t12[:, 0, 0:N], t12[:, 0, 0:N], t12[:, 1, 0:N], ALU.add, [t12.r], [t12.r])
            for h in range(4):
                pk, prk = self.pbank()
                self.mm(pk[:, 0:N], wkv[:, h * 128:(h + 1) * 128], ckvT[:, 0:N], True, True, [wkv.r, ckvT.r], prk)
                self.tt("dve", KTb[:, h, 0:N], pk[:, 0:N], rkvb[:, 0:N], ALU.mult, prk + [rkvb.r], [KTb.r])
                self.cp("pool", KTb[0:32, h, 0:N], t12[:, 0, 0:N], [t12.r], [KTb.r])
            yield
            if bi == 0:
                self.dma("pool", self.kv_ctx_k.t, KTb[:, :, 0:N], [KTb.r], [self.kv_ctx_k.r])
            else:
                lo = c0 - TC
                self.dma("pool", self.xk.t.rearrange("(h p) t -> p h t", p=128)[:, :, lo:lo + N], KTb[:, :, 0:N],
                         [KTb.r], [self.xk.r])
            for tl in range(nt):
                pv, prv = self.pbank()
                self.mm(pv[:, 0:256], ckvT[:, tl * 128:(tl + 1) * 128], wkv[:, 512:768], True, True, [ckvT.r, wkv.r], prv)
                self.ts("dve", Vb[:, tl, :].rearrange("p (h e) -> p h e", h=4)[:, :, 0:64],
                        pv[:, 0:256].rearrange("p (h e) -> p h e", h=4), rkc[:, tl:tl + 1], None, ALU.mult, None,
                        prv + [rkc.r], [Vb.r])
            if bi == 0:
                self.dma("pool", self.kv_ctx_v.t, Vb[:, 0:2, :], [Vb.r], [self.kv_ctx_v.r])
            else:
                lo = t0 - 2
                self.dma("pool", self.xv.t.rearrange("(t p) e -> p t e", p=128)[:, lo:lo + nt, :], Vb[:, 0:nt, :],
                         [Vb.r], [self.xv.r])
            yield

        def back1(bi):
            if bi == 0:
                return
            p = S["slot"][(bi - 1) % 2]
            pt0, pnt = blocks[bi - 1]
            pN = pnt * 128
            if 2 <= bi < nb:
                c = S["slot"][bi % 2]
                self.cp("pool", p["xb"][:, :, pN + 1:pN + 2], c["xb"][:, :, 1:2], [c["xb"].r], [p["xb"].r])
                self.cp("pool", p["pbf"][:, :, pN + 1:pN + 2], c["pbf"][:, :, 1:2], [c["pbf"].r], [p["pbf"].r])
                self.cp("pool", c["xb"][:, :, 0:1], p["xb"][:, :, pN:pN + 1], [p["xb"].r], [c["xb"].r])
                self.cp("pool", c["pbf"][:, :, 0:1], p["pbf"][:, :, pN:pN + 1], [p["pbf"].r], [c["pbf"].r])
            yield from self.stage_a2_conv(l, p["xb"], p["pbf"], p["gb"], pt0, pnt, W, bi == nb)

        def back2(bi):
            if bi < nb:
                yield from mla_post(bi)
            if bi >= 1:
                pt0, pnt = blocks[bi - 1]
                yield from self.stage_a2_scan(l, pt0, pnt, W, bi == nb)

        run_gen(front1(0))
        run_gen(front2(0))
        for bi in range(nb + 1):
            f1 = front1(bi + 1) if bi + 1 < nb else None
            f2 = front2(bi + 1) if bi + 1 < nb else None
            interleave((f1, 9), (back1(bi), 8))
            interleave((f2, 11), (back2(bi), 10))
        pxb, ppb = S["slot"][(nb - 1) % 2]["xb"], S["slot"][(nb - 1) % 2]["pbf"]
        pnt = blocks[-1][1]
        N = pnt * 128
        sm = self.sb("sm", [128, SM_W])
        self.ms("dve", sm[:], 0.0, [sm.r])
        self.cp("dve", sm[:, 0:256], HE[:], [HE.r], [sm.r])
        self.cp("dve", sm[:, 512:524].rearrange("p (c k) -> p c k", c=6), pxb[:, :, N - 1:N + 1], [pxb.r], [sm.r])
        self.cp("dve", sm[:, 524:526], ppb[:, :, N], [ppb.r], [sm.r])
        self.cp("dve", self.fix[:, 10:16], pxb[:, :, N], [pxb.r], [self.fix.r])
        self.dma("pool", self.xs_.t, sm[:], [sm.r], [self.xs_.r])
        self.dma("pool", self.xs_.t[0:1, 528:536], self.dt_all[127:128, NT - 1, 0:8], [self.dt_all.r], [self.xs_.r])
        self.dump(f"qT{l}", self.qT[:], self.qT.r, [128, 4, T], BF16)
        self.dump(f"ysc{l}", self.ysc[:], self.ysc.r, [128, 2, T], BF16)
        self.dump(f"dt{l}", self.dt_all[:], self.dt_all.r, [128, NT, 16])
        self.dump(f"HE{l}", HE[:], HE.r, [128, 256])
        self.dump(f"BT{l}", self.BT[:], self.BT.r, [128, T], BF16)
        self.dump(f"CT{l}", self.CT[:], self.CT.r, [128, T], BF16)
        self.dump(f"xk{l}", self.xk.t, self.xk.r, [512, TL], BF16)
        self.dump(f"xv{l}", self.xv.t, self.xv.r, [2048, 260], BF16)
        self.dump(f"scxs{l}", self.sc_xs.t, self.sc_xs.r, [T, 512])
        self.dump(f"scz{l}", self.sc_z.t, self.sc_z.r, [T, 512])

    def conv_consts(self, l):
        sccw = self.sb("sccw", [128, 2, 3])
        self.dma("sp", sccw[:], self.sc_cw.t[l].rearrange("(c p) k -> p c k", p=128), [self.sc_cw.r], [sccw.r])
        cw = self.sb("cw", [128, 6, 3])
        self.dma("sp", cw[:], self.ssd_cw.t[l].rearrange("(c p) k -> p c k", p=128), [self.ssd_cw.r], [cw.r])
        cb = self.sb("cb", [128, 6])
        self.dma("sp", cb[:], self.ssd_cb.t[l].rearrange("(c p) -> p c", p=128), [self.ssd_cb.r], [cb.r], slow_ok=True)
        self.ncb = self.sb("ncb", [128, 6])
        self.ts("dve", self.ncb[:], cb[:], -1.0, None, ALU.mult, None, [cb.r], [self.ncb.r])
        return sccw, cw, cb

    def ssd_consts(self, l):
        st = {}
        al = self.sb("al", [128, 16])
        self.dma("sp", al[:], self.a_log.t[l].partition_broadcast(128), [self.a_log.r], [al.r])
        ab = self.sb("ab", [128, 16])
        self.act(ab[:], al[:], AF.Exp, [al.r], [ab.r])
        self.ts("dve", ab[:], ab[:], -1.0, None, ALU.mult, None, [ab.r], [ab.r])
        st["ab"] = ab
        dtb = self.sb("dtb", [128, 16])
        self.dma("sp", dtb[:], self.dt_bias.t[l].partition_broadcast(128), [self.dt_bias.r], [dtb.r])
        st["dtb"] = dtb
        return st

    def stage_a2_conv(self, l, xb, pbf, gb, t0, nt, W, is_last):
        N = nt * 128
        c0 = t0 * 128
        xsT, BCf, sccw, cw, cb = W["xsT"], W["BCf"], W["sccw"], W["cw"], W["cb"]
        for c in range(2):
            cvo = W["cvo"].next()
            self.conv3(cvo, pbf, c, sccw, N)
            self.tt("dve", self.ysc[:, c, c0:c0 + N], cvo[:, 0:N], gb[:, c, 0:N], ALU.mult, [cvo.r, gb.r], [self.ysc.r])
            if is_last:
                self.cp("dve", self.fix[:, 6 + c:7 + c], cvo[:, N - 1:N], [cvo.r], [self.fix.r])
                self.cp("dve", self.fix[:, 8 + c:9 + c], gb[:, c, N - 1:N], [gb.r], [self.fix.r])
            yield
        for c in range(6):
            cvo = W["cvo"].next()
            self.conv3(cvo, xb, c, cw, N)
            if is_last:
                self.cp("dve", self.fix[:, c:c + 1], cvo[:, N - 1:N], [cvo.r], [self.fix.r])
            dst = xsT[:, c, 0:N] if c < 4 else BCf[:, c - 4, 0:N]
            dres = xsT.r if c < 4 else BCf.r
            sg = W["sg"].next()
            self.silu(dst, [dres], cvo[:, 0:N], [cvo.r], (sg[:, 0:N], sg.r), bias=(cb[:, c:c + 1], cb.r),
                      negbias=(self.ncb[:, c:c + 1], self.ncb.r))
            yield
        self.cp("pool", self.BT[:, c0:c0 + N], BCf[:, 0, 0:N], [BCf.r], [self.BT.r])
        self.cp("pool", self.CT[:, c0:c0 + N], BCf[:, 1, 0:N], [BCf.r], [self.CT.r])
        if is_last:
            self.cp("pool", self.xsT_last[:], xsT[:, :, N - 128:N], [xsT.r], [self.xsT_last.r])

    def stage_a2_scan(self, l, t0, nt, W, is_last):
        xsT, BCf = W["xsT"], W["BCf"]
        for tl in range(nt):
            t = t0 + tl
            xs_tm, b_tm = self.tokmajor_xs_b(xsT, BCf, tl, t, W["xstm"], W["btm"])
            self.cp("act", self.HEs[:, t, :], W["HE"][:], [W["HE"].r], [self.HEs.r])
            yield
            self.state_update(W["st"], xs_tm, b_tm, t, 0, W["sm16"], W["vt"], W["wE"], W["decE"], W["xdtw"], W["hetmp"],
                              W["HE"], mask_last=(is_last and tl == nt - 1))
            yield

    def conv3(self, cvo, buf, c, wts, N):
        self.ts("dve", cvo[:, 0:N], buf[:, c, 1:N + 1], wts[:, c, 1:2], None, ALU.mult, None, [buf.r, wts.r], [cvo.r])
        self.stt("dve", cvo[:, 0:N], buf[:, c, 0:N], wts[:, c, 0:1], cvo[:, 0:N], ALU.mult, ALU.add,
                 [buf.r, wts.r, cvo.r], [cvo.r])
        self.stt("dve", cvo[:, 0:N], buf[:, c, 2:N + 2], wts[:, c, 2:3], cvo[:, 0:N], ALU.mult, ALU.add,
                 [buf.r, wts.r, cvo.r], [cvo.r])

    def tokmajor_xs_b(self, xsT, BCf, tl, t, xstm, btm):
        pb, pr = self.pbank()
        for c in range(4):
            self.tr(pb[:, c * 128:(c + 1) * 128], xsT[:, c, tl * 128:(tl + 1) * 128], self.ident_f[:],
                    [xsT.r, self.cst.r], pr)
        xs_tm = xstm.next()
        self.cp("act", xs_tm[:], pb[:, :], pr, [xs_tm.r])
        self.dma("pool", self.sc_xs.t[t * 128:(t + 1) * 128, :], xs_tm[:], [xs_tm.r], [self.sc_xs.r])
        pb2, pr2 = self.pbank()
        self.tr(pb2[:, 0:128], BCf[:, 0, tl * 128:(tl + 1) * 128], self.ident_f[:], [BCf.r, self.cst.r], pr2)
        b_tm = btm.next()
        self.cp("act", b_tm[:], pb2[:, 0:128], pr2, [b_tm.r])
        self.dma("pool", self.sc_bt.t[t * 128:(t + 1) * 128, :], b_tm[:], [b_tm.r], [self.sc_bt.r])
        return xs_tm, b_tm

    def state_update(self, st, xs_tm, b_tm, t, d_, sm16, vt, wE, decE, xdtw, hetmp, H, mask_last=False, vt_ready=None):
        tri = self.triE if d_ == 0 else self.triL
        o = d_ * 8
        if vt_ready is None:
            la = sm16
            self.tt("dve", la[:, 0:8], self.dt_all[:, t, o:o + 8], st["ab"][:, o:o + 8], ALU.mult,
                    [self.dt_all.r, st["ab"].r], [la.r])
            pb, pr = self.pbank()
            self.mm(pb[:, 0:8], tri[:], la[:, 0:8], True, True, [self.cst.r, la.r], pr)
            self.mm(pb[:, 8:16], self.ones_f[:], la[:, 0:8], True, True, [self.ones_f.r, la.r], pr)
            self.cp("act", vt[:, 0:16], pb[:, 0:16], pr, [vt.r])
            vcol, tot = vt[:, 0:8], vt[:, 8:16]
            totlo, tothi = vt[0:64, 8:12], vt[64:128, 12:16]
            vr = vt.r
        else:
            vcol, tot, totlo, tothi, vr = vt_ready
        self.tt("dve", wE[:], tot, vcol, ALU.subtract, [vr], [wE.r])
        self.act(wE[:], wE[:], AF.Exp, [wE.r], [wE.r])
        self.tt("dve", wE[:], wE[:], self.dt_all[:, t, o:o + 8], ALU.mult, [wE.r, self.dt_all.r], [wE.r])
        if mask_last:
            self.ts("dve", wE[:], wE[:], self.lastmask[:, 0:1], None, ALU.mult, None, [wE.r, self.cst.r], [wE.r])
        self.act(decE[0:64, :], totlo, AF.Exp, [vr], [decE.r])
        self.act(decE[64:128, :], tothi, AF.Exp, [vr], [decE.r])
        xd = xdtw.next()
        self.tt("dve", xd[:].rearrange("p (h e) -> p h e", h=8), xs_tm[:].rearrange("p (h e) -> p h e", h=8),
                bc(wE[:], [128, 8, 64]), ALU.mult, [xs_tm.r, wE.r], [xd.r])
        pb, pr = self.pbank()
        self.mm(pb[:, :], b_tm[:], xd[:], True, True, [b_tm.r, xd.r], pr)
        self.tt("dve", hetmp[:].rearrange("p (h e) -> p h e", h=4), H[:].rearrange("p (h e) -> p h e", h=4),
                bc(decE[:], [128, 4, 64]), ALU.mult, [H.r, decE.r], [hetmp.r])
        self.tt("dve", H[0:64, :], hetmp[0:64, :], pb[0:64, 0:256], ALU.add, [hetmp.r] + pr, [H.r])
        self.tt("dve", H[64:128, :], hetmp[64:128, :], pb[64:128, 256:512], ALU.add, [hetmp.r] + pr, [H.r])

    def exchange(self, l):
        groups = [[2 * i, 2 * i + 1] for i in range(self.ncores // 2)]
        for src, dst in ((self.xk, self.xk_g), (self.xv, self.xv_g), (self.xs_, self.xs_g)):
            self.cc(src, dst, groups)

    def cc(self, src, dst, groups):
        if self.fake_cc:
            n = src.t.shape[0]
            for r in range(2):
                self.dma("pool", dst.t[r * n:(r + 1) * n, :], src.t, [src.r], [dst.r])
            return
        self.P.dma("pool", lambda e: e.collective_compute("AllGather", ALU.bypass, replica_groups=groups,
                                                           ins=[src.t.opt()], outs=[dst.t.opt()]),
                   [src.r], [dst.r], inc=1)

    def phase_b(self, l, xres, last):
        nc, P = self.nc, self.P
        self.new_scope()
        st = self.ssd_consts(l)
        sccw, cw, cb = self.conv_consts(l)
        wo_att = self.sb("wo_att", [64, 4, D], BF16)
        self.dma("pool", wo_att[:], self.w_out.t[l, 0:256, :].rearrange("(h p) n -> p h n", p=64), [self.w_out.r], [wo_att.r])
        wo_r = self.sb("wo_r", [128, 6, D], BF16)
        self.dma("pool", wo_r[:], self.w_out.t[l, 256:1024, :].rearrange("(c p) n -> p c n", p=128), [self.w_out.r], [wo_r.r])
        G1 = self.load_bv(2, "G1")
        dsk8 = self.sb("dsk8", [128, 8])
        self.dma("sp", dsk8[:], self.ssd_d.t[l].partition_broadcast(128), [self.ssd_d.r], [dsk8.r])
        ngb = self.sb("ngb", [128, 512])
        self.dma("sp", ngb[:], self.ssd_norm.t[l].partition_broadcast(128), [self.ssd_norm.r], [ngb.r])
        VA = self.sb("VA", [128, 34, 260], BF16)
        self.dma("sp", VA[:, 0:2, :], self.kv_ctx_v.t, [self.kv_ctx_v.r], [VA.r])
        xvv = self.xv_g.t.rearrange("(t p) e -> p t e", p=128)
        for q in range(4):
            self.dma("sp", VA[:, 2 + 8 * q:10 + 8 * q, :], xvv[:, 8 * q:8 * q + 8, :], [self.xv_g.r], [VA.r])
        KTs = self.rot("KT", 2, [128, TC + 2 * TL], BF16)
        smg = self.sb("smg", [128, 2, SM_W])
        self.dma("sp", smg[:], self.xs_g.t.rearrange("(r p) c -> p r c", p=128), [self.xs_g.r], [smg.r])
        smp = self.sb("smp", [128, SM_W])
        self.ts("dve", smp[:], smg[:, 1, :], self.esel[:, 0:1], None, ALU.mult, None, [smg.r, self.cst.r], [smp.r])
        self.stt("dve", smp[:], smg[:, 0, :], self.esel[:, 1:2], smp[:], ALU.mult, ALU.add, [smg.r, self.cst.r, smp.r], [smp.r])

        xt = self.sb("xt", [128, D])
        tmp = self.sb("tmp", [128, D])
        smalls = self.small3(2)
        x1 = self.sb("x1", [128, D])
        la = self.sb("la", [128, 16])
        vt = self.sb("vt", [128, 32])
        ev = self.sb("ev", [128, 16])
        wL = self.sb("wL", [128, 8])
        decL = self.sb("decL", [128, 4])
        Rm = self.sb("Rm", [128, 8, 128])
        D1 = self.sb("D1", [128, 8, 128])
        MT = self.sb("MT", [128, 16, 128], BF16)
        xdt = self.sb("xdt", [128, 16, 64], BF16)
        xs_r = self.rot("xsr", 2, [128, 512])
        bt_r = self.rot("btr", 2, [128, 128], BF16)
        z_r = self.rot("zr", 2, [128, 512])
        ya = self.sb("ya", [128, 512])
        yb = self.sb("yb", [128, 512])
        sz = self.sb("sz", [128, 512])
        yss = self.sb("yss", [128, 512], BF16)
        jk = self.sb("jk", [128, 512], BF16)
        jk32 = self.sb("jk32", [128, 512])
        ms2 = self.sb("ms2", [128, 2])
        ln2 = self.sb("ln2", [128, 2])
        rs2 = self.sb("rs2", [128, 2])
        xdtw = self.rot("xdtw", 2, [128, 512], BF16)
        hetmp = self.sb("hetmp", [128, 256])
        HL = self.sb("HL", [128, 256])
        HLb = self.sb("HLbd", [128, 512], BF16)
        self.ms("pool", HLb[:], 0.0, [HLb.r])
        Cbd = self.sb("Cbd", [128, 256], BF16)
        self.ms("pool", Cbd[:], 0.0, [Cbd.r])
        HbdE = self.sb("HbdE", [128, 512], BF16)
        self.ms("pool", HbdE[:], 0.0, [HbdE.r])
        ymx = self.sb("ymx", [128, 4, 512], BF16)
        yatt = self.sb("yatt", [64, 4, 512], BF16)
        PTs = self.rot("PT", 2, [128, 1024], BF16)
        Osb = self.sb("Osb", [64, 512])
        rec = self.sb("rec", [128, 512])
        dskb = self.sb("dskb", [128, 512])
        self.cp("dve", dskb[:].rearrange("p (h e) -> p h e", h=8), bc(dsk8[:], [128, 8, 64]), [dsk8.r], [dskb.r])

        self.l_init(l, st, smp, cw, cb, HL, HLb)
        self.fix_last(l, smp, sccw, cw, cb)
        self.dump(f"HL0{l}", HL[:], HL.r, [128, 256])
        if self.stop(l, "b0"):
            return

        order = [4, 3, 2, 1] + ([] if last else [0])
        ymxs = [ymx, self.sb("ymx2", [128, 4, 512], BF16)]
        yatts = [yatt, self.sb("yatt2", [64, 4, 512], BF16)]

        def geom(bi):
            t0, nt = BLOCKS[bi]
            return t0, nt, nt * 128, t0 * 128, (1 if bi == 0 else 0)

        def ssd_gen(bi, ymx):
            t0, nt, N, c0, isctx = geom(bi)
            if isctx:
                self.ms("dve", HL[:], 0.0, [HL.r])
                self.ms("dve", HLb[:], 0.0, [HLb.r])
            for tl in range(nt - 1, -1, -1):
                t = t0 + tl
                cs = slice(t * 128, (t + 1) * 128)
                xs = xs_r.next()
                self.dma("pool", xs[:], self.sc_xs.t[cs, :], [self.sc_xs.r], [xs.r])
                btk = bt_r.next()
                self.dma("pool", btk[:], self.sc_bt.t[cs, :], [self.sc_bt.r], [btk.r])
                z = z_r.next()
                self.dma("pool", z[:], self.sc_z.t[cs, :], [self.sc_z.r], [z.r])
                self.tt("dve", la[:], self.dt_all[:, t, :], st["ab"][:], ALU.mult, [self.dt_all.r, st["ab"].r], [la.r])
                pb, pr = self.pbank()
                self.mm(pb[:, 0:8], self.triE[:], la[:, 0:8], True, True, [self.cst.r, la.r], pr)
                self.mm(pb[:, 8:16], self.triL[:], la[:, 8:16], True, True, [self.cst.r, la.r], pr)
                self.mm(pb[:, 16:32], self.ones_f[:], la[:, 0:16], True, True, [self.ones_f.r, la.r], pr)
                self.cp("act", vt[:], pb[:, 0:32], pr, [vt.r])
                self.act(ev[:], vt[:, 0:16], AF.Exp, [vt.r], [ev.r])
                yield
                pg, prg = self.pbank(pin=True)
                for g in range(2):
                    self.cp("pool", Cbd[g * 64:(g + 1) * 64, g * 128:(g + 1) * 128], self.CT[g * 64:(g + 1) * 64, cs],
                            [self.CT.r], [Cbd.r])
                self.mm(pg[:, 0:256], self.BT[:, cs], Cbd[:], True, True, [self.BT.r, Cbd.r], prg)
                for d_ in range(2):
                    self.tt("dve", xdt[:, d_ * 8:(d_ + 1) * 8, :], xs[:].rearrange("p (h e) -> p h e", h=8),
                            bc(self.dt_all[:, t, d_ * 8:(d_ + 1) * 8], [128, 8, 64]), ALU.mult,
                            [xs.r, self.dt_all.r], [xdt.r])
                yield
                for d_, tri in enumerate((self.triE, self.triL)):
                    self.tt("pool", Rm[:], bcm(tri[:], [128, 8, 128]), bc(la[:, d_ * 8:(d_ + 1) * 8], [128, 8, 128]),
                            ALU.mult, [self.cst.r, la.r], [Rm.r])
                    pd, prd = self.pbank(2)
                    for q in range(2):
                        self.mm(pd[:, q * 512:(q + 1) * 512], self.ones_f[:],
                                Rm[:, q * 4:(q + 1) * 4, :].rearrange("p a b -> p (a b)"), True, False,
                                [self.ones_f.r, Rm.r], prd)
                        self.mm(pd[:, q * 512:(q + 1) * 512], self.ident_b[:], self.negm[:, d_, :], False, True,
                                [self.ident_b.r, self.negm.r], prd)
                    self.tt("dve", D1[:], pd[:, :].rearrange("p (a b) -> p a b", a=8),
                            bc(vt[:, d_ * 8:(d_ + 1) * 8], [128, 8, 128]), ALU.subtract, prd + [vt.r], [D1.r])
                    self.act(D1[:], D1[:], AF.Exp, [D1.r], [D1.r])
                    self.tt("dve", MT[:, d_ * 8:(d_ + 1) * 8, :].rearrange("p (g h) b -> p g h b", g=2),
                            D1[:].rearrange("p (g h) b -> p g h b", g=2),
                            pg[:, 0:256].rearrange("p (g b) -> p g b", g=2).unsqueeze(2).to_broadcast([128, 2, 4, 128]),
                            ALU.mult, [D1.r] + prg, [MT.r])
                    if d_ == 1:
                        self.unpin(prg)
                    yield
                yield
                py, pry = self.pbank()
                for h in range(8):
                    self.mm(py[:, h * 64:(h + 1) * 64], MT[:, h, :], xdt[:, h, :], True, False, [MT.r, xdt.r], pry)
                    self.mm(py[:, h * 64:(h + 1) * 64], MT[:, 8 + h, :], xdt[:, 8 + h, :], False, True, [MT.r, xdt.r], pry)
                pe_, pre = self.pbank()
                for g in range(2):
                    self.cp("act", HbdE[g * 64:(g + 1) * 64, g * 256:(g + 1) * 256], self.HEs[g * 64:(g + 1) * 64, t, :],
                            [self.HEs.r], [HbdE.r])
                self.mm(pe_[:, :], self.CT[:, cs], HbdE[:], True, True, [self.CT.r, HbdE.r], pre)
                self.tt("dve", ya[:].rearrange("p (h e) -> p h e", h=8), pe_[:, :].rearrange("p (h e) -> p h e", h=8),
                        bc(ev[:, 0:8], [128, 8, 64]), ALU.mult, pre + [ev.r], [ya.r])
                pl, prl = self.pbank()
                self.mm(pl[:, :], self.CT[:, cs], HLb[:], True, True, [self.CT.r, HLb.r], prl)
                self.tt("dve", yb[:].rearrange("p (h e) -> p h e", h=8), pl[:, :].rearrange("p (h e) -> p h e", h=8),
                        bc(ev[:, 8:16], [128, 8, 64]), ALU.mult, prl + [ev.r], [yb.r])
                self.tt("dve", ya[:], ya[:], yb[:], ALU.add, [ya.r, yb.r], [ya.r])
                self.tt("dve", ya[:], ya[:], py[:, :], ALU.add, [ya.r] + pry, [ya.r])
                self.tt("dve", yb[:], xs[:], dskb[:], ALU.mult, [xs.r, dskb.r], [yb.r])
                self.tt("dve", ya[:], ya[:], yb[:], ALU.add, [ya.r, yb.r], [ya.r])
                if l == 0 and t == NT - 1:
                    self.dump("yscan0", ya[:], ya.r, [128, 512])
                yield
                self.silu(sz[:], [sz.r], z[:], [z.r], (jk32[:], jk32.r))
                self.tt("dve", ya[:], ya[:], sz[:], ALU.mult, [ya.r, sz.r], [ya.r])
                for g in range(2):
                    self.act(jk[:, g * 256:(g + 1) * 256], ya[:, g * 256:(g + 1) * 256], AF.Square, [ya.r],
                             [jk.r, ms2.r], scale=1.0 / 16.0, accum=ms2[:, g:g + 1])
                self.act(ln2[:], ms2[:], AF.Ln, [ms2.r, self.eps_t.r], [ln2.r], bias=self.eps_t[:, 0:1])
                self.act(rs2[:], ln2[:], AF.Exp, [ln2.r], [rs2.r], scale=-0.5)
                for g in range(2):
                    self.stt("dve", yss[:, g * 256:(g + 1) * 256], ya[:, g * 256:(g + 1) * 256], rs2[:, g:g + 1],
                             ngb[:, g * 256:(g + 1) * 256], ALU.mult, ALU.mult, [ya.r, rs2.r, ngb.r], [yss.r])
                pb, pr = self.pbank()
                pbb = pb.bitcast(BF16)
                for c in range(4):
                    self.tr(pbb[:, c * 128:(c + 1) * 128], yss[:, c * 128:(c + 1) * 128], self.ident_b[:],
                            [yss.r, self.ident_b.r], pr)
                self.cp("act", ymx[:, :, tl * 128:(tl + 1) * 128], pbb[:, 0:512].rearrange("p (c t) -> p c t", c=4), pr, [ymx.r])
                yield
                vready = (vt[:, 8:16], vt[:, 24:32], vt[0:64, 24:28], vt[64:128, 28:32], vt.r)
                self.state_update(st, xs, btk, t, 1, None, vt, wL, decL, xdtw, hetmp, HL, vt_ready=vready)
                self.hl_bd(HL, HLb)
                yield

        def att_gen(bi, yatt):
            t0, nt, N, c0, isctx = geom(bi)
            nk = 2 if isctx else 34
            nkc = nk * 128
            for h in range(4):
                KT = KTs.next()
                self.dma("sp", KT[:, 0:TC], self.kv_ctx_k.t[:, h, :], [self.kv_ctx_k.r], [KT.r])
                if not isctx:
                    for r in range(2):
                        self.dma("sp", KT[:, TC + r * TL:TC + (r + 1) * TL],
                                 self.xk_g.t[r * 512 + h * 128:r * 512 + (h + 1) * 128, :], [self.xk_g.r], [KT.r])
                po, pro = self.pbank(pin=True)
                npair = nk // 2

                def qk_pair(j):
                    ps2, pr2 = self.pbank(2, pin=True)
                    for a in range(2):
                        kt = 2 * j + a
                        self.mm(ps2[:, a * 512:a * 512 + N], KT[:, kt * 128:(kt + 1) * 128], self.qT[:, h, c0:c0 + N],
                                True, True, [KT.r, self.qT.r], pr2)
                    return ps2, pr2

                cur = qk_pair(0)
                for j in range(npair):
                    ps2, pr2 = cur
                    pt_ = PTs.next()
                    self.act(pt_[:].rearrange("p (a n) -> p a n", a=2)[:, :, 0:N],
                             ps2[:, :].rearrange("p (a n) -> p a n", a=2)[:, :, 0:N], AF.Exp, pr2, [pt_.r], scale=MLA_SCALE)
                    self.unpin(pr2)
                    if j + 1 < npair:
                        cur = qk_pair(j + 1)
                    for a in range(2):
                        kt = 2 * j + a
                        self.mm(po[0:65, 0:N], VA[:, kt, h * 65:(h + 1) * 65], pt_[:, a * 512:a * 512 + N], kt == 0,
                                kt == nk - 1, [VA.r, pt_.r], pro)
                    yield
                self.unpin(pro)
                self.act(rec[64:65, 0:N], po[64:65, 0:N], AF.Ln, pro, [rec.r])
                self.act(rec[64:65, 0:N], rec[64:65, 0:N], AF.Exp, [rec.r], [rec.r], scale=-1.0)
                self.cp("act", Osb[:, 0:N], po[0:64, 0:N], pro, [Osb.r])
                yield
                pbc, prb = self.pbank()
                self.mm(pbc[0:64, 0:N], self.ones_f[64:65, 0:64], rec[64:65, 0:N], True, True, [self.ones_f.r, rec.r], prb)
                self.tt("dve", yatt[:, h, 0:N], pbc[0:64, 0:N], Osb[:, 0:N], ALU.mult, prb + [Osb.r], [yatt.r])
                yield

        def wout_gen(bi, ymx, yatt):
            t0, nt, N, c0, isctx = geom(bi)
            if isctx:
                self.dma("sp", G1[:], self.bv.t[8], [self.bv.r], [G1.r])
            for tl in range(nt):
                t = t0 + tl
                cs = slice(t * 128, (t + 1) * 128)
                ls = slice(tl * 128, (tl + 1) * 128)
                pw, prw = self.pbank(2)
                for hf in range(2):
                    ns = slice(hf * 512, (hf + 1) * 512)
                    for h in range(4):
                        self.mm(pw[:, ns], yatt[:, h, ls], wo_att[:, h, ns], h == 0, False, [yatt.r, wo_att.r], prw)
                    for c in range(2):
                        self.mm(pw[:, ns], self.ysc[:, c, cs], wo_r[:, c, ns], False, False, [self.ysc.r, wo_r.r], prw)
                    for c in range(4):
                        self.mm(pw[:, ns], ymx[:, c, ls], wo_r[:, 2 + c, ns], False, c == 3, [ymx.r, wo_r.r], prw)
                self.dma("sp", xt[:], xres.t[cs, :], [xres.r], [xt.r])
                self.post_tile(pw, prw, xt, G1, tmp, smalls.next(), x1)
                self.dma("sp", self.x1s.t[cs, :], x1[:], [x1.r], [self.x1s.r])
                yield

        prev = None
        for it, bi in enumerate(order):
            ymx_c, yatt_c = ymxs[it % 2], yatts[it % 2]
            nk_ = 2 if bi == 0 else 34
            gens = [(ssd_gen(bi, ymx_c), 9 * BLOCKS[bi][1]), (att_gen(bi, yatt_c), 4 * (nk_ // 2 + 2))]
            if prev is not None:
                gens.append((wout_gen(*prev), BLOCKS[prev[0]][1]))
            interleave(*gens)
            if l == 0 and bi == 4:
                self.dump("yatt0", yatt_c[:], yatt_c.r, [64, 4, 512], BF16)
                self.dump("ymx0", ymx_c[:], ymx_c.r, [128, 4, 512], BF16)
            if self.stop(l, "b1") or self.stop(l, "b2"):
                return
            prev = (bi, ymx_c, yatt_c)
        run_gen(wout_gen(*prev))
        self.dump(f"x1_{l}", self.x1s.t, self.x1s.r, [T, D])

    def l_init(self, l, st, smp, cw, cb, HL, HLb):
        c3 = self.sb("c3", [128, 6, 3])
        pv = smp[:, 512:524].rearrange("p (c k) -> p c k", c=6)
        self.cp("dve", c3[:, :, 0], self.fix[:, 10:16], [self.fix.r], [c3.r])
        self.cp("dve", c3[:, :, 1], pv[:, :, 1], [smp.r], [c3.r])
        self.cp("dve", c3[:, :, 2], pv[:, :, 0], [smp.r], [c3.r])
        t6 = self.sb("t6", [128, 6, 3])
        cv = self.sb("cv6", [128, 6])
        self.tt("dve", t6[:], c3[:], cw[:], ALU.mult, [c3.r, cw.r], [t6.r])
        self.tt("dve", cv[:], t6[:, :, 0], t6[:, :, 1], ALU.add, [t6.r], [cv.r])
        self.tt("dve", cv[:], cv[:], t6[:, :, 2], ALU.add, [t6.r, cv.r], [cv.r])
        self.tt("dve", cv[:], cv[:], cb[:], ALU.add, [cv.r, cb.r], [cv.r])
        sv = self.sb("sv6", [128, 6])
        sg6 = self.sb("sg6", [128, 6])
        self.silu(sv[:], [sv.r], cv[:], [cv.r], (sg6[:], sg6.r))
        pb, pr = self.pbank()
        for c in range(4):
            self.tr(pb[0:1, c * 128:(c + 1) * 128], sv[:, c:c + 1], self.ident_f[:], [sv.r, self.cst.r], pr)
        pb2, pr2 = self.pbank()
        self.tr(pb2[0:1, 0:128], sv[:, 4:5], self.ident_f[:], [sv.r, self.cst.r], pr2)
        row = self.sb("row", [1, 640])
        self.cp("act", row[:, 0:512], pb[0:1, 0:512], pr, [row.r])
        self.cp("act", row[:, 512:640], pb2[0:1, 0:128], pr2, [row.r])
        dtr = self.sb("dtr", [1, 8])
        self.cp("dve", dtr[:], smp[0:1, 528:536], [smp.r], [dtr.r])
        xr = self.sb("xr", [1, 512])
        self.tt("dve", xr[:].rearrange("p (h e) -> p h e", h=8), row[:, 0:512].rearrange("p (h e) -> p h e", h=8),
                bc(dtr[:], [1, 8, 64]), ALU.mult, [row.r, dtr.r], [xr.r])
        pb3, pr3 = self.pbank()
        self.mm(pb3[:, :], row[:, 512:640], xr[:], True, True, [row.r, xr.r], pr3)
        self.tt("dve", HL[0:64, :], smp[0:64, 0:256], pb3[0:64, 0:256], ALU.add, [smp.r] + pr3, [HL.r])
        self.tt("dve", HL[64:128, :], smp[64:128, 0:256], pb3[64:128, 256:512], ALU.add, [smp.r] + pr3, [HL.r])
        self.hl_bd(HL, HLb)

    def hl_bd(self, HL, HLb):
        for g in range(2):
            self.cp("act", HLb[g * 64:(g + 1) * 64, g * 256:(g + 1) * 256], HL[g * 64:(g + 1) * 64, :], [HL.r], [HLb.r])

    def fix_last(self, l, smp, sccw, cw, cb):
        tcol = T - 1
        f2 = self.sb("f2", [128, 2])
        self.tt("dve", f2[:], smp[:, 524:526], sccw[:, :, 2], ALU.mult, [smp.r, sccw.r], [f2.r])
        self.tt("dve", f2[:], f2[:], self.fix[:, 6:8], ALU.add, [f2.r, self.fix.r], [f2.r])
        self.tt("dve", self.ysc[:, :, tcol], f2[:], self.fix[:, 8:10], ALU.mult, [f2.r, self.fix.r], [self.ysc.r])
        f6 = self.sb("f6", [128, 6])
        pv = smp[:, 512:524].rearrange("p (c k) -> p c k", c=6)
        self.tt("dve", f6[:], pv[:, :, 1], cw[:, :, 2], ALU.mult, [smp.r, cw.r], [f6.r])
        self.tt("dve", f6[:], f6[:], self.fix[:, 0:6], ALU.add, [f6.r, self.fix.r], [f6.r])
        self.tt("dve", f6[:], f6[:], cb[:], ALU.add, [f6.r, cb.r], [f6.r])
        s6 = self.sb("s6", [128, 6])
        sg6b = self.sb("sg6b", [128, 6])
        self.silu(s6[:], [s6.r], f6[:], [f6.r], (sg6b[:], sg6b.r))
        self.cp("dve", self.xsT_last[:, :, 127], s6[:, 0:4], [s6.r], [self.xsT_last.r])
        self.cp("dve", self.BT[:, tcol:tcol + 1], s6[:, 4:5], [s6.r], [self.BT.r])
        self.cp("dve", self.CT[:, tcol:tcol + 1], s6[:, 5:6], [s6.r], [self.CT.r])
        bl = self.sb("bl", [128, 128])
        self.cp("dve", bl[:], self.BT[:, T - 128:T], [self.BT.r], [bl.r])
        self.cp("dve", bl[:, 127:128], s6[:, 4:5], [s6.r], [bl.r])
        pb, pr = self.pbank()
        for c in range(4):
            self.tr(pb[:, c * 128:(c + 1) * 128], self.xsT_last[:, c, :], self.ident_f[:], [self.xsT_last.r, self.cst.r], pr)
        xl = self.sb("xl", [128, 512])
        self.cp("act", xl[:], pb[:, :], pr, [xl.r])
        self.dma("pool", self.sc_xs.t[T - 128:T, :], xl[:], [xl.r], [self.sc_xs.r])
        pb2, pr2 = self.pbank()
        self.tr(pb2[:, 0:128], bl[:], self.ident_f[:], [bl.r, self.cst.r], pr2)
        bl2 = self.sb("bl2", [128, 128], BF16)
        self.cp("act", bl2[:], pb2[:, 0:128], pr2, [bl2.r])
        self.dma("pool", self.sc_bt.t[T - 128:T, :], bl2[:], [bl2.r], [self.sc_bt.r])

    def phase_c(self, l, xnext, last):
        self.close_ab()
        self.new_scope()
        w1g = [self.sb(f"w1g{g}", [128, 8, 512], BF16) for g in range(8)]
        w2g = [self.sb(f"w2g{q}", [128, 8, D], BF16) for q in range(4)]
        w1v = self.w_ff1.t[l].rearrange("(k p) n -> p k n", p=128)
        w2v = self.w_ff2.t[l].rearrange("(c p) n -> p c n", p=128)
        for kind, g in (("1", 0), ("2", 0), ("1", 1), ("1", 2), ("2", 1), ("1", 3), ("1", 4), ("2", 2), ("1", 5), ("1", 6),
                        ("2", 3), ("1", 7)):
            if kind == "1":
                self.dma("pool", w1g[g][:], w1v[:, :, g * 512:(g + 1) * 512], [self.w_ff1.r], [w1g[g].r])
            else:
                self.dma("pool", w2g[g][:], w2v[:, g * 8:(g + 1) * 8, :], [self.w_ff2.r], [w2g[g].r])
        A2 = self.load_bv(3, "A2")
        B2 = self.load_bv(4, "B2")
        G2 = self.load_bv(5, "G2")
        A2c = B2c = G2c = None
        if not last:
            A2c = self.load_bv(9, "A2c")
            B2c = self.load_bv(10, "B2c")
            G2c = self.load_bv(11, "G2c")
        NBC = 256
        xts = self.rot("xt", 2, [128, D])
        xps = self.rot("xp", 1, [128, D])
        tmp = self.sb("tmp", [128, D])
        tmp2 = self.sb("tmp2", [128, D])
        hb = self.sb("hb", [128, D], BF16)
        hTs = [self.sb("hTa", [128, 8, NBC], BF16), self.sb("hTb", [128, 8, NBC], BF16)]
        a_r = self.rot("afo", 3, [128, NBC], BF16)
        rl_r = self.rot("rl", 2, [128, NBC])
        smalls = self.small3(2)
        smalls2 = self.small3(2)
        x2s = self.rot("x2", 2, [128, D])
        ysbs = self.rot("ysb", 1, [128, D])
        blocks = [(2 + 2 * i, 2) for i in range(8)] + ([] if last else [(0, 2)])

        def front(i):
            t0, nt = blocks[i]
            isctx = (t0 == 0)
            hT = hTs[i % 2]
            for tl in range(nt):
                t = t0 + tl
                xt = xts.next()
                self.dma("sp", xt[:], self.x1s.t[t * 128:(t + 1) * 128, :], [self.x1s.r], [xt.r])
                self.norm_tile(xt, A2c if isctx else A2, B2c if isctx else B2, hb, tmp, smalls.next())
                self.transpose_tile(hb, hT, tl)
                yield

        def ffn(i):
            t0, nt = blocks[i]
            isctx = (t0 == 0)
            N = nt * 128
            hT = hTs[i % 2]
            pws = [self.pbank(2, pin=True) for _ in range(nt)]

            def ffn1(fo):
                pb, pr = self.pbank(pin=True)
                for k in range(8):
                    wg = w1g[fo // 4]
                    self.mm(pb[:, 0:N], wg[:, k, (fo % 4) * 128:(fo % 4 + 1) * 128], hT[:, k, 0:N], k == 0, k == 7,
                            [wg.r, hT.r], pr)
                return pb, pr

            cur = ffn1(0)
            for fo in range(32):
                pb, pr = cur
                rl = rl_r.next()
                a = a_r.next()
                self.act(rl[:, 0:N], pb[:, 0:N], AF.Relu, pr, [rl.r])
                self.unpin(pr)
                if fo + 1 < 32:
                    cur = ffn1(fo + 1)
                self.tt("dve", a[:, 0:N], rl[:, 0:N], rl[:, 0:N], ALU.mult, [rl.r], [a.r])
                for tl in range(nt):
                    pw, prw = pws[tl]
                    for hf in range(2):
                        ns = slice(hf * 512, (hf + 1) * 512)
                        self.mm(pw[:, ns], a[:, tl * 128:(tl + 1) * 128], w2g[fo // 8][:, fo % 8, ns], fo == 0, fo == 31,
                                [a.r, w2g[fo // 8].r], prw)
                yield
            for tl in range(nt):
                t = t0 + tl
                cs = slice(t * 128, (t + 1) * 128)
                pw, prw = pws[tl]
                ysb = ysbs.next()
                self.cp("act", ysb[:], pw[:, :], prw, [ysb.r])
                self.unpin(prw)
                xp = xps.next()
                self.dma("sp", xp[:], self.x1s.t[cs, :], [self.x1s.r], [xp.r])
                x2 = x2s.next()
                self.post_tile(ysb, [ysb.r], xp, G2c if isctx else G2, tmp2, smalls2.next(), x2)
                if last:
                    self.dma("pool", self.y_out.t[(t - 2) * 128:(t - 1) * 128, :], x2[:], [x2.r], [self.y_out.r])
                else:
                    self.dma("pool", xnext.t[cs, :], x2[:], [x2.r], [xnext.r])
                yield

        run_gen(front(0))
        for i in range(len(blocks)):
            f = front(i + 1) if i + 1 < len(blocks) else None
            interleave((f, 2, 0.7), (ffn(i), 34))
        if not last:
            self.dump(f"xres1_{l}", xnext.t, xnext.r, [T, D])


def run_gen(g):
    if g is not None:
        for _ in g:
            pass


def interleave(*pairs):
    live = [[p[0], (p[2] if len(p) > 2 else 0.0), max(1, p[1])] for p in pairs if p[0] is not None]
    while live:
        live.sort(key=lambda x: x[1] / x[2])
        it = live[0]
        try:
            next(it[0])
            it[1] += 1
        except StopIteration:
            live.remove(it)


class Rot:
    def __init__(self, items):
        self.items = items
        self.i = -1

    def next(self):
        self.i = (self.i + 1) % len(self.items)
        return self.items[self.i]


_PERM = np.r_[8:16, 0:8, 24:32, 16:24]


def _rope_tables(pos):
    half = 16
    inv = (10000.0 ** (-(np.arange(0, half, 2, dtype=np.float32)) / half)).astype(np.float32)
    row = (pos // 64).astype(np.float32)
    col = (pos % 64).astype(np.float32)
    ar = row[:, None] * inv
    ac = col[:, None] * inv
    ang = np.concatenate([ar, ar, ac, ac], -1).astype(np.float32)
    sign = np.concatenate([-np.ones(8), np.ones(8), -np.ones(8), np.ones(8)]).astype(np.float32)
    return np.cos(ang).astype(np.float32), (np.sin(ang) * sign).astype(np.float32)


def host_layout(inp):
    f = lambda a: np.ascontiguousarray(np.asarray(a, dtype=np.float32))
    x, c, ctx, c_ctx = f(inp["x"]), f(inp["c"]), f(inp["ctx"]), f(inp["c_ctx"])
    L = DEPTH
    w_in = f(inp["w_in"])
    s0 = 416 + 768
    shared = {}
    per_rev = {}
    for rev in (0, 1):
        wa = []
        for l in range(L):
            w = w_in[l]
            dtc = w[:, s0 + 512 + 768:s0 + 512 + 768 + 16]
            if rev:
                dtc = np.concatenate([dtc[:, 8:16], dtc[:, 0:8]], 1)
            wa.append(np.concatenate([w[:, 0:256], w[:, 256:384], w[:, 384:416], w[:, 384 + _PERM],
                                      w[:, 416:416 + 768], w[:, s0 + 512:s0 + 512 + 768], w[:, s0:s0 + 512], dtc], 1))
        d = {"w_in_aug": f(np.stack(wa))}
        al = f(inp["ssd_a_log"]); db = f(inp["ssd_dt_bias"])
        order = [1, 0] if rev else [0, 1]
        d["a_log"] = f(np.concatenate([al[:, order[0]], al[:, order[1]]], -1))
        d["dt_bias"] = f(np.concatenate([db[:, order[0]], db[:, order[1]]], -1))
        scw = f(inp["sc_conv_w"]); sw = f(inp["ssd_conv_w"])
        if rev:
            scw = scw[:, ::-1]; sw = sw[:, ::-1]
        d["sc_cw"] = f(np.transpose(scw, (0, 2, 1)))
        d["ssd_cw"] = f(np.transpose(sw, (0, 2, 1)))
        per_rev[rev] = d
    wuq = f(inp["w_uq"])
    wq = np.zeros((L, 256, 4, 160), np.float32)
    for h in range(4):
        wq[:, :, h, 0:32] = wuq[:, :, h * 96 + 64:h * 96 + 96]
        wq[:, :, h, 64:128] = wuq[:, :, h * 96:h * 96 + 64]
        wq[:, :, h, 128:160] = wuq[:, :, h * 96 + 64 + _PERM]
    shared["w_uq_aug"] = f(wq.reshape(L, 256, 640))
    wukv = f(inp["w_ukv"])
    wk = np.zeros((L, 128, 4, 128), np.float32)
    wv = np.zeros((L, 128, 4, 64), np.float32)
    for h in range(4):
        wk[:, :, h, 64:128] = wukv[:, :, h * 128:h * 128 + 64]
        wv[:, :, h, :] = wukv[:, :, h * 128 + 64:h * 128 + 128]
    shared["w_ukv_k"] = f(wk.reshape(L, 128, 512))
    shared["w_ukv_v"] = f(wv.reshape(L, 128, 256))
    shared["q_norm"] = f(inp["mla_q_norm"])
    shared["kv_norm"] = f(inp["mla_kv_norm"])
    shared["w_mod"] = f(inp["w_mod"])
    shared["b_mod"] = f(inp["b_mod"])
    shared["gvecs"] = f(np.stack([inp["g_pre_mix"], inp["g_post_mix"], inp["g_pre_ffn"], inp["g_post_ffn"]], 1))
    shared["ssd_cb"] = f(inp["ssd_conv_b"])
    shared["ssd_d"] = f(inp["ssd_d"])
    shared["ssd_norm"] = f(inp["ssd_norm"])
    shared["w_out"] = f(inp["w_out"])
    shared["w_ff1"] = f(inp["w_ff1"])
    shared["w_ff2"] = f(inp["w_ff2"])
    j = np.arange(128)
    triE = (j[:, None] <= j[None, :]).astype(np.float32)
    triL = (j[:, None] >= j[None, :]).astype(np.float32)
    in_maps = []
    for core in range(NCORES):
        b, hf = core // 2, core % 2
        xl = x[b, hf * TL:(hf + 1) * TL]
        cl = ctx[b]
        pos = np.arange(hf * TL, (hf + 1) * TL)
        if hf:
            xl, cl, pos = xl[::-1], cl[::-1], pos[::-1]
        cos, sin = _rope_tables(pos)
        rope = np.zeros((2, 32, T), np.float32)
        rope[0, :, :TC] = 1.0
        rope[0, :, TC:] = cos.T
        rope[1, :, TC:] = sin.T
        cst = np.zeros((128, 3 * 128 + 4), np.float32)
        cst[:, 0:128] = np.eye(128)
        cst[:, 128:256] = triE
        cst[:, 256:384] = triL
        cst[:, 384] = 1.0
        cst[127, 384] = 0.0
        cst[:, 385] = 1.0 - hf
        cst[:, 386] = float(hf)
        m = {"x_in": f(np.concatenate([cl, xl], 0)), "cvecT": f(np.stack([c[b], c_ctx], 1)), "ropeT": rope, "consts": cst}
        m.update(shared)
        m.update(per_rev[hf])
        in_maps.append(m)
    return in_maps


_NC_CACHE = {}


def kernel(**inputs):
    in_maps = host_layout(inputs)
    if "nc" not in _NC_CACHE:
        _NC_CACHE["nc"] = Builder().build()
    res = run_bass_kernel_spmd(_NC_CACHE["nc"], in_maps, core_ids=list(range(NCORES)))
    out = np.zeros((4, 2 * TL, D), np.float32)
    for core in range(NCORES):
        b, hf = core // 2, core % 2
        y = np.asarray(res.results[core]["y_out"], dtype=np.float32)
        out[b, hf * TL:(hf + 1) * TL] = y[::-1] if hf else y
    return out
```

```python
import numpy as np
import ml_dtypes
from contextlib import ExitStack
import concourse.bass as bass
import concourse.mybir as mybir
from concourse.bass_utils import run_bass_kernel_spmd

F32 = mybir.dt.float32
BF16 = mybir.dt.bfloat16
AF = mybir.ActivationFunctionType
ALU = mybir.AluOpType

NCORES = 8
D = 1024
DEPTH = 2
TC = 256
TL = 2048
T = TC + TL
NT = T // 128
BLOCKS = [(0, 2), (2, 4), (6, 4), (10, 4), (14, 4)]
NU = 2512
EPS = 1e-6
MLA_SCALE = 96.0 ** -0.5
NEG = -30000.0
FM = [(0, 128), (128, 128), (256, 128), (384, 32), (416, 32)] + [(448 + 128 * i, 128) for i in range(12)]
Z_OFF = 1984
DT_OFF = 2496
SM_W = 544

ENGS = ("pe", "act", "dve", "pool", "sp")


class Res:
    __slots__ = ("name", "w", "r", "sem", "ndma")

    def __init__(self, name):
        self.name = name
        self.w = None
        self.r = []
        self.sem = None
        self.ndma = 0


class Prog:
    def __init__(self, nc, strict=True):
        self.nc = nc
        self.ops = {e: [] for e in ENGS}
        self.strict = strict
        self.nres = 0
        self.dma_res = []
        self.slot_count = []
        self.slot_cls = []
        self.free_slots = {}

    def res(self, name=None):
        self.nres += 1
        return Res(name or f"r{self.nres}")

    def _deps(self, reads, writes, eng=None):
        ev = []
        for r in reads:
            if r.w is not None:
                ev.append(r.w)
        for w in writes:
            for e in ([w.w] if w.w is not None else []) + w.r:
                if eng is not None and e[0] == "E" and e[1] == eng:
                    continue
                ev.append(e)
        return ev

    def op(self, eng, fn, reads=(), writes=()):
        lst = self.ops[eng]
        idx = len(lst)
        waits = self._collect(eng, self._deps(reads, writes, eng))
        lst.append({"fn": fn, "waits": waits, "signal": False, "dma": None})
        me = ("E", eng, idx)
        for r in reads:
            r.r.append(me)
        for w in writes:
            w.w = me
            w.r = []

    def dma(self, eng, fn, src, dst, inc=16):
        srcs = src if isinstance(src, (list, tuple)) else [src]
        dsts = dst if isinstance(dst, (list, tuple)) else [dst]
        main = dsts[0]
        cls = eng if inc == 16 else "cc"
        if main.sem is None:
            fl = self.free_slots.setdefault(cls, [])
            if fl and cls != "cc":
                main.sem = fl.pop()
            else:
                main.sem = len(self.slot_count)
                self.slot_count.append(0)
                self.slot_cls.append(cls)
            main.ndma = self.slot_count[main.sem]
            self.dma_res.append(main)
        assert self.slot_cls[main.sem] == cls, (main.name, self.slot_cls[main.sem], cls)
        waits = self._collect(eng, self._deps(srcs, dsts))
        main.ndma += inc
        self.slot_count[main.sem] = main.ndma
        self.ops[eng].append({"fn": fn, "waits": waits, "signal": False, "dma": main.sem, "inc": inc})
        me = ("D", main.sem, main.ndma)
        for r in srcs:
            r.r.append(me)
        for w in dsts:
            w.w = me
            w.r = []

    def release(self, resources):
        for r in resources:
            if r.sem is not None:
                self.free_slots.setdefault(self.slot_cls[r.sem], []).append(r.sem)
                self.dma_res.remove(r)
                r.sem = None

    def _collect(self, eng, events, all_same=False):
        out = {}
        for ev in events:
            if ev[0] == "E":
                _, e2, j = ev
                if e2 == eng and (eng == "pe" or not self.strict) and not all_same:
                    continue
                key = ("E", e2)
            else:
                key = ("D", ev[1])
            if key not in out or out[key][2] < ev[2]:
                out[key] = ev
        evs = list(out.values())
        for ev in evs:
            if ev[0] == "E":
                self.ops[ev[1]][ev[2]]["signal"] = True
        return evs

    def final_wait(self, eng, resources):
        evs = []
        for r in resources:
            if r.w is not None:
                evs.append(r.w)
            evs.extend(r.r)
        self.ops[eng].append({"fn": None, "waits": self._collect(eng, evs), "signal": False, "dma": None})

    def barrier(self):
        evs = []
        for e in ENGS:
            for idx in range(len(self.ops[e]) - 1, -1, -1):
                rec = self.ops[e][idx]
                if rec["dma"] is None and rec["fn"] is not None:
                    evs.append(("E", e, idx))
                    break
        for r in self.dma_res:
            evs.append(("D", r.sem, r.ndma))
        for e in ENGS:
            self.ops[e].append({"fn": None, "waits": self._collect(e, evs, all_same=(e != "pe")),
                                "signal": False, "dma": None})

    def emit(self):
        nc = self.nc
        esem = {e: nc.alloc_semaphore(f"s_{e}") for e in ENGS}
        dsem = [nc.alloc_semaphore(f"d{i}") for i in range(len(self.slot_count))]
        cnt = {}
        for e in ENGS:
            c = 0
            arr = []
            for rec in self.ops[e]:
                if rec["signal"] and rec["dma"] is None:
                    c += 1
                arr.append(c)
            cnt[e] = arr
        ops = self.ops

        def run(e, eh):
            waited = {}
            for rec in ops[e]:
                for ev in rec["waits"]:
                    if ev[0] == "E":
                        sem = esem[ev[1]]
                        val = cnt[ev[1]][ev[2]]
                    else:
                        sem = dsem[ev[1]]
                        val = ev[2]
                    if waited.get(sem.num, 0) >= val:
                        continue
                    waited[sem.num] = val
                    eh.wait_ge(sem, val)
                if rec["fn"] is None:
                    continue
                ins = rec["fn"](eh)
                if rec["dma"] is not None:
                    ins.then_inc(dsem[rec["dma"]], rec["inc"])
                elif rec["signal"]:
                    ins.then_inc(esem[e], 1)

        with nc.Block() as block:
            @block.tensor
            def _(eh):
                run("pe", eh)

            @block.scalar
            def _(eh):
                run("act", eh)

            @block.vector
            def _(eh):
                run("dve", eh)

            @block.gpsimd
            def _(eh):
                run("pool", eh)

            @block.sync
            def _(eh):
                run("sp", eh)


class Buf:
    def __init__(self, t, r):
        self.t = t
        self.r = r

    def __getitem__(self, k):
        return self.t[k]


def bc(ap, shape):
    return ap.unsqueeze(len(ap.shape)).to_broadcast(list(shape))


def bcm(ap, shape):
    return ap.unsqueeze(1).to_broadcast(list(shape))


class Builder:
    def __init__(self, dbg=None, nlayers=DEPTH, stop_after=None, fake_cc=False, ncores=NCORES):
        self.fake_cc = fake_cc
        self.ncores = ncores
        self.nc = bass.Bass("TRN2", target_bir_lowering=False)
        self.P = Prog(self.nc)
        self.dbg = dbg or []
        self.dbg_out = {}
        self.nlayers = nlayers
        self.stop_after = stop_after
        self.scope = None
        self.scope_ab = None
        self.scope_res = []
        self.ab_res = []
        self.pcur = 0
        self.pinned = set()
        self.uid = 0

    def din(self, name, shape, dt=F32):
        return Buf(self.nc.dram_tensor(name, list(shape), dt, kind="ExternalInput").ap(), self.P.res(name))

    def dout(self, name, shape, dt=F32):
        return Buf(self.nc.dram_tensor(name, list(shape), dt, kind="ExternalOutput").ap(), self.P.res(name))

    def dscr(self, name, shape, dt=F32):
        return Buf(self.nc.dram_tensor(name, list(shape), dt, kind="Internal").ap(), self.P.res(name))

    def sb(self, name, shape, dt=F32, persist=False, ab=False):
        self.uid += 1
        nm = f"{name}_{self.uid}"
        r = self.P.res(name)
        if persist:
            t = self.nc.alloc_sbuf_tensor(nm, list(shape), dt)
        elif ab:
            t = self.scope_ab.enter_context(self.nc.sbuf_tensor(nm, list(shape), dt))
            self.ab_res.append(r)
        else:
            t = self.scope.enter_context(self.nc.sbuf_tensor(nm, list(shape), dt))
            self.scope_res.append(r)
        return Buf(t, r)

    def rot(self, name, n, shape, dt=F32):
        return Rot([self.sb(f"{name}{i}", shape, dt) for i in range(n)])

    def pbank(self, n=1, pin=False):
        while True:
            if self.pcur % n:
                self.pcur += n - self.pcur % n
            if self.pcur + n > 8:
                self.pcur = 0
            b0 = self.pcur
            self.pcur = (self.pcur + n) % 8
            if not any((b0 + i) in self.pinned for i in range(n)):
                break
        if pin:
            for i in range(n):
                self.pinned.add(b0 + i)
        self.last_b0 = b0
        return self.ps_all[:, b0 * 512:(b0 + n) * 512], [self.ps_res[b0 + i] for i in range(n)]

    def unpin(self, res_list):
        for r in res_list:
            self.pinned.discard(self.ps_res.index(r))

    def mm(self, out, lhsT, rhs, start, stop, R, W):
        self.P.op("pe", lambda e: e.matmul(out, lhsT, rhs, start=start, stop=stop), R, W)

    def tr(self, out, in_, ident, R, W):
        self.P.op("pe", lambda e: e.transpose(out, in_, ident), R, W)

    def act(self, out, in_, func, R, W, bias=None, scale=None, accum=None, eng="act"):
        kw = {}
        if bias is not None:
            kw["bias"] = bias
        if scale is not None:
            kw["scale"] = scale
        if accum is not None:
            kw["accum_out"] = accum
        self.P.op("act", lambda e: e.activation(out, in_, func, **kw), R, W)

    def tt(self, eng, out, a, b, op, R, W):
        self.P.op(eng, lambda e: e.tensor_tensor(out, a, b, op), R, W)

    def ts(self, eng, out, a, s1, s2, op0, op1, R, W):
        if op1 is None:
            self.P.op(eng, lambda e: e.tensor_scalar(out, a, s1, s2, op0), R, W)
        else:
            self.P.op(eng, lambda e: e.tensor_scalar(out, a, s1, s2, op0, op1), R, W)

    def stt(self, eng, out, a, s, b, op0, op1, R, W):
        self.P.op(eng, lambda e: e.scalar_tensor_tensor(out, a, s, b, op0, op1), R, W)

    def cp(self, eng, out, in_, R, W):
        if eng == "act":
            self.P.op("act", lambda e: e.copy(out, in_), R, W)
        else:
            self.P.op(eng, lambda e: e.tensor_copy(out, in_), R, W)

    def ms(self, eng, ap, val, W):
        self.P.op(eng, lambda e: e.memset(ap, val), [], W)

    def recip(self, out, in_, R, W):
        self.P.op("dve", lambda e: e.reciprocal(out, in_), R, W)

    def dma(self, eng, out, in_, R, W, slow_ok=False):
        if slow_ok:
            self.P.dma(eng, lambda e: e.dma_start(out=out, in_=in_, allow_slow_non_contiguous=True), R, W)
        else:
            self.P.dma(eng, lambda e: e.dma_start(out=out, in_=in_), R, W)

    def rstd(self, out, ms_ap, R, W, tmp):
        self.act(tmp[:], ms_ap, AF.Ln, R, [tmp.r], bias=self.eps_t[:, 0:1])
        self.act(out, tmp[:], AF.Exp, [tmp.r], W, scale=-0.5)

    def silu(self, out, out_res, in_, in_res, sg, bias=None, negbias=None, mul_eng="dve"):
        kw = {} if negbias is None else {"bias": negbias[0]}
        rr = list(in_res) + ([] if negbias is None else [negbias[1]])
        self.act(sg[0], in_, AF.Exp, rr, [sg[1]], scale=-1.0, **kw)
        self.act(sg[0], sg[0], AF.Ln, [sg[1]], [sg[1]], bias=1.0)
        self.act(sg[0], sg[0], AF.Exp, [sg[1]], [sg[1]], scale=-1.0)
        if bias is not None:
            self.stt(mul_eng, out, in_, bias[0], sg[0], ALU.add, ALU.mult, list(in_res) + [bias[1], sg[1]], out_res)
        else:
            self.tt(mul_eng, out, in_, sg[0], ALU.mult, list(in_res) + [sg[1]], out_res)

    def dump(self, key, src_ap, src_res, shape, dt=F32):
        if key not in self.dbg or key in self.dbg_out:
            return
        o = self.dout("dbg_" + key, shape, dt)
        self.dma("pool", o.t, src_ap, [src_res] if not isinstance(src_res, list) else src_res, [o.r])
        self.dbg_out[key] = o

    def build(self):
        nc, P = self.nc, self.P
        L = DEPTH
        self.x_in = self.din("x_in", [T, D])
        self.cvecT = self.din("cvecT", [D, 2])
        self.w_mod = self.din("w_mod", [L, D, 6 * D])
        self.b_mod = self.din("b_mod", [L, 6 * D])
        self.gvecs = self.din("gvecs", [L, 4, D])
        self.w_in = self.din("w_in_aug", [L, D, NU])
        self.q_norm = self.din("q_norm", [L, 256])
        self.w_uq = self.din("w_uq_aug", [L, 256, 640])
        self.kv_norm = self.din("kv_norm", [L, 128])
        self.w_ukv_k = self.din("w_ukv_k", [L, 128, 512])
        self.w_ukv_v = self.din("w_ukv_v", [L, 128, 256])
        self.sc_cw = self.din("sc_cw", [L, 256, 3])
        self.ssd_cw = self.din("ssd_cw", [L, 768, 3])
        self.ssd_cb = self.din("ssd_cb", [L, 768])
        self.a_log = self.din("a_log", [L, 16])
        self.dt_bias = self.din("dt_bias", [L, 16])
        self.ssd_d = self.din("ssd_d", [L, 8])
        self.ssd_norm = self.din("ssd_norm", [L, 512])
        self.w_out = self.din("w_out", [L, D, D])
        self.w_ff1 = self.din("w_ff1", [L, D, 4 * D])
        self.w_ff2 = self.din("w_ff2", [L, 4 * D, D])
        self.ropeT = self.din("ropeT", [2, 32, T])
        self.consts = self.din("consts", [128, 3 * 128 + 4])
        self.y_out = self.dout("y_out", [TL, D])
        self.xres1 = self.dscr("xres1", [T, D])
        self.x1s = self.dscr("x1s", [T, D])
        self.sc_z = self.dscr("sc_z", [T, 512])
        self.sc_xs = self.dscr("sc_xs", [T, 512])
        self.sc_bt = self.dscr("sc_bt", [T, 128], BF16)
        self.bv = self.dscr("bv", [12, 128, D])
        self.kv_ctx_k = self.dscr("kv_ctx_k", [128, 4, TC], BF16)
        self.kv_ctx_v = self.dscr("kv_ctx_v", [128, 2, 260], BF16)
        self.xk = self.dscr("xk", [512, TL], BF16)
        self.xk_g = self.dscr("xk_g", [1024, TL], BF16)
        self.xv = self.dscr("xv", [2048, 260], BF16)
        self.xv_g = self.dscr("xv_g", [4096, 260], BF16)
        self.xs_ = self.dscr("xsm", [128, SM_W])
        self.xs_g = self.dscr("xsm_g", [256, SM_W])

        self.ps_all = nc.alloc_psum_tensor("ps_all", [128, 8 * 512], F32)
        self.ps_res = [P.res(f"bank{i}") for i in range(8)]

        cst = self.sb("cst", [128, 3 * 128 + 4], persist=True)
        self.dma("sp", cst[:], self.consts.t, [self.consts.r], [cst.r])
        self.cst = cst
        self.ident_f = Buf(cst.t[:, 0:128], cst.r)
        self.triE = Buf(cst.t[:, 128:256], cst.r)
        self.triL = Buf(cst.t[:, 256:384], cst.r)
        self.lastmask = Buf(cst.t[:, 384:385], cst.r)
        self.esel = Buf(cst.t[:, 385:387], cst.r)
        self.ident_b = self.sb("ident_b", [128, 128], BF16, persist=True)
        self.cp("dve", self.ident_b[:], self.ident_f[:], [cst.r], [self.ident_b.r])
        self.ones_f = self.sb("ones_f", [128, 128], persist=True)
        self.ms("pool", self.ones_f[:], 1.0, [self.ones_f.r])
        self.eps_t = self.sb("eps_t", [128, 1], persist=True)
        self.ms("pool", self.eps_t[:], EPS, [self.eps_t.r])
        self.negm = self.sb("negm", [128, 2, 512], BF16, persist=True)
        for d_, tri in enumerate((self.triE, self.triL)):
            self.ts("dve", self.negm[:, d_, :].rearrange("p (a b) -> p a b", a=4), bcm(tri[:], [128, 4, 128]),
                    -1.0, -NEG, ALU.add, ALU.mult, [cst.r], [self.negm.r])
        self.sT = self.sb("sT", [128, 8, 2], persist=True)
        cv = self.sb("cv", [128, 8, 2], persist=True)
        self.dma("sp", cv[:], self.cvecT.t.rearrange("(k p) r -> p k r", p=128), [self.cvecT.r], [cv.r])
        self.act(self.sT[:], cv[:], AF.Silu, [cv.r], [self.sT.r])
        self.dt_all = self.sb("dt_all", [128, NT, 16], persist=True)

        out_res = []
        xres = self.x_in
        for l in range(self.nlayers):
            last = (l == DEPTH - 1)
            xnext = None if last else self.xres1
            self.layer(l, xres, xnext, last)
            xres = xnext
            if self.stop_after is not None and self.stop_after[0] == l:
                break
        P.final_wait("sp", [self.y_out.r] + [o.r for o in self.dbg_out.values()])
        P.final_wait("pool", [self.y_out.r] + [o.r for o in self.dbg_out.values()])
        P.emit()
        return nc

    def end_scope(self):
        if self.scope is not None:
            self.P.barrier()
            self.scope.close()
            self.scope = None
            self.P.release(self.scope_res)
            self.scope_res = []

    def new_scope(self):
        self.end_scope()
        self.scope = ExitStack()

    def open_ab(self):
        self.end_scope()
        self.scope_ab = ExitStack()

    def close_ab(self):
        self.end_scope()
        if self.scope_ab is not None:
            self.scope_ab.close()
            self.scope_ab = None
            self.P.release(self.ab_res)
            self.ab_res = []

    def stop(self, l, tag):
        return self.stop_after is not None and self.stop_after == (l, tag)

    def layer(self, l, xres, xnext, last):
        self.phase_mod(l)
        if self.stop(l, "mod"):
            return
        self.phase_a(l, xres, last)
        if self.stop(l, "a"):
            return
        self.exchange(l)
        self.phase_b(l, xres, last)
        if self.stop_after is not None and self.stop_after[0] == l and self.stop_after[1] in ("b", "b0", "b1", "b2", "t1", "t2", "t3", "t4", "u1", "u2", "u3", "v1", "v2"):
            return
        self.phase_c(l, xnext, last)

    def phase_mod(self, l):
        self.new_scope()
        modsb = self.sb("modsb", [2, 6 * D])
        bm = self.sb("bm", [2, 6 * D])
        self.dma("sp", bm[:], self.b_mod.t[l].partition_broadcast(2), [self.b_mod.r], [bm.r])
        gv = self.sb("gv", [2, 4 * D])
        self.dma("sp", gv[:], self.gvecs.t[l].rearrange("a d -> (a d)").partition_broadcast(2),
                 [self.gvecs.r], [gv.r])
        slabs = self.rot("wm", 2, [128, 8, 512])
        wmv = self.w_mod.t[l].rearrange("(k p) n -> p k n", p=128)
        for n in range(12):
            sl = slabs.next()
            self.dma("sp", sl[:], wmv[:, :, n * 512:(n + 1) * 512], [self.w_mod.r], [sl.r])
            pb, pr = self.pbank()
            for k in range(8):
                self.mm(pb[0:2, :], self.sT[:, k, :], sl[:, k, :], k == 0, k == 7, [self.sT.r, sl.r], pr)
            self.tt("dve", modsb[:, n * 512:(n + 1) * 512], pb[0:2, :], bm[:, n * 512:(n + 1) * 512], ALU.add,
                    pr + [bm.r], [modsb.r])
        cmb = self.sb("cmb", [2, 6, D])
        m = lambda j: modsb[:, j * D:(j + 1) * D]
        g = lambda j: gv[:, j * D:(j + 1) * D]
        R = [modsb.r, gv.r]
        self.stt("dve", cmb[:, 0, :], m(1), 1.0, g(0), ALU.add, ALU.mult, R, [cmb.r])
        self.cp("dve", cmb[:, 1, :], m(0), R, [cmb.r])
        self.tt("dve", cmb[:, 2, :], m(2), g(1), ALU.mult, R, [cmb.r])
        self.stt("dve", cmb[:, 3, :], m(4), 1.0, g(2), ALU.add, ALU.mult, R, [cmb.r])
        self.cp("dve", cmb[:, 4, :], m(3), R, [cmb.r])
        self.tt("dve", cmb[:, 5, :], m(5), g(3), ALU.mult, R, [cmb.r])
        sel = self.sb("sel", [2, 2, 128])
        self.cp("dve", sel[:, 0, :], self.ident_f.t[0:2, 0:1].to_broadcast([2, 128]), [self.cst.r], [sel.r])
        self.cp("dve", sel[:, 1, :], self.ident_f.t[0:2, 1:2].to_broadcast([2, 128]), [self.cst.r], [sel.r])
        stg = self.rot("bvst", 2, [128, D])
        for s in range(2):
            for j in range(6):
                st = stg.next()
                for hlf in range(2):
                    pb, pr = self.pbank()
                    self.mm(pb[:, :], sel[:, s, :], cmb[:, j, hlf * 512:(hlf + 1) * 512], True, True,
                            [sel.r, cmb.r], pr)
                    self.cp("act", st[:, hlf * 512:(hlf + 1) * 512], pb[:, :], pr, [st.r])
                self.dma("sp", self.bv.t[6 * s + j], st[:], [st.r], [self.bv.r])
        self.dump(f"bv{l}", self.bv.t, self.bv.r, [12, 128, D])

    def esel_rows(self):
        return self.ident_f.t[0:2, 0:2]

    def load_bv(self, idx, name):
        b = self.sb(name, [128, D])
        self.dma("sp", b[:], self.bv.t[idx], [self.bv.r], [b.r])
        return b

    def norm_tile(self, xt, A, Bv, hb, tmp, small):
        ms_, ln_, rs_ = small
        self.act(hb[:], xt[:], AF.Square, [xt.r], [hb.r, ms_.r], scale=1.0 / 32.0, accum=ms_[:])
        self.rstd(rs_[:], ms_[:], [ms_.r], [rs_.r], ln_)
        self.stt("dve", tmp[:], xt[:], rs_[:, 0:1], A[:], ALU.mult, ALU.mult, [xt.r, rs_.r, A.r], [tmp.r])
        self.tt("dve", hb[:], tmp[:], Bv[:], ALU.add, [tmp.r, Bv.r], [hb.r])

    def post_tile(self, pw, prw, xt, G, tmp, small, xo):
        ms_, ln_, rs_ = small
        self.act(tmp[:], pw[:, :], AF.Square, prw, [tmp.r, ms_.r], scale=1.0 / 32.0, accum=ms_[:])
        self.rstd(rs_[:], ms_[:], [ms_.r], [rs_.r], ln_)
        self.stt("dve", tmp[:], pw[:, :], rs_[:, 0:1], G[:], ALU.mult, ALU.mult, prw + [rs_.r, G.r], [tmp.r])
        self.tt("dve", xo[:], xt[:], tmp[:], ALU.add, [xt.r, tmp.r], [xo.r])

    def transpose_tile(self, hb, hT, tl):
        pb, pr = self.pbank()
        pbb = pb.bitcast(BF16)
        for k in range(8):
            self.tr(pbb[:, k * 128:(k + 1) * 128], hb[:, k * 128:(k + 1) * 128], self.ident_b[:],
                    [hb.r, self.ident_b.r], pr)
        self.cp("act", hT[:, :, tl * 128:(tl + 1) * 128], pbb[:, :].rearrange("p (k t) -> p k t", k=8), pr, [hT.r])

    def small3(self, n):
        return Rot([(self.sb("ms", [128, 1]), self.sb("ln", [128, 1]), self.sb("rs", [128, 1])) for _ in range(n)])

    def phase_a(self, l, xres, last):
        nc, P = self.nc, self.P
        self.open_ab()
        self.qT = self.sb("qT", [128, 4, T], BF16, ab=True)
        self.ysc = self.sb("ysc", [128, 2, T], BF16, ab=True)
        self.BT = self.sb("BT", [128, T], BF16, ab=True)
        self.CT = self.sb("CT", [128, T], BF16, ab=True)
        self.HEs = self.sb("HEs", [128, NT, 256], BF16, ab=True)
        self.fix = self.sb("fix", [128, 16], ab=True)
        self.xsT_last = self.sb("xsT_last", [128, 4, 128], ab=True)
        self.new_scope()
        NB = 256
        wgrp = [(Z_OFF, NU), (0, 448), (448, 1216), (1216, Z_OFF)]
        wslab = []
        winv = self.w_in.t[l].rearrange("(k p) n -> p k n", p=128)
        for gi, (a_, b_) in enumerate(wgrp):
            wt = self.sb(f"win{gi}", [128, 8, b_ - a_], BF16)
            self.dma("pool", wt[:], winv[:, :, a_:b_], [self.w_in.r], [wt.r])
            wslab.append((a_, b_, wt))

        class _Win:
            def cols(self_, k, off, w):
                for a_, b_, wt in wslab:
                    if a_ <= off and off + w <= b_:
                        return wt[:, k, off - a_:off - a_ + w], wt.r
                raise AssertionError((off, w))
        win = _Win()
        wst = self.sb("wst", [128, 2, 640])
        self.dma("sp", wst[:], self.w_uq.t[l].rearrange("(c p) n -> p c n", p=128), [self.w_uq.r], [wst.r])
        gq = self.sb("gq", [128, 2])
        self.dma("sp", gq[:], self.q_norm.t[l].rearrange("(c p) -> p c", p=128), [self.q_norm.r], [gq.r], slow_ok=True)
        wuq = self.sb("wuq", [128, 2, 640], BF16)
        for c in range(2):
            self.ts("dve", wuq[:, c, :], wst[:, c, :], gq[:, c:c + 1], None, ALU.mult, None, [wst.r, gq.r], [wuq.r])
        wst2 = self.sb("wst2", [128, 768])
        self.dma("sp", wst2[:, 0:512], self.w_ukv_k.t[l], [self.w_ukv_k.r], [wst2.r])
        self.dma("sp", wst2[:, 512:768], self.w_ukv_v.t[l], [self.w_ukv_v.r], [wst2.r])
        gkv = self.sb("gkv", [128, 1])
        self.dma("sp", gkv[:], self.kv_norm.t[l].rearrange("(p o) -> p o", o=1), [self.kv_norm.r], [gkv.r])
        wkv = self.sb("wkv", [128, 768], BF16)
        self.ts("dve", wkv[:], wst2[:], gkv[:, 0:1], None, ALU.mult, None, [wst2.r, gkv.r], [wkv.r])
        sccw, cw, cb = self.conv_consts(l)
        st = self.ssd_consts(l)
        A1 = self.load_bv(6, "A1")
        B1 = self.load_bv(7, "B1")
        HE = self.sb("HE", [128, 256])
        self.ms("dve", HE[:], 0.0, [HE.r])

        xts = self.rot("xt", 2, [128, D])
        tmp = self.sb("tmp", [128, D])
        hb = self.sb("hb", [128, D], BF16)
        hT = self.sb("hT", [128, 8, NB], BF16)
        smalls = self.small3(2)
        zs = self.rot("zs", 2, [128, 512])
        dtt = self.sb("dtt", [128, 16])
        rqb = self.sb("rqb", [128, NB])
        rkvb = self.sb("rkvb", [128, NB])
        lnb = self.sb("lnb", [128, NB])
        cosr = self.sb("cosr", [32, 2, NB])
        t12 = self.sb("t12", [32, 2, NB])
        rkc = self.sb("rkc", [128, 4])
        rkl = self.sb("rkl", [128, 4])
        KTb = self.sb("KTb", [128, 4, NB], BF16)
        Vb = self.sb("Vb", [128, 2, 260], BF16)
        self.ms("pool", Vb[:], 1.0, [Vb.r])
        xbc = self.rot("xbc", 2, [128, 6, NB + 2])
        pbuf = self.rot("pbuf", 2, [128, 2, NB + 2])
        gbb = self.rot("gbb", 2, [128, 2, NB])
        gcs = self.sb("gcs", [128, NB])
        W = dict(sg=self.rot("sg", 2, [128, NB]), cvo=self.rot("cvo", 2, [128, NB]), xsT=self.sb("xsT", [128, 4, NB]), BCf=self.sb("BCf", [128, 2, NB]),
                 xstm=self.rot("xstm", 2, [128, 512]), btm=self.rot("btm", 2, [128, 128], BF16),
                 sm16=self.sb("sm16", [128, 16]), vt=self.sb("vt", [128, 16]), wE=self.sb("wE", [128, 8]),
                 decE=self.sb("decE", [128, 4]), xdtw=self.rot("xdtw", 2, [128, 512], BF16),
                 hetmp=self.sb("hetmp", [128, 256]), HE=HE, sccw=sccw, cw=cw, cb=cb, st=st)

        blocks = [(2 * i, 2) for i in range(NT // 2)]
        nb = len(blocks)
        S = dict(win=win, wuq=wuq, wkv=wkv, A1=A1, B1=B1, xts=xts, tmp=tmp, hb=hb, hT=hT, smalls=smalls, zs=zs, dtt=dtt,
                 st=st, rqb=rqb, rkvb=rkvb, lnb=lnb, cosr=cosr, t12=t12, rkc=rkc, rkl=rkl, KTb=KTb, Vb=Vb, gcs=gcs,
                 mla=[dict(cqT=self.sb("cqT", [128, 2, NB], BF16), cqsq=self.sb("cqsq", [128, 2, NB]),
                           ckvT=self.sb("ckvT", [128, NB], BF16), ckvsq=self.sb("ckvsq", [128, NB]),
                           krs=self.sb("krs", [32, 2, NB]), rope=self.sb("rope", [32, 2, NB])) for _ in range(2)],
                 slot=[dict(xb=xbc.next(), pbf=pbuf.next(), gb=gbb.next()) for _ in range(2)])

        def front1(bi):
            t0, nt = blocks[bi]
            N = nt * 128
            c0 = t0 * 128
            M = S["mla"][bi % 2]
            if bi == 1:
                self.dma("sp", A1[:], self.bv.t[0], [self.bv.r], [A1.r])
                self.dma("sp", B1[:], self.bv.t[1], [self.bv.r], [B1.r])
            for tl in range(nt):
                t = t0 + tl
                xt = xts.next()
                self.dma("sp", xt[:], xres.t[t * 128:(t + 1) * 128, :], [xres.r], [xt.r])
                self.norm_tile(xt, A1, B1, hb, tmp, smalls.next())
                self.transpose_tile(hb, hT, tl)
                yield
            for tl in range(nt):
                t = t0 + tl
                pb, pr = self.pbank()
                for k in range(8):
                    wa, wr = win.cols(k, Z_OFF, 512)
                    self.mm(pb[:, :], hT[:, k, tl * 128:(tl + 1) * 128], wa, k == 0, k == 7, [hT.r, wr], pr)
                z = zs.next()
                self.cp("act", z[:], pb[:, :], pr, [z.r])
                self.dma("pool", self.sc_z.t[t * 128:(t + 1) * 128, :], z[:], [z.r], [self.sc_z.r])
                pb, pr = self.pbank()
                for k in range(8):
                    wa, wr = win.cols(k, DT_OFF, 16)
                    self.mm(pb[:, 0:16], hT[:, k, tl * 128:(tl + 1) * 128], wa, k == 0, k == 7, [hT.r, wr], pr)
                self.tt("dve", dtt[:], pb[:, 0:16], st["dtb"][:], ALU.add, pr + [st["dtb"].r], [dtt.r])
                self.act(dtt[:], dtt[:], AF.Exp, [dtt.r], [dtt.r])
                self.act(self.dt_all[:, t, :], dtt[:], AF.Ln, [dtt.r], [self.dt_all.r], bias=1.0)
                yield
            self.dma("sp", M["rope"][:, :, 0:N], self.ropeT.t[:, :, c0:c0 + N].rearrange("a p t -> p a t"),
                     [self.ropeT.r], [M["rope"].r])
            for ci in range(5):
                off, wdt = FM[ci]
                pb, pr = self.pbank()
                for k in range(8):
                    wa, wr = win.cols(k, off, wdt)
                    self.mm(pb[0:wdt, 0:N], wa, hT[:, k, 0:N], k == 0, k == 7, [hT.r, wr], pr)
                src = pb[0:wdt, 0:N]
                if ci < 2:
                    self.cp("act", M["cqT"][:, ci, 0:N], src, pr, [M["cqT"].r])
                    self.act(M["cqsq"][:, ci, 0:N], src, AF.Square, pr, [M["cqsq"].r])
                elif ci == 2:
                    self.cp("act", M["ckvT"][:, 0:N], src, pr, [M["ckvT"].r])
                    self.act(M["ckvsq"][:, 0:N], src, AF.Square, pr, [M["ckvsq"].r])
                else:
                    self.cp("act", M["krs"][:, ci - 3, 0:N], src, pr, [M["krs"].r])
                yield

        def front2(bi):
            t0, nt = blocks[bi]
            N = nt * 128
            sl = S["slot"][bi % 2]
            xb, pbf, gb = sl["xb"], sl["pbf"], sl["gb"]
            for ci in (5, 6, 7, 8, 11, 12, 13, 14, 15, 16):
                off, wdt = FM[ci]
                pb, pr = self.pbank()
                for k in range(8):
                    wa, wr = win.cols(k, off, wdt)
                    self.mm(pb[0:wdt, 0:N], wa, hT[:, k, 0:N], k == 0, k == 7, [hT.r, wr], pr)
                src = pb[0:wdt, 0:N]
                if ci < 7:
                    self.cp("act", gb[:, ci - 5, 0:N], src, pr, [gb.r])
                elif ci < 9:
                    c = ci - 7
                    pbv, prv = self.pbank()
                    o2, w2 = FM[ci + 2]
                    for k in range(8):
                        wa, wr = win.cols(k, o2, w2)
                        self.mm(pbv[:, 0:N], wa, hT[:, k, 0:N], k == 0, k == 7, [hT.r, wr], prv)
                    self.cp("act", gcs[:, 0:N], src, pr, [gcs.r])
                    self.tt("dve", pbf[:, c, 1:N + 1], gcs[:, 0:N], pbv[:, 0:N], ALU.mult, [gcs.r] + prv, [pbf.r])
                else:
                    self.cp("act", xb[:, ci - 11, 1:N + 1], src, pr, [xb.r])
                yield
            self.ms("pool", xb[:, :, 0:1], 0.0, [xb.r])
            self.ms("pool", pbf[:, :, 0:1], 0.0, [pbf.r])
            self.ms("pool", xb[:, :, N + 1:N + 2], 0.0, [xb.r])
            self.ms("pool", pbf[:, :, N + 1:N + 2], 0.0, [pbf.r])
            yield

        def mla_post(bi):
            t0, nt = blocks[bi]
            N = nt * 128
            c0 = t0 * 128
            M = S["mla"][bi % 2]
            cqT, cqsq, ckvT, ckvsq, krs, rope = M["cqT"], M["cqsq"], M["ckvT"], M["ckvsq"], M["krs"], M["rope"]
            pb, pr = self.pbank()
            for c in range(2):
                self.mm(pb[:, 0:N], self.ones_f[:], cqsq[:, c, 0:N], c == 0, c == 1, [self.ones_f.r, cqsq.r], pr)
            self.act(lnb[:, 0:N], pb[:, 0:N], AF.Ln, pr + [self.eps_t.r], [lnb.r], bias=self.eps_t[:, 0:1], scale=1.0 / 256)
            self.act(rqb[:, 0:N], lnb[:, 0:N], AF.Exp, [lnb.r], [rqb.r], scale=-0.5)
            pb, pr = self.pbank()
            self.mm(pb[:, 0:N], self.ones_f[:], ckvsq[:, 0:N], True, True, [self.ones_f.r, ckvsq.r], pr)
            self.act(lnb[:, 0:N], pb[:, 0:N], AF.Ln, pr + [self.eps_t.r], [lnb.r], bias=self.eps_t[:, 0:1], scale=1.0 / 128)
            self.act(rkvb[:, 0:N], lnb[:, 0:N], AF.Exp, [lnb.r], [rkvb.r], scale=-0.5)
            pb, pr = self.pbank()
            for tl in range(nt):
                self.mm(pb[:, tl:tl + 1], ckvsq[:, tl * 128:(tl + 1) * 128], self.ones_f[:, 0:1], True, True,
                        [ckvsq.r, self.ones_f.r], pr)
            self.act(rkl[:, 0:nt], pb[:, 0:nt], AF.Ln, pr + [self.eps_t.r], [rkl.r], bias=self.eps_t[:, 0:1], scale=1.0 / 128)
            self.act(rkc[:, 0:nt], rkl[:, 0:nt], AF.Exp, [rkl.r], [rkc.r], scale=-0.5)
            yield
            for a in range(2):
                self.tt("pool", cosr[:, a, 0:N], rope[:, a, 0:N], rqb[0:32, 0:N], ALU.mult, [rope.r, rqb.r], [cosr.r])
            for h in range(4):
                pa, pra = self.pbank()
                for c in range(2):
                    self.mm(pa[:, 0:N], wuq[:, c, h * 160:h * 160 + 128], cqT[:, c, 0:N], c == 0, c == 1, [wuq.r, cqT.r], pra)
                pq, prq = self.pbank()
                for c in range(2):
                    self.mm(pq[0:32, 0:N], wuq[:, c, h * 160 + 128:h * 160 + 160], cqT[:, c, 0:N], c == 0, c == 1,
                            [wuq.r, cqT.r], prq)
                self.tt("dve", self.qT[:, h, c0:c0 + N], pa[:, 0:N], rqb[:, 0:N], ALU.mult, pra + [rqb.r], [self.qT.r])
                self.tt("dve", t12[:, 0, 0:N], pa[0:32, 0:N], cosr[:, 0, 0:N], ALU.mult, pra + [cosr.r], [t12.r])
                self.tt("dve", t12[:, 1, 0:N], pq[0:32, 0:N], cosr[:, 1, 0:N], ALU.mult, prq + [cosr.r], [t12.r])
                self.tt("dve", self.qT[0:32, h, c0:c0 + N], t12[:, 0, 0:N], t12[:, 1, 0:N], ALU.add, [t12.r], [self.qT.r])
                yield
            self.tt("pool", t12[:, 0, 0:N], krs[:, 0, 0:N], rope[:, 0, 0:N], ALU.mult, [krs.r, rope.r], [t12.r])
            self.tt("pool", t12[:, 1, 0:N], krs[:, 1, 0:N], rope[:, 1, 0:N], ALU.mult, [krs.r, rope.r], [t12.r])
            self.tt("pool", t12[:, 0, 0:N], t12[:, 0, 0:N], t12[:, 1, 0:N], ALU.add, [t12.r], [t12.r])
            for h in range(4):
                pk, prk = self.pbank()
                self.mm(pk[:, 0:N], wkv[:, h * 128:(h + 1) * 128], ckvT[:, 0:N], True, True, [wkv.r, ckvT.r], prk)
                self.tt("dve", KTb[:, h, 0:N], pk[:, 0:N], rkvb[:, 0:N], ALU.mult, prk + [rkvb.r], [KTb.r])
                self.cp("pool", KTb[0:32, h, 0:N], t12[:, 0, 0:N], [t12.r], [KTb.r])
            yield
            if bi == 0:
                self.dma("pool", self.kv_ctx_k.t, KTb[:, :, 0:N], [KTb.r], [self.kv_ctx_k.r])
            else:
                lo = c0 - TC
                self.dma("pool", self.xk.t.rearrange("(h p) t -> p h t", p=128)[:, :, lo:lo + N], KTb[:, :, 0:N],
                         [KTb.r], [self.xk.r])
            for tl in range(nt):
                pv, prv = self.pbank()
                self.mm(pv[:, 0:256], ckvT[:, tl * 128:(tl + 1) * 128], wkv[:, 512:768], True, True, [ckvT.r, wkv.r], prv)
                self.ts("dve", Vb[:, tl, :].rearrange("p (h e) -> p h e", h=4)[:, :, 0:64],
                        pv[:, 0:256].rearrange("p (h e) -> p h e", h=4), rkc[:, tl:tl + 1], None, ALU.mult, None,
                        prv + [rkc.r], [Vb.r])
            if bi == 0:
                self.dma("pool", self.kv_ctx_v.t, Vb[:, 0:2, :], [Vb.r], [self.kv_ctx_v.r])
            else:
                lo = t0 - 2
                self.dma("pool", self.xv.t.rearrange("(t p) e -> p t e", p=128)[:, lo:lo + nt, :], Vb[:, 0:nt, :],
                         [Vb.r], [self.xv.r])
            yield

        def back1(bi):
            if bi == 0:
                return
            p = S["slot"][(bi - 1) % 2]
            pt0, pnt = blocks[bi - 1]
            pN = pnt * 128
            if 2 <= bi < nb:
                c = S["slot"][bi % 2]
                self.cp("pool", p["xb"][:, :, pN + 1:pN + 2], c["xb"][:, :, 1:2], [c["xb"].r], [p["xb"].r])
                self.cp("pool", p["pbf"][:, :, pN + 1:pN + 2], c["pbf"][:, :, 1:2], [c["pbf"].r], [p["pbf"].r])
                self.cp("pool", c["xb"][:, :, 0:1], p["xb"][:, :, pN:pN + 1], [p["xb"].r], [c["xb"].r])
                self.cp("pool", c["pbf"][:, :, 0:1], p["pbf"][:, :, pN:pN + 1], [p["pbf"].r], [c["pbf"].r])
            yield from self.stage_a2_conv(l, p["xb"], p["pbf"], p["gb"], pt0, pnt, W, bi == nb)

        def back2(bi):
            if bi < nb:
                yield from mla_post(bi)
            if bi >= 1:
                pt0, pnt = blocks[bi - 1]
                yield from self.stage_a2_scan(l, pt0, pnt, W, bi == nb)

        run_gen(front1(0))
        run_gen(front2(0))
        for bi in range(nb + 1):
            f1 = front1(bi + 1) if bi + 1 < nb else None
            f2 = front2(bi + 1) if bi + 1 < nb else None
            interleave((f1, 9), (back1(bi), 8))
            interleave((f2, 11), (back2(bi), 10))
        pxb, ppb = S["slot"][(nb - 1) % 2]["xb"], S["slot"][(nb - 1) % 2]["pbf"]
        pnt = blocks[-1][1]
        N = pnt * 128
        sm = self.sb("sm", [128, SM_W])
        self.ms("dve", sm[:], 0.0, [sm.r])
        self.cp("dve", sm[:, 0:256], HE[:], [HE.r], [sm.r])
        self.cp("dve", sm[:, 512:524].rearrange("p (c k) -> p c k", c=6), pxb[:, :, N - 1:N + 1], [pxb.r], [sm.r])
        self.cp("dve", sm[:, 524:526], ppb[:, :, N], [ppb.r], [sm.r])
        self.cp("dve", self.fix[:, 10:16], pxb[:, :, N], [pxb.r], [self.fix.r])
        self.dma("pool", self.xs_.t, sm[:], [sm.r], [self.xs_.r])
        self.dma("pool", self.xs_.t[0:1, 528:536], self.dt_all[127:128, NT - 1, 0:8], [self.dt_all.r], [self.xs_.r])
        self.dump(f"qT{l}", self.qT[:], self.qT.r, [128, 4, T], BF16)
        self.dump(f"ysc{l}", self.ysc[:], self.ysc.r, [128, 2, T], BF16)
        self.dump(f"dt{l}", self.dt_all[:], self.dt_all.r, [128, NT, 16])
        self.dump(f"HE{l}", HE[:], HE.r, [128, 256])
        self.dump(f"BT{l}", self.BT[:], self.BT.r, [128, T], BF16)
        self.dump(f"CT{l}", self.CT[:], self.CT.r, [128, T], BF16)
        self.dump(f"xk{l}", self.xk.t, self.xk.r, [512, TL], BF16)
        self.dump(f"xv{l}", self.xv.t, self.xv.r, [2048, 260], BF16)
        self.dump(f"scxs{l}", self.sc_xs.t, self.sc_xs.r, [T, 512])
        self.dump(f"scz{l}", self.sc_z.t, self.sc_z.r, [T, 512])

    def conv_consts(self, l):
        sccw = self.sb("sccw", [128, 2, 3])
        self.dma("sp", sccw[:], self.sc_cw.t[l].rearrange("(c p) k -> p c k", p=128), [self.sc_cw.r], [sccw.r])
        cw = self.sb("cw", [128, 6, 3])
        self.dma("sp", cw[:], self.ssd_cw.t[l].rearrange("(c p) k -> p c k", p=128), [self.ssd_cw.r], [cw.r])
        cb = self.sb("cb", [128, 6])
        self.dma("sp", cb[:], self.ssd_cb.t[l].rearrange("(c p) -> p c", p=128), [self.ssd_cb.r], [cb.r], slow_ok=True)
        self.ncb = self.sb("ncb", [128, 6])
        self.ts("dve", self.ncb[:], cb[:], -1.0, None, ALU.mult, None, [cb.r], [self.ncb.r])
        return sccw, cw, cb

    def ssd_consts(self, l):
        st = {}
        al = self.sb("al", [128, 16])
        self.dma("sp", al[:], self.a_log.t[l].partition_broadcast(128), [self.a_log.r], [al.r])
        ab = self.sb("ab", [128, 16])
        self.act(ab[:], al[:], AF.Exp, [al.r], [ab.r])
        self.ts("dve", ab[:], ab[:], -1.0, None, ALU.mult, None, [ab.r], [ab.r])
        st["ab"] = ab
        dtb = self.sb("dtb", [128, 16])
        self.dma("sp", dtb[:], self.dt_bias.t[l].partition_broadcast(128), [self.dt_bias.r], [dtb.r])
        st["dtb"] = dtb
        return st

    def stage_a2_conv(self, l, xb, pbf, gb, t0, nt, W, is_last):
        N = nt * 128
        c0 = t0 * 128
        xsT, BCf, sccw, cw, cb = W["xsT"], W["BCf"], W["sccw"], W["cw"], W["cb"]
        for c in range(2):
            cvo = W["cvo"].next()
            self.conv3(cvo, pbf, c, sccw, N)
            self.tt("dve", self.ysc[:, c, c0:c0 + N], cvo[:, 0:N], gb[:, c, 0:N], ALU.mult, [cvo.r, gb.r], [self.ysc.r])
            if is_last:
                self.cp("dve", self.fix[:, 6 + c:7 + c], cvo[:, N - 1:N], [cvo.r], [self.fix.r])
                self.cp("dve", self.fix[:, 8 + c:9 + c], gb[:, c, N - 1:N], [gb.r], [self.fix.r])
            yield
        for c in range(6):
            cvo = W["cvo"].next()
            self.conv3(cvo, xb, c, cw, N)
            if is_last:
                self.cp("dve", self.fix[:, c:c + 1], cvo[:, N - 1:N], [cvo.r], [self.fix.r])
            dst = xsT[:, c, 0:N] if c < 4 else BCf[:, c - 4, 0:N]
            dres = xsT.r if c < 4 else BCf.r
            sg = W["sg"].next()
            self.silu(dst, [dres], cvo[:, 0:N], [cvo.r], (sg[:, 0:N], sg.r), bias=(cb[:, c:c + 1], cb.r),
                      negbias=(self.ncb[:, c:c + 1], self.ncb.r))
            yield
        self.cp("pool", self.BT[:, c0:c0 + N], BCf[:, 0, 0:N], [BCf.r], [self.BT.r])
        self.cp("pool", self.CT[:, c0:c0 + N], BCf[:, 1, 0:N], [BCf.r], [self.CT.r])
        if is_last:
            self.cp("pool", self.xsT_last[:], xsT[:, :, N - 128:N], [xsT.r], [self.xsT_last.r])

    def stage_a2_scan(self, l, t0, nt, W, is_last):
        xsT, BCf = W["xsT"], W["BCf"]
        for tl in range(nt):
            t = t0 + tl
            xs_tm, b_tm = self.tokmajor_xs_b(xsT, BCf, tl, t, W["xstm"], W["btm"])
            self.cp("act", self.HEs[:, t, :], W["HE"][:], [W["HE"].r], [self.HEs.r])
            yield
            self.state_update(W["st"], xs_tm, b_tm, t, 0, W["sm16"], W["vt"], W["wE"], W["decE"], W["xdtw"], W["hetmp"],
                              W["HE"], mask_last=(is_last and tl == nt - 1))
            yield

    def conv3(self, cvo, buf, c, wts, N):
        self.ts("dve", cvo[:, 0:N], buf[:, c, 1:N + 1], wts[:, c, 1:2], None, ALU.mult, None, [buf.r, wts.r], [cvo.r])
        self.stt("dve", cvo[:, 0:N], buf[:, c, 0:N], wts[:, c, 0:1], cvo[:, 0:N], ALU.mult, ALU.add,
                 [buf.r, wts.r, cvo.r], [cvo.r])
        self.stt("dve", cvo[:, 0:N], buf[:, c, 2:N + 2], wts[:, c, 2:3], cvo[:, 0:N], ALU.mult, ALU.add,
                 [buf.r, wts.r, cvo.r], [cvo.r])

    def tokmajor_xs_b(self, xsT, BCf, tl, t, xstm, btm):
        pb, pr = self.pbank()
        for c in range(4):
            self.tr(pb[:, c * 128:(c + 1) * 128], xsT[:, c, tl * 128:(tl + 1) * 128], self.ident_f[:],
                    [xsT.r, self.cst.r], pr)
        xs_tm = xstm.next()
        self.cp("act", xs_tm[:], pb[:, :], pr, [xs_tm.r])
        self.dma("pool", self.sc_xs.t[t * 128:(t + 1) * 128, :], xs_tm[:], [xs_tm.r], [self.sc_xs.r])
        pb2, pr2 = self.pbank()
        self.tr(pb2[:, 0:128], BCf[:, 0, tl * 128:(tl + 1) * 128], self.ident_f[:], [BCf.r, self.cst.r], pr2)
        b_tm = btm.next()
        self.cp("act", b_tm[:], pb2[:, 0:128], pr2, [b_tm.r])
        self.dma("pool", self.sc_bt.t[t * 128:(t + 1) * 128, :], b_tm[:], [b_tm.r], [self.sc_bt.r])
        return xs_tm, b_tm

    def state_update(self, st, xs_tm, b_tm, t, d_, sm16, vt, wE, decE, xdtw, hetmp, H, mask_last=False, vt_ready=None):
        tri = self.triE if d_ == 0 else self.triL
        o = d_ * 8
        if vt_ready is None:
            la = sm16
            self.tt("dve", la[:, 0:8], self.dt_all[:, t, o:o + 8], st["ab"][:, o:o + 8], ALU.mult,
                    [self.dt_all.r, st["ab"].r], [la.r])
            pb, pr = self.pbank()
            self.mm(pb[:, 0:8], tri[:], la[:, 0:8], True, True, [self.cst.r, la.r], pr)
            self.mm(pb[:, 8:16], self.ones_f[:], la[:, 0:8], True, True, [self.ones_f.r, la.r], pr)
            self.cp("act", vt[:, 0:16], pb[:, 0:16], pr, [vt.r])
            vcol, tot = vt[:, 0:8], vt[:, 8:16]
            totlo, tothi = vt[0:64, 8:12], vt[64:128, 12:16]
            vr = vt.r
        else:
            vcol, tot, totlo, tothi, vr = vt_ready
        self.tt("dve", wE[:], tot, vcol, ALU.subtract, [vr], [wE.r])
        self.act(wE[:], wE[:], AF.Exp, [wE.r], [wE.r])
        self.tt("dve", wE[:], wE[:], self.dt_all[:, t, o:o + 8], ALU.mult, [wE.r, self.dt_all.r], [wE.r])
        if mask_last:
            self.ts("dve", wE[:], wE[:], self.lastmask[:, 0:1], None, ALU.mult, None, [wE.r, self.cst.r], [wE.r])
        self.act(decE[0:64, :], totlo, AF.Exp, [vr], [decE.r])
        self.act(decE[64:128, :], tothi, AF.Exp, [vr], [decE.r])
        xd = xdtw.next()
        self.tt("dve", xd[:].rearrange("p (h e) -> p h e", h=8), xs_tm[:].rearrange("p (h e) -> p h e", h=8),
                bc(wE[:], [128, 8, 64]), ALU.mult, [xs_tm.r, wE.r], [xd.r])
        pb, pr = self.pbank()
        self.mm(pb[:, :], b_tm[:], xd[:], True, True, [b_tm.r, xd.r], pr)
        self.tt("dve", hetmp[:].rearrange("p (h e) -> p h e", h=4), H[:].rearrange("p (h e) -> p h e", h=4),
                bc(decE[:], [128, 4, 64]), ALU.mult, [H.r, decE.r], [hetmp.r])
        self.tt("dve", H[0:64, :], hetmp[0:64, :], pb[0:64, 0:256], ALU.add, [hetmp.r] + pr, [H.r])
        self.tt("dve", H[64:128, :], hetmp[64:128, :], pb[64:128, 256:512], ALU.add, [hetmp.r] + pr, [H.r])

    def exchange(self, l):
        groups = [[2 * i, 2 * i + 1] for i in range(self.ncores // 2)]
        for src, dst in ((self.xk, self.xk_g), (self.xv, self.xv_g), (self.xs_, self.xs_g)):
            self.cc(src, dst, groups)

    def cc(self, src, dst, groups):
        if self.fake_cc:
            n = src.t.shape[0]
            for r in range(2):
                self.dma("pool", dst.t[r * n:(r + 1) * n, :], src.t, [src.r], [dst.r])
            return
        self.P.dma("pool", lambda e: e.collective_compute("AllGather", ALU.bypass, replica_groups=groups,
                                                           ins=[src.t.opt()], outs=[dst.t.opt()]),
                   [src.r], [dst.r], inc=1)

    def phase_b(self, l, xres, last):
        nc, P = self.nc, self.P
        self.new_scope()
        st = self.ssd_consts(l)
        sccw, cw, cb = self.conv_consts(l)
        wo_att = self.sb("wo_att", [64, 4, D], BF16)
        wo_r = self.sb("wo_r", [128, 6, D], BF16)

        def load_wo():
            self.dma("pool", wo_att[:], self.w_out.t[l, 0:256, :].rearrange("(h p) n -> p h n", p=64), [self.w_out.r], [wo_att.r])
            self.dma("pool", wo_r[:], self.w_out.t[l, 256:1024, :].rearrange("(c p) n -> p c n", p=128), [self.w_out.r], [wo_r.r])
        G1 = self.load_bv(2, "G1")
        dsk8 = self.sb("dsk8", [128, 8])
        self.dma("sp", dsk8[:], self.ssd_d.t[l].partition_broadcast(128), [self.ssd_d.r], [dsk8.r])
        ngb = self.sb("ngb", [128, 512])
        self.dma("sp", ngb[:], self.ssd_norm.t[l].partition_broadcast(128), [self.ssd_norm.r], [ngb.r])
        VA = self.sb("VA", [128, 34, 260], BF16)
        self.dma("sp", VA[:, 0:2, :], self.kv_ctx_v.t, [self.kv_ctx_v.r], [VA.r])
        xvv = self.xv_g.t.rearrange("(t p) e -> p t e", p=128)
        for q in range(4):
            self.dma("sp", VA[:, 2 + 8 * q:10 + 8 * q, :], xvv[:, 8 * q:8 * q + 8, :], [self.xv_g.r], [VA.r])
        KTs = self.rot("KT", 2, [128, TC + 2 * TL], BF16)
        smg = self.sb("smg", [128, 2, SM_W])
        self.dma("sp", smg[:], self.xs_g.t.rearrange("(r p) c -> p r c", p=128), [self.xs_g.r], [smg.r])
        smp = self.sb("smp", [128, SM_W])
        self.ts("dve", smp[:], smg[:, 1, :], self.esel[:, 0:1], None, ALU.mult, None, [smg.r, self.cst.r], [smp.r])
        self.stt("dve", smp[:], smg[:, 0, :], self.esel[:, 1:2], smp[:], ALU.mult, ALU.add, [smg.r, self.cst.r, smp.r], [smp.r])

        xt = self.sb("xt", [128, D])
        tmp = self.sb("tmp", [128, D])
        smalls = self.small3(2)
        x1 = self.sb("x1", [128, D])
        la = self.sb("la", [128, 16])
        vt = self.sb("vt", [128, 32])
        ev = self.sb("ev", [128, 16])
        wL = self.sb("wL", [128, 8])
        decL = self.sb("decL", [128, 4])
        Rm = self.sb("Rm", [128, 8, 128])
        D1 = self.sb("D1", [128, 8, 128])
        MT = self.sb("MT", [128, 16, 128], BF16)
        xdt = self.sb("xdt", [128, 16, 64], BF16)
        xs_r = self.rot("xsr", 2, [128, 512])
        bt_r = self.rot("btr", 2, [128, 128], BF16)
        z_r = self.rot("zr", 2, [128, 512])
        ya = self.sb("ya", [128, 512])
        yb = self.sb("yb", [128, 512])
        sz = self.sb("sz", [128, 512])
        yss = self.sb("yss", [128, 512], BF16)
        jk = self.sb("jk", [128, 512], BF16)
        jk32 = self.sb("jk32", [128, 512])
        ms2 = self.sb("ms2", [128, 2])
        ln2 = self.sb("ln2", [128, 2])
        rs2 = self.sb("rs2", [128, 2])
        xdtw = self.rot("xdtw", 2, [128, 512], BF16)
        hetmp = self.sb("hetmp", [128, 256])
        HL = self.sb("HL", [128, 256])
        HLb = self.sb("HLbd", [128, 512], BF16)
        self.ms("pool", HLb[:], 0.0, [HLb.r])
        Cbd = self.sb("Cbd", [128, 256], BF16)
        self.ms("pool", Cbd[:], 0.0, [Cbd.r])
        HbdE = self.sb("HbdE", [128, 512], BF16)
        self.ms("pool", HbdE[:], 0.0, [HbdE.r])
        ymx = self.sb("ymx", [128, 4, 512], BF16)
        yatt = self.sb("yatt", [64, 4, 512], BF16)
        PTs = self.rot("PT", 2, [128, 1024], BF16)
        Osb = self.sb("Osb", [64, 512])
        rec = self.sb("rec", [128, 512])
        dskb = self.sb("dskb", [128, 512])
        self.cp("dve", dskb[:].rearrange("p (h e) -> p h e", h=8), bc(dsk8[:], [128, 8, 64]), [dsk8.r], [dskb.r])

        self.l_init(l, st, smp, cw, cb, HL, HLb)
        self.fix_last(l, smp, sccw, cw, cb)
        self.dump(f"HL0{l}", HL[:], HL.r, [128, 256])
        if self.stop(l, "b0"):
            return

        order = [4, 3, 2, 1] + ([] if last else [0])
        ymxs = [ymx, self.sb("ymx2", [128, 4, 512], BF16)]
        yatts = [yatt, self.sb("yatt2", [64, 4, 512], BF16)]

        def geom(bi):
            t0, nt = BLOCKS[bi]
            return t0, nt, nt * 128, t0 * 128, (1 if bi == 0 else 0)

        def ssd_gen(bi, ymx):
            t0, nt, N, c0, isctx = geom(bi)
            if isctx:
                self.ms("dve", HL[:], 0.0, [HL.r])
                self.ms("dve", HLb[:], 0.0, [HLb.r])
            for tl in range(nt - 1, -1, -1):
                t = t0 + tl
                cs = slice(t * 128, (t + 1) * 128)
                xs = xs_r.next()
                self.dma("pool", xs[:], self.sc_xs.t[cs, :], [self.sc_xs.r], [xs.r])
                btk = bt_r.next()
                self.dma("pool", btk[:], self.sc_bt.t[cs, :], [self.sc_bt.r], [btk.r])
                z = z_r.next()
                self.dma("pool", z[:], self.sc_z.t[cs, :], [self.sc_z.r], [z.r])
                self.tt("dve", la[:], self.dt_all[:, t, :], st["ab"][:], ALU.mult, [self.dt_all.r, st["ab"].r], [la.r])
                pb, pr = self.pbank()
                self.mm(pb[:, 0:8], self.triE[:], la[:, 0:8], True, True, [self.cst.r, la.r], pr)
                self.mm(pb[:, 8:16], self.triL[:], la[:, 8:16], True, True, [self.cst.r, la.r], pr)
                self.mm(pb[:, 16:32], self.ones_f[:], la[:, 0:16], True, True, [self.ones_f.r, la.r], pr)
                self.cp("act", vt[:], pb[:, 0:32], pr, [vt.r])
                self.act(ev[:], vt[:, 0:16], AF.Exp, [vt.r], [ev.r])
                yield
                pg, prg = self.pbank(pin=True)
                for g in range(2):
                    self.cp("pool", Cbd[g * 64:(g + 1) * 64, g * 128:(g + 1) * 128], self.CT[g * 64:(g + 1) * 64, cs],
                            [self.CT.r], [Cbd.r])
                self.mm(pg[:, 0:256], self.BT[:, cs], Cbd[:], True, True, [self.BT.r, Cbd.r], prg)
                for d_ in range(2):
                    self.tt("dve", xdt[:, d_ * 8:(d_ + 1) * 8, :], xs[:].rearrange("p (h e) -> p h e", h=8),
                            bc(self.dt_all[:, t, d_ * 8:(d_ + 1) * 8], [128, 8, 64]), ALU.mult,
                            [xs.r, self.dt_all.r], [xdt.r])
                yield
                for d_, tri in enumerate((self.triE, self.triL)):
                    self.tt("pool", Rm[:], bcm(tri[:], [128, 8, 128]), bc(la[:, d_ * 8:(d_ + 1) * 8], [128, 8, 128]),
                            ALU.mult, [self.cst.r, la.r], [Rm.r])
                    pd, prd = self.pbank(2)
                    for q in range(2):
                        self.mm(pd[:, q * 512:(q + 1) * 512], self.ones_f[:],
                                Rm[:, q * 4:(q + 1) * 4, :].rearrange("p a b -> p (a b)"), True, False,
                                [self.ones_f.r, Rm.r], prd)
                        self.mm(pd[:, q * 512:(q + 1) * 512], self.ident_b[:], self.negm[:, d_, :], False, True,
                                [self.ident_b.r, self.negm.r], prd)
                    self.tt("dve", D1[:], pd[:, :].rearrange("p (a b) -> p a b", a=8),
                            bc(vt[:, d_ * 8:(d_ + 1) * 8], [128, 8, 128]), ALU.subtract, prd + [vt.r], [D1.r])
                    self.act(D1[:], D1[:], AF.Exp, [D1.r], [D1.r])
                    self.tt("dve", MT[:, d_ * 8:(d_ + 1) * 8, :].rearrange("p (g h) b -> p g h b", g=2),
                            D1[:].rearrange("p (g h) b -> p g h b", g=2),
                            pg[:, 0:256].rearrange("p (g b) -> p g b", g=2).unsqueeze(2).to_broadcast([128, 2, 4, 128]),
                            ALU.mult, [D1.r] + prg, [MT.r])
                    if d_ == 1:
                        self.unpin(prg)
                    yield
                yield
                py, pry = self.pbank()
                for h in range(8):
                    self.mm(py[:, h * 64:(h + 1) * 64], MT[:, h, :], xdt[:, h, :], True, False, [MT.r, xdt.r], pry)
                    self.mm(py[:, h * 64:(h + 1) * 64], MT[:, 8 + h, :], xdt[:, 8 + h, :], False, True, [MT.r, xdt.r], pry)
                pe_, pre = self.pbank()
                for g in range(2):
                    self.cp("act", HbdE[g * 64:(g + 1) * 64, g * 256:(g + 1) * 256], self.HEs[g * 64:(g + 1) * 64, t, :],
                            [self.HEs.r], [HbdE.r])
                self.mm(pe_[:, :], self.CT[:, cs], HbdE[:], True, True, [self.CT.r, HbdE.r], pre)
                self.tt("dve", ya[:].rearrange("p (h e) -> p h e", h=8), pe_[:, :].rearrange("p (h e) -> p h e", h=8),
                        bc(ev[:, 0:8], [128, 8, 64]), ALU.mult, pre + [ev.r], [ya.r])
                pl, prl = self.pbank()
                self.mm(pl[:, :], self.CT[:, cs], HLb[:], True, True, [self.CT.r, HLb.r], prl)
                self.tt("dve", yb[:].rearrange("p (h e) -> p h e", h=8), pl[:, :].rearrange("p (h e) -> p h e", h=8),
                        bc(ev[:, 8:16], [128, 8, 64]), ALU.mult, prl + [ev.r], [yb.r])
                self.tt("dve", ya[:], ya[:], yb[:], ALU.add, [ya.r, yb.r], [ya.r])
                self.tt("dve", ya[:], ya[:], py[:, :], ALU.add, [ya.r] + pry, [ya.r])
                self.tt("dve", yb[:], xs[:], dskb[:], ALU.mult, [xs.r, dskb.r], [yb.r])
                self.tt("dve", ya[:], ya[:], yb[:], ALU.add, [ya.r, yb.r], [ya.r])
                if l == 0 and t == NT - 1:
                    self.dump("yscan0", ya[:], ya.r, [128, 512])
                yield
                self.silu(sz[:], [sz.r], z[:], [z.r], (jk32[:], jk32.r))
                self.tt("dve", ya[:], ya[:], sz[:], ALU.mult, [ya.r, sz.r], [ya.r])
                for g in range(2):
                    self.act(jk[:, g * 256:(g + 1) * 256], ya[:, g * 256:(g + 1) * 256], AF.Square, [ya.r],
                             [jk.r, ms2.r], scale=1.0 / 16.0, accum=ms2[:, g:g + 1])
                self.act(ln2[:], ms2[:], AF.Ln, [ms2.r, self.eps_t.r], [ln2.r], bias=self.eps_t[:, 0:1])
                self.act(rs2[:], ln2[:], AF.Exp, [ln2.r], [rs2.r], scale=-0.5)
                for g in range(2):
                    self.stt("dve", yss[:, g * 256:(g + 1) * 256], ya[:, g * 256:(g + 1) * 256], rs2[:, g:g + 1],
                             ngb[:, g * 256:(g + 1) * 256], ALU.mult, ALU.mult, [ya.r, rs2.r, ngb.r], [yss.r])
                pb, pr = self.pbank()
                pbb = pb.bitcast(BF16)
                for c in range(4):
                    self.tr(pbb[:, c * 128:(c + 1) * 128], yss[:, c * 128:(c + 1) * 128], self.ident_b[:],
                            [yss.r, self.ident_b.r], pr)
                self.cp("act", ymx[:, :, tl * 128:(tl + 1) * 128], pbb[:, 0:512].rearrange("p (c t) -> p c t", c=4), pr, [ymx.r])
                yield
                vready = (vt[:, 8:16], vt[:, 24:32], vt[0:64, 24:28], vt[64:128, 28:32], vt.r)
                self.state_update(st, xs, btk, t, 1, None, vt, wL, decL, xdtw, hetmp, HL, vt_ready=vready)
                self.hl_bd(HL, HLb)
                yield

        def att_gen(bi, yatt):
            t0, nt, N, c0, isctx = geom(bi)
            nk = 2 if isctx else 34
            nkc = nk * 128
            for h in range(4):
                KT = KTs.next()
                self.dma("sp", KT[:, 0:TC], self.kv_ctx_k.t[:, h, :], [self.kv_ctx_k.r], [KT.r])
                if not isctx:
                    for r in range(2):
                        self.dma("sp", KT[:, TC + r * TL:TC + (r + 1) * TL],
                                 self.xk_g.t[r * 512 + h * 128:r * 512 + (h + 1) * 128, :], [self.xk_g.r], [KT.r])
                po, pro = self.pbank(pin=True)
                npair = nk // 2

                def qk_pair(j):
                    ps2, pr2 = self.pbank(2, pin=True)
                    for a in range(2):
                        kt = 2 * j + a
                        self.mm(ps2[:, a * 512:a * 512 + N], KT[:, kt * 128:(kt + 1) * 128], self.qT[:, h, c0:c0 + N],
                                True, True, [KT.r, self.qT.r], pr2)
                    return ps2, pr2

                cur = qk_pair(0)
                for j in range(npair):
                    ps2, pr2 = cur
                    pt_ = PTs.next()
                    self.act(pt_[:].rearrange("p (a n) -> p a n", a=2)[:, :, 0:N],
                             ps2[:, :].rearrange("p (a n) -> p a n", a=2)[:, :, 0:N], AF.Exp, pr2, [pt_.r], scale=MLA_SCALE)
                    self.unpin(pr2)
                    if j + 1 < npair:
                        cur = qk_pair(j + 1)
                    for a in range(2):
                        kt = 2 * j + a
                        self.mm(po[0:65, 0:N], VA[:, kt, h * 65:(h + 1) * 65], pt_[:, a * 512:a * 512 + N], kt == 0,
                                kt == nk - 1, [VA.r, pt_.r], pro)
                    yield
                self.unpin(pro)
                self.act(rec[64:65, 0:N], po[64:65, 0:N], AF.Ln, pro, [rec.r])
                self.act(rec[64:65, 0:N], rec[64:65, 0:N], AF.Exp, [rec.r], [rec.r], scale=-1.0)
                self.cp("act", Osb[:, 0:N], po[0:64, 0:N], pro, [Osb.r])
                yield
                pbc, prb = self.pbank()
                self.mm(pbc[0:64, 0:N], self.ones_f[64:65, 0:64], rec[64:65, 0:N], True, True, [self.ones_f.r, rec.r], prb)
                self.tt("dve", yatt[:, h, 0:N], pbc[0:64, 0:N], Osb[:, 0:N], ALU.mult, prb + [Osb.r], [yatt.r])
                yield

        def wout_gen(bi, ymx, yatt):
            t0, nt, N, c0, isctx = geom(bi)
            if isctx:
                self.dma("sp", G1[:], self.bv.t[8], [self.bv.r], [G1.r])
            for tl in range(nt):
                t = t0 + tl
                cs = slice(t * 128, (t + 1) * 128)
                ls = slice(tl * 128, (tl + 1) * 128)
                pw, prw = self.pbank(2)
                for hf in range(2):
                    ns = slice(hf * 512, (hf + 1) * 512)
                    for h in range(4):
                        self.mm(pw[:, ns], yatt[:, h, ls], wo_att[:, h, ns], h == 0, False, [yatt.r, wo_att.r], prw)
                    for c in range(2):
                        self.mm(pw[:, ns], self.ysc[:, c, cs], wo_r[:, c, ns], False, False, [self.ysc.r, wo_r.r], prw)
                    for c in range(4):
                        self.mm(pw[:, ns], ymx[:, c, ls], wo_r[:, 2 + c, ns], False, c == 3, [ymx.r, wo_r.r], prw)
                self.dma("sp", xt[:], xres.t[cs, :], [xres.r], [xt.r])
                self.post_tile(pw, prw, xt, G1, tmp, smalls.next(), x1)
                self.dma("sp", self.x1s.t[cs, :], x1[:], [x1.r], [self.x1s.r])
                yield

        prev = None
        for it, bi in enumerate(order):
            ymx_c, yatt_c = ymxs[it % 2], yatts[it % 2]
            nk_ = 2 if bi == 0 else 34
            gens = [(ssd_gen(bi, ymx_c), 9 * BLOCKS[bi][1]), (att_gen(bi, yatt_c), 4 * (nk_ // 2 + 2))]
            if prev is not None:
                gens.append((wout_gen(*prev), BLOCKS[prev[0]][1]))
            if it == 0:
                gens.insert(0, ((lambda: (yield load_wo()))(), 1, 0.4))
            interleave(*gens)
            if l == 0 and bi == 4:
                self.dump("yatt0", yatt_c[:], yatt_c.r, [64, 4, 512], BF16)
                self.dump("ymx0", ymx_c[:], ymx_c.r, [128, 4, 512], BF16)
            if self.stop(l, "b1") or self.stop(l, "b2"):
                return
            prev = (bi, ymx_c, yatt_c)
        run_gen(wout_gen(*prev))
        self.dump(f"x1_{l}", self.x1s.t, self.x1s.r, [T, D])

    def l_init(self, l, st, smp, cw, cb, HL, HLb):
        c3 = self.sb("c3", [128, 6, 3])
        pv = smp[:, 512:524].rearrange("p (c k) -> p c k", c=6)
        self.cp("dve", c3[:, :, 0], self.fix[:, 10:16], [self.fix.r], [c3.r])
        self.cp("dve", c3[:, :, 1], pv[:, :, 1], [smp.r], [c3.r])
        self.cp("dve", c3[:, :, 2], pv[:, :, 0], [smp.r], [c3.r])
        t6 = self.sb("t6", [128, 6, 3])
        cv = self.sb("cv6", [128, 6])
        self.tt("dve", t6[:], c3[:], cw[:], ALU.mult, [c3.r, cw.r], [t6.r])
        self.tt("dve", cv[:], t6[:, :, 0], t6[:, :, 1], ALU.add, [t6.r], [cv.r])
        self.tt("dve", cv[:], cv[:], t6[:, :, 2], ALU.add, [t6.r, cv.r], [cv.r])
        self.tt("dve", cv[:], cv[:], cb[:], ALU.add, [cv.r, cb.r], [cv.r])
        sv = self.sb("sv6", [128, 6])
        sg6 = self.sb("sg6", [128, 6])
        self.silu(sv[:], [sv.r], cv[:], [cv.r], (sg6[:], sg6.r))
        pb, pr = self.pbank()
        for c in range(4):
            self.tr(pb[0:1, c * 128:(c + 1) * 128], sv[:, c:c + 1], self.ident_f[:], [sv.r, self.cst.r], pr)
        pb2, pr2 = self.pbank()
        self.tr(pb2[0:1, 0:128], sv[:, 4:5], self.ident_f[:], [sv.r, self.cst.r], pr2)
        row = self.sb("row", [1, 640])
        self.cp("act", row[:, 0:512], pb[0:1, 0:512], pr, [row.r])
        self.cp("act", row[:, 512:640], pb2[0:1, 0:128], pr2, [row.r])
        dtr = self.sb("dtr", [1, 8])
        self.cp("dve", dtr[:], smp[0:1, 528:536], [smp.r], [dtr.r])
        xr = self.sb("xr", [1, 512])
        self.tt("dve", xr[:].rearrange("p (h e) -> p h e", h=8), row[:, 0:512].rearrange("p (h e) -> p h e", h=8),
                bc(dtr[:], [1, 8, 64]), ALU.mult, [row.r, dtr.r], [xr.r])
        pb3, pr3 = self.pbank()
        self.mm(pb3[:, :], row[:, 512:640], xr[:], True, True, [row.r, xr.r], pr3)
        self.tt("dve", HL[0:64, :], smp[0:64, 0:256], pb3[0:64, 0:256], ALU.add, [smp.r] + pr3, [HL.r])
        self.tt("dve", HL[64:128, :], smp[64:128, 0:256], pb3[64:128, 256:512], ALU.add, [smp.r] + pr3, [HL.r])
        self.hl_bd(HL, HLb)

    def hl_bd(self, HL, HLb):
        for g in range(2):
            self.cp("act", HLb[g * 64:(g + 1) * 64, g * 256:(g + 1) * 256], HL[g * 64:(g + 1) * 64, :], [HL.r], [HLb.r])

    def fix_last(self, l, smp, sccw, cw, cb):
        tcol = T - 1
        f2 = self.sb("f2", [128, 2])
        self.tt("dve", f2[:], smp[:, 524:526], sccw[:, :, 2], ALU.mult, [smp.r, sccw.r], [f2.r])
        self.tt("dve", f2[:], f2[:], self.fix[:, 6:8], ALU.add, [f2.r, self.fix.r], [f2.r])
        self.tt("dve", self.ysc[:, :, tcol], f2[:], self.fix[:, 8:10], ALU.mult, [f2.r, self.fix.r], [self.ysc.r])
        f6 = self.sb("f6", [128, 6])
        pv = smp[:, 512:524].rearrange("p (c k) -> p c k", c=6)
        self.tt("dve", f6[:], pv[:, :, 1], cw[:, :, 2], ALU.mult, [smp.r, cw.r], [f6.r])
        self.tt("dve", f6[:], f6[:], self.fix[:, 0:6], ALU.add, [f6.r, self.fix.r], [f6.r])
        self.tt("dve", f6[:], f6[:], cb[:], ALU.add, [f6.r, cb.r], [f6.r])
        s6 = self.sb("s6", [128, 6])
        sg6b = self.sb("sg6b", [128, 6])
        self.silu(s6[:], [s6.r], f6[:], [f6.r], (sg6b[:], sg6b.r))
        self.cp("dve", self.xsT_last[:, :, 127], s6[:, 0:4], [s6.r], [self.xsT_last.r])
        self.cp("dve", self.BT[:, tcol:tcol + 1], s6[:, 4:5], [s6.r], [self.BT.r])
        self.cp("dve", self.CT[:, tcol:tcol + 1], s6[:, 5:6], [s6.r], [self.CT.r])
        bl = self.sb("bl", [128, 128])
        self.cp("dve", bl[:], self.BT[:, T - 128:T], [self.BT.r], [bl.r])
        self.cp("dve", bl[:, 127:128], s6[:, 4:5], [s6.r], [bl.r])
        pb, pr = self.pbank()
        for c in range(4):
            self.tr(pb[:, c * 128:(c + 1) * 128], self.xsT_last[:, c, :], self.ident_f[:], [self.xsT_last.r, self.cst.r], pr)
        xl = self.sb("xl", [128, 512])
        self.cp("act", xl[:], pb[:, :], pr, [xl.r])
        self.dma("pool", self.sc_xs.t[T - 128:T, :], xl[:], [xl.r], [self.sc_xs.r])
        pb2, pr2 = self.pbank()
        self.tr(pb2[:, 0:128], bl[:], self.ident_f[:], [bl.r, self.cst.r], pr2)
        bl2 = self.sb("bl2", [128, 128], BF16)
        self.cp("act", bl2[:], pb2[:, 0:128], pr2, [bl2.r])
        self.dma("pool", self.sc_bt.t[T - 128:T, :], bl2[:], [bl2.r], [self.sc_bt.r])

    def phase_c(self, l, xnext, last):
        self.close_ab()
        self.new_scope()
        w1g = [self.sb(f"w1g{g}", [128, 8, 512], BF16) for g in range(8)]
        w2g = [self.sb(f"w2g{q}", [128, 8, D], BF16) for q in range(4)]
        w1v = self.w_ff1.t[l].rearrange("(k p) n -> p k n", p=128)
        w2v = self.w_ff2.t[l].rearrange("(c p) n -> p c n", p=128)
        for kind, g in (("1", 0), ("2", 0), ("1", 1), ("1", 2), ("2", 1), ("1", 3), ("1", 4), ("2", 2), ("1", 5), ("1", 6),
                        ("2", 3), ("1", 7)):
            if kind == "1":
                self.dma("pool", w1g[g][:], w1v[:, :, g * 512:(g + 1) * 512], [self.w_ff1.r], [w1g[g].r])
            else:
                self.dma("pool", w2g[g][:], w2v[:, g * 8:(g + 1) * 8, :], [self.w_ff2.r], [w2g[g].r])
        A2 = self.load_bv(3, "A2")
        B2 = self.load_bv(4, "B2")
        G2 = self.load_bv(5, "G2")
        A2c = B2c = G2c = None
        if not last:
            A2c = self.load_bv(9, "A2c")
            B2c = self.load_bv(10, "B2c")
            G2c = self.load_bv(11, "G2c")
        NBC = 256
        xts = self.rot("xt", 2, [128, D])
        xps = self.rot("xp", 1, [128, D])
        tmp = self.sb("tmp", [128, D])
        tmp2 = self.sb("tmp2", [128, D])
        hb = self.sb("hb", [128, D], BF16)
        hTs = [self.sb("hTa", [128, 8, NBC], BF16), self.sb("hTb", [128, 8, NBC], BF16)]
        a_r = self.rot("afo", 3, [128, NBC], BF16)
        rl_r = self.rot("rl", 2, [128, NBC])
        smalls = self.small3(2)
        smalls2 = self.small3(2)
        x2s = self.rot("x2", 2, [128, D])
        ysbs = self.rot("ysb", 1, [128, D])
        blocks = [(2 + 2 * i, 2) for i in range(8)] + ([] if last else [(0, 2)])

        def front(i):
            t0, nt = blocks[i]
            isctx = (t0 == 0)
            hT = hTs[i % 2]
            for tl in range(nt):
                t = t0 + tl
                xt = xts.next()
                self.dma("sp", xt[:], self.x1s.t[t * 128:(t + 1) * 128, :], [self.x1s.r], [xt.r])
                self.norm_tile(xt, A2c if isctx else A2, B2c if isctx else B2, hb, tmp, smalls.next())
                self.transpose_tile(hb, hT, tl)
                yield

        def ffn(i):
            t0, nt = blocks[i]
            isctx = (t0 == 0)
            N = nt * 128
            hT = hTs[i % 2]
            pws = [self.pbank(2, pin=True) for _ in range(nt)]

            def ffn1(fo):
                pb, pr = self.pbank(pin=True)
                for k in range(8):
                    wg = w1g[fo // 4]
                    self.mm(pb[:, 0:N], wg[:, k, (fo % 4) * 128:(fo % 4 + 1) * 128], hT[:, k, 0:N], k == 0, k == 7,
                            [wg.r, hT.r], pr)
                return pb, pr

            cur = ffn1(0)
            for fo in range(32):
                pb, pr = cur
                rl = rl_r.next()
                a = a_r.next()
                self.act(rl[:, 0:N], pb[:, 0:N], AF.Relu, pr, [rl.r])
                self.unpin(pr)
                if fo + 1 < 32:
                    cur = ffn1(fo + 1)
                self.tt("dve", a[:, 0:N], rl[:, 0:N], rl[:, 0:N], ALU.mult, [rl.r], [a.r])
                for tl in range(nt):
                    pw, prw = pws[tl]
                    for hf in range(2):
                        ns = slice(hf * 512, (hf + 1) * 512)
                        self.mm(pw[:, ns], a[:, tl * 128:(tl + 1) * 128], w2g[fo // 8][:, fo % 8, ns], fo == 0, fo == 31,
                                [a.r, w2g[fo // 8].r], prw)
                yield
            for tl in range(nt):
                t = t0 + tl
                cs = slice(t * 128, (t + 1) * 128)
                pw, prw = pws[tl]
                ysb = ysbs.next()
                self.cp("act", ysb[:], pw[:, :], prw, [ysb.r])
                self.unpin(prw)
                xp = xps.next()
                self.dma("sp", xp[:], self.x1s.t[cs, :], [self.x1s.r], [xp.r])
                x2 = x2s.next()
                self.post_tile(ysb, [ysb.r], xp, G2c if isctx else G2, tmp2, smalls2.next(), x2)
                if last:
                    self.dma("pool", self.y_out.t[(t - 2) * 128:(t - 1) * 128, :], x2[:], [x2.r], [self.y_out.r])
                else:
                    self.dma("pool", xnext.t[cs, :], x2[:], [x2.r], [xnext.r])
                yield

        run_gen(front(0))
        for i in range(len(blocks)):
            f = front(i + 1) if i + 1 < len(blocks) else None
            interleave((f, 2, 0.7), (ffn(i), 34))
        if not last:
            self.dump(f"xres1_{l}", xnext.t, xnext.r, [T, D])


def run_gen(g):
    if g is not None:
        for _ in g:
            pass


def interleave(*pairs):
    live = [[p[0], (p[2] if len(p) > 2 else 0.0), max(1, p[1])] for p in pairs if p[0] is not None]
    while live:
        live.sort(key=lambda x: x[1] / x[2])
        it = live[0]
        try:
            next(it[0])
            it[1] += 1
        except StopIteration:
            live.remove(it)


class Rot:
    def __init__(self, items):
        self.items = items
        self.i = -1

    def next(self):
        self.i = (self.i + 1) % len(self.items)
        return self.items[self.i]


_PERM = np.r_[8:16, 0:8, 24:32, 16:24]


def _rope_tables(pos):
    half = 16
    inv = (10000.0 ** (-(np.arange(0, half, 2, dtype=np.float32)) / half)).astype(np.float32)
    row = (pos // 64).astype(np.float32)
    col = (pos % 64).astype(np.float32)
    ar = row[:, None] * inv
    ac = col[:, None] * inv
    ang = np.concatenate([ar, ar, ac, ac], -1).astype(np.float32)
    sign = np.concatenate([-np.ones(8), np.ones(8), -np.ones(8), np.ones(8)]).astype(np.float32)
    return np.cos(ang).astype(np.float32), (np.sin(ang) * sign).astype(np.float32)


def host_layout(inp):
    f = lambda a: np.ascontiguousarray(np.asarray(a, dtype=np.float32))
    x, c, ctx, c_ctx = f(inp["x"]), f(inp["c"]), f(inp["ctx"]), f(inp["c_ctx"])
    L = DEPTH
    w_in = f(inp["w_in"])
    s0 = 416 + 768
    shared = {}
    per_rev = {}
    for rev in (0, 1):
        wa = []
        for l in range(L):
            w = w_in[l]
            dtc = w[:, s0 + 512 + 768:s0 + 512 + 768 + 16]
            if rev:
                dtc = np.concatenate([dtc[:, 8:16], dtc[:, 0:8]], 1)
            wa.append(np.concatenate([w[:, 0:256], w[:, 256:384], w[:, 384:416], w[:, 384 + _PERM],
                                      w[:, 416:416 + 768], w[:, s0 + 512:s0 + 512 + 768], w[:, s0:s0 + 512], dtc], 1))
        d = {"w_in_aug": f(np.stack(wa))}
        al = f(inp["ssd_a_log"]); db = f(inp["ssd_dt_bias"])
        order = [1, 0] if rev else [0, 1]
        d["a_log"] = f(np.concatenate([al[:, order[0]], al[:, order[1]]], -1))
        d["dt_bias"] = f(np.concatenate([db[:, order[0]], db[:, order[1]]], -1))
        scw = f(inp["sc_conv_w"]); sw = f(inp["ssd_conv_w"])
        if rev:
            scw = scw[:, ::-1]; sw = sw[:, ::-1]
        d["sc_cw"] = f(np.transpose(scw, (0, 2, 1)))
        d["ssd_cw"] = f(np.transpose(sw, (0, 2, 1)))
        per_rev[rev] = d
    wuq = f(inp["w_uq"])
    wq = np.zeros((L, 256, 4, 160), np.float32)
    for h in range(4):
        wq[:, :, h, 0:32] = wuq[:, :, h * 96 + 64:h * 96 + 96]
        wq[:, :, h, 64:128] = wuq[:, :, h * 96:h * 96 + 64]
        wq[:, :, h, 128:160] = wuq[:, :, h * 96 + 64 + _PERM]
    shared["w_uq_aug"] = f(wq.reshape(L, 256, 640))
    wukv = f(inp["w_ukv"])
    wk = np.zeros((L, 128, 4, 128), np.float32)
    wv = np.zeros((L, 128, 4, 64), np.float32)
    for h in range(4):
        wk[:, :, h, 64:128] = wukv[:, :, h * 128:h * 128 + 64]
        wv[:, :, h, :] = wukv[:, :, h * 128 + 64:h * 128 + 128]
    shared["w_ukv_k"] = f(wk.reshape(L, 128, 512))
    shared["w_ukv_v"] = f(wv.reshape(L, 128, 256))
    shared["q_norm"] = f(inp["mla_q_norm"])
    shared["kv_norm"] = f(inp["mla_kv_norm"])
    shared["w_mod"] = f(inp["w_mod"])
    shared["b_mod"] = f(inp["b_mod"])
    shared["gvecs"] = f(np.stack([inp["g_pre_mix"], inp["g_post_mix"], inp["g_pre_ffn"], inp["g_post_ffn"]], 1))
    shared["ssd_cb"] = f(inp["ssd_conv_b"])
    shared["ssd_d"] = f(inp["ssd_d"])
    shared["ssd_norm"] = f(inp["ssd_norm"])
    shared["w_out"] = f(inp["w_out"])
    shared["w_ff1"] = f(inp["w_ff1"])
    shared["w_ff2"] = f(inp["w_ff2"])
    j = np.arange(128)
    triE = (j[:, None] <= j[None, :]).astype(np.float32)
    triL = (j[:, None] >= j[None, :]).astype(np.float32)
    in_maps = []
    for core in range(NCORES):
        b, hf = core // 2, core % 2
        xl = x[b, hf * TL:(hf + 1) * TL]
        cl = ctx[b]
        pos = np.arange(hf * TL, (hf + 1) * TL)
        if hf:
            xl, cl, pos = xl[::-1], cl[::-1], pos[::-1]
        cos, sin = _rope_tables(pos)
        rope = np.zeros((2, 32, T), np.float32)
        rope[0, :, :TC] = 1.0
        rope[0, :, TC:] = cos.T
        rope[1, :, TC:] = sin.T
        cst = np.zeros((128, 3 * 128 + 4), np.float32)
        cst[:, 0:128] = np.eye(128)
        cst[:, 128:256] = triE
        cst[:, 256:384] = triL
        cst[:, 384] = 1.0
        cst[127, 384] = 0.0
        cst[:, 385] = 1.0 - hf
        cst[:, 386] = float(hf)
        m = {"x_in": f(np.concatenate([cl, xl], 0)), "cvecT": f(np.stack([c[b], c_ctx], 1)), "ropeT": rope, "consts": cst}
        m.update(shared)
        m.update(per_rev[hf])
        in_maps.append(m)
    return in_maps


_NC_CACHE = {}


def kernel(**inputs):
    in_maps = host_layout(inputs)
    if "nc" not in _NC_CACHE:
        _NC_CACHE["nc"] = Builder().build()
    res = run_bass_kernel_spmd(_NC_CACHE["nc"], in_maps, core_ids=list(range(NCORES)))
    out = np.zeros((4, 2 * TL, D), np.float32)
    for core in range(NCORES):
        b, hf = core // 2, core % 2
        y = np.asarray(res.results[core]["y_out"], dtype=np.float32)
        out[b, hf * TL:(hf + 1) * TL] = y[::-1] if hf else y
    return out
```
